# Optimizing a Trainium2 kernel written in Bass

```python
import math
import jax
import jax.numpy as jnp
from jax import lax
import numpy as np

D_MODEL = 2048
BATCH = 8
SEQ = 2048
DEPTH = 4
DEC_BATCH = 4
DEC_SEQ = 4096
PAST_LEN = 128

N_MIXERS = 4
Q_BLOCK = 128
NORM_EPS = 1e-6
F32 = jnp.float32

MLA_HEADS = 16
MLA_Q_RANK = 512
MLA_KV_RANK = 512
MLA_NOPE = 128
MLA_ROPE = 64
MLA_V = 128
MLA_THETA = 10000.0
MLA_IN = MLA_Q_RANK + MLA_KV_RANK + MLA_ROPE + MLA_HEADS * MLA_V

DIFF_HEADS = 16
DIFF_HD = D_MODEL // DIFF_HEADS // 2
DIFF_ROT = DIFF_HD // 4
ROPE_THETA = 500000.0
DIFF_QK = 2 * DIFF_HEADS * DIFF_HD
DIFF_VW = DIFF_HEADS * 2 * DIFF_HD

LRU_WIDTH = D_MODEL
LRU_BLOCKS = 16
LRU_BW = LRU_WIDTH // LRU_BLOCKS
LRU_CONV = 4
LRU_C = 8.0

RWKV_N = 64
RWKV_HEADS = D_MODEL // RWKV_N
RWKV_DECAY_LORA = 96
RWKV_A_LORA = 96
RWKV_GN_EPS = 64e-5

N_A = (DEPTH + 3) // 4
N_B = (DEPTH + 2) // 4
N_C = (DEPTH + 1) // 4
N_D = DEPTH // 4

kernel_name = 'hybrid_bidir_mla_diff_rglru_rwkv7_encoder'


def rms_norm(x, g, eps=NORM_EPS):
    xf = x.astype(F32)
    y = xf * lax.rsqrt(jnp.mean(xf * xf, axis=-1, keepdims=True) + eps)
    return (y * g.astype(F32)).astype(x.dtype)


def rope_tables(seq, dim, theta):
    inv = 1.0 / (theta ** (jnp.arange(0, dim, 2, dtype=F32) / dim))
    ang = jnp.arange(seq, dtype=F32)[:, None] * inv[None, :]
    ang = jnp.concatenate([ang, ang], axis=-1)
    return jnp.cos(ang), jnp.sin(ang)


def apply_rope(x, cos, sin):
    xf = x.astype(F32)
    half = x.shape[-1] // 2
    rot = jnp.concatenate([-xf[..., half:], xf[..., :half]], axis=-1)
    return (xf * cos[None, :, None, :] + rot * sin[None, :, None, :]).astype(x.dtype)


def blocked_attention(q, k, v, scale, mix_probs):
    b, s, hq, dq = q.shape
    nb = s // Q_BLOCK
    qb = jnp.moveaxis(q.reshape(b, nb, Q_BLOCK, hq, dq), 1, 0)
    kf = k.astype(F32)

    def one_block(qblk):
        sc = jnp.einsum('bqhd,bkhd->bhqk', qblk.astype(F32), kf) * scale
        p = mix_probs(jax.nn.softmax(sc, axis=-1))
        return jnp.einsum('bhqk,bkhd->bqhd', p.astype(v.dtype), v)

    out = jnp.moveaxis(lax.map(one_block, qb), 0, 1)
    return out.reshape(b, s, out.shape[3], out.shape[4])


def mla_mixer(h, w_in, q_norm_g, kv_norm_g, w_q_up, w_kv_up, w_out):
    b, s, _ = h.shape
    z = h @ w_in
    q_lat, kv_lat, k_pe, gate = jnp.split(
        z, [MLA_Q_RANK, MLA_Q_RANK + MLA_KV_RANK, MLA_Q_RANK + MLA_KV_RANK + MLA_ROPE], axis=-1)
    q = (rms_norm(q_lat, q_norm_g) @ w_q_up).reshape(b, s, MLA_HEADS, MLA_NOPE + MLA_ROPE)
    kv = (rms_norm(kv_lat, kv_norm_g) @ w_kv_up).reshape(b, s, MLA_HEADS, MLA_NOPE + MLA_V)
    k_nope, v = jnp.split(kv, [MLA_NOPE], axis=-1)
    cos, sin = rope_tables(s, MLA_ROPE, MLA_THETA)
    q_pe = apply_rope(q[..., MLA_NOPE:], cos, sin)
    k_pe = apply_rope(k_pe[:, :, None, :], cos, sin)
    q = jnp.concatenate([q[..., :MLA_NOPE], q_pe], axis=-1)
    k = jnp.concatenate([k_nope, jnp.broadcast_to(k_pe, (b, s, MLA_HEADS, MLA_ROPE))], axis=-1)
    o = blocked_attention(q, k, v, (MLA_NOPE + MLA_ROPE) ** -0.5, lambda p: p)
    o = o.reshape(b, s, MLA_HEADS * MLA_V) * jax.nn.silu(gate)
    return o @ w_out


def diff_mixer(h, layer_idx, w_in, lam, subln_g, w_out):
    b, s, _ = h.shape
    q, k, v, gate = jnp.split(h @ w_in, [DIFF_QK, 2 * DIFF_QK, 2 * DIFF_QK + DIFF_VW], axis=-1)
    q = q.reshape(b, s, 2 * DIFF_HEADS, DIFF_HD)
    k = k.reshape(b, s, 2 * DIFF_HEADS, DIFF_HD)
    v = v.reshape(b, s, DIFF_HEADS, 2 * DIFF_HD)
    cos, sin = rope_tables(s, DIFF_ROT, ROPE_THETA)
    q = jnp.concatenate([apply_rope(q[..., :DIFF_ROT], cos, sin), q[..., DIFF_ROT:]], axis=-1)
    k = jnp.concatenate([apply_rope(k[..., :DIFF_ROT], cos, sin), k[..., DIFF_ROT:]], axis=-1)
    lam_init = 0.8 - 0.6 * math.exp(-0.3 * layer_idx)
    lf = lam.astype(F32)
    lam_full = jnp.exp(jnp.sum(lf[0] * lf[1])) - jnp.exp(jnp.sum(lf[2] * lf[3])) + lam_init

    def diff_probs(p):
        p = p.reshape(p.shape[0], DIFF_HEADS, 2, p.shape[2], p.shape[3])
        return p[:, :, 0] - lam_full * p[:, :, 1]

    o = blocked_attention(q, k, v, DIFF_HD ** -0.5, diff_probs)
    o = rms_norm(o, subln_g, eps=1e-5) * (1.0 - lam_init)
    o = o.reshape(b, s, DIFF_VW) * jax.nn.silu(gate)
    return o @ w_out


def rglru_scan(xc, gate_w, gate_b, lam, reverse):
    b, s, w = xc.shape
    xr = xc.reshape(b, s, LRU_BLOCKS, LRU_BW)
    gates = jnp.einsum('bsnh,gnhk->gbsnk', xr, gate_w.astype(F32)).reshape(2, b, s, w)
    gates = gates + gate_b.astype(F32)[:, None, None, :]
    r_t = jax.nn.sigmoid(gates[0])
    i_t = jax.nn.sigmoid(gates[1])
    log_a = -LRU_C * r_t * jax.nn.softplus(-lam.astype(F32))
    a_t = jnp.exp(log_a)
    mult = jnp.sqrt(-jnp.expm1(2.0 * log_a))
    first = s - 1 if reverse else 0
    mult = jnp.where((jnp.arange(s) == first)[None, :, None], 1.0, mult)
    u = mult * i_t * xc

    def combine(e1, e2):
        a1, b1 = e1
        a2, b2 = e2
        return a1 * a2, a2 * b1 + b2

    _, hs = lax.associative_scan(combine, (a_t, u), reverse=reverse, axis=1)
    return hs


def lru_mixer(h, w_in, conv_w, conv_b, gate_w, gate_b, lam, w_out):
    xb, gate = jnp.split(h @ w_in, [LRU_WIDTH], axis=-1)
    lo = (LRU_CONV - 1) // 2
    hi = LRU_CONV - 1 - lo
    xc = lax.conv_general_dilated(xb, conv_w[:, None, :], window_strides=(1,), padding=[(lo, hi)],
                                  dimension_numbers=('NWC', 'WIO', 'NWC'),
                                  feature_group_count=LRU_WIDTH) + conv_b
    xf = xc.astype(F32)
    y = (rglru_scan(xf, gate_w[0], gate_b[0], lam[0], False)
         + rglru_scan(xf, gate_w[1], gate_b[1], lam[1], True))
    return (y.astype(h.dtype) * jax.nn.silu(gate)) @ w_out


def rwkv_scan(decay, kneg, kb, v, k, r, reverse):
    b, s, hh, n = v.shape

    def step(state, inp):
        w_t, ka_t, kb_t, v_t, k_t, r_t = inp
        sa = jnp.einsum('bhij,bhj->bhi', state, ka_t)
        state = (state * w_t[:, :, None, :] + sa[..., None] * kb_t[:, :, None, :]
                 + v_t[..., None] * k_t[:, :, None, :])
        return state, jnp.einsum('bhij,bhj->bhi', state, r_t)

    xs = tuple(jnp.moveaxis(t, 1, 0) for t in (decay, kneg, kb, v, k, r))
    _, out = lax.scan(step, jnp.zeros((b, hh, n, n), F32), xs, reverse=reverse)
    return jnp.moveaxis(out, 0, 1)


def rwkv_mixer(h, mu, w_in, w0, w1, w2, a0, a1, a2, k_k, k_a, r_k, ln_g, ln_b, w_out):
    b, s, d = h.shape

    def heads(t):
        return t.reshape(t.shape[0], t.shape[1], RWKV_HEADS, RWKV_N)

    zero = jnp.zeros_like(h[:, :1])
    x_prev = jnp.concatenate([zero, h[:, :-1]], axis=1)
    x_next = jnp.concatenate([h[:, 1:], zero], axis=1)
    xx = 0.5 * (x_prev + x_next) - h
    xs = h[:, :, None, :] + xx[:, :, None, :] * mu
    rkvg = jnp.einsum('bsmd,mde->bsme', xs[:, :, :4], w_in)
    r = heads(rkvg[:, :, 0].astype(F32))
    k = heads(rkvg[:, :, 1].astype(F32))
    v = heads(rkvg[:, :, 2].astype(F32))
    g = rkvg[:, :, 3]
    xw = xs[:, :, 4]
    xa = xs[:, :, 5]
    kk = k * k_k.astype(F32).reshape(RWKV_HEADS, RWKV_N)
    kk = kk / jnp.maximum(jnp.sqrt(jnp.sum(kk * kk, axis=-1, keepdims=True)), 1e-12)
    k_af = k_a.astype(F32).reshape(RWKV_HEADS, RWKV_N)
    r_kf = r_k.astype(F32)

    def direction(dr):
        wl = -jax.nn.softplus(-(w0[dr] + jnp.tanh(xw @ w1[dr]) @ w2[dr]).astype(F32)) - 0.5
        decay = heads(jnp.exp(-jnp.exp(wl)))
        a = heads(jax.nn.sigmoid((a0[dr] + (xa @ a1[dr]) @ a2[dr]).astype(F32)))
        kd = k * (1.0 + (a - 1.0) * k_af)
        o = rwkv_scan(decay, -kk, kk * a, v, kd, r, reverse=(dr == 1))
        bonus = jnp.sum(r * kd * r_kf, axis=-1, keepdims=True) * v
        return o, bonus

    o_f, bonus_f = direction(0)
    o_b, bonus_b = direction(1)
    o = o_f + o_b
    mean = jnp.mean(o, axis=-1, keepdims=True)
    var = jnp.mean(jnp.square(o - mean), axis=-1, keepdims=True)
    o = (o - mean) * lax.rsqrt(var + RWKV_GN_EPS)
    o = o.reshape(b, s, d) * ln_g.astype(F32) + ln_b.astype(F32) + (bonus_f + bonus_b).reshape(b, s, d)
    o = o.astype(h.dtype) * jax.nn.silu(g)
    return o @ w_out


def run_trunk(x, c, p):
    cs = jax.nn.silu(c)
    for i in range(DEPTH):
        mod = cs @ p['ada_w'][i] + p['ada_b'][i]
        shift, scale, gate = jnp.split(mod[:, None, :], 3, axis=-1)
        hdn = rms_norm(x, p['norm_pre_g'][i]) * (1.0 + scale) + shift
        m, j = i % N_MIXERS, i // N_MIXERS
        if m == 0:
            y = mla_mixer(hdn, p['mla_w_in'][j], p['mla_q_norm_g'][j], p['mla_kv_norm_g'][j],
                          p['mla_w_q_up'][j], p['mla_w_kv_up'][j], p['mla_w_out'][j])
        elif m == 1:
            y = diff_mixer(hdn, i, p['diff_w_in'][j], p['diff_lambda'][j], p['diff_subln_g'][j],
                           p['diff_w_out'][j])
        elif m == 2:
            y = lru_mixer(hdn, p['lru_w_in'][j], p['lru_conv_w'][j], p['lru_conv_b'][j],
                          p['lru_gate_w'][j], p['lru_gate_b'][j], p['lru_lambda'][j], p['lru_w_out'][j])
        else:
            y = rwkv_mixer(hdn, p['rwkv_mu'][j], p['rwkv_w_in'][j], p['rwkv_w0'][j], p['rwkv_w1'][j],
                           p['rwkv_w2'][j], p['rwkv_a0'][j], p['rwkv_a1'][j], p['rwkv_a2'][j],
                           p['rwkv_k_k'][j], p['rwkv_k_a'][j], p['rwkv_r_k'][j], p['rwkv_ln_g'][j],
                           p['rwkv_ln_b'][j], p['rwkv_w_out'][j])
        x = x + gate * rms_norm(y, p['norm_post_g'][i])
    return x


def setup_inputs(seed: int = 0) -> dict:
    key = jax.random.key(seed)
    ks = iter(jax.random.split(key, 64))
    D = D_MODEL

    def nrm(shape, scale):
        return scale * jax.random.normal(next(ks), shape, F32)

    def gain(shape):
        return 1.0 + nrm(shape, 0.02)

    u = jax.random.uniform(next(ks), (N_C, 2, LRU_WIDTH), F32, 0.9, 0.999)
    a_base = u ** (1.0 / LRU_C)
    lru_lambda = jnp.log(a_base) - jnp.log1p(-a_base)
    return {
        'x_prompt': nrm((BATCH, SEQ, D), 1.0),
        'x_sample': nrm((DEC_BATCH, DEC_SEQ, D), 1.0),
        'c_prompt': nrm((BATCH, D), 1.0),
        'c_sample': nrm((DEC_BATCH, D), 1.0),
        'ada_w': nrm((DEPTH, D, 3 * D), D ** -0.5),
        'ada_b': nrm((DEPTH, 3 * D), 0.02),
        'norm_pre_g': gain((DEPTH, D)),
        'norm_post_g': gain((DEPTH, D)),
        'mla_w_in': nrm((N_A, D, MLA_IN), D ** -0.5),
        'mla_q_norm_g': gain((N_A, MLA_Q_RANK)),
        'mla_kv_norm_g': gain((N_A, MLA_KV_RANK)),
        'mla_w_q_up': nrm((N_A, MLA_Q_RANK, MLA_HEADS * (MLA_NOPE + MLA_ROPE)), MLA_Q_RANK ** -0.5),
        'mla_w_kv_up': nrm((N_A, MLA_KV_RANK, MLA_HEADS * (MLA_NOPE + MLA_V)), MLA_KV_RANK ** -0.5),
        'mla_w_out': nrm((N_A, MLA_HEADS * MLA_V, D), (MLA_HEADS * MLA_V) ** -0.5),
        'diff_w_in': nrm((N_B, D, 2 * DIFF_QK + 2 * DIFF_VW), D ** -0.5),
        'diff_lambda': nrm((N_B, 4, DIFF_HD), 0.1),
        'diff_subln_g': gain((N_B, 2 * DIFF_HD)),
        'diff_w_out': nrm((N_B, DIFF_VW, D), DIFF_VW ** -0.5),
        'lru_w_in': nrm((N_C, D, 2 * LRU_WIDTH), D ** -0.5),
        'lru_conv_w': nrm((N_C, LRU_CONV, LRU_WIDTH), LRU_CONV ** -0.5),
        'lru_conv_b': nrm((N_C, LRU_WIDTH), 0.02),
        'lru_gate_w': nrm((N_C, 2, 2, LRU_BLOCKS, LRU_BW, LRU_BW), LRU_BW ** -0.5),
        'lru_gate_b': nrm((N_C, 2, 2, LRU_WIDTH), 0.02),
        'lru_lambda': lru_lambda,
        'lru_w_out': nrm((N_C, LRU_WIDTH, D), LRU_WIDTH ** -0.5),
        'rwkv_mu': jax.random.uniform(next(ks), (N_D, 6, D), F32),
        'rwkv_w_in': nrm((N_D, 4, D, D), D ** -0.5),
        'rwkv_w0': jax.random.uniform(next(ks), (N_D, 2, D), F32, -6.0, -1.0),
        'rwkv_w1': nrm((N_D, 2, D, RWKV_DECAY_LORA), D ** -0.5),
        'rwkv_w2': nrm((N_D, 2, RWKV_DECAY_LORA, D), 0.1 * RWKV_DECAY_LORA ** -0.5),
        'rwkv_a0': nrm((N_D, 2, D), 0.1),
        'rwkv_a1': nrm((N_D, 2, D, RWKV_A_LORA), D ** -0.5),
        'rwkv_a2': nrm((N_D, 2, RWKV_A_LORA, D), 0.1 * RWKV_A_LORA ** -0.5),
        'rwkv_k_k': 0.85 + nrm((N_D, D), 0.02),
        'rwkv_k_a': gain((N_D, D)),
        'rwkv_r_k': -0.04 + nrm((N_D, RWKV_HEADS, RWKV_N), 0.05),
        'rwkv_ln_g': gain((N_D, D)),
        'rwkv_ln_b': nrm((N_D, D), 0.02),
        'rwkv_w_out': nrm((N_D, D, D), D ** -0.5),
    }


def reference(x_prompt, x_sample, c_prompt, c_sample, ada_w, ada_b, norm_pre_g, norm_post_g,
              mla_w_in, mla_q_norm_g, mla_kv_norm_g, mla_w_q_up, mla_w_kv_up, mla_w_out,
              diff_w_in, diff_lambda, diff_subln_g, diff_w_out,
              lru_w_in, lru_conv_w, lru_conv_b, lru_gate_w, lru_gate_b, lru_lambda, lru_w_out,
              rwkv_mu, rwkv_w_in, rwkv_w0, rwkv_w1, rwkv_w2, rwkv_a0, rwkv_a1, rwkv_a2,
              rwkv_k_k, rwkv_k_a, rwkv_r_k, rwkv_ln_g, rwkv_ln_b, rwkv_w_out):
    params = {
        'ada_w': ada_w, 'ada_b': ada_b, 'norm_pre_g': norm_pre_g, 'norm_post_g': norm_post_g,
        'mla_w_in': mla_w_in, 'mla_q_norm_g': mla_q_norm_g, 'mla_kv_norm_g': mla_kv_norm_g,
        'mla_w_q_up': mla_w_q_up, 'mla_w_kv_up': mla_w_kv_up, 'mla_w_out': mla_w_out,
        'diff_w_in': diff_w_in, 'diff_lambda': diff_lambda, 'diff_subln_g': diff_subln_g,
        'diff_w_out': diff_w_out,
        'lru_w_in': lru_w_in, 'lru_conv_w': lru_conv_w, 'lru_conv_b': lru_conv_b,
        'lru_gate_w': lru_gate_w, 'lru_gate_b': lru_gate_b, 'lru_lambda': lru_lambda,
        'lru_w_out': lru_w_out,
        'rwkv_mu': rwkv_mu, 'rwkv_w_in': rwkv_w_in, 'rwkv_w0': rwkv_w0, 'rwkv_w1': rwkv_w1,
        'rwkv_w2': rwkv_w2, 'rwkv_a0': rwkv_a0, 'rwkv_a1': rwkv_a1, 'rwkv_a2': rwkv_a2,
        'rwkv_k_k': rwkv_k_k, 'rwkv_k_a': rwkv_k_a, 'rwkv_r_k': rwkv_r_k, 'rwkv_ln_g': rwkv_ln_g,
        'rwkv_ln_b': rwkv_ln_b, 'rwkv_w_out': rwkv_w_out,
    }
    y_prompt = run_trunk(x_prompt, c_prompt, params)
    y_sample = run_trunk(x_sample, c_sample, params)
    return (y_prompt, y_sample)
```

```python
import contextlib
import math
import numpy as np
import ml_dtypes
import concourse.bass as bass
import concourse.mybir as mybir
from concourse.bass_utils import run_bass_kernel_spmd

F32 = mybir.dt.float32
BF16 = mybir.dt.bfloat16
AF = mybir.ActivationFunctionType
ALU = mybir.AluOpType
AX = mybir.AxisListType

EPOCH = 60000
DEPOCH = 3500
COMPUTE = ('pe', 'act', 'dve', 'pool')
QUEUES = COMPUTE + ('sp',)

T = 4096
D = 2048
KC = 16
NT = 32
HALF = 2048
BIG = 30000.0


class DSem:
    def __init__(self, S, name):
        self.S = S
        self.name = name
        self.handles = []
        self.counts = []
        self._new()

    def _new(self):
        self.handles.append(self.S._sem(f"d_{self.name}_{len(self.handles)}"))
        self.counts.append(0)


class Sched:
    def __init__(self, nc):
        self.nc = nc
        self.stack = contextlib.ExitStack()
        self.ops = {e: [] for e in QUEUES}
        self.csem = {e: [] for e in COMPUTE}
        self.ccnt = {e: [] for e in COMPUTE}
        self.last_w = {}
        self.readers = {}
        self.dsems = []
        self.free_ds = {}
        self.tile_ds = {}
        self.nsem = 0
        self.uid = 0
        self.waited = {e: {} for e in QUEUES}
        self.psum = {}
        self.last_acc = {}
        for e in COMPUTE:
            self._new_epoch(e)

    def _sem(self, name):
        self.nsem += 1
        return self.stack.enter_context(self.nc.semaphore(name))

    def _new_epoch(self, e):
        self.csem[e].append(self._sem(f"c_{e}_{len(self.csem[e])}"))
        self.ccnt[e].append(0)

    def get_ds(self, q):
        fl = self.free_ds.setdefault(q, [])
        if fl:
            return fl.pop()
        d = DSem(self, f"{q}{len(self.dsems)}")
        d.q = q
        self.dsems.append(d)
        return d

    def ds_of(self, tile, q):
        k = (id(tile), q)
        if k not in self.tile_ds:
            self.tile_ds[k] = self.get_ds(q)
        return self.tile_ds[k]

    def sb(self, sc, shape, dtype, name=None):
        self.uid += 1
        return sc.enter_context(self.nc.sbuf_tensor(f"{name or 't'}_{self.uid}", list(shape), dtype))

    def ps(self, sc, shape, dtype, name=None):
        self.uid += 1
        t = sc.enter_context(self.nc.psum_tensor(f"{name or 'p'}_{self.uid}", list(shape), dtype))
        nbytes = int(np.prod(shape[1:])) * (2 if dtype == BF16 else 4)
        assert nbytes % 2048 == 0, "PSUM tiles must be whole banks"
        self.psum[id(t)] = nbytes // 2048
        return t

    def _split(self, keys):
        norm, banks = [], []
        for k in keys:
            base = k[0] if isinstance(k, tuple) else k
            nb = self.psum.get(id(base)) if not isinstance(base, (str, int)) else None
            if nb is None:
                norm.append(self._key(k))
            elif nb == 1:
                banks.append(('B', id(base)))
            else:
                banks.append(('B', id(base), k[1]))
        return norm, banks

    @staticmethod
    def _key(k):
        if isinstance(k, tuple):
            return tuple(Sched._key(x) for x in k)
        if isinstance(k, (str, int)):
            return k
        return id(k)

    def _waits(self, eng, toks):
        w = {}
        for t in toks:
            if t[0] == 'c':
                _, e, ep, idx = t
                if e == eng and e == 'pe':
                    continue
                h = self.csem[e][ep]
                v = idx
            else:
                _, d, ep, cnt = t
                h = d.handles[ep]
                v = d.counts[ep]
            k = id(h)
            if k not in w or w[k][1] < v:
                w[k] = (h, v)
        return list(w.values())

    def _deps(self, reads, writes):
        toks = []
        for k in reads + writes:
            t = self.last_w.get(k)
            if t is not None:
                toks.append(t)
        for k in writes:
            r = self.readers.get(k)
            if r:
                toks.extend(r.values())
        return toks

    def _update(self, tok, reads, writes):
        for k in writes:
            self.last_w[k] = tok
            self.readers[k] = {}
        for k in reads:
            r = self.readers.setdefault(k, {})
            if tok[0] == 'c':
                r[(tok[1], tok[2])] = tok
            else:
                r[(id(tok[1]), tok[2])] = tok

    def op(self, eng, fn, reads=(), writes=()):
        reads, b1 = self._split(reads)
        writes, b2 = self._split(writes)
        toks = self._deps(reads, writes)
        for bk in b1 + b2:
            for e2, t in self.last_acc.get(bk, {}).items():
                if e2 != eng:
                    toks.append(t)
        waits = self._waits(eng, toks)
        if self.ccnt[eng][-1] >= EPOCH:
            self._new_epoch(eng)
        ep = len(self.ccnt[eng]) - 1
        self.ccnt[eng][ep] += 1
        tok = ('c', eng, ep, self.ccnt[eng][ep])
        self.ops[eng].append((waits, fn, (self.csem[eng][ep], 1)))
        self._update(tok, reads, writes)
        for bk in b1 + b2:
            self.last_acc.setdefault(bk, {})[eng] = tok
        return tok

    def dma(self, q, out, in_, tile, reads=(), writes=()):
        ds = self.ds_of(tile, q)
        reads = [self._key(k) for k in reads]
        writes = [self._key(k) for k in writes]
        waits = self._waits(q, self._deps(reads, writes))
        if ds.counts[-1] >= DEPOCH * 16:
            ds._new()
        ep = len(ds.counts) - 1
        ds.counts[ep] += 16
        tok = ('d', ds, ep, ds.counts[ep])
        self.ops[q].append((waits, (lambda e, o=out, i=in_: e.dma_start(out=o, in_=i)), (ds.handles[ep], 16)))
        self._update(tok, reads, writes)
        return tok

    def barrier(self):
        toks = []
        for e in COMPUTE:
            for ep in range(len(self.ccnt[e])):
                if self.ccnt[e][ep] > 0:
                    toks.append(('c', e, ep, self.ccnt[e][ep]))
        for d in self.dsems:
            for ep in range(len(d.counts)):
                if d.counts[ep] > 0:
                    toks.append(('d', d, ep, d.counts[ep]))
        for e in QUEUES:
            tk = [t for t in toks if not (t[0] == 'c' and t[1] == e)]
            self.ops[e].append((self._waits(e, tk), None, None))

    def release(self, tiles):
        for t in tiles:
            for q in QUEUES:
                k = (id(t), q)
                if k in self.tile_ds:
                    self.free_ds.setdefault(q, []).append(self.tile_ds.pop(k))

    def emit(self):
        self.barrier()
        nc = self.nc
        ops = self.ops
        self.ops = {e: [] for e in QUEUES}
        with nc.Block() as block:
            def run(e, name):
                waited = self.waited[name]
                for waits, fn, inc in ops[name]:
                    for h, v in waits:
                        k = id(h)
                        if waited.get(k, 0) < v:
                            e.wait_ge(h, v)
                            waited[k] = v
                    if fn is not None:
                        fn(e).then_inc(inc[0], inc[1])

            @block.tensor
            def _(e):
                run(e, 'pe')

            @block.scalar
            def _(e):
                run(e, 'act')

            @block.vector
            def _(e):
                run(e, 'dve')

            @block.gpsimd
            def _(e):
                run(e, 'pool')

            @block.sync
            def _(e):
                run(e, 'sp')


class Ring:
    def __init__(self, S, sc, shape, dtype, n, psum=False, name=None):
        self.tiles = [(S.ps if psum else S.sb)(sc, shape, dtype, name) for _ in range(n)]
        self.i = 0

    def next(self):
        t = self.tiles[self.i % len(self.tiles)]
        self.i += 1
        return t


class Stage:
    def __init__(self, S):
        self.S = S
        self.sc = contextlib.ExitStack()
        self.tiles = []

    def __enter__(self):
        self.sc.__enter__()
        return self

    def sb(self, shape, dtype, name=None):
        t = self.S.sb(self.sc, shape, dtype, name)
        self.tiles.append(t)
        return t

    def ps(self, shape, dtype, name=None):
        return self.S.ps(self.sc, shape, dtype, name)

    def ring(self, shape, dtype, n, psum=False, name=None):
        r = Ring(self.S, self.sc, shape, dtype, n, psum, name)
        if not psum:
            self.tiles.extend(r.tiles)
        return r

    def __exit__(self, *a):
        self.S.emit()
        self.S.release(self.tiles)
        return self.sc.__exit__(*a)


class K:
    pass


class Em:
    def __init__(self, S):
        self.S = S

    def mm(self, out, lhsT, rhs, start, stop, r, w):
        return self.S.op('pe', lambda e: e.matmul(out, lhsT=lhsT, rhs=rhs, start=start, stop=stop), r, w)

    def tr(self, out, in_, ident, r, w):
        return self.S.op('pe', lambda e: e.transpose(out=out, in_=in_, identity=ident), r, w)

    def act(self, out, in_, func, r, w, scale=1.0, bias=0.0, accum=None, eng='act'):
        if accum is None:
            return self.S.op(eng, lambda e: e.activation(out=out, in_=in_, func=func, scale=scale, bias=bias), r, w)
        return self.S.op(eng, lambda e: e.activation(out=out, in_=in_, func=func, scale=scale, bias=bias,
                                                     accum_out=accum), r, w)

    def tt(self, eng, out, in0, in1, op, r, w):
        return self.S.op(eng, lambda e: e.tensor_tensor(out=out, in0=in0, in1=in1, op=op), r, w)

    def ts(self, eng, out, in0, s1, s2, op0, op1, r, w):
        if s2 is None:
            return self.S.op(eng, lambda e: e.tensor_scalar(out=out, in0=in0, scalar1=s1, scalar2=None, op0=op0), r, w)
        return self.S.op(eng, lambda e: e.tensor_scalar(out=out, in0=in0, scalar1=s1, scalar2=s2, op0=op0, op1=op1), r, w)

    def stt(self, out, in0, scalar, in1, op0, op1, r, w):
        return self.S.op('dve', lambda e: e.scalar_tensor_tensor(out=out, in0=in0, scalar=scalar, in1=in1,
                                                                 op0=op0, op1=op1), r, w)

    def cp(self, eng, out, in_, r, w):
        return self.S.op(eng, lambda e: e.tensor_copy(out=out, in_=in_), r, w)

    def recip(self, out, in_, r, w):
        return self.S.op('dve', lambda e: e.reciprocal(out=out, in_=in_), r, w)

    def red(self, out, in_, op, r, w):
        return self.S.op('dve', lambda e: e.tensor_reduce(out=out, in_=in_, axis=AX.X, op=op), r, w)

    def memset(self, eng, ap, val, w):
        return self.S.op(eng, lambda e: e.memset(ap, val), (), w)

    def scan(self, out, d0, d1, init, r, w):
        return self.S.op('dve', lambda e: e.tensor_tensor_scan(out=out, data0=d0, data1=d1, initial=init,
                                                               op0=ALU.mult, op1=ALU.add), r, w)


def wview(w_ap):
    return w_ap.rearrange("(c p) n -> p c n", p=128)


def fmv(d_ap):
    return d_ap.rearrange("c p t -> p c t")


class WLoader:
    def __init__(self, S, E, st, kc=KC, nb=256):
        self.S, self.E = S, E
        self.ring = st.ring([128, kc, nb], F32, 2, name="wst")
        self.nb = nb

    def load(self, dst_tile, dst_fn, src_view, kc, ncols, scale=None):
        for c0 in range(0, ncols, self.nb):
            c1 = min(ncols, c0 + self.nb)
            w = self.ring.next()
            self.S.dma('sp', w[:, 0:kc, 0:c1 - c0], src_view[:, :, c0:c1], w, writes=[w])
            if scale is None:
                self.E.cp('pool', dst_fn(c0, c1), w[:, 0:kc, 0:c1 - c0], [w], [dst_tile])
            else:
                self.E.ts('pool', dst_fn(c0, c1), w[:, 0:kc, 0:c1 - c0], scale, None, ALU.mult, None, [w], [dst_tile])


class G:
    pass


def build(NL=4):
    nc = bass.Bass("TRN2", target_bir_lowering=False)
    S = Sched(nc)
    E = Em(S)
    g = G()
    g.nc, g.S, g.E = nc, S, E

    g.in_names = []

    def din(name, shape, dt=F32):
        g.in_names.append(name)
        return nc.dram_tensor(name, list(shape), dt, kind="ExternalInput").ap()

    def dscr(name, shape, dt):
        return nc.dram_tensor(name, list(shape), dt).ap()

    g.din, g.dscr = din, dscr
    g.x_in = din("x", [T, D])
    g.cT_in = din("cT", [128, KC, 2])
    g.cont_in = din("cont", [128, 2])
    g.seg_q = din("seg_q", [2, T], BF16)
    g.seg_k = din("seg_k", [2, T], BF16)
    g.ident_in = din("ident", [128, 128], BF16)
    g.identf_in = din("identf", [128, 128], F32)
    g.ada_w = din("ada_w", [4, D, 3 * D])
    g.ada_bP = din("ada_bP", [128, 4, 48])
    g.pre_gP = din("pre_gP", [128, 4, KC])
    g.post_g = din("post_g", [4, D])
    g.y_out = nc.dram_tensor("y", [T, D], F32, kind="ExternalOutput").ap()
    g.hT_d = dscr("hT_d", [KC, 128, T], BF16)
    g.oT_d = dscr("oT_d", [KC, 128, T], BF16)
    g.gT_d = dscr("gT_d", [KC, 128, T], BF16)
    g.gg_d = dscr("gg_d", [4, 2, D], F32)

    glob = contextlib.ExitStack()
    g.glob = glob
    g.ident = S.sb(glob, [128, 128], BF16, "ident")
    g.identf = S.sb(glob, [128, 128], F32, "identf")
    g.ones_b = S.sb(glob, [128, 128], BF16, "ones")
    g.cont = S.sb(glob, [128, 2], F32, "cont")
    g.preA = S.sb(glob, [128, 4, 2, KC], F32, "preA")
    g.preB = S.sb(glob, [128, 4, 2, KC], F32, "preB")
    S.dma('sp', g.ident[:], g.ident_in[:, :], g.ident, writes=[g.ident])
    S.dma('sp', g.identf[:], g.identf_in[:, :], g.identf, writes=[g.identf])
    S.dma('sp', g.cont[:], g.cont_in[:, :], g.cont, writes=[g.cont])
    E.memset('dve', g.ones_b[:], 1.0, [g.ones_b])

    import os
    g.stop = int(os.environ.get("KSTOP", "99"))
    prologue(g, NL)
    mixers = [mla_mixer, diff_mixer, lru_mixer, rwkv_mixer]
    for l in range(NL):
        if g.stop < 1:
            break
        stage_pre(g, l, g.x_in if l == 0 else g.y_out)
        if g.stop < 2:
            break
        w_out = mixers[l % 4](g, l)
        if g.stop < 4:
            break
        stage_post(g, l, w_out, g.x_in if l == 0 else g.y_out)
    if NL == 0:
        raise ValueError
    S.emit()
    return g


def prologue(g, NL):
    S, E = g.S, g.E
    with Stage(S) as st:
        cT = st.sb([128, KC, 2], F32)
        csT = st.sb([128, KC, 2], F32)
        adab = st.sb([128, 4, 48], F32)
        preg = st.sb([128, 4, KC], F32)
        modT = st.sb([128, 4, 2, 48], F32)
        gsb = st.sb([128, 2, 16], F32)
        ggs = st.sb([32, 128], F32)
        S.dma('sp', cT[:], g.cT_in[:, :, :], cT, writes=[cT])
        S.dma('sp', adab[:], g.ada_bP[:, :, :], adab, writes=[adab])
        S.dma('sp', preg[:], g.pre_gP[:, :, :], preg, writes=[preg])
        E.act(csT[:], cT[:], AF.Silu, [cT], [csT])
        wring = st.ring([128, KC, 512], F32, 2, name="adaw")
        pm = st.ring([128, 512], F32, 2, psum=True)
        for l in range(NL):
            pmod = pm.next()
            for blk in range(12):
                w = wring.next()
                S.dma('sp', w[:], wview(g.ada_w[l])[:, :, blk * 512:(blk + 1) * 512], w, writes=[w])
                for fc in range(4):
                    ch = blk * 4 + fc
                    for kc in range(KC):
                        E.mm(pmod[:, ch * 2:ch * 2 + 2], w[:, kc, fc * 128:(fc + 1) * 128], csT[:, kc, :],
                             kc == 0, kc == KC - 1, [w, csT], [pmod])
            E.tt('dve', modT[:, l, :, :], pmod[:, 0:96].rearrange("p (c h) -> p h c", h=2),
                 adab[:, l, :].unsqueeze(1).to_broadcast([128, 2, 48]), ALU.add, [pmod, adab], [modT])
            E.ts('dve', g.preA[:, l, :, :], modT[:, l, :, 16:32], 1.0, None, ALU.add, None, [modT], [g.preA])
            E.tt('dve', g.preA[:, l, :, :], g.preA[:, l, :, :],
                 preg[:, l, :].unsqueeze(1).to_broadcast([128, 2, KC]), ALU.mult, [g.preA, preg], [g.preA])
            E.cp('dve', g.preB[:, l, :, :], modT[:, l, :, 0:16], [modT], [g.preB])
            E.cp('dve', gsb[:], modT[:, l, :, 32:48], [modT], [gsb])
            pt = pm.next()
            E.tr(pt[0:32, 0:128], gsb[:].rearrange("p h c -> p (h c)"), g.identf[:], [gsb, g.identf], [pt])
            E.cp('dve', ggs[:], pt[0:32, 0:128], [pt], [ggs])
            S.dma('pool', g.gg_d[l].rearrange("h (c p) -> (h c) p", p=128), ggs[:], ggs, reads=[ggs])


def stage_pre(g, l, x_src):
    S, E = g.S, g.E
    with Stage(S) as st:
        xr = st.ring([128, D], F32, 3, name="x")
        junkr = st.ring([128, D], BF16, 2, name="junk")
        xs = st.ring([128, D], BF16, 2, name="xs")
        ssr = st.ring([128, 4], F32, 4, name="ss")
        ptr = st.ring([128, 1024], BF16, 4, psum=True)
        tmpr = st.ring([128, 8, 128], F32, 3, name="tmp")
        hst = st.ring([128, KC, 512], BF16, 2, name="hst")
        for grp in range(8):
            hs = hst.next()
            for sub in range(4):
                tt = grp * 4 + sub
                half = tt // 16
                x = xr.next()
                S.dma('sp', x[:], x_src[tt * 128:(tt + 1) * 128, :], x, writes=[x])
                ss = ssr.next()
                junk = junkr.next()
                E.act(junk[:], x[:], AF.Square, [x], [ss, junk], accum=ss[:, 0:1])
                E.act(ss[:, 1:2], ss[:, 0:1], AF.Sqrt, [ss], [ss], scale=1.0 / D, bias=1e-6)
                E.recip(ss[:, 2:3], ss[:, 1:2], [ss], [ss])
                xb = xs.next()
                E.ts('dve', xb[:], x[:], ss[:, 2:3], None, ALU.mult, None, [x, ss], [xb])
                for hc in range(2):
                    pt = ptr.next()
                    for c8 in range(8):
                        c = hc * 8 + c8
                        E.tr(pt[:, c8 * 128:(c8 + 1) * 128], xb[:, c * 128:(c + 1) * 128], g.ident[:],
                             [xb, g.ident], [pt])
                    tm = tmpr.next()
                    E.tt('dve', tm[:], pt[:].rearrange("p (c t) -> p c t", t=128),
                         g.preA[:, l, half, hc * 8:hc * 8 + 8].unsqueeze(2).to_broadcast([128, 8, 128]),
                         ALU.mult, [pt, g.preA], [tm])
                    E.tt('pool', hs[:, hc * 8:hc * 8 + 8, sub * 128:(sub + 1) * 128], tm[:],
                         g.preB[:, l, half, hc * 8:hc * 8 + 8].unsqueeze(2).to_broadcast([128, 8, 128]),
                         ALU.add, [tm, g.preB], [hs])
            S.dma('pool', fmv(g.hT_d)[:, :, grp * 512:(grp + 1) * 512], hs[:], hs, reads=[hs])


def stage_post(g, l, w_out, x_src):
    S, E = g.S, g.E
    with Stage(S) as st:
        wl = WLoader(S, E, st)
        wo = st.sb([128, KC, D], BF16, "wo")
        wl.load(wo, lambda c0, c1: wo[:, :, c0:c1], wview(w_out), KC, D)
        gg = st.sb([128, 2, D], F32, "gg")
        pg = st.sb([128, D], F32, "pg")
        for h in range(2):
            S.dma('sp', gg[:, h, :], g.gg_d[l, h:h + 1, :].partition_broadcast(128), gg, writes=[gg])
        S.dma('sp', pg[:], g.post_g[l:l + 1, :].partition_broadcast(128), pg, writes=[pg])
        E.tt('dve', gg[:], gg[:], pg[:].unsqueeze(1).to_broadcast([128, 2, D]), ALU.mult, [gg, pg], [gg])
        oring = st.ring([128, KC, 512], BF16, 2, name="o")
        xr = st.ring([128, D], F32, 2, name="x")
        tmp = st.ring([128, D], F32, 2, name="t")
        pyr = st.ring([128, D], F32, 2, psum=True)
        junkr = st.ring([128, 512], BF16, 4, name="junk")
        ssr = st.ring([128, 8], F32, 4, name="ss")
        for grp in range(8):
            o = oring.next()
            S.dma('sp', o[:], fmv(g.oT_d)[:, :, grp * 512:(grp + 1) * 512], o, writes=[o])
            for sub in range(4):
                tt = grp * 4 + sub
                half = tt // 16
                y = pyr.next()
                for n4 in range(4):
                    for kc in range(KC):
                        E.mm(y[:, n4 * 512:(n4 + 1) * 512], o[:, kc, sub * 128:(sub + 1) * 128],
                             wo[:, kc, n4 * 512:(n4 + 1) * 512], kc == 0, kc == KC - 1, [o, wo], [(y, n4)])
                ss = ssr.next()
                for n4 in range(4):
                    junk = junkr.next()
                    E.act(junk[:], y[:, n4 * 512:(n4 + 1) * 512], AF.Square, [(y, n4)], [(ss, n4), junk],
                          accum=ss[:, n4:n4 + 1])
                E.red(ss[:, 4:5], ss[:, 0:4], ALU.add, [(ss, 0), (ss, 1), (ss, 2), (ss, 3)], [(ss, 4)])
                E.act(ss[:, 5:6], ss[:, 4:5], AF.Sqrt, [(ss, 4)], [(ss, 5)], scale=1.0 / D, bias=1e-6)
                E.recip(ss[:, 6:7], ss[:, 5:6], [(ss, 5)], [(ss, 6)])
                x = xr.next()
                S.dma('sp', x[:], x_src[tt * 128:(tt + 1) * 128, :], x, writes=[x])
                t = tmp.next()
                for n4 in range(4):
                    sl = slice(n4 * 512, (n4 + 1) * 512)
                    E.stt(t[:, sl], y[:, sl], ss[:, 6:7], gg[:, half, sl], ALU.mult, ALU.mult,
                          [(y, n4), (ss, 6), gg], [(t, n4)])
                    E.tt('pool', t[:, sl], t[:, sl], x[:, sl], ALU.add, [(t, n4), x], [(t, n4)])
                S.dma('pool', g.y_out[tt * 128:(tt + 1) * 128, :], t[:], t,
                      reads=[(t, 0), (t, 1), (t, 2), (t, 3)])


def mla_mixer(g, l):
    S, E = g.S, g.E
    din, dscr = g.din, g.dscr
    w_in = din("mla_w_in", [D, 3136])
    qgP = din("mla_qgP", [128, 4])
    kvgP = din("mla_kvgP", [128, 4])
    w_q = din("mla_w_q_up", [512, 3072])
    w_kv = din("mla_w_kv_up", [512, 4096])
    w_out = din("mla_w_out", [D, D])
    cos_in = din("mla_cosT", [64, T])
    sin_in = din("mla_sinT", [64, T])
    lat_d = dscr("lat_d", [8, 128, T], BF16)
    kpe_d = dscr("kpe_d", [64, T], BF16)
    scale = 192 ** -0.5

    for hf in range(2):
      with Stage(S) as st:
        T0 = hf * HALF
        hT = st.sb([128, KC, HALF], BF16, "hT")
        for c in range(KC):
            S.dma('sp', hT[:, c, :], g.hT_d[c, :, T0:T0 + HALF], hT, writes=[(hT, c)])
        wl = WLoader(S, E, st, nb=128)
        wq = st.ring([128, KC, 512], BF16, 2, name="wq")
        pz = st.ring([128, 512], F32, 4, psum=True)
        pss = st.ring([128, 512], F32, 2, psum=True)
        zr = st.ring([128, 512], F32, 6, name="z")
        sqr = st.ring([128, 512], BF16, 2, name="sq")
        sdr = st.ring([128, 512], F32, 2, name="sd")
        ostr = st.ring([128, 4, 512], BF16, 2, name="ost")
        gt = st.sb([128, 8], F32, "gt")
        S.dma('sp', gt[:, 0:4], qgP[:, :], gt, writes=[gt])
        S.dma('sp', gt[:, 4:8], kvgP[:, :], gt, writes=[gt])
        for gi in range(2):
            w = wq.next()
            wl.load(w, lambda c0, c1, w=w: w[:, :, c0:c1], wview(w_in)[:, :, gi * 512:(gi + 1) * 512], KC, 512)
            for tq in range(4):
                tsl = slice(tq * 512, (tq + 1) * 512)
                gsl = slice(T0 + tq * 512, T0 + (tq + 1) * 512)
                zc = [zr.next() for _ in range(4)]
                psum_ss = pss.next()
                for fc in range(4):
                    p = pz.next()
                    for kc in range(KC):
                        E.mm(p[:], w[:, kc, fc * 128:(fc + 1) * 128], hT[:, kc, tsl], kc == 0, kc == KC - 1,
                             [w, (hT, kc)], [p])
                    sq = sqr.next()
                    E.act(sq[:], p[:], AF.Square, [p], [sq])
                    E.cp('dve', zc[fc][:], p[:], [p], [zc[fc]])
                    E.mm(psum_ss[:], g.ones_b[:], sq[:], fc == 0, fc == 3, [g.ones_b, sq], [psum_ss])
                sd = sdr.next()
                E.act(sd[:], psum_ss[:], AF.Sqrt, [psum_ss], [sd], scale=1.0 / 512, bias=1e-6)
                E.recip(sd[:], sd[:], [sd], [sd])
                ost = ostr.next()
                for fc in range(4):
                    E.stt(ost[:, fc, :], zc[fc][:], gt[:, gi * 4 + fc:gi * 4 + fc + 1], sd[:], ALU.mult, ALU.mult,
                          [zc[fc], gt, sd], [ost])
                S.dma('pool', fmv(lat_d)[:, gi * 4:gi * 4 + 4, gsl], ost[:], ost, reads=[ost])
        wk = st.sb([128, KC, 128], BF16, "wk")
        wv = wview(w_in)
        wl.load(wk, lambda c0, c1: wk[:, :, c0:c1], wv[:, :, 1024:1088], KC, 64)
        wl.load(wk, lambda c0, c1: wk[:, :, 64 + c0:64 + c1], wv[:, :, 1056:1088], KC, 32, scale=-1.0)
        wl.load(wk, lambda c0, c1: wk[:, :, 96 + c0:96 + c1], wv[:, :, 1024:1056], KC, 32)
        cosr = st.ring([64, 512], F32, 2, name="cos")
        sinr = st.ring([64, 512], F32, 2, name="sin")
        kstr = st.ring([64, 512], BF16, 2, name="kst")
        t1r = st.ring([64, 512], F32, 2, name="t1")
        t2r = st.ring([64, 512], F32, 2, name="t2")
        for tq in range(4):
            tsl = slice(tq * 512, (tq + 1) * 512)
            gsl = slice(T0 + tq * 512, T0 + (tq + 1) * 512)
            cos, sin = cosr.next(), sinr.next()
            S.dma('sp', cos[:], cos_in[:, gsl], cos, writes=[cos])
            S.dma('sp', sin[:], sin_in[:, gsl], sin, writes=[sin])
            pa, pb = pz.next(), pz.next()
            for kc in range(KC):
                E.mm(pa[0:64, :], wk[:, kc, 0:64], hT[:, kc, tsl], kc == 0, kc == KC - 1, [wk, (hT, kc)], [pa])
            for kc in range(KC):
                E.mm(pb[0:64, :], wk[:, kc, 64:128], hT[:, kc, tsl], kc == 0, kc == KC - 1, [wk, (hT, kc)], [pb])
            t1, t2, kst = t1r.next(), t2r.next(), kstr.next()
            E.tt('dve', t1[:], pa[0:64, :], cos[:], ALU.mult, [pa, cos], [t1])
            E.tt('dve', t2[:], pb[0:64, :], sin[:], ALU.mult, [pb, sin], [t2])
            E.tt('pool', kst[:], t1[:], t2[:], ALU.add, [t1, t2], [kst])
            S.dma('pool', kpe_d[:, gsl], kst[:], kst, reads=[kst])
        gstr = st.ring([128, 4, 512], BF16, 2, name="gst")
        for gq in range(4):
            w = wq.next()
            wl.load(w, lambda c0, c1, w=w: w[:, :, c0:c1], wv[:, :, 1088 + gq * 512:1088 + (gq + 1) * 512], KC, 512)
            for tq in range(4):
                tsl = slice(tq * 512, (tq + 1) * 512)
                gsl = slice(T0 + tq * 512, T0 + (tq + 1) * 512)
                gs = gstr.next()
                for fc in range(4):
                    p = pz.next()
                    for kc in range(KC):
                        E.mm(p[:], w[:, kc, fc * 128:(fc + 1) * 128], hT[:, kc, tsl], kc == 0, kc == KC - 1,
                             [w, (hT, kc)], [p])
                    E.act(gs[:, fc, :], p[:], AF.Silu, [p], [gs])
                S.dma('pool', fmv(g.gT_d)[:, gq * 4:gq * 4 + 4, gsl], gs[:], gs, reads=[gs])

    if g.stop < 3:
        return w_out
    with Stage(S) as st:
        qn = st.sb([128, 4, T], BF16, "qn")
        kvn = st.sb([128, 4, T], BF16, "kvn")
        for c in range(4):
            S.dma('sp', qn[:, c, :], lat_d[c], qn, writes=[qn])
            S.dma('sp', kvn[:, c, :], lat_d[4 + c], kvn, writes=[kvn])
        kaug = st.sb([66, T], BF16, "kaug")
        S.dma('sp', kaug[0:64, :], kpe_d[:, :], kaug, writes=[kaug])
        S.dma('sp', kaug[64:66, :], g.seg_k[:, :], kaug, writes=[kaug])
        qaugr = st.ring([66, T], BF16, 2, name="qaug")
        for qa in qaugr.tiles:
            S.dma('sp', qa[64:66, :], g.seg_q[:, :], qa, writes=[(qa, 'seg')])
        cosr = st.ring([64, 512], F32, 2, name="cos")
        sinr = st.ring([64, 512], F32, 2, name="sin")
        wl = WLoader(S, E, st, kc=4, nb=256)
        wqr = st.ring([128, 4, 256], BF16, 2, name="wqh")
        wkvr = st.ring([128, 4, 256], BF16, 2, name="wkvh")
        qnpr = st.ring([128, T], BF16, 2, name="qnp")
        knpr = st.ring([128, T], BF16, 2, name="knp")
        Vr = st.ring([128, NT, 128], BF16, 1, name="V")
        ps_s = st.ring([128, 512], F32, 2, psum=True)
        ps_o = st.ring([128, 512], F32, 1, psum=True)
        ps_m = st.ring([128, 512], F32, 1, psum=True)
        ps_x = st.ring([128, 512], F32, 4, psum=True)
        sqr = st.ring([128, 512], BF16, 4, name="sq")
        t1r = st.ring([64, 512], F32, 2, name="t1")
        t2r = st.ring([64, 512], F32, 2, name="t2")
        pTr = st.ring([128, 512], BF16, 8, name="pT")
        sgr = st.ring([128, 512], BF16, 4, name="sg")
        rsr = st.ring([128, 512], F32, 2, name="rs")
        ofr = st.ring([128, 512], F32, 2, name="of")
        obr = st.ring([128, 512], BF16, 2, name="ob")
        glr = st.ring([128, 512], BF16, 2, name="gl")
        statr = st.ring([128, 24], F32, 2, name="stat")
        wqv = w_q.rearrange("(c p) n -> p c n", p=128)
        wkvv = w_kv.rearrange("(c p) n -> p c n", p=128)
        for h in range(16):
            wqh, wkvh = wqr.next(), wkvr.next()
            wl.load(wqh, lambda c0, c1, t=wqh: t[:, :, c0:c1], wqv[:, :, 192 * h:192 * h + 192], 4, 192)
            wl.load(wqh, lambda c0, c1, t=wqh: t[:, :, 192 + c0:192 + c1], wqv[:, :, 192 * h + 160:192 * h + 192],
                    4, 32, scale=-1.0)
            wl.load(wqh, lambda c0, c1, t=wqh: t[:, :, 224 + c0:224 + c1], wqv[:, :, 192 * h + 128:192 * h + 160],
                    4, 32)
            wl.load(wkvh, lambda c0, c1, t=wkvh: t[:, :, c0:c1], wkvv[:, :, 256 * h:256 * h + 256], 4, 256)
            qnp, knp, V, qaug, stat = qnpr.next(), knpr.next(), Vr.next(), qaugr.next(), statr.next()
            for tq in range(8):
                tsl = slice(tq * 512, (tq + 1) * 512)
                p = ps_x.next()
                for kc in range(4):
                    E.mm(p[:], wqh[:, kc, 0:128], qn[:, kc, tsl], kc == 0, kc == 3, [wqh, qn], [p])
                E.act(qnp[:, tsl], p[:], AF.Copy, [p], [(qnp, tq)])
                sq1 = sqr.next()
                E.cp('dve', sq1[:], p[:], [p], [sq1])
                E.tt('pool', sq1[:], sq1[:], sq1[:], ALU.mult, [sq1], [sq1])
                pa, pb = ps_x.next(), ps_x.next()
                for kc in range(4):
                    E.mm(pa[0:64, :], wqh[:, kc, 128:192], qn[:, kc, tsl], kc == 0, kc == 3, [wqh, qn], [pa])
                for kc in range(4):
                    E.mm(pb[0:64, :], wqh[:, kc, 192:256], qn[:, kc, tsl], kc == 0, kc == 3, [wqh, qn], [pb])
                t1, t2 = t1r.next(), t2r.next()
                cos, sin = cosr.next(), sinr.next()
                S.dma('sp', cos[:], cos_in[:, tsl], cos, writes=[cos])
                S.dma('sp', sin[:], sin_in[:, tsl], sin, writes=[sin])
                E.tt('dve', t1[:], pa[0:64, :], cos[:], ALU.mult, [pa, cos], [t1])
                E.tt('dve', t2[:], pb[0:64, :], sin[:], ALU.mult, [pb, sin], [t2])
                E.tt('pool', qaug[0:64, tsl], t1[:], t2[:], ALU.add, [t1, t2], [(qaug, tq)])
                sq2 = sqr.next()
                E.tt('pool', sq2[0:64, :], qaug[0:64, tsl], qaug[0:64, tsl], ALU.mult, [(qaug, tq)], [sq2])
                pn = ps_x.next()
                E.mm(pn[:], g.ones_b[:], sq1[:], True, False, [g.ones_b, sq1], [pn])
                E.mm(pn[:], g.ones_b[0:64, :], sq2[0:64, :], False, True, [g.ones_b, sq2], [pn])
                E.red(stat[:, tq:tq + 1], pn[:], ALU.max, [pn], [(stat, 'q', tq)])
                p = ps_x.next()
                for kc in range(4):
                    E.mm(p[:], wkvh[:, kc, 0:128], kvn[:, kc, tsl], kc == 0, kc == 3, [wkvh, kvn], [p])
                E.act(knp[:, tsl], p[:], AF.Copy, [p], [(knp, tq)])
                sq3 = sqr.next()
                E.cp('dve', sq3[:], p[:], [p], [sq3])
                E.tt('pool', sq3[:], sq3[:], sq3[:], ALU.mult, [sq3], [sq3])
                sq4 = sqr.next()
                E.tt('pool', sq4[0:64, :], kaug[0:64, tsl], kaug[0:64, tsl], ALU.mult, [kaug], [sq4])
                pn = ps_x.next()
                E.mm(pn[:], g.ones_b[:], sq3[:], True, False, [g.ones_b, sq3], [pn])
                E.mm(pn[:], g.ones_b[0:64, :], sq4[0:64, :], False, True, [g.ones_b, sq4], [pn])
                E.red(stat[:, 8 + tq:9 + tq], pn[:], ALU.max, [pn], [(stat, 'k', tq)])
                p = ps_x.next()
                for j in range(4):
                    tt = tq * 4 + j
                    for kc in range(4):
                        E.mm(p[:, j * 128:(j + 1) * 128], kvn[:, kc, tt * 128:(tt + 1) * 128], wkvh[:, kc, 128:256],
                             kc == 0, kc == 3, [wkvh, kvn], [p])
                E.act(V[:, tq * 4:tq * 4 + 4, :], p[:].rearrange("p (j d) -> p j d", d=128), AF.Copy, [p], [(V, tq)])
            qk = [(stat, 'q', t) for t in range(8)]
            kk = [(stat, 'k', t) for t in range(8)]
            E.red(stat[:, 16:17], stat[:, 0:8], ALU.max, qk, [(stat, 16)])
            E.red(stat[:, 17:18], stat[:, 8:16], ALU.max, kk, [(stat, 17)])
            E.tt('dve', stat[:, 18:19], stat[:, 16:17], stat[:, 17:18], ALU.mult, [(stat, 16), (stat, 17)], [(stat, 18)])
            E.act(stat[:, 19:20], stat[:, 18:19], AF.Sqrt, [(stat, 18)], [(stat, 19)])
            E.ts('dve', stat[:, 20:21], stat[:, 19:20], -scale, None, ALU.mult, None, [(stat, 19)], [(stat, 20)])
            negB = stat[:, 20:21]
            qkeys = [(qnp, t) for t in range(8)] + [(qaug, t) for t in range(8)] + [(qaug, 'seg')]
            kkeys = [(knp, t) for t in range(8)] + [kaug]
            vkeys = [(V, t) for t in range(8)]
            units = [(tq, kb) for tq in range(8) for kb in range(NT)]

            def qk_mm(u):
                tq, kb = units[u]
                ps = ps_s.next()
                tsl = slice(tq * 512, (tq + 1) * 512)
                ksl = slice(kb * 128, (kb + 1) * 128)
                E.mm(ps[:], knp[:, ksl], qnp[:, tsl], True, False, [(knp, kb // 4), (qnp, tq)], [ps])
                E.mm(ps[:], kaug[0:66, ksl], qaug[0:66, tsl], False, True, [kaug, (qaug, tq), (qaug, 'seg')], [ps])
                return ps

            nxt = qk_mm(0)
            po = psm = None
            for u, (tq, kb) in enumerate(units):
                ps = nxt
                if u + 1 < len(units):
                    nxt = qk_mm(u + 1)
                if kb == 0:
                    po, psm = ps_o.next(), ps_m.next()
                pT = pTr.next()
                E.act(pT[:], ps[:], AF.Exp, [ps, (stat, 20)], [pT], scale=scale, bias=negB)
                E.mm(po[:], V[:, kb, :], pT[:], kb == 0, kb == NT - 1, [(V, kb // 4), pT], [po])
                if kb % 4 == 0:
                    grp = []
                grp.append(pT)
                if kb % 4 == 3:
                    s1, s2 = sgr.next(), sgr.next()
                    E.tt('pool', s1[:], grp[0][:], grp[1][:], ALU.add, [grp[0], grp[1]], [s1])
                    E.tt('pool', s2[:], grp[2][:], grp[3][:], ALU.add, [grp[2], grp[3]], [s2])
                    E.tt('pool', s1[:], s1[:], s2[:], ALU.add, [s1, s2], [s1])
                    E.mm(psm[:], g.ones_b[:], s1[:], kb == 3, kb == NT - 1, [g.ones_b, s1], [psm])
                if kb == NT - 1:
                    tsl = slice(tq * 512, (tq + 1) * 512)
                    rs, of, ob, gl = rsr.next(), ofr.next(), obr.next(), glr.next()
                    S.dma('sp', gl[:], g.gT_d[h, :, tsl], gl, writes=[gl])
                    E.recip(rs[:], psm[:], [psm], [rs])
                    E.tt('dve', of[:], po[:], rs[:], ALU.mult, [po, rs], [of])
                    E.tt('pool', ob[:], of[:], gl[:], ALU.mult, [of, gl], [ob])
                    S.dma('pool', g.oT_d[h, :, tsl], ob[:], ob, reads=[ob])
    return w_out


def diff_mixer(g, l):
    S, E = g.S, g.E
    din, dscr = g.din, g.dscr
    w_in = din("diff_w_in", [D, 8192])
    lam_in = din("diff_lam", [1, 256])
    gsub_in = din("diff_gsubP", [128, 1])
    w_out = din("diff_w_out", [D, D])
    cos_in = din("diff_cos", [128, NT, 8])
    sin_in = din("diff_sin", [128, NT, 8])
    qk_d = dscr("dqk_d", [2, KC, 128, T], BF16)
    v_d = dscr("dv_d", [T, D], BF16)
    scale = 64 ** -0.5
    lam_init = 0.8 - 0.6 * math.exp(-0.3 * l)
    wv = wview(w_in)

    for hf in range(2):
      with Stage(S) as st:
        T0 = hf * HALF
        hT = st.sb([128, KC, HALF], BF16, "hT")
        for c in range(KC):
            S.dma('sp', hT[:, c, :], g.hT_d[c, :, T0:T0 + HALF], hT, writes=[(hT, c)])
        wl = WLoader(S, E, st, nb=128)
        wq = st.ring([128, KC, 512], BF16, 2, name="wq")
        pz = st.ring([128, 512], F32, 4, psum=True)
        ptr = st.ring([128, 8, 128], BF16, 2, psum=True)
        ctab = st.sb([128, NT, 8], F32, "ctab")
        stab = st.sb([128, NT, 8], F32, "stab")
        S.dma('sp', ctab[:], cos_in[:, :, :], ctab, writes=[ctab])
        S.dma('sp', stab[:], sin_in[:, :, :], stab, writes=[stab])
        qtmr = st.ring([128, 512], BF16, 3, name="qtm")
        rr = st.ring([128, 4, 8, 8], F32, 3, name="rope")
        qTr = st.ring([128, 4, HALF], BF16, 2, name="qTst")
        for sel in range(2):
            for cb in range(4):
                w = wq.next()
                col0 = sel * 2048 + cb * 512
                wl.load(w, lambda c0, c1, w=w: w[:, :, c0:c1], wv[:, :, col0:col0 + 512], KC, 512)
                qT = qTr.next()
                for tt in range(16):
                    gtt = hf * 16 + tt
                    p = pz.next()
                    for kc in range(KC):
                        E.mm(p[:], hT[:, kc, tt * 128:(tt + 1) * 128], w[:, kc, :], kc == 0, kc == KC - 1,
                             [w, (hT, kc)], [p])
                    qtm = qtmr.next()
                    E.act(qtm[:], p[:], AF.Copy, [p], [qtm])
                    p3 = p[:].rearrange("p (h d) -> p h d", d=64)
                    q3 = qtm[:].rearrange("p (h d) -> p h d", d=64)
                    cs = ctab[:, gtt, :].unsqueeze(1).to_broadcast([128, 8, 8])
                    sn = stab[:, gtt, :].unsqueeze(1).to_broadcast([128, 8, 8])
                    r = rr.next()
                    E.tt('dve', r[:, 0], p3[:, :, 0:8], cs, ALU.mult, [p, ctab], [(r, 0)])
                    E.tt('dve', r[:, 1], p3[:, :, 8:16], sn, ALU.mult, [p, stab], [(r, 1)])
                    E.tt('dve', r[:, 2], p3[:, :, 8:16], cs, ALU.mult, [p, ctab], [(r, 2)])
                    E.tt('dve', r[:, 3], p3[:, :, 0:8], sn, ALU.mult, [p, stab], [(r, 3)])
                    E.tt('pool', q3[:, :, 0:8], r[:, 0], r[:, 1], ALU.subtract, [(r, 0), (r, 1), qtm], [qtm])
                    E.tt('pool', q3[:, :, 8:16], r[:, 2], r[:, 3], ALU.add, [(r, 2), (r, 3), qtm], [qtm])
                    pt = ptr.next()
                    for j in range(4):
                        E.tr(pt[:, j, :], qtm[:, j * 128:(j + 1) * 128], g.ident[:], [qtm, g.ident], [pt])
                    E.cp('dve', qT[:, :, tt * 128:(tt + 1) * 128], pt[:, 0:4, :], [pt], [(qT, tt)])
                S.dma('pool', fmv(qk_d[sel])[:, cb * 4:cb * 4 + 4, T0:T0 + HALF], qT[:], qT,
                      reads=[(qT, t) for t in range(16)])
        vstr = st.ring([128, 512], BF16, 3, name="vst")
        for cb in range(4):
            w = wq.next()
            col0 = 4096 + cb * 512
            wl.load(w, lambda c0, c1, w=w: w[:, :, c0:c1], wv[:, :, col0:col0 + 512], KC, 512)
            for tt in range(16):
                p = pz.next()
                for kc in range(KC):
                    E.mm(p[:], hT[:, kc, tt * 128:(tt + 1) * 128], w[:, kc, :], kc == 0, kc == KC - 1,
                         [w, (hT, kc)], [p])
                vs = vstr.next()
                E.act(vs[:], p[:], AF.Copy, [p], [vs])
                S.dma('pool', v_d[T0 + tt * 128:T0 + (tt + 1) * 128, cb * 512:(cb + 1) * 512], vs[:], vs, reads=[vs])
        gstr = st.ring([128, 4, 512], BF16, 2, name="gst")
        for gq in range(4):
            w = wq.next()
            wl.load(w, lambda c0, c1, w=w: w[:, :, c0:c1], wv[:, :, 6144 + gq * 512:6144 + (gq + 1) * 512], KC, 512)
            for tq in range(4):
                tsl = slice(tq * 512, (tq + 1) * 512)
                gsl = slice(T0 + tq * 512, T0 + (tq + 1) * 512)
                gs = gstr.next()
                for fc in range(4):
                    p = pz.next()
                    for kc in range(KC):
                        E.mm(p[:], w[:, kc, fc * 128:(fc + 1) * 128], hT[:, kc, tsl], kc == 0, kc == KC - 1,
                             [w, (hT, kc)], [p])
                    E.act(gs[:, fc, :], p[:], AF.Silu, [p], [gs])
                S.dma('pool', fmv(g.gT_d)[:, gq * 4:gq * 4 + 4, gsl], gs[:], gs, reads=[gs])
    if g.stop < 3:
        return w_out

    with Stage(S) as st:
        lam = st.sb([128, 256], F32, "lam")
        S.dma('sp', lam[:], lam_in[0:1, :].partition_broadcast(128), lam, writes=[lam])
        lt = st.sb([128, 128], F32, "lt")
        ls = st.sb([128, 8], F32, "ls")
        E.tt('dve', lt[:, 0:64], lam[:, 0:64], lam[:, 64:128], ALU.mult, [lam], [lt])
        E.tt('dve', lt[:, 64:128], lam[:, 128:192], lam[:, 192:256], ALU.mult, [lam], [lt])
        E.red(ls[:, 0:1], lt[:, 0:64], ALU.add, [lt], [(ls, 0)])
        E.red(ls[:, 1:2], lt[:, 64:128], ALU.add, [lt], [(ls, 1)])
        E.act(ls[:, 2:4], ls[:, 0:2], AF.Exp, [(ls, 0), (ls, 1)], [(ls, 2)])
        E.tt('dve', ls[:, 4:5], ls[:, 3:4], ls[:, 2:3], ALU.subtract, [(ls, 2)], [(ls, 4)])
        E.ts('dve', ls[:, 5:6], ls[:, 4:5], -lam_init, None, ALU.add, None, [(ls, 4)], [(ls, 5)])
        neglam = ls[:, 5:6]
        gsub = st.sb([128, 2], F32, "gsub")
        S.dma('sp', gsub[:, 0:1], gsub_in[:, :], gsub, writes=[gsub])
        E.ts('dve', gsub[:, 1:2], gsub[:, 0:1], 1.0 - lam_init, None, ALU.mult, None, [gsub], [(gsub, 1)])
        qr = [st.ring([66, T], BF16, 2, name=f"q{c}") for c in range(2)]
        kr = [st.ring([66, T], BF16, 2, name=f"k{c}") for c in range(2)]
        for c in range(2):
            for t_ in qr[c].tiles:
                S.dma('sp', t_[64:66, :], g.seg_q[:, :], t_, writes=[(t_, 'seg')])
            for t_ in kr[c].tiles:
                S.dma('sp', t_[64:66, :], g.seg_k[:, :], t_, writes=[(t_, 'seg')])
        Vr = st.ring([128, NT, 128], BF16, 2, name="V")
        ps_s = st.ring([128, 512], F32, 2, psum=True)
        ps_o = [st.ring([128, 512], F32, 1, psum=True) for _ in range(2)]
        ps_m = [st.ring([128, 512], F32, 1, psum=True) for _ in range(2)]
        ps_x = st.ring([128, 512], F32, 2, psum=True)
        sqr = st.ring([128, 512], BF16, 3, name="sq")
        pTr = st.ring([128, 512], BF16, 14, name="pT")
        sgr = st.ring([128, 512], BF16, 6, name="sg")
        rsr = st.ring([128, 512], F32, 2, name="rs")
        ofr = st.ring([128, 512], F32, 4, name="of")
        obr = st.ring([128, 512], BF16, 2, name="ob")
        glr = st.ring([128, 512], BF16, 2, name="gl")
        statr = st.ring([128, 48], F32, 2, name="stat")
        vview = v_d.rearrange("(n p) f -> p n f", p=128)
        for h in range(16):
            q = [qr[c].next() for c in range(2)]
            k = [kr[c].next() for c in range(2)]
            V, stat = Vr.next(), statr.next()
            for c in range(2):
                S.dma('sp', q[c][0:64, :], qk_d[0, h, 64 * c:64 * c + 64, :], q[c], writes=[(q[c], 'd')])
                S.dma('sp', k[c][0:64, :], qk_d[1, h, 64 * c:64 * c + 64, :], k[c], writes=[(k[c], 'd')])
            for j in range(4):
                S.dma('sp', V[:, j * 8:j * 8 + 8, :], vview[:, j * 8:j * 8 + 8, h * 128:(h + 1) * 128], V, writes=[V])
            for ti, tl in enumerate((q[0], q[1], k[0], k[1])):
                for tq in range(8):
                    tsl = slice(tq * 512, (tq + 1) * 512)
                    sq = sqr.next()
                    E.tt('pool', sq[0:64, :], tl[0:64, tsl], tl[0:64, tsl], ALU.mult, [(tl, 'd')], [sq])
                    pn = ps_x.next()
                    E.mm(pn[:], g.ones_b[0:64, :], sq[0:64, :], True, True, [g.ones_b, sq], [pn])
                    E.red(stat[:, ti * 8 + tq:ti * 8 + tq + 1], pn[:], ALU.max, [pn], [(stat, ti, tq)])
                E.red(stat[:, 32 + ti:33 + ti], stat[:, ti * 8:ti * 8 + 8], ALU.max,
                      [(stat, ti, t) for t in range(8)], [(stat, 32 + ti)])
            negB = []
            for c in range(2):
                E.tt('dve', stat[:, 36 + c:37 + c], stat[:, 32 + c:33 + c], stat[:, 34 + c:35 + c], ALU.mult,
                     [(stat, 32 + c), (stat, 34 + c)], [(stat, 36 + c)])
                E.act(stat[:, 38 + c:39 + c], stat[:, 36 + c:37 + c], AF.Sqrt, [(stat, 36 + c)], [(stat, 38 + c)])
                E.ts('dve', stat[:, 40 + c:41 + c], stat[:, 38 + c:39 + c], -scale, None, ALU.mult, None,
                     [(stat, 38 + c)], [(stat, 40 + c)])
                negB.append(stat[:, 40 + c:41 + c])
            units = [(tq, kb, c) for tq in range(8) for kb in range(NT) for c in range(2)]

            def qk_mm(u):
                tq, kb, c = units[u]
                ps = ps_s.next()
                E.mm(ps[:], k[c][0:66, kb * 128:(kb + 1) * 128], q[c][0:66, tq * 512:(tq + 1) * 512], True, True,
                     [(k[c], 'd'), (k[c], 'seg'), (q[c], 'd'), (q[c], 'seg')], [ps])
                return ps

            nxt = qk_mm(0)
            po = [None, None]
            psm = [None, None]
            grp = [[], []]
            for u, (tq, kb, c) in enumerate(units):
                ps = nxt
                if u + 1 < len(units):
                    nxt = qk_mm(u + 1)
                if kb == 0:
                    po[c], psm[c] = ps_o[c].next(), ps_m[c].next()
                pT = pTr.next()
                E.act(pT[:], ps[:], AF.Exp, [ps, (stat, 40 + c)], [pT], scale=scale, bias=negB[c])
                E.mm(po[c][:], V[:, kb, :], pT[:], kb == 0, kb == NT - 1, [V, pT], [po[c]])
                if kb % 4 == 0:
                    grp[c] = []
                grp[c].append(pT)
                if kb % 4 == 3:
                    s1, s2 = sgr.next(), sgr.next()
                    gg_ = grp[c]
                    E.tt('pool', s1[:], gg_[0][:], gg_[1][:], ALU.add, [gg_[0], gg_[1]], [s1])
                    E.tt('pool', s2[:], gg_[2][:], gg_[3][:], ALU.add, [gg_[2], gg_[3]], [s2])
                    E.tt('pool', s1[:], s1[:], s2[:], ALU.add, [s1, s2], [s1])
                    E.mm(psm[c][:], g.ones_b[:], s1[:], kb == 3, kb == NT - 1, [g.ones_b, s1], [psm[c]])
                if kb == NT - 1 and c == 1:
                    tsl = slice(tq * 512, (tq + 1) * 512)
                    gl = glr.next()
                    S.dma('sp', gl[:], g.gT_d[h, :, tsl], gl, writes=[gl])
                    o = []
                    for cc in range(2):
                        rs, of = rsr.next(), ofr.next()
                        E.recip(rs[:], psm[cc][:], [psm[cc]], [rs])
                        E.tt('dve', of[:], po[cc][:], rs[:], ALU.mult, [po[cc], rs], [of])
                        o.append(of)
                    od = ofr.next()
                    E.stt(od[:], o[1][:], neglam, o[0][:], ALU.mult, ALU.add, [o[0], o[1], (ls, 5)], [od])
                    sq = sqr.next()
                    E.tt('pool', sq[:], od[:], od[:], ALU.mult, [od], [sq])
                    pn = ps_x.next()
                    E.mm(pn[:], g.ones_b[:], sq[:], True, True, [g.ones_b, sq], [pn])
                    rs = rsr.next()
                    E.act(rs[:], pn[:], AF.Sqrt, [pn], [rs], scale=1.0 / 128, bias=1e-5)
                    E.recip(rs[:], rs[:], [rs], [rs])
                    on = ofr.next()
                    E.stt(on[:], od[:], gsub[:, 1:2], rs[:], ALU.mult, ALU.mult, [od, (gsub, 1), rs], [on])
                    ob = obr.next()
                    E.tt('pool', ob[:], on[:], gl[:], ALU.mult, [on, gl], [ob])
                    S.dma('pool', g.oT_d[h, :, tsl], ob[:], ob, reads=[ob])
    return w_out


def lru_mixer(g, l):
    S, E = g.S, g.E
    din, dscr = g.din, g.dscr
    w_in = din("lru_w_in", [D, 2 * D])
    w_out = din("lru_w_out", [D, D])
    conv_in = din("lru_convP", [128, KC, 5])
    gw_in = din("lru_gate_w", [128, 2, 2, KC, 128])
    gb_in = din("lru_gate_bP", [128, 2, 2, KC])
    lam_in = din("lru_lamP", [128, 2, KC])
    xb_d = dscr("xb_d", [KC, 128, T], F32)
    wv = wview(w_in)

    for hf in range(2):
      with Stage(S) as st:
        T0 = hf * HALF
        hT = st.sb([128, KC, HALF], BF16, "hT")
        for c in range(KC):
            S.dma('sp', hT[:, c, :], g.hT_d[c, :, T0:T0 + HALF], hT, writes=[(hT, c)])
        wl = WLoader(S, E, st, nb=128)
        wq = st.ring([128, KC, 512], BF16, 2, name="wq")
        pz = st.ring([128, 512], F32, 4, psum=True)
        xstr = st.ring([128, 4, 512], F32, 2, name="xst")
        gstr = st.ring([128, 4, 512], BF16, 2, name="gst")
        for part in range(2):
            for gq in range(4):
                w = wq.next()
                col0 = part * 2048 + gq * 512
                wl.load(w, lambda c0, c1, w=w: w[:, :, c0:c1], wv[:, :, col0:col0 + 512], KC, 512)
                for tq in range(4):
                    tsl = slice(tq * 512, (tq + 1) * 512)
                    gsl = slice(T0 + tq * 512, T0 + (tq + 1) * 512)
                    stt_ = (xstr if part == 0 else gstr).next()
                    for fc in range(4):
                        p = pz.next()
                        for kc in range(KC):
                            E.mm(p[:], w[:, kc, fc * 128:(fc + 1) * 128], hT[:, kc, tsl], kc == 0, kc == KC - 1,
                                 [w, (hT, kc)], [p])
                        E.act(stt_[:, fc, :], p[:], AF.Copy if part == 0 else AF.Silu, [p], [stt_])
                    dst = fmv(xb_d if part == 0 else g.gT_d)[:, gq * 4:gq * 4 + 4, gsl]
                    S.dma('pool', dst, stt_[:], stt_, reads=[stt_])
    if g.stop < 3:
        return w_out

    with Stage(S) as st:
        cvp = st.sb([128, KC, 5], F32, "cvp")
        gb = st.sb([128, 2, 2, KC], F32, "gb")
        lam = st.sb([128, 2, KC], F32, "lam")
        sc = st.sb([128, 2, KC], F32, "sc")
        S.dma('sp', cvp[:], conv_in[:, :, :], cvp, writes=[cvp])
        S.dma('sp', gb[:], gb_in[:, :, :, :], gb, writes=[gb])
        S.dma('sp', lam[:], lam_in[:, :, :], lam, writes=[lam])
        E.act(sc[:], lam[:], AF.Exp, [lam], [sc], scale=-1.0)
        E.act(sc[:], sc[:], AF.Ln, [sc], [sc], bias=1.0)
        E.ts('dve', sc[:], sc[:], -8.0, None, ALU.mult, None, [sc], [sc])
        cont = g.cont
        xbp = st.sb([128, 2, HALF + 3], F32, "xbp")
        xc = st.sb([128, 2, HALF], F32, "xc")
        xcb = st.sb([128, T], BF16, "xcb")
        gTr = st.ring([128, T], BF16, 2, name="gT")
        ar = st.ring([128, T], F32, 2, name="a")
        ir = st.ring([128, T], F32, 2, name="i")
        mr = st.ring([128, T], F32, 2, name="m")
        hr = st.ring([128, T], F32, 2, name="hs")
        obr = st.ring([128, T], BF16, 1, name="ob")
        gwsr = st.ring([128, 2, 128], F32, 2, name="gws")
        gwr = st.ring([128, 2, 128], BF16, 2, name="gw")
        inr = st.ring([128, 2], F32, 4, name="init")
        pg = st.ring([128, 512], F32, 4, psum=True)
        E.memset('dve', xbp[:, 0, 0:1], 0.0, [(xbp, 'z')])
        E.memset('dve', xbp[:, 1, HALF + 1:HALF + 3], 0.0, [(xbp, 'z')])
        for n in range(KC):
            for h in range(2):
                S.dma('sp', xbp[:, h, 1:HALF + 1], xb_d[n, :, h * HALF:(h + 1) * HALF], xbp, writes=[(xbp, h)])
            gT = gTr.next()
            S.dma('sp', gT[:], g.gT_d[n], gT, writes=[gT])
            E.ts('dve', xbp[:, 0, HALF + 1:HALF + 3], xbp[:, 1, 1:3], cont[:, 0:1], None, ALU.mult, None,
                 [(xbp, 1), cont], [(xbp, 'h0')])
            E.ts('dve', xbp[:, 1, 0:1], xbp[:, 0, HALF:HALF + 1], cont[:, 0:1], None, ALU.mult, None,
                 [(xbp, 0), cont], [(xbp, 'h1')])
            xk = [(xbp, 0), (xbp, 1), (xbp, 'h0'), (xbp, 'h1'), (xbp, 'z')]
            E.ts('dve', xc[:], xbp[:, :, 0:HALF], cvp[:, n, 0:1], cvp[:, n, 4:5], ALU.mult, ALU.add,
                 xk + [cvp], [xc])
            for j in range(1, 4):
                E.stt(xc[:], xbp[:, :, j:j + HALF], cvp[:, n, j:j + 1], xc[:], ALU.mult, ALU.add, xk + [cvp, xc], [xc])
            xcf = xc[:].rearrange("p h t -> p (h t)")
            E.act(xcb[:], xcf, AF.Copy, [xc], [xcb])
            hs2 = []
            for d in range(2):
                gws, gw = gwsr.next(), gwr.next()
                S.dma('sp', gws[:], gw_in[:, d, :, n, :], gws, writes=[gws])
                E.cp('pool', gw[:], gws[:], [gws], [gw])
                a, it, mt, hs = ar.next(), ir.next(), mr.next(), hr.next()
                for tq in range(8):
                    tsl = slice(tq * 512, (tq + 1) * 512)
                    p_r, p_i = pg.next(), pg.next()
                    E.mm(p_r[:], gw[:, 0, :], xcb[:, tsl], True, True, [gw, xcb], [p_r])
                    E.mm(p_i[:], gw[:, 1, :], xcb[:, tsl], True, True, [gw, xcb], [p_i])
                    E.act(a[:, tsl], p_r[:], AF.Sigmoid, [p_r, gb], [(a, tq)], bias=gb[:, d, 0, n:n + 1])
                    E.act(it[:, tsl], p_i[:], AF.Sigmoid, [p_i, gb], [(it, tq)], bias=gb[:, d, 1, n:n + 1])
                ak = [(a, t) for t in range(8)]
                ik = [(it, t) for t in range(8)]
                E.act(a[:], a[:], AF.Exp, ak + [sc], ak, scale=sc[:, d, n:n + 1])
                E.tt('pool', mt[:], a[:], a[:], ALU.mult, ak, [mt])
                E.act(mt[:], mt[:], AF.Sqrt, [mt], [mt], scale=-1.0, bias=1.0)
                first, mid = (0, HALF) if d == 0 else (T - 1, HALF - 1)
                E.memset('dve', mt[:, first:first + 1], 1.0, [mt])
                E.ts('dve', mt[:, mid:mid + 1], mt[:, mid:mid + 1], cont[:, 0:1], cont[:, 1:2], ALU.mult, ALU.add,
                     [mt, cont], [mt])
                E.tt('pool', it[:], it[:], xcf, ALU.mult, ik + [xc], ik)
                E.tt('dve', it[:], it[:], mt[:], ALU.mult, ik + [mt], ik)
                ini = inr.next()
                if d == 0:
                    E.scan(hs[:, 0:HALF], a[:, 0:HALF], it[:, 0:HALF], 0.0, ak + ik, [(hs, 0)])
                    E.ts('dve', ini[:, 0:1], hs[:, HALF - 1:HALF], cont[:, 0:1], None, ALU.mult, None,
                         [(hs, 0), cont], [ini])
                    E.scan(hs[:, HALF:T], a[:, HALF:T], it[:, HALF:T], ini[:, 0:1], ak + ik + [ini], [(hs, 1)])
                else:
                    E.scan(hs[:, HALF:T][:, ::-1], a[:, HALF:T][:, ::-1], it[:, HALF:T][:, ::-1], 0.0,
                           ak + ik, [(hs, 1)])
                    E.ts('dve', ini[:, 0:1], hs[:, HALF:HALF + 1], cont[:, 0:1], None, ALU.mult, None,
                         [(hs, 1), cont], [ini])
                    E.scan(hs[:, 0:HALF][:, ::-1], a[:, 0:HALF][:, ::-1], it[:, 0:HALF][:, ::-1], ini[:, 0:1],
                           ak + ik + [ini], [(hs, 0)])
                hs2.append(hs)
            hk = [(hs2[0], 0), (hs2[0], 1), (hs2[1], 0), (hs2[1], 1)]
            E.tt('pool', hs2[0][:], hs2[0][:], hs2[1][:], ALU.add, hk, [(hs2[0], 0), (hs2[0], 1)])
            ob = obr.next()
            E.tt('dve', ob[:], hs2[0][:], gT[:], ALU.mult, [(hs2[0], 0), (hs2[0], 1), gT], [ob])
            S.dma('pool', g.oT_d[n], ob[:], ob, reads=[ob])
    return w_out


CH = 64
NCH = T // CH


def rwkv_mixer(g, l):
    S, E = g.S, g.E
    din, dscr = g.din, g.dscr
    mu_in = din("rwkv_muP", [128, 6, KC])
    w_in = din("rwkv_w_in", [4, D, D])
    w0_in = din("rwkv_w0P", [128, 2, KC])
    a0_in = din("rwkv_a0P", [128, 2, KC])
    w1_in = din("rwkv_w1", [2, D, 96])
    w2_in = din("rwkv_w2", [2, 96, D])
    a1_in = din("rwkv_a1", [2, D, 96])
    a2_in = din("rwkv_a2", [2, 96, D])
    kk_in = din("rwkv_kkP", [128, KC])
    ka_in = din("rwkv_kaP", [128, KC])
    rk_in = din("rwkv_rkP", [128, KC])
    lng_in = din("rwkv_lngS", [2, 1024])
    lnb_in = din("rwkv_lnbS", [2, 1024])
    w_out = din("rwkv_w_out_perm", [D, D])
    mask4_in = din("rw_mask4", [2, 128, 512], BF16)
    maskT_in = din("rw_maskT", [2, 128, 128], BF16)
    bdm_in = din("rw_bdmask", [128, 128], BF16)
    lvm_in = din("rw_lvmask", [4, 128, 128], BF16)
    bones_in = din("rw_bones", [128, 128], BF16)
    r_d = dscr("rw_r", [KC, 128, T], F32)
    k_d = dscr("rw_k", [KC, 128, T], F32)
    a_d = dscr("rw_a", [2, KC, 128, T], F32)
    lw_d = dscr("rw_lw", [2, KC, 128, T], F32)
    v_st = dscr("rw_v", [2, T, 1024], BF16)
    sg_st = dscr("rw_sg", [2, T, 1024], BF16)
    X_d = dscr("rw_X", [2, NCH, 128, 16 * 4 * CH], BF16)
    rkr_d = dscr("rw_rkr", [NCH, 128, 16 * CH], BF16)
    o_st = dscr("rw_of", [2, T, 1024], F32)
    of_st = dscr("rw_ofin", [2, T, 1024], BF16)
    wc_d = dscr("rw_wc", [16, 128, KC * 512], BF16)
    w1c_d = dscr("rw_w1c", [4, 128, KC * 96], BF16)
    w2c_d = dscr("rw_w2c", [4, 96, D], BF16)
    cont = g.cont
    pcg = S.sb(g.glob, [128, 2, NCH, 16], F32, "pcg")

    for b in range(8):
      with Stage(S) as st:
        T0 = b * 512
        hx = st.sb([128, KC, 514], BF16, "hx")
        lo, hi = max(T0 - 1, 0), min(T0 + 513, T)
        S.dma('sp', hx[:, :, lo - T0 + 1:hi - T0 + 1], fmv(g.hT_d)[:, :, lo:hi], hx, writes=[hx])
        if b == 0:
            E.memset('dve', hx[:, :, 0:1], 0.0, [hx])
        if b == 7:
            E.memset('dve', hx[:, :, 513:514], 0.0, [hx])
        if b == 4:
            E.ts('dve', hx[:, :, 0:1], hx[:, :, 0:1], cont[:, 0:1], None, ALU.mult, None, [hx, cont], [hx])
        if b == 3:
            E.ts('dve', hx[:, :, 513:514], hx[:, :, 513:514], cont[:, 0:1], None, ALU.mult, None, [hx, cont], [hx])
        mu = st.sb([128, 6, KC], F32, "mu")
        S.dma('sp', mu[:], mu_in[:, :, :], mu, writes=[mu])
        bias0 = st.sb([128, 2, 2, KC], F32, "bias0")
        S.dma('sp', bias0[:, 0], w0_in[:, :, :], bias0, writes=[bias0])
        S.dma('sp', bias0[:, 1], a0_in[:, :, :], bias0, writes=[bias0])
        xx = st.sb([128, KC, 512], F32, "xx")
        tmpr = st.ring([128, 512], F32, 2, name="tmp")
        for kc in range(KC):
            tm = tmpr.next()
            E.tt('dve', tm[:], hx[:, kc, 0:512], hx[:, kc, 2:514], ALU.add, [hx], [tm])
            E.stt(xx[:, kc, :], tm[:], 0.5, hx[:, kc, 1:513], ALU.mult, ALU.subtract, [tm, hx], [(xx, kc)])
        xsr = st.ring([128, KC, 512], BF16, 2, name="xs")
        wl = WLoader(S, E, st, nb=256)
        wq = st.ring([128, KC, 512], BF16, 2, name="wq")
        pz = st.ring([128, 512], F32, 4, psum=True)
        pl = st.ring([128, 512], F32, 2, psum=True)

        def cached(tile, flat, cache_ap, fill):
            if b == 0:
                fill()
                S.dma('pool', cache_ap, flat, tile, reads=[tile])
            else:
                S.dma('sp', flat, cache_ap, tile, writes=[tile])

        fstr = st.ring([128, 4, 512], F32, 2, name="fst")
        vstr = [st.sb([128, 2, 16, 64], BF16, f"vst{i}") for i in range(4)]
        for m in range(6):
            xs = xsr.next()
            for kc in range(KC):
                E.stt(xs[:, kc, :], xx[:, kc, :], mu[:, m, kc:kc + 1], hx[:, kc, 1:513], ALU.mult, ALU.add,
                      [(xx, kc), mu, hx], [(xs, kc)])
            xk = [(xs, kc) for kc in range(KC)]
            if m < 2:
                dst_d = r_d if m == 0 else k_d
                for gq in range(4):
                    w = wq.next()
                    cached(w, w[:].rearrange("p c n -> p (c n)"), wc_d[m * 4 + gq],
                           lambda w=w, gq=gq: wl.load(w, lambda c0, c1, w=w: w[:, :, c0:c1],
                                                      wview(w_in[m])[:, :, gq * 512:(gq + 1) * 512], KC, 512))
                    fs = fstr.next()
                    for fc in range(4):
                        p = pz.next()
                        for kc in range(KC):
                            E.mm(p[:], w[:, kc, fc * 128:(fc + 1) * 128], xs[:, kc, :], kc == 0, kc == KC - 1,
                                 [w, (xs, kc)], [p])
                        E.act(fs[:, fc, :], p[:], AF.Copy, [p], [(fs, fc)])
                    S.dma('pool', fmv(dst_d)[:, gq * 4:gq * 4 + 4, T0:T0 + 512], fs[:], fs,
                          reads=[(fs, f) for f in range(4)])
            elif m < 4:
                dst_d = v_st if m == 2 else sg_st
                for n4 in range(4):
                    w = wq.next()
                    cached(w, w[:].rearrange("p c n -> p (c n)"), wc_d[m * 4 + n4],
                           lambda w=w, n4=n4: wl.load(w, lambda c0, c1, w=w: w[:, :, c0:c1],
                                                      wview(w_in[m])[:, :, n4 * 512:(n4 + 1) * 512], KC, 512))
                    for tt in range(4):
                        p = pz.next()
                        for kc in range(KC):
                            E.mm(p[:], xs[:, kc, tt * 128:(tt + 1) * 128], w[:, kc, :], kc == 0, kc == KC - 1,
                                 [w, (xs, kc)], [p])
                        vs = vstr[tt]
                        E.act(vs[:, :, n4 * 4:n4 * 4 + 4, :].rearrange("p h q i -> p q h i"),
                              p[:].rearrange("p (q h i) -> p q h i", h=2, i=64),
                              AF.Copy if m == 2 else AF.Silu, [p], [(vs, n4)])
                for tt in range(4):
                    for hh in range(2):
                        S.dma('pool', dst_d[hh, T0 + tt * 128:T0 + (tt + 1) * 128, :],
                              vstr[tt][:, hh, :, :].rearrange("p q i -> p (q i)"), vstr[tt],
                              reads=[(vstr[tt], n) for n in range(4)])
            else:
                which = m - 4
                l1_in, l2_in = (w1_in, w2_in) if which == 0 else (a1_in, a2_in)
                dst_d = lw_d if which == 0 else a_d
                for dr in range(2):
                    w1b = st.sb([128, KC, 96], BF16, "w1b") if (m == 4 and dr == 0) else w1b
                    cached(w1b, w1b[:].rearrange("p c n -> p (c n)"), w1c_d[which * 2 + dr],
                           lambda dr=dr: wl.load(w1b, lambda c0, c1: w1b[:, :, c0:c1], wview(l1_in[dr]), KC, 96))
                    p1 = pl.next()
                    for kc in range(KC):
                        E.mm(p1[0:96, :], w1b[:, kc, :], xs[:, kc, :], kc == 0, kc == KC - 1, [w1b, (xs, kc)], [p1])
                    tw = st.sb([96, 512], BF16, "tw") if (m == 4 and dr == 0) else tw
                    E.act(tw[:], p1[0:96, :], AF.Tanh if which == 0 else AF.Copy, [p1], [tw])
                    w2s = st.sb([96, D], F32, "w2s") if (m == 4 and dr == 0) else w2s
                    w2b = st.sb([96, D], BF16, "w2b") if (m == 4 and dr == 0) else w2b
                    def fill2(dr=dr):
                        S.dma('sp', w2s[:], l2_in[dr], w2s, writes=[w2s])
                        E.cp('pool', w2b[:], w2s[:], [w2s], [w2b])
                    cached(w2b, w2b[:], w2c_d[which * 2 + dr], fill2)
                    for gq in range(4):
                        fs = fstr.next()
                        for fc in range(4):
                            oc = gq * 4 + fc
                            p = pz.next()
                            E.mm(p[:], w2b[0:96, oc * 128:(oc + 1) * 128], tw[0:96, :], True, True, [w2b, tw], [p])
                            E.act(fs[:, fc, :], p[:], AF.Sigmoid, [p, bias0], [(fs, fc)],
                                  bias=bias0[:, which, dr, oc:oc + 1])
                            if which == 0:
                                E.ts('pool', fs[:, fc, :], fs[:, fc, :], -math.exp(-0.5), None, ALU.mult, None,
                                     [(fs, fc)], [(fs, fc)])
                        S.dma('pool', fmv(dst_d[dr])[:, gq * 4:gq * 4 + 4, T0:T0 + 512], fs[:], fs,
                              reads=[(fs, f) for f in range(4)])
    if g.stop < 3:
        return w_out

    with Stage(S) as st:
        vecs = st.sb([128, 3, KC], F32, "vecs")
        S.dma('sp', vecs[:, 0], kk_in[:, :], vecs, writes=[vecs])
        S.dma('sp', vecs[:, 1], ka_in[:, :], vecs, writes=[vecs])
        S.dma('sp', vecs[:, 2], rk_in[:, :], vecs, writes=[vecs])
        bones = st.sb([128, 128], BF16, "bones")
        S.dma('sp', bones[:], bones_in[:, :], bones, writes=[bones])
        cm = st.sb([128, 2, 4, 256], F32, "cm")
        E.memset('dve', cm[:], 1.0, [cm])
        for q in range(4):
            E.memset('dve', cm[:, 0, q, 0:256:64], 0.0, [cm])
            E.memset('dve', cm[:, 1, q, 63:256:64], 0.0, [cm])
        Xst = [st.sb([128, 4, 16, 4, CH], BF16, f"Xst{d}") for d in range(2)]
        rkst = st.sb([128, 4, 16, CH], BF16, "rkst")
        inr = st.ring([128, 6, 4, 256], F32, 2, name="inp")
        pn = st.ring([128, 512], F32, 2, psum=True)
        R = lambda nm, dt=F32, n=1: st.ring([128, 4, 256], dt, n, name=nm)
        nkr = R("nkkn")
        kkr, sqr, nrr, kknr, cr, e1r, e2r, e3r, t1r, kdr, kdsr = (R("kk"), R("sq", BF16), R("nr"), R("kkn"), R("c"),
                                                                 R("e1"), R("e2"), R("e3"), R("t1", F32, 3), R("kd"),
                                                                 R("kds"))
        v5 = lambda ap: ap.rearrange("p q (c t) -> p c q t", t=CH)
        for blk in range(16):
            tsl = slice(blk * 256, (blk + 1) * 256)
            for pg in range(4):
                it_ = inr.next()
                srcs = [r_d, k_d, a_d[0], a_d[1], lw_d[0], lw_d[1]]
                for i_, sd_ in enumerate(srcs):
                    S.dma('sp', it_[:, i_], fmv(sd_)[:, pg * 4:pg * 4 + 4, tsl], it_, writes=[(it_, i_)])
                ps4 = slice(pg * 4, pg * 4 + 4)
                vb = lambda j: vecs[:, j, ps4].unsqueeze(2).to_broadcast([128, 4, 256])
                r_, k_ = it_[:, 0], it_[:, 1]
                kk, sq, nr, kkn, kds = kkr.next(), sqr.next(), nrr.next(), kknr.next(), kdsr.next()
                E.tt('pool', kk[:], k_, vb(0), ALU.mult, [(it_, 1), vecs], [kk])
                E.tt('pool', sq[:], kk[:], kk[:], ALU.mult, [kk], [sq])
                for hq in range(2):
                    p = pn.next()
                    E.mm(p[:], bones[:], sq[:, 2 * hq:2 * hq + 2, :].rearrange("p q t -> p (q t)"), True, True,
                         [bones, sq], [p])
                    E.act(nr[:, 2 * hq:2 * hq + 2, :].rearrange("p q t -> p (q t)"), p[:], AF.Sqrt, [p], [(nr, hq)])
                nk = [(nr, 0), (nr, 1)]
                E.ts('dve', nr[:], nr[:], 1e-12, None, ALU.max, None, nk, nk)
                E.recip(nr[:], nr[:], nk, nk)
                E.tt('dve', kkn[:], kk[:], nr[:], ALU.mult, [kk] + nk, [kkn])
                nkkn = nkr.next()
                E.ts('pool', nkkn[:], kkn[:], -1.0, None, ALU.mult, None, [kkn], [nkkn])
                for dr in range(2):
                    a_, lw_ = it_[:, 2 + dr], it_[:, 4 + dr]
                    c, e1, e2, e3, t1, kd = cr.next(), e1r.next(), e2r.next(), e3r.next(), t1r.next(), kdr.next()
                    fl = lambda ap: ap.rearrange("p q t -> p (q t)")
                    if dr == 0:
                        E.scan(fl(c[:]), fl(cm[:, 0]), fl(lw_), 0.0, [cm, (it_, 4)], [c])
                    else:
                        E.scan(fl(c[:])[:, ::-1], fl(cm[:, 1])[:, ::-1], fl(lw_)[:, ::-1], 0.0, [cm, (it_, 5)], [c])
                    E.act(e1[:], c[:], AF.Exp, [c], [e1])
                    E.tt('pool', t1[:], c[:], lw_, ALU.subtract, [c, (it_, 4 + dr)], [t1])
                    E.act(e2[:], t1[:], AF.Exp, [t1], [e2])
                    E.act(e3[:], c[:], AF.Exp, [c], [e3], scale=-1.0)
                    X = Xst[dr]
                    E.tt('dve', X[:, :, ps4, 1, :], v5(r_), v5(e1[:]), ALU.mult, [(it_, 0), e1], [(X, pg, 1)])
                    E.tt('dve', X[:, :, ps4, 0, :], v5(nkkn[:]), v5(e2[:]), ALU.mult, [nkkn, e2], [(X, pg, 0)])
                    t2 = t1r.next()
                    E.tt('pool', t2[:], kkn[:], a_, ALU.mult, [kkn, (it_, 2 + dr)], [t2])
                    E.tt('dve', X[:, :, ps4, 2, :], v5(t2[:]), v5(e3[:]), ALU.mult, [t2, e3], [(X, pg, 2)])
                    t3 = t1r.next()
                    E.ts('dve', t3[:], a_, -1.0, None, ALU.add, None, [(it_, 2 + dr)], [t3])
                    E.tt('pool', t3[:], t3[:], vb(1), ALU.mult, [t3, vecs], [t3])
                    E.stt(kd[:], t3[:], 1.0, k_, ALU.add, ALU.mult, [t3, (it_, 1)], [kd])
                    E.tt('dve', X[:, :, ps4, 3, :], v5(kd[:]), v5(e3[:]), ALU.mult, [kd, e3], [(X, pg, 3)])
                    col = 63 if dr == 0 else 0
                    E.cp('pool', pcg[:, dr, blk * 4:blk * 4 + 4, ps4], e1[:, :, col:256:64].rearrange("p q c -> p c q"),
                         [e1], [(pcg, dr, blk, pg)])
                    if dr == 0:
                        E.cp('pool', kds[:], kd[:], [kd], [kds])
                    else:
                        E.tt('pool', kds[:], kds[:], kd[:], ALU.add, [kds, kd], [kds])
                E.tt('pool', kds[:], kds[:], vb(2), ALU.mult, [kds, vecs], [kds])
                E.tt('dve', rkst[:, :, ps4, :], v5(kds[:]), v5(r_), ALU.mult, [kds, (it_, 0)], [(rkst, pg)])
            for dr in range(2):
                X = Xst[dr]
                S.dma('pool', X_d[dr, blk * 4:blk * 4 + 4].rearrange("c p x -> p c x"),
                      X[:].rearrange("p c q x t -> p c (q x t)"), X,
                      reads=[(X, pg_, x) for pg_ in range(4) for x in range(4)])
            S.dma('pool', rkr_d[blk * 4:blk * 4 + 4].rearrange("c p x -> p c x"),
                  rkst[:].rearrange("p c q t -> p c (q t)"), rkst, reads=[(rkst, pg_) for pg_ in range(4)])

    import os
    if os.environ.get("RW_STOP") == "C1":
        return w_out
    for dr in range(1 if os.environ.get("RW_STOP") == "C2f" else 2):
      with Stage(S) as st:
        mask4 = st.sb([128, 512], BF16, "mask4")
        maskT = st.sb([128, 128], BF16, "maskT")
        bdm = st.sb([128, 2, 64], BF16, "bdm")
        S.dma('sp', mask4[:], mask4_in[dr], mask4, writes=[mask4])
        S.dma('sp', maskT[:], maskT_in[dr], maskT, writes=[maskT])
        S.dma('sp', bdm[:].rearrange("p h t -> p (h t)"), bdm_in[:, :], bdm, writes=[bdm])
        S32 = st.sb([128, 16, 64], F32, "S32")
        Sb = st.sb([128, 16, 64], BF16, "Sb")
        E.memset('dve', S32[:], 0.0, [(S32, 0), (S32, 1)])
        E.memset('dve', Sb[:], 0.0, [(Sb, 0), (Sb, 1)])
        Xr = st.ring([128, 16, 4, CH], BF16, 2, name="X")
        Vr = st.ring([128, 16, 64], BF16, 3, name="V")
        Ostr = st.ring([128, 16, 64], F32, 2, name="Ost")
        tmpr = st.ring([128, 8, 64], F32, 2, name="stmp")
        two = lambda shape, nm, dt=BF16: [st.sb(shape, dt, nm)] * 2
        ARs, Bbs, Kbs = two([128, 16, 256], "AR"), two([128, 16, 128], "Bb"), two([128, 16, 128], "Kb")
        G1s = two([128, 16, 512], "G1")
        sq16 = lambda nm: st.sb([128, 16, 128], BF16, nm)
        NTt, Nd, NTd, No, NTo, Mt, MTt, Tt, TTt, Yt, Zt = [sq16(n_) for n_ in
                                                          ("NT", "Nd", "NTd", "No", "NTo", "M", "MT", "T", "TT", "Y", "Z")]
        lvm = st.sb([128, 4, 128], BF16, "lvm")
        for i_ in range(4):
            S.dma('sp', lvm[:, i_, :], lvm_in[i_], lvm, writes=[lvm])
        Tbs = two([128, 16, 128], "Tb")
        BKTs, XTs, UTs = two([128, 16, 2, 128], "BKT"), two([128, 16, 64], "XT"), two([128, 16, 64], "UT")
        P1r = st.ring([128, 512], F32, 2, psum=True)
        Qr = st.ring([128, 512], F32, 4, psum=True)
        Sr = st.ring([128, 512], F32, 2, psum=True)
        if dr == 1:
            lng = st.sb([128, 16, 64], F32, "lng")
            lnb = st.sb([128, 16, 64], F32, "lnb")
            for hh in range(2):
                S.dma('sp', lng[hh * 64:(hh + 1) * 64].rearrange("p q i -> p (q i)"),
                      lng_in[hh:hh + 1, :].partition_broadcast(64), lng, writes=[lng])
                S.dma('sp', lnb[hh * 64:(hh + 1) * 64].rearrange("p q i -> p (q i)"),
                      lnb_in[hh:hh + 1, :].partition_broadcast(64), lnb, writes=[lnb])
            Ofr = st.ring([128, 16, 64], F32, 1, name="Of")
            sgr = st.ring([128, 16, 64], BF16, 2, name="sg")
            rkrr = st.ring([128, 16, CH], BF16, 2, name="rkr")
            rkbd = st.ring([128, 16, 2, 64], BF16, 1, name="rkbd")
            o2r = st.ring([128, 16, 64], F32, 1, name="o2")
            o3r = st.ring([128, 16, 64], F32, 1, name="o3")
            ofr = st.ring([128, 16, 64], BF16, 2, name="ofin")
            stt_r = st.ring([128, 8, 16], F32, 2, name="gnst")
        order = list(range(NCH)) if dr == 0 else list(range(NCH - 1, -1, -1))
        bd4 = bdm[:].unsqueeze(1).to_broadcast([128, 16, 2, 64])
        G4 = [list(range(4 * g_, 4 * g_ + 4)) for g_ in range(4)]
        q4 = lambda bank: bank[:].rearrange("p (j t) -> p j t", t=128)
        q8 = lambda bank: bank[:].rearrange("p (j t) -> p j t", t=64)

        def phaseG(ci, c):
            par = ci % 2
            AR, Bb, Kb, G1, BKT, Tb = ARs[par], Bbs[par], Kbs[par], G1s[par], BKTs[par], Tbs[par]
            X, V = Xr.next(), Vr.next()
            S.dma('sp', X[:].rearrange("p q x t -> p (q x t)"), X_d[dr, c], X, writes=[X])
            for hh in range(2):
                S.dma('sp', V[hh * 64:(hh + 1) * 64].rearrange("p q i -> p (q i)"),
                      v_st[hh, c * CH:(c + 1) * CH, :], V, writes=[V])
            x4 = lambda x: X[:, :, x, :].unsqueeze(2).to_broadcast([128, 16, 2, 64])
            ARv = AR[:].rearrange("p q (x h t) -> p q x h t", x=2, h=2)
            E.tt('pool', ARv[:, :, 0], x4(0), bd4, ALU.mult, [X, bdm], [(AR, 0)])
            E.tt('pool', ARv[:, :, 1], x4(1), bd4, ALU.mult, [X, bdm], [(AR, 1)])
            E.tt('pool', Bb[:].rearrange("p q (h t) -> p q h t", h=2), x4(2), bd4, ALU.mult, [X, bdm], [Bb])
            E.tt('pool', Kb[:].rearrange("p q (h t) -> p q h t", h=2), x4(3), bd4, ALU.mult, [X, bdm], [Kb])
            for q in range(16):
                p1 = P1r.next()
                E.mm(p1[:, 0:256], Bb[:, q, :], AR[:, q, :], True, True, [Bb, (AR, 0), (AR, 1)], [p1])
                E.mm(p1[:, 256:512], Kb[:, q, :], AR[:, q, :], True, True, [Kb, (AR, 0), (AR, 1)], [p1])
                E.tt('dve', G1[:, q, :], p1[:], mask4[:], ALU.mult, [p1, mask4], [(G1, q // 4)])
            m4 = lambda i_: lvm[:, i_, :].unsqueeze(1).to_broadcast([128, 4, 128])
            idb = g.ident[:].unsqueeze(1).to_broadcast([128, 4, 128])
            def chain(g_):
                gs_ = slice(4 * g_, 4 * g_ + 4)
                K_ = lambda t_: (t_, g_)

                def mm4(lhs, rhs, rk):
                    bank = Qr.next()
                    for j, q in enumerate(G4[g_]):
                        E.mm(bank[:, j * 128:(j + 1) * 128], lhs[:, q, :], rhs[:, q, :], True, True, rk, [bank])
                    return bank

                qb = Qr.next()
                for j, q in enumerate(G4[g_]):
                    E.mm(qb[:, j * 128:(j + 1) * 128], AR[:, q, 0:128], Bb[:, q, :], True, True, [(AR, 0), Bb], [qb])
                E.tt('dve', NTt[:, gs_, :], q4(qb), maskT[:].unsqueeze(1).to_broadcast([128, 4, 128]),
                     ALU.mult, [qb, maskT], [K_(NTt)])
                yield
                qt = Qr.next()
                qtb = qt[:].bitcast(BF16)
                for j, q in enumerate(G4[g_]):
                    E.tr(qtb[:, (2 * j) * 128:(2 * j + 1) * 128], Bb[:, q, :], g.ident[:], [Bb, g.ident], [qt])
                    E.tr(qtb[:, (2 * j + 1) * 128:(2 * j + 2) * 128], Kb[:, q, :], g.ident[:], [Kb, g.ident], [qt])
                E.act(BKT[:, gs_, :, :], qtb.rearrange("p (j w t) -> p j w t", w=2, t=128), AF.Copy, [qt], [K_(BKT)])
                Nn = G1[:, gs_, 0:128]
                E.tt('pool', Nd[:, gs_, :], Nn, m4(0), ALU.mult, [K_(G1), lvm], [K_(Nd)])
                E.tt('pool', NTd[:, gs_, :], NTt[:, gs_, :], m4(0), ALU.mult, [K_(NTt), lvm], [K_(NTd)])
                E.tt('pool', Tt[:, gs_, :], Nd[:, gs_, :], idb, ALU.add, [K_(Nd), g.ident], [K_(Tt)])
                E.tt('pool', TTt[:, gs_, :], NTd[:, gs_, :], idb, ALU.add, [K_(NTd), g.ident], [K_(TTt)])
                yield
                Mc, MTc = Nd, NTd
                for lvl in range(2):
                    ba = mm4(MTc, Mc, [K_(MTc), K_(Mc)])
                    bb = mm4(Mc, MTc, [K_(MTc), K_(Mc)])
                    E.act(Mt[:, gs_, :], q4(ba), AF.Copy, [ba], [K_(Mt)])
                    E.act(MTt[:, gs_, :], q4(bb), AF.Copy, [bb], [K_(MTt)])
                    yield
                    Mc, MTc = Mt, MTt
                    bc_ = mm4(MTc, Tt, [K_(MTc), K_(Tt)])
                    bd_ = mm4(Mc, TTt, [K_(Mc), K_(TTt)])
                    E.tt('dve', Tt[:, gs_, :], q4(bc_), Tt[:, gs_, :], ALU.add, [bc_, K_(Tt)], [K_(Tt)])
                    E.tt('dve', TTt[:, gs_, :], q4(bd_), TTt[:, gs_, :], ALU.add, [bd_, K_(TTt)], [K_(TTt)])
                    yield
                for mi in range(1, 4):
                    lastm = mi == 3
                    E.tt('pool', NTo[:, gs_, :], NTt[:, gs_, :], m4(mi), ALU.mult, [K_(NTt), lvm], [K_(NTo)])
                    by = mm4(NTo, Tt, [K_(NTo), K_(Tt)])
                    E.act(Yt[:, gs_, :], q4(by), AF.Copy, [by], [K_(Yt)])
                    if not lastm:
                        E.tt('pool', No[:, gs_, :], Nn, m4(mi), ALU.mult, [K_(G1), lvm], [K_(No)])
                        bz = mm4(No, TTt, [K_(No), K_(TTt)])
                        E.act(Zt[:, gs_, :], q4(bz), AF.Copy, [bz], [K_(Zt)])
                    yield
                    bc_ = mm4(TTt, Yt, [K_(TTt), K_(Yt)])
                    if not lastm:
                        bd_ = mm4(Tt, Zt, [K_(Tt), K_(Zt)])
                        E.tt('dve', Tt[:, gs_, :], q4(bc_), Tt[:, gs_, :], ALU.add, [bc_, K_(Tt)], [K_(Tt)])
                        E.tt('dve', TTt[:, gs_, :], q4(bd_), TTt[:, gs_, :], ALU.add, [bd_, K_(TTt)], [K_(TTt)])
                    else:
                        E.tt('dve', Tb[:, gs_, :], q4(bc_), Tt[:, gs_, :], ALU.add, [bc_, K_(Tt)], [K_(Tb)])
                    yield

            alive = [chain(g_) for g_ in range(4)]
            while alive:
                for gen in list(alive):
                    try:
                        next(gen)
                    except StopIteration:
                        alive.remove(gen)
            return V

        def phaseS(ci, c, V):
            par = ci % 2
            AR, G1, Tm, BKT, XT, UT = ARs[par], G1s[par], Tbs[par], BKTs[par], XTs[par], UTs[par]
            Ost = Ostr.next()
            H8 = [list(range(8 * h_, 8 * h_ + 8)) for h_ in range(2)]
            sk = lambda t_, h_: [(t_, 2 * h_), (t_, 2 * h_ + 1)]
            for h_ in range(2):
                sb = Sr.next()
                for j, q in enumerate(H8[h_]):
                    sl = sb[:, j * 64:(j + 1) * 64]
                    E.mm(sl, AR[:, q, 0:128], Sb[:, q, :], True, False, [(AR, 0), (Sb, h_)], [sb])
                    E.mm(sl, G1[:, q, 256:384], V[:, q, :], False, True, sk(G1, h_) + [V], [sb])
                E.act(XT[:, 8 * h_:8 * h_ + 8, :], q8(sb), AF.Copy, [sb], [(XT, h_)])
            for h_ in range(2):
                sb = Sr.next()
                for j, q in enumerate(H8[h_]):
                    E.mm(sb[:, j * 64:(j + 1) * 64], Tm[:, q, :], XT[:, q, :], True, True, sk(Tm, h_) + [(XT, h_)], [sb])
                E.act(UT[:, 8 * h_:8 * h_ + 8, :], q8(sb), AF.Copy, [sb], [(UT, h_)])
            for h_ in range(2):
                sb = Sr.next()
                for j, q in enumerate(H8[h_]):
                    sl = sb[:, j * 64:(j + 1) * 64]
                    E.mm(sl, AR[:, q, 128:256], Sb[:, q, :], True, False, [(AR, 1), (Sb, h_)], [sb])
                    E.mm(sl, G1[:, q, 128:256], UT[:, q, :], False, False, sk(G1, h_) + [(UT, h_)], [sb])
                    E.mm(sl, G1[:, q, 384:512], V[:, q, :], False, True, sk(G1, h_) + [V], [sb])
                E.cp('dve', Ost[:, 8 * h_:8 * h_ + 8, :], q8(sb), [sb], [(Ost, h_)])
            for h_ in range(2):
                sb = Sr.next()
                for j, q in enumerate(H8[h_]):
                    sl = sb[:, j * 64:(j + 1) * 64]
                    E.mm(sl, BKT[:, q, 0, :], UT[:, q, :], True, False, sk(BKT, h_) + [(UT, h_)], [sb])
                    E.mm(sl, BKT[:, q, 1, :], V[:, q, :], False, True, sk(BKT, h_) + [V], [sb])
                pcb = pcg[:, dr, c, 8 * h_:8 * h_ + 8].unsqueeze(2).to_broadcast([128, 8, 64])
                pk = [(pcg, dr, c // 4, q // 4) for q in H8[h_]]
                S8 = S32[:, 8 * h_:8 * h_ + 8, :]
                tm = tmpr.next()
                E.tt('pool', S8, S8, pcb, ALU.mult, [(S32, h_)] + pk, [(S32, h_)])
                E.tt('dve', tm[:], q8(sb), pcb, ALU.mult, [sb] + pk, [tm])
                E.tt('pool', S8, S8, tm[:], ALU.add, [(S32, h_), tm], [(S32, h_)])
                E.act(Sb[:, 8 * h_:8 * h_ + 8, :], S8, AF.Copy, [(S32, h_)], [(Sb, h_)])
            return Ost

        def reset_state():
            allk = [(S32, 0), (S32, 1)]
            E.ts('pool', S32[:], S32[:], cont[:, 0:1], None, ALU.mult, None, allk + [cont], allk)
            E.act(Sb[:], S32[:], AF.Copy, allk, [(Sb, 0), (Sb, 1)])

        def finalize(c, Ost, V):
            ok = [(Ost, 0), (Ost, 1)]
            if dr == 0:
                for hh in range(2):
                    S.dma('pool', o_st[hh, c * CH:(c + 1) * CH, :],
                          Ost[hh * 64:(hh + 1) * 64].rearrange("p q i -> p (q i)"), Ost, reads=ok)
                return
            Of, sg, rk, o2, o3, ofin, gs, rb = (Ofr.next(), sgr.next(), rkrr.next(), o2r.next(), o3r.next(), ofr.next(),
                                                stt_r.next(), rkbd.next())
            for hh in range(2):
                S.dma('sp', Of[hh * 64:(hh + 1) * 64].rearrange("p q i -> p (q i)"),
                      o_st[hh, c * CH:(c + 1) * CH, :], Of, writes=[Of])
                S.dma('sp', sg[hh * 64:(hh + 1) * 64].rearrange("p q i -> p (q i)"),
                      sg_st[hh, c * CH:(c + 1) * CH, :], sg, writes=[sg])
            S.dma('sp', rk[:].rearrange("p q t -> p (q t)"), rkr_d[c], rk, writes=[rk])
            E.tt('pool', rb[:], rk[:].unsqueeze(2).to_broadcast([128, 16, 2, 64]), bd4, ALU.mult, [rk, bdm], [rb])
            sb = Sr.next()
            for q in range(16):
                E.mm(sb[:, q:q + 1], rb[:, q].rearrange("p h t -> p (h t)"), g.ones_b[:, 0:1], True, True,
                     [rb, g.ones_b], [sb])
            E.cp('dve', gs[:, 7, :], sb[:, 0:16], [sb], [(gs, 7)])
            E.tt('pool', o2[:], Ost[:], Of[:], ALU.add, ok + [Of], [o2])
            bc = lambda j: gs[:, j, :].unsqueeze(2).to_broadcast([128, 16, 64])
            E.red(gs[:, 0, :], o2[:], ALU.add, [o2], [(gs, 0)])
            E.ts('dve', gs[:, 2, :], gs[:, 0, :], 1.0 / 64, None, ALU.mult, None, [(gs, 0)], [(gs, 2)])
            E.tt('dve', o2[:], o2[:], bc(2), ALU.subtract, [o2, (gs, 2)], [o2])
            E.tt('pool', o3[:], o2[:], o2[:], ALU.mult, [o2], [o3])
            E.red(gs[:, 1, :], o3[:], ALU.add, [o3], [(gs, 1)])
            E.act(gs[:, 5, :], gs[:, 1, :], AF.Sqrt, [(gs, 1)], [(gs, 5)], scale=1.0 / 64, bias=64e-5)
            E.recip(gs[:, 6, :], gs[:, 5, :], [(gs, 5)], [(gs, 6)])
            E.tt('dve', o2[:], o2[:], bc(6), ALU.mult, [o2, (gs, 6)], [o2])
            E.tt('pool', o2[:], o2[:], lng[:], ALU.mult, [o2, lng], [o2])
            E.tt('pool', o2[:], o2[:], lnb[:], ALU.add, [o2, lnb], [o2])
            E.tt('dve', o3[:], V[:], bc(7), ALU.mult, [V, (gs, 7)], [o3])
            E.tt('pool', o2[:], o2[:], o3[:], ALU.add, [o2, o3], [o2])
            E.tt('dve', ofin[:], o2[:], sg[:], ALU.mult, [o2, sg], [ofin])
            for hh in range(2):
                S.dma('pool', of_st[hh, c * CH:(c + 1) * CH, :],
                      ofin[hh * 64:(hh + 1) * 64].rearrange("p q i -> p (q i)"), ofin, reads=[ofin])

        for ci, c in enumerate(order):
            V = phaseG(ci, c)
            if ci == NCH // 2:
                reset_state()
            Ost = phaseS(ci, c, V)
            finalize(c, Ost, V)

    with Stage(S) as st:
        inr = st.ring([128, 1024], BF16, 3, name="oin")
        ptr = st.ring([128, 8, 128], BF16, 2, psum=True)
        ostr = [st.ring([128, 8, 512], BF16, 2, name=f"ost{hh}") for hh in range(2)]
        for grp in range(8):
            os_ = [ostr[hh].next() for hh in range(2)]
            for sub in range(4):
                tt = grp * 4 + sub
                for hh in range(2):
                    it_ = inr.next()
                    S.dma('sp', it_[:], of_st[hh, tt * 128:(tt + 1) * 128, :], it_, writes=[it_])
                    pt = ptr.next()
                    for q in range(8):
                        E.tr(pt[:, q, :], it_[:, q * 128:(q + 1) * 128], g.ident[:], [it_, g.ident], [pt])
                    if hh == 0:
                        E.cp('dve', os_[hh][:, :, sub * 128:(sub + 1) * 128], pt[:], [pt], [(os_[hh], sub)])
                    else:
                        E.act(os_[hh][:, :, sub * 128:(sub + 1) * 128], pt[:], AF.Copy, [pt], [(os_[hh], sub)])
            for hh in range(2):
                S.dma('pool', fmv(g.oT_d)[:, hh * 8:hh * 8 + 8, grp * 512:(grp + 1) * 512], os_[hh][:], os_[hh],
                      reads=[(os_[hh], s_) for s_ in range(4)])
    return w_out


def core_inputs_more(n, inp, pos, cont):
    if n == 'diff_w_in':
        return inp['diff_w_in'][0]
    if n == 'diff_lam':
        return inp['diff_lambda'][0].reshape(1, 256)
    if n == 'diff_gsubP':
        return _P(inp['diff_subln_g'][0])
    if n == 'diff_w_out':
        return inp['diff_w_out'][0]
    if n in ('diff_cos', 'diff_sin'):
        cs, sn = _rope_tables(pos, 16, 500000.0)
        t = (cs if n == 'diff_cos' else sn)[:, 0:8]
        return np.ascontiguousarray(t.reshape(NT, 128, 8).transpose(1, 0, 2))
    if n == 'lru_w_in':
        return inp['lru_w_in'][0]
    if n == 'lru_w_out':
        return inp['lru_w_out'][0]
    if n == 'lru_convP':
        cw = inp['lru_conv_w'][0]
        cb = inp['lru_conv_b'][0]
        a = np.concatenate([cw, cb[None, :]], axis=0)
        return np.ascontiguousarray(a.reshape(5, KC, 128).transpose(2, 1, 0))
    if n == 'lru_gate_w':
        return np.ascontiguousarray(inp['lru_gate_w'][0].transpose(3, 0, 1, 2, 4))
    if n == 'lru_gate_bP':
        return np.ascontiguousarray(inp['lru_gate_b'][0].reshape(2, 2, KC, 128).transpose(3, 0, 1, 2))
    if n == 'lru_lamP':
        return np.ascontiguousarray(inp['lru_lambda'][0].reshape(2, KC, 128).transpose(2, 0, 1))
    return core_inputs_rwkv(n, inp, pos, cont)


def _tri_masks():
    p = np.arange(128)
    same = (p[:, None] // 64) == (p[None, :] // 64)
    s_, t_ = p[:, None] % 64, p[None, :] % 64
    out4, outT = [], []
    for dr in range(2):
        strict = same & ((s_ < t_) if dr == 0 else (s_ > t_))
        incl = same & ((s_ <= t_) if dr == 0 else (s_ >= t_))
        out4.append(np.concatenate([strict, incl, strict, incl], axis=1))
        outT.append(strict.T)
    return np.stack(out4).astype(np.float32), np.stack(outT).astype(np.float32), same.astype(np.float32)


def core_inputs_rwkv(n, inp, pos, cont):
    if n == 'rwkv_muP':
        return np.ascontiguousarray(inp['rwkv_mu'][0].reshape(6, KC, 128).transpose(2, 0, 1))
    if n == 'rwkv_w_in':
        return inp['rwkv_w_in'][0]
    if n in ('rwkv_w0P', 'rwkv_a0P'):
        k = 'rwkv_w0' if n == 'rwkv_w0P' else 'rwkv_a0'
        return np.ascontiguousarray(inp[k][0].reshape(2, KC, 128).transpose(2, 0, 1))
    if n in ('rwkv_w1', 'rwkv_w2', 'rwkv_a1', 'rwkv_a2'):
        return inp[n][0]
    if n in ('rwkv_kkP', 'rwkv_kaP', 'rwkv_rkP'):
        k = {'rwkv_kkP': 'rwkv_k_k', 'rwkv_kaP': 'rwkv_k_a', 'rwkv_rkP': 'rwkv_r_k'}[n]
        return _P(inp[k][0].reshape(-1))
    if n in ('rwkv_lngS', 'rwkv_lnbS'):
        v = inp['rwkv_ln_g' if n == 'rwkv_lngS' else 'rwkv_ln_b'][0]
        return np.ascontiguousarray(v.reshape(16, 2, 64).transpose(1, 0, 2).reshape(2, 1024))
    if n == 'rwkv_w_out_perm':
        cp, e, i = np.meshgrid(np.arange(16), np.arange(2), np.arange(64), indexing='ij')
        hh, q = cp // 8, cp % 8
        perm = ((2 * q + e) * 128 + hh * 64 + i).reshape(-1)
        return np.ascontiguousarray(inp['rwkv_w_out'][0][perm, :])
    if n == 'rw_mask4':
        return _bf(_tri_masks()[0])
    if n == 'rw_maskT':
        return _bf(_tri_masks()[1])
    if n == 'rw_lvmask':
        p = np.arange(128)
        blk = lambda b: ((p[:, None] // b) == (p[None, :] // b)).astype(np.float32)
        return _bf(np.stack([blk(8), blk(16) - blk(8), blk(32) - blk(16), blk(64) - blk(32)]))
    if n in ('rw_bdmask', 'rw_bones'):
        return _bf(_tri_masks()[2])
    raise KeyError(n)


def _P(v):
    return np.ascontiguousarray(np.asarray(v, np.float32).reshape(-1, 128).T)


def _bf(a):
    return np.asarray(a, np.float32).astype(ml_dtypes.bfloat16)


def _rope_tables(pos, dim, theta):
    inv = (1.0 / (np.float32(theta) ** (np.arange(0, dim, 2, dtype=np.float32) / np.float32(dim)))).astype(np.float32)
    ang = pos.astype(np.float32)[:, None] * inv[None, :]
    ang = np.concatenate([ang, ang], axis=-1).astype(np.float32)
    return np.cos(ang).astype(np.float32), np.sin(ang).astype(np.float32)


def core_inputs(i, inp, names):
    if i < 4:
        x = inp['x_sample'][i]
        c2 = np.stack([inp['c_sample'][i], inp['c_sample'][i]])
        cont = 1.0
        pos = np.arange(T)
    else:
        j = i - 4
        x = np.concatenate([inp['x_prompt'][2 * j], inp['x_prompt'][2 * j + 1]], axis=0)
        c2 = np.stack([inp['c_prompt'][2 * j], inp['c_prompt'][2 * j + 1]])
        cont = 0.0
        pos = np.arange(T) % HALF
    hq = (np.arange(T) >= HALF).astype(np.float32)
    d = {}
    for n in names:
        if n == 'x':
            v = np.ascontiguousarray(x, np.float32)
        elif n == 'cT':
            v = np.ascontiguousarray(c2.reshape(2, KC, 128).transpose(2, 1, 0), np.float32)
        elif n == 'cont':
            v = np.tile(np.array([[cont, 1.0 - cont]], np.float32), (128, 1))
        elif n == 'seg_q':
            v = _bf(np.stack([BIG * hq, BIG * (1 - hq)]) * (1.0 - cont))
        elif n == 'seg_k':
            v = _bf(np.stack([-(1 - hq), -hq]))
        elif n == 'ident':
            v = _bf(np.eye(128))
        elif n == 'identf':
            v = np.eye(128, dtype=np.float32)
        elif n == 'ada_w':
            v = inp['ada_w']
        elif n == 'ada_bP':
            v = np.ascontiguousarray(np.stack([_P(inp['ada_b'][l]) for l in range(4)], axis=1))
        elif n == 'pre_gP':
            v = np.ascontiguousarray(np.stack([_P(inp['norm_pre_g'][l]) for l in range(4)], axis=1))
        elif n == 'post_g':
            v = inp['norm_post_g']
        elif n == 'mla_w_in':
            v = inp['mla_w_in'][0]
        elif n == 'mla_qgP':
            v = _P(inp['mla_q_norm_g'][0])
        elif n == 'mla_kvgP':
            v = _P(inp['mla_kv_norm_g'][0])
        elif n == 'mla_w_q_up':
            v = inp['mla_w_q_up'][0]
        elif n == 'mla_w_kv_up':
            v = inp['mla_w_kv_up'][0]
        elif n == 'mla_w_out':
            v = inp['mla_w_out'][0]
        elif n in ('mla_cosT', 'mla_sinT'):
            cs, sn = _rope_tables(pos, 64, 10000.0)
            v = np.ascontiguousarray((cs if n == 'mla_cosT' else sn).T)
        else:
            v = core_inputs_more(n, inp, pos, cont)
        d[n] = np.ascontiguousarray(v)
    return d


_PROG = {}


def run_layers(inputs, NL=4):
    if NL not in _PROG:
        _PROG[NL] = build(NL)
    g = _PROG[NL]
    inp = {k: np.asarray(v) for k, v in inputs.items()}
    in_maps = [core_inputs(i, inp, g.in_names) for i in range(8)]
    res = run_bass_kernel_spmd(g.nc, in_maps, core_ids=list(range(8)))
    return [r["y"] for r in res.results]


def kernel(**inputs):
    ys = run_layers(inputs, 4)
    y_sample = np.stack([np.asarray(ys[i], np.float32) for i in range(4)])
    y_prompt = np.stack([np.asarray(ys[4 + j // 2], np.float32)[(j % 2) * HALF:(j % 2 + 1) * HALF] for j in range(8)])
    return (y_prompt, y_sample)
```

```python
import contextlib
import math
import numpy as np
import ml_dtypes
import concourse.bass as bass
import concourse.mybir as mybir
from concourse.bass_utils import run_bass_kernel_spmd

F32 = mybir.dt.float32
BF16 = mybir.dt.bfloat16
AF = mybir.ActivationFunctionType
ALU = mybir.AluOpType
AX = mybir.AxisListType

EPOCH = 60000
DEPOCH = 3500
COMPUTE = ('pe', 'act', 'dve', 'pool')
QUEUES = COMPUTE + ('sp',)

T = 4096
D = 2048
KC = 16
NT = 32
HALF = 2048
BIG = 30000.0


class DSem:
    def __init__(self, S, name):
        self.S = S
        self.name = name
        self.handles = []
        self.counts = []
        self._new()

    def _new(self):
        self.handles.append(self.S._sem(f"d_{self.name}_{len(self.handles)}"))
        self.counts.append(0)


class Sched:
    def __init__(self, nc):
        self.nc = nc
        self.stack = contextlib.ExitStack()
        self.ops = {e: [] for e in QUEUES}
        self.csem = {e: [] for e in COMPUTE}
        self.ccnt = {e: [] for e in COMPUTE}
        self.last_w = {}
        self.readers = {}
        self.dsems = []
        self.free_ds = {}
        self.tile_ds = {}
        self.nsem = 0
        self.uid = 0
        self.waited = {e: {} for e in QUEUES}
        self.psum = {}
        self.last_acc = {}
        for e in COMPUTE:
            self._new_epoch(e)

    def _sem(self, name):
        self.nsem += 1
        return self.stack.enter_context(self.nc.semaphore(name))

    def _new_epoch(self, e):
        self.csem[e].append(self._sem(f"c_{e}_{len(self.csem[e])}"))
        self.ccnt[e].append(0)

    def get_ds(self, q):
        fl = self.free_ds.setdefault(q, [])
        if fl:
            return fl.pop()
        d = DSem(self, f"{q}{len(self.dsems)}")
        d.q = q
        self.dsems.append(d)
        return d

    def ds_of(self, tile, q):
        k = (id(tile), q)
        if k not in self.tile_ds:
            self.tile_ds[k] = self.get_ds(q)
        return self.tile_ds[k]

    def sb(self, sc, shape, dtype, name=None):
        self.uid += 1
        return sc.enter_context(self.nc.sbuf_tensor(f"{name or 't'}_{self.uid}", list(shape), dtype))

    def ps(self, sc, shape, dtype, name=None):
        self.uid += 1
        t = sc.enter_context(self.nc.psum_tensor(f"{name or 'p'}_{self.uid}", list(shape), dtype))
        nbytes = int(np.prod(shape[1:])) * (2 if dtype == BF16 else 4)
        assert nbytes % 2048 == 0, "PSUM tiles must be whole banks"
        self.psum[id(t)] = nbytes // 2048
        return t

    def _split(self, keys):
        norm, banks = [], []
        for k in keys:
            base = k[0] if isinstance(k, tuple) else k
            nb = self.psum.get(id(base)) if not isinstance(base, (str, int)) else None
            if nb is None:
                norm.append(self._key(k))
            elif nb == 1:
                banks.append(('B', id(base)))
            else:
                banks.append(('B', id(base), k[1]))
        return norm, banks

    @staticmethod
    def _key(k):
        if isinstance(k, tuple):
            return tuple(Sched._key(x) for x in k)
        if isinstance(k, (str, int)):
            return k
        return id(k)

    def _waits(self, eng, toks):
        w = {}
        for t in toks:
            if t[0] == 'c':
                _, e, ep, idx = t
                if e == eng and e == 'pe':
                    continue
                h = self.csem[e][ep]
                v = idx
            else:
                _, d, ep, cnt = t
                h = d.handles[ep]
                v = d.counts[ep]
            k = id(h)
            if k not in w or w[k][1] < v:
                w[k] = (h, v)
        return list(w.values())

    def _deps(self, reads, writes):
        toks = []
        for k in reads + writes:
            t = self.last_w.get(k)
            if t is not None:
                toks.append(t)
        for k in writes:
            r = self.readers.get(k)
            if r:
                toks.extend(r.values())
        return toks

    def _update(self, tok, reads, writes):
        for k in writes:
            self.last_w[k] = tok
            self.readers[k] = {}
        for k in reads:
            r = self.readers.setdefault(k, {})
            if tok[0] == 'c':
                r[(tok[1], tok[2])] = tok
            else:
                r[(id(tok[1]), tok[2])] = tok

    def op(self, eng, fn, reads=(), writes=()):
        reads, b1 = self._split(reads)
        writes, b2 = self._split(writes)
        toks = self._deps(reads, writes)
        for bk in b1 + b2:
            for e2, t in self.last_acc.get(bk, {}).items():
                if e2 != eng:
                    toks.append(t)
        waits = self._waits(eng, toks)
        if self.ccnt[eng][-1] >= EPOCH:
            self._new_epoch(eng)
        ep = len(self.ccnt[eng]) - 1
        self.ccnt[eng][ep] += 1
        tok = ('c', eng, ep, self.ccnt[eng][ep])
        self.ops[eng].append((waits, fn, (self.csem[eng][ep], 1)))
        self._update(tok, reads, writes)
        for bk in b1 + b2:
            self.last_acc.setdefault(bk, {})[eng] = tok
        return tok

    def dma(self, q, out, in_, tile, reads=(), writes=()):
        ds = self.ds_of(tile, q)
        reads = [self._key(k) for k in reads]
        writes = [self._key(k) for k in writes]
        waits = self._waits(q, self._deps(reads, writes))
        if ds.counts[-1] >= DEPOCH * 16:
            ds._new()
        ep = len(ds.counts) - 1
        ds.counts[ep] += 16
        tok = ('d', ds, ep, ds.counts[ep])
        self.ops[q].append((waits, (lambda e, o=out, i=in_: e.dma_start(out=o, in_=i)), (ds.handles[ep], 16)))
        self._update(tok, reads, writes)
        return tok

    def barrier(self):
        toks = []
        for e in COMPUTE:
            for ep in range(len(self.ccnt[e])):
                if self.ccnt[e][ep] > 0:
                    toks.append(('c', e, ep, self.ccnt[e][ep]))
        for d in self.dsems:
            for ep in range(len(d.counts)):
                if d.counts[ep] > 0:
                    toks.append(('d', d, ep, d.counts[ep]))
        for e in QUEUES:
            tk = [t for t in toks if not (t[0] == 'c' and t[1] == e)]
            self.ops[e].append((self._waits(e, tk), None, None))

    def release(self, tiles):
        for t in tiles:
            for q in QUEUES:
                k = (id(t), q)
                if k in self.tile_ds:
                    self.free_ds.setdefault(q, []).append(self.tile_ds.pop(k))

    def emit(self):
        self.barrier()
        nc = self.nc
        ops = self.ops
        self.ops = {e: [] for e in QUEUES}
        with nc.Block() as block:
            def run(e, name):
                waited = self.waited[name]
                for waits, fn, inc in ops[name]:
                    for h, v in waits:
                        k = id(h)
                        if waited.get(k, 0) < v:
                            e.wait_ge(h, v)
                            waited[k] = v
                    if fn is not None:
                        fn(e).then_inc(inc[0], inc[1])

            @block.tensor
            def _(e):
                run(e, 'pe')

            @block.scalar
            def _(e):
                run(e, 'act')

            @block.vector
            def _(e):
                run(e, 'dve')

            @block.gpsimd
            def _(e):
                run(e, 'pool')

            @block.sync
            def _(e):
                run(e, 'sp')


class Ring:
    def __init__(self, S, sc, shape, dtype, n, psum=False, name=None):
        self.tiles = [(S.ps if psum else S.sb)(sc, shape, dtype, name) for _ in range(n)]
        self.i = 0

    def next(self):
        t = self.tiles[self.i % len(self.tiles)]
        self.i += 1
        return t


class Stage:
    def __init__(self, S):
        self.S = S
        self.sc = contextlib.ExitStack()
        self.tiles = []

    def __enter__(self):
        self.sc.__enter__()
        return self

    def sb(self, shape, dtype, name=None):
        t = self.S.sb(self.sc, shape, dtype, name)
        self.tiles.append(t)
        return t

    def ps(self, shape, dtype, name=None):
        return self.S.ps(self.sc, shape, dtype, name)

    def ring(self, shape, dtype, n, psum=False, name=None):
        r = Ring(self.S, self.sc, shape, dtype, n, psum, name)
        if not psum:
            self.tiles.extend(r.tiles)
        return r

    def __exit__(self, *a):
        self.S.emit()
        self.S.release(self.tiles)
        return self.sc.__exit__(*a)


class K:
    pass


class Em:
    def __init__(self, S):
        self.S = S

    def mm(self, out, lhsT, rhs, start, stop, r, w):
        return self.S.op('pe', lambda e: e.matmul(out, lhsT=lhsT, rhs=rhs, start=start, stop=stop), r, w)

    def tr(self, out, in_, ident, r, w):
        return self.S.op('pe', lambda e: e.transpose(out=out, in_=in_, identity=ident), r, w)

    def act(self, out, in_, func, r, w, scale=1.0, bias=0.0, accum=None, eng='act'):
        if accum is None:
            return self.S.op(eng, lambda e: e.activation(out=out, in_=in_, func=func, scale=scale, bias=bias), r, w)
        return self.S.op(eng, lambda e: e.activation(out=out, in_=in_, func=func, scale=scale, bias=bias,
                                                     accum_out=accum), r, w)

    def tt(self, eng, out, in0, in1, op, r, w):
        return self.S.op(eng, lambda e: e.tensor_tensor(out=out, in0=in0, in1=in1, op=op), r, w)

    def ts(self, eng, out, in0, s1, s2, op0, op1, r, w):
        if s2 is None:
            return self.S.op(eng, lambda e: e.tensor_scalar(out=out, in0=in0, scalar1=s1, scalar2=None, op0=op0), r, w)
        return self.S.op(eng, lambda e: e.tensor_scalar(out=out, in0=in0, scalar1=s1, scalar2=s2, op0=op0, op1=op1), r, w)

    def stt(self, out, in0, scalar, in1, op0, op1, r, w):
        return self.S.op('dve', lambda e: e.scalar_tensor_tensor(out=out, in0=in0, scalar=scalar, in1=in1,
                                                                 op0=op0, op1=op1), r, w)

    def cp(self, eng, out, in_, r, w):
        return self.S.op(eng, lambda e: e.tensor_copy(out=out, in_=in_), r, w)

    def recip(self, out, in_, r, w):
        return self.S.op('dve', lambda e: e.reciprocal(out=out, in_=in_), r, w)

    def red(self, out, in_, op, r, w):
        return self.S.op('dve', lambda e: e.tensor_reduce(out=out, in_=in_, axis=AX.X, op=op), r, w)

    def memset(self, eng, ap, val, w):
        return self.S.op(eng, lambda e: e.memset(ap, val), (), w)

    def scan(self, out, d0, d1, init, r, w):
        return self.S.op('dve', lambda e: e.tensor_tensor_scan(out=out, data0=d0, data1=d1, initial=init,
                                                               op0=ALU.mult, op1=ALU.add), r, w)


def wview(w_ap):
    return w_ap.rearrange("(c p) n -> p c n", p=128)


def fmv(d_ap):
    return d_ap.rearrange("c p t -> p c t")


class WLoader:
    def __init__(self, S, E, st, kc=KC, nb=256):
        self.S, self.E = S, E
        self.ring = st.ring([128, kc, nb], F32, 2, name="wst")
        self.nb = nb

    def load(self, dst_tile, dst_fn, src_view, kc, ncols, scale=None):
        for c0 in range(0, ncols, self.nb):
            c1 = min(ncols, c0 + self.nb)
            w = self.ring.next()
            self.S.dma('sp', w[:, 0:kc, 0:c1 - c0], src_view[:, :, c0:c1], w, writes=[w])
            if scale is None:
                self.E.cp('pool', dst_fn(c0, c1), w[:, 0:kc, 0:c1 - c0], [w], [dst_tile])
            else:
                self.E.ts('pool', dst_fn(c0, c1), w[:, 0:kc, 0:c1 - c0], scale, None, ALU.mult, None, [w], [dst_tile])


class G:
    pass


def build(NL=4):
    nc = bass.Bass("TRN2", target_bir_lowering=False)
    S = Sched(nc)
    E = Em(S)
    g = G()
    g.nc, g.S, g.E = nc, S, E

    g.in_names = []

    def din(name, shape, dt=F32):
        g.in_names.append(name)
        return nc.dram_tensor(name, list(shape), dt, kind="ExternalInput").ap()

    def dscr(name, shape, dt):
        return nc.dram_tensor(name, list(shape), dt).ap()

    g.din, g.dscr = din, dscr
    g.x_in = din("x", [T, D])
    g.cT_in = din("cT", [128, KC, 2])
    g.cont_in = din("cont", [128, 2])
    g.seg_q = din("seg_q", [2, T], BF16)
    g.seg_k = din("seg_k", [2, T], BF16)
    g.ident_in = din("ident", [128, 128], BF16)
    g.identf_in = din("identf", [128, 128], F32)
    g.ada_w = din("ada_w", [4, D, 3 * D])
    g.ada_bP = din("ada_bP", [128, 4, 48])
    g.pre_gP = din("pre_gP", [128, 4, KC])
    g.post_g = din("post_g", [4, D])
    g.y_out = nc.dram_tensor("y", [T, D], F32, kind="ExternalOutput").ap()
    g.hT_d = dscr("hT_d", [KC, 128, T], BF16)
    g.oT_d = dscr("oT_d", [KC, 128, T], BF16)
    g.gT_d = dscr("gT_d", [KC, 128, T], BF16)
    g.gg_d = dscr("gg_d", [4, 2, D], F32)

    glob = contextlib.ExitStack()
    g.glob = glob
    g.ident = S.sb(glob, [128, 128], BF16, "ident")
    g.identf = S.sb(glob, [128, 128], F32, "identf")
    g.ones_b = S.sb(glob, [128, 128], BF16, "ones")
    g.cont = S.sb(glob, [128, 2], F32, "cont")
    g.preA = S.sb(glob, [128, 4, 2, KC], F32, "preA")
    g.preB = S.sb(glob, [128, 4, 2, KC], F32, "preB")
    S.dma('sp', g.ident[:], g.ident_in[:, :], g.ident, writes=[g.ident])
    S.dma('sp', g.identf[:], g.identf_in[:, :], g.identf, writes=[g.identf])
    S.dma('sp', g.cont[:], g.cont_in[:, :], g.cont, writes=[g.cont])
    E.memset('dve', g.ones_b[:], 1.0, [g.ones_b])

    import os
    g.stop = int(os.environ.get("KSTOP", "99"))
    prologue(g, NL)
    mixers = [mla_mixer, diff_mixer, lru_mixer, rwkv_mixer]
    for l in range(NL):
        if g.stop < 1:
            break
        stage_pre(g, l, g.x_in if l == 0 else g.y_out)
        if g.stop < 2:
            break
        w_out = mixers[l % 4](g, l)
        if g.stop < 4:
            break
        stage_post(g, l, w_out, g.x_in if l == 0 else g.y_out)
    if NL == 0:
        raise ValueError
    S.emit()
    return g


def prologue(g, NL):
    S, E = g.S, g.E
    with Stage(S) as st:
        cT = st.sb([128, KC, 2], F32)
        csT = st.sb([128, KC, 2], F32)
        adab = st.sb([128, 4, 48], F32)
        preg = st.sb([128, 4, KC], F32)
        modT = st.sb([128, 4, 2, 48], F32)
        gsb = st.sb([128, 2, 16], F32)
        ggs = st.sb([32, 128], F32)
        S.dma('sp', cT[:], g.cT_in[:, :, :], cT, writes=[cT])
        S.dma('sp', adab[:], g.ada_bP[:, :, :], adab, writes=[adab])
        S.dma('sp', preg[:], g.pre_gP[:, :, :], preg, writes=[preg])
        E.act(csT[:], cT[:], AF.Silu, [cT], [csT])
        wring = st.ring([128, KC, 512], F32, 2, name="adaw")
        pm = st.ring([128, 512], F32, 2, psum=True)
        for l in range(NL):
            pmod = pm.next()
            for blk in range(12):
                w = wring.next()
                S.dma('sp', w[:], wview(g.ada_w[l])[:, :, blk * 512:(blk + 1) * 512], w, writes=[w])
                for fc in range(4):
                    ch = blk * 4 + fc
                    for kc in range(KC):
                        E.mm(pmod[:, ch * 2:ch * 2 + 2], w[:, kc, fc * 128:(fc + 1) * 128], csT[:, kc, :],
                             kc == 0, kc == KC - 1, [w, csT], [pmod])
            E.tt('dve', modT[:, l, :, :], pmod[:, 0:96].rearrange("p (c h) -> p h c", h=2),
                 adab[:, l, :].unsqueeze(1).to_broadcast([128, 2, 48]), ALU.add, [pmod, adab], [modT])
            E.ts('dve', g.preA[:, l, :, :], modT[:, l, :, 16:32], 1.0, None, ALU.add, None, [modT], [g.preA])
            E.tt('dve', g.preA[:, l, :, :], g.preA[:, l, :, :],
                 preg[:, l, :].unsqueeze(1).to_broadcast([128, 2, KC]), ALU.mult, [g.preA, preg], [g.preA])
            E.cp('dve', g.preB[:, l, :, :], modT[:, l, :, 0:16], [modT], [g.preB])
            E.cp('dve', gsb[:], modT[:, l, :, 32:48], [modT], [gsb])
            pt = pm.next()
            E.tr(pt[0:32, 0:128], gsb[:].rearrange("p h c -> p (h c)"), g.identf[:], [gsb, g.identf], [pt])
            E.cp('dve', ggs[:], pt[0:32, 0:128], [pt], [ggs])
            S.dma('pool', g.gg_d[l].rearrange("h (c p) -> (h c) p", p=128), ggs[:], ggs, reads=[ggs])


def stage_pre(g, l, x_src):
    S, E = g.S, g.E
    with Stage(S) as st:
        xr = st.ring([128, D], F32, 3, name="x")
        junkr = st.ring([128, D], BF16, 2, name="junk")
        xs = st.ring([128, D], BF16, 2, name="xs")
        ssr = st.ring([128, 4], F32, 4, name="ss")
        ptr = st.ring([128, 1024], BF16, 4, psum=True)
        tmpr = st.ring([128, 8, 128], F32, 3, name="tmp")
        hst = st.ring([128, KC, 512], BF16, 2, name="hst")
        for grp in range(8):
            hs = hst.next()
            for sub in range(4):
                tt = grp * 4 + sub
                half = tt // 16
                x = xr.next()
                S.dma('sp', x[:], x_src[tt * 128:(tt + 1) * 128, :], x, writes=[x])
                ss = ssr.next()
                junk = junkr.next()
                E.act(junk[:], x[:], AF.Square, [x], [ss, junk], accum=ss[:, 0:1])
                E.act(ss[:, 1:2], ss[:, 0:1], AF.Sqrt, [ss], [ss], scale=1.0 / D, bias=1e-6)
                E.recip(ss[:, 2:3], ss[:, 1:2], [ss], [ss])
                xb = xs.next()
                E.ts('dve', xb[:], x[:], ss[:, 2:3], None, ALU.mult, None, [x, ss], [xb])
                for hc in range(2):
                    pt = ptr.next()
                    for c8 in range(8):
                        c = hc * 8 + c8
                        E.tr(pt[:, c8 * 128:(c8 + 1) * 128], xb[:, c * 128:(c + 1) * 128], g.ident[:],
                             [xb, g.ident], [pt])
                    tm = tmpr.next()
                    E.tt('dve', tm[:], pt[:].rearrange("p (c t) -> p c t", t=128),
                         g.preA[:, l, half, hc * 8:hc * 8 + 8].unsqueeze(2).to_broadcast([128, 8, 128]),
                         ALU.mult, [pt, g.preA], [tm])
                    E.tt('pool', hs[:, hc * 8:hc * 8 + 8, sub * 128:(sub + 1) * 128], tm[:],
                         g.preB[:, l, half, hc * 8:hc * 8 + 8].unsqueeze(2).to_broadcast([128, 8, 128]),
                         ALU.add, [tm, g.preB], [hs])
            S.dma('pool', fmv(g.hT_d)[:, :, grp * 512:(grp + 1) * 512], hs[:], hs, reads=[hs])


def stage_post(g, l, w_out, x_src):
    S, E = g.S, g.E
    with Stage(S) as st:
        wl = WLoader(S, E, st)
        wo = st.sb([128, KC, D], BF16, "wo")
        wl.load(wo, lambda c0, c1: wo[:, :, c0:c1], wview(w_out), KC, D)
        gg = st.sb([128, 2, D], F32, "gg")
        pg = st.sb([128, D], F32, "pg")
        for h in range(2):
            S.dma('sp', gg[:, h, :], g.gg_d[l, h:h + 1, :].partition_broadcast(128), gg, writes=[gg])
        S.dma('sp', pg[:], g.post_g[l:l + 1, :].partition_broadcast(128), pg, writes=[pg])
        E.tt('dve', gg[:], gg[:], pg[:].unsqueeze(1).to_broadcast([128, 2, D]), ALU.mult, [gg, pg], [gg])
        oring = st.ring([128, KC, 512], BF16, 2, name="o")
        xr = st.ring([128, D], F32, 2, name="x")
        tmp = st.ring([128, D], F32, 2, name="t")
        pyr = st.ring([128, D], F32, 2, psum=True)
        junkr = st.ring([128, 512], BF16, 4, name="junk")
        ssr = st.ring([128, 8], F32, 4, name="ss")
        for grp in range(8):
            o = oring.next()
            S.dma('sp', o[:], fmv(g.oT_d)[:, :, grp * 512:(grp + 1) * 512], o, writes=[o])
            for sub in range(4):
                tt = grp * 4 + sub
                half = tt // 16
                y = pyr.next()
                for n4 in range(4):
                    for kc in range(KC):
                        E.mm(y[:, n4 * 512:(n4 + 1) * 512], o[:, kc, sub * 128:(sub + 1) * 128],
                             wo[:, kc, n4 * 512:(n4 + 1) * 512], kc == 0, kc == KC - 1, [o, wo], [(y, n4)])
                ss = ssr.next()
                for n4 in range(4):
                    junk = junkr.next()
                    E.act(junk[:], y[:, n4 * 512:(n4 + 1) * 512], AF.Square, [(y, n4)], [(ss, n4), junk],
                          accum=ss[:, n4:n4 + 1])
                E.red(ss[:, 4:5], ss[:, 0:4], ALU.add, [(ss, 0), (ss, 1), (ss, 2), (ss, 3)], [(ss, 4)])
                E.act(ss[:, 5:6], ss[:, 4:5], AF.Sqrt, [(ss, 4)], [(ss, 5)], scale=1.0 / D, bias=1e-6)
                E.recip(ss[:, 6:7], ss[:, 5:6], [(ss, 5)], [(ss, 6)])
                x = xr.next()
                S.dma('sp', x[:], x_src[tt * 128:(tt + 1) * 128, :], x, writes=[x])
                t = tmp.next()
                for n4 in range(4):
                    sl = slice(n4 * 512, (n4 + 1) * 512)
                    E.stt(t[:, sl], y[:, sl], ss[:, 6:7], gg[:, half, sl], ALU.mult, ALU.mult,
                          [(y, n4), (ss, 6), gg], [(t, n4)])
                    E.tt('pool', t[:, sl], t[:, sl], x[:, sl], ALU.add, [(t, n4), x], [(t, n4)])
                S.dma('pool', g.y_out[tt * 128:(tt + 1) * 128, :], t[:], t,
                      reads=[(t, 0), (t, 1), (t, 2), (t, 3)])


def mla_mixer(g, l):
    S, E = g.S, g.E
    din, dscr = g.din, g.dscr
    w_in = din("mla_w_in", [D, 3136])
    qgP = din("mla_qgP", [128, 4])
    kvgP = din("mla_kvgP", [128, 4])
    w_q = din("mla_w_q_up", [512, 3072])
    w_kv = din("mla_w_kv_up", [512, 4096])
    w_out = din("mla_w_out", [D, D])
    cos_in = din("mla_cosT", [64, T])
    sin_in = din("mla_sinT", [64, T])
    lat_d = dscr("lat_d", [8, 128, T], BF16)
    kpe_d = dscr("kpe_d", [64, T], BF16)
    scale = 192 ** -0.5

    for hf in range(2):
      with Stage(S) as st:
        T0 = hf * HALF
        hT = st.sb([128, KC, HALF], BF16, "hT")
        for c in range(KC):
            S.dma('sp', hT[:, c, :], g.hT_d[c, :, T0:T0 + HALF], hT, writes=[(hT, c)])
        wl = WLoader(S, E, st, nb=128)
        wq = st.ring([128, KC, 512], BF16, 2, name="wq")
        pz = st.ring([128, 512], F32, 4, psum=True)
        pss = st.ring([128, 512], F32, 2, psum=True)
        zr = st.ring([128, 512], F32, 6, name="z")
        sqr = st.ring([128, 512], BF16, 2, name="sq")
        sdr = st.ring([128, 512], F32, 2, name="sd")
        ostr = st.ring([128, 4, 512], BF16, 2, name="ost")
        gt = st.sb([128, 8], F32, "gt")
        S.dma('sp', gt[:, 0:4], qgP[:, :], gt, writes=[gt])
        S.dma('sp', gt[:, 4:8], kvgP[:, :], gt, writes=[gt])
        for gi in range(2):
            w = wq.next()
            wl.load(w, lambda c0, c1, w=w: w[:, :, c0:c1], wview(w_in)[:, :, gi * 512:(gi + 1) * 512], KC, 512)
            for tq in range(4):
                tsl = slice(tq * 512, (tq + 1) * 512)
                gsl = slice(T0 + tq * 512, T0 + (tq + 1) * 512)
                zc = [zr.next() for _ in range(4)]
                psum_ss = pss.next()
                for fc in range(4):
                    p = pz.next()
                    for kc in range(KC):
                        E.mm(p[:], w[:, kc, fc * 128:(fc + 1) * 128], hT[:, kc, tsl], kc == 0, kc == KC - 1,
                             [w, (hT, kc)], [p])
                    sq = sqr.next()
                    E.act(sq[:], p[:], AF.Square, [p], [sq])
                    E.cp('dve', zc[fc][:], p[:], [p], [zc[fc]])
                    E.mm(psum_ss[:], g.ones_b[:], sq[:], fc == 0, fc == 3, [g.ones_b, sq], [psum_ss])
                sd = sdr.next()
                E.act(sd[:], psum_ss[:], AF.Sqrt, [psum_ss], [sd], scale=1.0 / 512, bias=1e-6)
                E.recip(sd[:], sd[:], [sd], [sd])
                ost = ostr.next()
                for fc in range(4):
                    E.stt(ost[:, fc, :], zc[fc][:], gt[:, gi * 4 + fc:gi * 4 + fc + 1], sd[:], ALU.mult, ALU.mult,
                          [zc[fc], gt, sd], [ost])
                S.dma('pool', fmv(lat_d)[:, gi * 4:gi * 4 + 4, gsl], ost[:], ost, reads=[ost])
        wk = st.sb([128, KC, 128], BF16, "wk")
        wv = wview(w_in)
        wl.load(wk, lambda c0, c1: wk[:, :, c0:c1], wv[:, :, 1024:1088], KC, 64)
        wl.load(wk, lambda c0, c1: wk[:, :, 64 + c0:64 + c1], wv[:, :, 1056:1088], KC, 32, scale=-1.0)
        wl.load(wk, lambda c0, c1: wk[:, :, 96 + c0:96 + c1], wv[:, :, 1024:1056], KC, 32)
        cosr = st.ring([64, 512], F32, 2, name="cos")
        sinr = st.ring([64, 512], F32, 2, name="sin")
        kstr = st.ring([64, 512], BF16, 2, name="kst")
        t1r = st.ring([64, 512], F32, 2, name="t1")
        t2r = st.ring([64, 512], F32, 2, name="t2")
        for tq in range(4):
            tsl = slice(tq * 512, (tq + 1) * 512)
            gsl = slice(T0 + tq * 512, T0 + (tq + 1) * 512)
            cos, sin = cosr.next(), sinr.next()
            S.dma('sp', cos[:], cos_in[:, gsl], cos, writes=[cos])
            S.dma('sp', sin[:], sin_in[:, gsl], sin, writes=[sin])
            pa, pb = pz.next(), pz.next()
            for kc in range(KC):
                E.mm(pa[0:64, :], wk[:, kc, 0:64], hT[:, kc, tsl], kc == 0, kc == KC - 1, [wk, (hT, kc)], [pa])
            for kc in range(KC):
                E.mm(pb[0:64, :], wk[:, kc, 64:128], hT[:, kc, tsl], kc == 0, kc == KC - 1, [wk, (hT, kc)], [pb])
            t1, t2, kst = t1r.next(), t2r.next(), kstr.next()
            E.tt('dve', t1[:], pa[0:64, :], cos[:], ALU.mult, [pa, cos], [t1])
            E.tt('dve', t2[:], pb[0:64, :], sin[:], ALU.mult, [pb, sin], [t2])
            E.tt('pool', kst[:], t1[:], t2[:], ALU.add, [t1, t2], [kst])
            S.dma('pool', kpe_d[:, gsl], kst[:], kst, reads=[kst])
        gstr = st.ring([128, 4, 512], BF16, 2, name="gst")
        for gq in range(4):
            w = wq.next()
            wl.load(w, lambda c0, c1, w=w: w[:, :, c0:c1], wv[:, :, 1088 + gq * 512:1088 + (gq + 1) * 512], KC, 512)
            for tq in range(4):
                tsl = slice(tq * 512, (tq + 1) * 512)
                gsl = slice(T0 + tq * 512, T0 + (tq + 1) * 512)
                gs = gstr.next()
                for fc in range(4):
                    p = pz.next()
                    for kc in range(KC):
                        E.mm(p[:], w[:, kc, fc * 128:(fc + 1) * 128], hT[:, kc, tsl], kc == 0, kc == KC - 1,
                             [w, (hT, kc)], [p])
                    E.act(gs[:, fc, :], p[:], AF.Silu, [p], [gs])
                S.dma('pool', fmv(g.gT_d)[:, gq * 4:gq * 4 + 4, gsl], gs[:], gs, reads=[gs])

    if g.stop < 3:
        return w_out
    with Stage(S) as st:
        qn = st.sb([128, 4, T], BF16, "qn")
        kvn = st.sb([128, 4, T], BF16, "kvn")
        for c in range(4):
            S.dma('sp', qn[:, c, :], lat_d[c], qn, writes=[qn])
            S.dma('sp', kvn[:, c, :], lat_d[4 + c], kvn, writes=[kvn])
        kaug = st.sb([66, T], BF16, "kaug")
        S.dma('sp', kaug[0:64, :], kpe_d[:, :], kaug, writes=[kaug])
        S.dma('sp', kaug[64:66, :], g.seg_k[:, :], kaug, writes=[kaug])
        qaugr = st.ring([66, T], BF16, 2, name="qaug")
        for qa in qaugr.tiles:
            S.dma('sp', qa[64:66, :], g.seg_q[:, :], qa, writes=[(qa, 'seg')])
        cosr = st.ring([64, 512], F32, 2, name="cos")
        sinr = st.ring([64, 512], F32, 2, name="sin")
        wl = WLoader(S, E, st, kc=4, nb=256)
        wqr = st.ring([128, 4, 256], BF16, 2, name="wqh")
        wkvr = st.ring([128, 4, 256], BF16, 2, name="wkvh")
        qnpr = st.ring([128, T], BF16, 2, name="qnp")
        knpr = st.ring([128, T], BF16, 2, name="knp")
        Vr = st.ring([128, NT, 128], BF16, 1, name="V")
        ps_s = st.ring([128, 512], F32, 3, psum=True)
        ps_o = st.ring([128, 512], F32, 1, psum=True)
        ps_m = st.ring([128, 512], F32, 1, psum=True)
        ps_x = st.ring([128, 512], F32, 3, psum=True)
        sqr = st.ring([128, 512], BF16, 4, name="sq")
        t1r = st.ring([64, 512], F32, 2, name="t1")
        t2r = st.ring([64, 512], F32, 2, name="t2")
        pTr = st.ring([128, 512], BF16, 8, name="pT")
        sgr = st.ring([128, 512], BF16, 4, name="sg")
        rsr = st.ring([128, 512], F32, 2, name="rs")
        ofr = st.ring([128, 512], F32, 2, name="of")
        obr = st.ring([128, 512], BF16, 2, name="ob")
        glr = st.ring([128, 512], BF16, 2, name="gl")
        statr = st.ring([128, 24], F32, 2, name="stat")
        wqv = w_q.rearrange("(c p) n -> p c n", p=128)
        wkvv = w_kv.rearrange("(c p) n -> p c n", p=128)
        for h in range(16):
            wqh, wkvh = wqr.next(), wkvr.next()
            wl.load(wqh, lambda c0, c1, t=wqh: t[:, :, c0:c1], wqv[:, :, 192 * h:192 * h + 192], 4, 192)
            wl.load(wqh, lambda c0, c1, t=wqh: t[:, :, 192 + c0:192 + c1], wqv[:, :, 192 * h + 160:192 * h + 192],
                    4, 32, scale=-1.0)
            wl.load(wqh, lambda c0, c1, t=wqh: t[:, :, 224 + c0:224 + c1], wqv[:, :, 192 * h + 128:192 * h + 160],
                    4, 32)
            wl.load(wkvh, lambda c0, c1, t=wkvh: t[:, :, c0:c1], wkvv[:, :, 256 * h:256 * h + 256], 4, 256)
            qnp, knp, V, qaug, stat = qnpr.next(), knpr.next(), Vr.next(), qaugr.next(), statr.next()
            for tq in range(8):
                tsl = slice(tq * 512, (tq + 1) * 512)
                p = ps_x.next()
                for kc in range(4):
                    E.mm(p[:], wqh[:, kc, 0:128], qn[:, kc, tsl], kc == 0, kc == 3, [wqh, qn], [p])
                E.act(qnp[:, tsl], p[:], AF.Copy, [p], [(qnp, tq)])
                sq1 = sqr.next()
                E.cp('dve', sq1[:], p[:], [p], [sq1])
                E.tt('pool', sq1[:], sq1[:], sq1[:], ALU.mult, [sq1], [sq1])
                pa, pb = ps_x.next(), ps_x.next()
                for kc in range(4):
                    E.mm(pa[0:64, :], wqh[:, kc, 128:192], qn[:, kc, tsl], kc == 0, kc == 3, [wqh, qn], [pa])
                for kc in range(4):
                    E.mm(pb[0:64, :], wqh[:, kc, 192:256], qn[:, kc, tsl], kc == 0, kc == 3, [wqh, qn], [pb])
                t1, t2 = t1r.next(), t2r.next()
                cos, sin = cosr.next(), sinr.next()
                S.dma('sp', cos[:], cos_in[:, tsl], cos, writes=[cos])
                S.dma('sp', sin[:], sin_in[:, tsl], sin, writes=[sin])
                E.tt('dve', t1[:], pa[0:64, :], cos[:], ALU.mult, [pa, cos], [t1])
                E.tt('dve', t2[:], pb[0:64, :], sin[:], ALU.mult, [pb, sin], [t2])
                E.tt('pool', qaug[0:64, tsl], t1[:], t2[:], ALU.add, [t1, t2], [(qaug, tq)])
                sq2 = sqr.next()
                E.tt('pool', sq2[0:64, :], qaug[0:64, tsl], qaug[0:64, tsl], ALU.mult, [(qaug, tq)], [sq2])
                pn = ps_x.next()
                E.mm(pn[:], g.ones_b[:], sq1[:], True, False, [g.ones_b, sq1], [pn])
                E.mm(pn[:], g.ones_b[0:64, :], sq2[0:64, :], False, True, [g.ones_b, sq2], [pn])
                E.red(stat[:, tq:tq + 1], pn[:], ALU.max, [pn], [(stat, 'q', tq)])
                p = ps_x.next()
                for kc in range(4):
                    E.mm(p[:], wkvh[:, kc, 0:128], kvn[:, kc, tsl], kc == 0, kc == 3, [wkvh, kvn], [p])
                E.act(knp[:, tsl], p[:], AF.Copy, [p], [(knp, tq)])
                sq3 = sqr.next()
                E.cp('dve', sq3[:], p[:], [p], [sq3])
                E.tt('pool', sq3[:], sq3[:], sq3[:], ALU.mult, [sq3], [sq3])
                sq4 = sqr.next()
                E.tt('pool', sq4[0:64, :], kaug[0:64, tsl], kaug[0:64, tsl], ALU.mult, [kaug], [sq4])
                pn = ps_x.next()
                E.mm(pn[:], g.ones_b[:], sq3[:], True, False, [g.ones_b, sq3], [pn])
                E.mm(pn[:], g.ones_b[0:64, :], sq4[0:64, :], False, True, [g.ones_b, sq4], [pn])
                E.red(stat[:, 8 + tq:9 + tq], pn[:], ALU.max, [pn], [(stat, 'k', tq)])
                p = ps_x.next()
                for j in range(4):
                    tt = tq * 4 + j
                    for kc in range(4):
                        E.mm(p[:, j * 128:(j + 1) * 128], kvn[:, kc, tt * 128:(tt + 1) * 128], wkvh[:, kc, 128:256],
                             kc == 0, kc == 3, [wkvh, kvn], [p])
                E.act(V[:, tq * 4:tq * 4 + 4, :], p[:].rearrange("p (j d) -> p j d", d=128), AF.Copy, [p], [(V, tq)])
            qk = [(stat, 'q', t) for t in range(8)]
            kk = [(stat, 'k', t) for t in range(8)]
            E.red(stat[:, 16:17], stat[:, 0:8], ALU.max, qk, [(stat, 16)])
            E.red(stat[:, 17:18], stat[:, 8:16], ALU.max, kk, [(stat, 17)])
            E.tt('dve', stat[:, 18:19], stat[:, 16:17], stat[:, 17:18], ALU.mult, [(stat, 16), (stat, 17)], [(stat, 18)])
            E.act(stat[:, 19:20], stat[:, 18:19], AF.Sqrt, [(stat, 18)], [(stat, 19)])
            E.ts('dve', stat[:, 20:21], stat[:, 19:20], -scale, None, ALU.mult, None, [(stat, 19)], [(stat, 20)])
            negB = stat[:, 20:21]
            qkeys = [(qnp, t) for t in range(8)] + [(qaug, t) for t in range(8)] + [(qaug, 'seg')]
            kkeys = [(knp, t) for t in range(8)] + [kaug]
            vkeys = [(V, t) for t in range(8)]
            units = [(tq, kb) for tq in range(8) for kb in range(NT)]

            def qk_mm(u):
                tq, kb = units[u]
                ps = ps_s.next()
                tsl = slice(tq * 512, (tq + 1) * 512)
                ksl = slice(kb * 128, (kb + 1) * 128)
                E.mm(ps[:], knp[:, ksl], qnp[:, tsl], True, False, [(knp, kb // 4), (qnp, tq)], [ps])
                E.mm(ps[:], kaug[0:66, ksl], qaug[0:66, tsl], False, True, [kaug, (qaug, tq), (qaug, 'seg')], [ps])
                return ps

            pend = [qk_mm(0), qk_mm(1)]
            po = psm = None
            for u, (tq, kb) in enumerate(units):
                ps = pend.pop(0)
                if u + 2 < len(units):
                    pend.append(qk_mm(u + 2))
                if kb == 0:
                    po, psm = ps_o.next(), ps_m.next()
                pT = pTr.next()
                E.act(pT[:], ps[:], AF.Exp, [ps, (stat, 20)], [pT], scale=scale, bias=negB)
                E.mm(po[:], V[:, kb, :], pT[:], kb == 0, kb == NT - 1, [(V, kb // 4), pT], [po])
                E.mm(psm[:], g.ones_b[:], pT[:], kb == 0, kb == NT - 1, [g.ones_b, pT], [psm])
                if kb == NT - 1:
                    tsl = slice(tq * 512, (tq + 1) * 512)
                    rs, of, ob, gl = rsr.next(), ofr.next(), obr.next(), glr.next()
                    S.dma('sp', gl[:], g.gT_d[h, :, tsl], gl, writes=[gl])
                    E.recip(rs[:], psm[:], [psm], [rs])
                    E.tt('dve', of[:], po[:], rs[:], ALU.mult, [po, rs], [of])
                    E.tt('pool', ob[:], of[:], gl[:], ALU.mult, [of, gl], [ob])
                    S.dma('pool', g.oT_d[h, :, tsl], ob[:], ob, reads=[ob])
    return w_out


def diff_mixer(g, l):
    S, E = g.S, g.E
    din, dscr = g.din, g.dscr
    w_in = din("diff_w_in", [D, 8192])
    lam_in = din("diff_lam", [1, 256])
    gsub_in = din("diff_gsubP", [128, 1])
    w_out = din("diff_w_out", [D, D])
    cos_in = din("diff_cos", [128, NT, 8])
    sin_in = din("diff_sin", [128, NT, 8])
    qk_d = dscr("dqk_d", [2, KC, 128, T], BF16)
    v_d = dscr("dv_d", [T, D], BF16)
    scale = 64 ** -0.5
    lam_init = 0.8 - 0.6 * math.exp(-0.3 * l)
    wv = wview(w_in)

    for hf in range(2):
      with Stage(S) as st:
        T0 = hf * HALF
        hT = st.sb([128, KC, HALF], BF16, "hT")
        for c in range(KC):
            S.dma('sp', hT[:, c, :], g.hT_d[c, :, T0:T0 + HALF], hT, writes=[(hT, c)])
        wl = WLoader(S, E, st, nb=128)
        wq = st.ring([128, KC, 512], BF16, 2, name="wq")
        pz = st.ring([128, 512], F32, 4, psum=True)
        ptr = st.ring([128, 8, 128], BF16, 2, psum=True)
        ctab = st.sb([128, NT, 8], F32, "ctab")
        stab = st.sb([128, NT, 8], F32, "stab")
        S.dma('sp', ctab[:], cos_in[:, :, :], ctab, writes=[ctab])
        S.dma('sp', stab[:], sin_in[:, :, :], stab, writes=[stab])
        qtmr = st.ring([128, 512], BF16, 3, name="qtm")
        rr = st.ring([128, 4, 8, 8], F32, 3, name="rope")
        qTr = st.ring([128, 4, HALF], BF16, 2, name="qTst")
        for sel in range(2):
            for cb in range(4):
                w = wq.next()
                col0 = sel * 2048 + cb * 512
                wl.load(w, lambda c0, c1, w=w: w[:, :, c0:c1], wv[:, :, col0:col0 + 512], KC, 512)
                qT = qTr.next()
                for tt in range(16):
                    gtt = hf * 16 + tt
                    p = pz.next()
                    for kc in range(KC):
                        E.mm(p[:], hT[:, kc, tt * 128:(tt + 1) * 128], w[:, kc, :], kc == 0, kc == KC - 1,
                             [w, (hT, kc)], [p])
                    qtm = qtmr.next()
                    E.act(qtm[:], p[:], AF.Copy, [p], [qtm])
                    p3 = p[:].rearrange("p (h d) -> p h d", d=64)
                    q3 = qtm[:].rearrange("p (h d) -> p h d", d=64)
                    cs = ctab[:, gtt, :].unsqueeze(1).to_broadcast([128, 8, 8])
                    sn = stab[:, gtt, :].unsqueeze(1).to_broadcast([128, 8, 8])
                    r = rr.next()
                    E.tt('dve', r[:, 0], p3[:, :, 0:8], cs, ALU.mult, [p, ctab], [(r, 0)])
                    E.tt('dve', r[:, 1], p3[:, :, 8:16], sn, ALU.mult, [p, stab], [(r, 1)])
                    E.tt('dve', r[:, 2], p3[:, :, 8:16], cs, ALU.mult, [p, ctab], [(r, 2)])
                    E.tt('dve', r[:, 3], p3[:, :, 0:8], sn, ALU.mult, [p, stab], [(r, 3)])
                    E.tt('pool', q3[:, :, 0:8], r[:, 0], r[:, 1], ALU.subtract, [(r, 0), (r, 1), qtm], [qtm])
                    E.tt('pool', q3[:, :, 8:16], r[:, 2], r[:, 3], ALU.add, [(r, 2), (r, 3), qtm], [qtm])
                    pt = ptr.next()
                    for j in range(4):
                        E.tr(pt[:, j, :], qtm[:, j * 128:(j + 1) * 128], g.ident[:], [qtm, g.ident], [pt])
                    E.cp('dve', qT[:, :, tt * 128:(tt + 1) * 128], pt[:, 0:4, :], [pt], [(qT, tt)])
                S.dma('pool', fmv(qk_d[sel])[:, cb * 4:cb * 4 + 4, T0:T0 + HALF], qT[:], qT,
                      reads=[(qT, t) for t in range(16)])
        vstr = st.ring([128, 512], BF16, 3, name="vst")
        for cb in range(4):
            w = wq.next()
            col0 = 4096 + cb * 512
            wl.load(w, lambda c0, c1, w=w: w[:, :, c0:c1], wv[:, :, col0:col0 + 512], KC, 512)
            for tt in range(16):
                p = pz.next()
                for kc in range(KC):
                    E.mm(p[:], hT[:, kc, tt * 128:(tt + 1) * 128], w[:, kc, :], kc == 0, kc == KC - 1,
                         [w, (hT, kc)], [p])
                vs = vstr.next()
                E.act(vs[:], p[:], AF.Copy, [p], [vs])
                S.dma('pool', v_d[T0 + tt * 128:T0 + (tt + 1) * 128, cb * 512:(cb + 1) * 512], vs[:], vs, reads=[vs])
        gstr = st.ring([128, 4, 512], BF16, 2, name="gst")
        for gq in range(4):
            w = wq.next()
            wl.load(w, lambda c0, c1, w=w: w[:, :, c0:c1], wv[:, :, 6144 + gq * 512:6144 + (gq + 1) * 512], KC, 512)
            for tq in range(4):
                tsl = slice(tq * 512, (tq + 1) * 512)
                gsl = slice(T0 + tq * 512, T0 + (tq + 1) * 512)
                gs = gstr.next()
                for fc in range(4):
                    p = pz.next()
                    for kc in range(KC):
                        E.mm(p[:], w[:, kc, fc * 128:(fc + 1) * 128], hT[:, kc, tsl], kc == 0, kc == KC - 1,
                             [w, (hT, kc)], [p])
                    E.act(gs[:, fc, :], p[:], AF.Silu, [p], [gs])
                S.dma('pool', fmv(g.gT_d)[:, gq * 4:gq * 4 + 4, gsl], gs[:], gs, reads=[gs])
    if g.stop < 3:
        return w_out

    with Stage(S) as st:
        lam = st.sb([128, 256], F32, "lam")
        S.dma('sp', lam[:], lam_in[0:1, :].partition_broadcast(128), lam, writes=[lam])
        lt = st.sb([128, 128], F32, "lt")
        ls = st.sb([128, 8], F32, "ls")
        E.tt('dve', lt[:, 0:64], lam[:, 0:64], lam[:, 64:128], ALU.mult, [lam], [lt])
        E.tt('dve', lt[:, 64:128], lam[:, 128:192], lam[:, 192:256], ALU.mult, [lam], [lt])
        E.red(ls[:, 0:1], lt[:, 0:64], ALU.add, [lt], [(ls, 0)])
        E.red(ls[:, 1:2], lt[:, 64:128], ALU.add, [lt], [(ls, 1)])
        E.act(ls[:, 2:4], ls[:, 0:2], AF.Exp, [(ls, 0), (ls, 1)], [(ls, 2)])
        E.tt('dve', ls[:, 4:5], ls[:, 3:4], ls[:, 2:3], ALU.subtract, [(ls, 2)], [(ls, 4)])
        E.ts('dve', ls[:, 5:6], ls[:, 4:5], -lam_init, None, ALU.add, None, [(ls, 4)], [(ls, 5)])
        neglam = ls[:, 5:6]
        gsub = st.sb([128, 2], F32, "gsub")
        S.dma('sp', gsub[:, 0:1], gsub_in[:, :], gsub, writes=[gsub])
        E.ts('dve', gsub[:, 1:2], gsub[:, 0:1], 1.0 - lam_init, None, ALU.mult, None, [gsub], [(gsub, 1)])
        qr = [st.ring([66, T], BF16, 2, name=f"q{c}") for c in range(2)]
        kr = [st.ring([66, T], BF16, 2, name=f"k{c}") for c in range(2)]
        for c in range(2):
            for t_ in qr[c].tiles:
                S.dma('sp', t_[64:66, :], g.seg_q[:, :], t_, writes=[(t_, 'seg')])
            for t_ in kr[c].tiles:
                S.dma('sp', t_[64:66, :], g.seg_k[:, :], t_, writes=[(t_, 'seg')])
        Vr = st.ring([128, NT, 128], BF16, 2, name="V")
        ps_s = st.ring([128, 512], F32, 3, psum=True)
        ps_o = [st.ring([128, 512], F32, 1, psum=True) for _ in range(2)]
        ps_m = [st.ring([128, 512], F32, 1, psum=True) for _ in range(2)]
        ps_x = st.ring([128, 512], F32, 1, psum=True)
        sqr = st.ring([128, 512], BF16, 3, name="sq")
        pTr = st.ring([128, 512], BF16, 14, name="pT")
        sgr = st.ring([128, 512], BF16, 6, name="sg")
        rsr = st.ring([128, 512], F32, 2, name="rs")
        ofr = st.ring([128, 512], F32, 4, name="of")
        obr = st.ring([128, 512], BF16, 2, name="ob")
        glr = st.ring([128, 512], BF16, 2, name="gl")
        statr = st.ring([128, 48], F32, 2, name="stat")
        vview = v_d.rearrange("(n p) f -> p n f", p=128)
        for h in range(16):
            q = [qr[c].next() for c in range(2)]
            k = [kr[c].next() for c in range(2)]
            V, stat = Vr.next(), statr.next()
            for c in range(2):
                S.dma('sp', q[c][0:64, :], qk_d[0, h, 64 * c:64 * c + 64, :], q[c], writes=[(q[c], 'd')])
                S.dma('sp', k[c][0:64, :], qk_d[1, h, 64 * c:64 * c + 64, :], k[c], writes=[(k[c], 'd')])
            for j in range(4):
                S.dma('sp', V[:, j * 8:j * 8 + 8, :], vview[:, j * 8:j * 8 + 8, h * 128:(h + 1) * 128], V, writes=[V])
            for ti, tl in enumerate((q[0], q[1], k[0], k[1])):
                for tq in range(8):
                    tsl = slice(tq * 512, (tq + 1) * 512)
                    sq = sqr.next()
                    E.tt('pool', sq[0:64, :], tl[0:64, tsl], tl[0:64, tsl], ALU.mult, [(tl, 'd')], [sq])
                    pn = ps_x.next()
                    E.mm(pn[:], g.ones_b[0:64, :], sq[0:64, :], True, True, [g.ones_b, sq], [pn])
                    E.red(stat[:, ti * 8 + tq:ti * 8 + tq + 1], pn[:], ALU.max, [pn], [(stat, ti, tq)])
                E.red(stat[:, 32 + ti:33 + ti], stat[:, ti * 8:ti * 8 + 8], ALU.max,
                      [(stat, ti, t) for t in range(8)], [(stat, 32 + ti)])
            negB = []
            for c in range(2):
                E.tt('dve', stat[:, 36 + c:37 + c], stat[:, 32 + c:33 + c], stat[:, 34 + c:35 + c], ALU.mult,
                     [(stat, 32 + c), (stat, 34 + c)], [(stat, 36 + c)])
                E.act(stat[:, 38 + c:39 + c], stat[:, 36 + c:37 + c], AF.Sqrt, [(stat, 36 + c)], [(stat, 38 + c)])
                E.ts('dve', stat[:, 40 + c:41 + c], stat[:, 38 + c:39 + c], -scale, None, ALU.mult, None,
                     [(stat, 38 + c)], [(stat, 40 + c)])
                negB.append(stat[:, 40 + c:41 + c])
            units = [(tq, kb, c) for tq in range(8) for kb in range(NT) for c in range(2)]

            def qk_mm(u):
                tq, kb, c = units[u]
                ps = ps_s.next()
                E.mm(ps[:], k[c][0:66, kb * 128:(kb + 1) * 128], q[c][0:66, tq * 512:(tq + 1) * 512], True, True,
                     [(k[c], 'd'), (k[c], 'seg'), (q[c], 'd'), (q[c], 'seg')], [ps])
                return ps

            pend = [qk_mm(0), qk_mm(1)]
            po = [None, None]
            psm = [None, None]
            grp = [[], []]
            for u, (tq, kb, c) in enumerate(units):
                ps = pend.pop(0)
                if u + 2 < len(units):
                    pend.append(qk_mm(u + 2))
                if kb == 0:
                    po[c], psm[c] = ps_o[c].next(), ps_m[c].next()
                pT = pTr.next()
                E.act(pT[:], ps[:], AF.Exp, [ps, (stat, 40 + c)], [pT], scale=scale, bias=negB[c])
                E.mm(po[c][:], V[:, kb, :], pT[:], kb == 0, kb == NT - 1, [V, pT], [po[c]])
                E.mm(psm[c][:], g.ones_b[:], pT[:], kb == 0, kb == NT - 1, [g.ones_b, pT], [psm[c]])
                if kb == NT - 1 and c == 1:
                    tsl = slice(tq * 512, (tq + 1) * 512)
                    gl = glr.next()
                    S.dma('sp', gl[:], g.gT_d[h, :, tsl], gl, writes=[gl])
                    o = []
                    for cc in range(2):
                        rs, of = rsr.next(), ofr.next()
                        E.recip(rs[:], psm[cc][:], [psm[cc]], [rs])
                        E.tt('dve', of[:], po[cc][:], rs[:], ALU.mult, [po[cc], rs], [of])
                        o.append(of)
                    od = ofr.next()
                    E.stt(od[:], o[1][:], neglam, o[0][:], ALU.mult, ALU.add, [o[0], o[1], (ls, 5)], [od])
                    sq = sqr.next()
                    E.tt('pool', sq[:], od[:], od[:], ALU.mult, [od], [sq])
                    pn = ps_x.next()
                    E.mm(pn[:], g.ones_b[:], sq[:], True, True, [g.ones_b, sq], [pn])
                    rs = rsr.next()
                    E.act(rs[:], pn[:], AF.Sqrt, [pn], [rs], scale=1.0 / 128, bias=1e-5)
                    E.recip(rs[:], rs[:], [rs], [rs])
                    on = ofr.next()
                    E.stt(on[:], od[:], gsub[:, 1:2], rs[:], ALU.mult, ALU.mult, [od, (gsub, 1), rs], [on])
                    ob = obr.next()
                    E.tt('pool', ob[:], on[:], gl[:], ALU.mult, [on, gl], [ob])
                    S.dma('pool', g.oT_d[h, :, tsl], ob[:], ob, reads=[ob])
    return w_out


def lru_mixer(g, l):
    S, E = g.S, g.E
    din, dscr = g.din, g.dscr
    w_in = din("lru_w_in", [D, 2 * D])
    w_out = din("lru_w_out", [D, D])
    conv_in = din("lru_convP", [128, KC, 5])
    gw_in = din("lru_gate_w", [128, 2, 2, KC, 128])
    gb_in = din("lru_gate_bP", [128, 2, 2, KC])
    lam_in = din("lru_lamP", [128, 2, KC])
    xb_d = dscr("xb_d", [KC, 128, T], F32)
    wv = wview(w_in)

    for hf in range(2):
      with Stage(S) as st:
        T0 = hf * HALF
        hT = st.sb([128, KC, HALF], BF16, "hT")
        for c in range(KC):
            S.dma('sp', hT[:, c, :], g.hT_d[c, :, T0:T0 + HALF], hT, writes=[(hT, c)])
        wl = WLoader(S, E, st, nb=128)
        wq = st.ring([128, KC, 512], BF16, 2, name="wq")
        pz = st.ring([128, 512], F32, 4, psum=True)
        xstr = st.ring([128, 4, 512], F32, 2, name="xst")
        gstr = st.ring([128, 4, 512], BF16, 2, name="gst")
        for part in range(2):
            for gq in range(4):
                w = wq.next()
                col0 = part * 2048 + gq * 512
                wl.load(w, lambda c0, c1, w=w: w[:, :, c0:c1], wv[:, :, col0:col0 + 512], KC, 512)
                for tq in range(4):
                    tsl = slice(tq * 512, (tq + 1) * 512)
                    gsl = slice(T0 + tq * 512, T0 + (tq + 1) * 512)
                    stt_ = (xstr if part == 0 else gstr).next()
                    for fc in range(4):
                        p = pz.next()
                        for kc in range(KC):
                            E.mm(p[:], w[:, kc, fc * 128:(fc + 1) * 128], hT[:, kc, tsl], kc == 0, kc == KC - 1,
                                 [w, (hT, kc)], [p])
                        E.act(stt_[:, fc, :], p[:], AF.Copy if part == 0 else AF.Silu, [p], [stt_])
                    dst = fmv(xb_d if part == 0 else g.gT_d)[:, gq * 4:gq * 4 + 4, gsl]
                    S.dma('pool', dst, stt_[:], stt_, reads=[stt_])
    if g.stop < 3:
        return w_out

    with Stage(S) as st:
        cvp = st.sb([128, KC, 5], F32, "cvp")
        gb = st.sb([128, 2, 2, KC], F32, "gb")
        lam = st.sb([128, 2, KC], F32, "lam")
        sc = st.sb([128, 2, KC], F32, "sc")
        S.dma('sp', cvp[:], conv_in[:, :, :], cvp, writes=[cvp])
        S.dma('sp', gb[:], gb_in[:, :, :, :], gb, writes=[gb])
        S.dma('sp', lam[:], lam_in[:, :, :], lam, writes=[lam])
        E.act(sc[:], lam[:], AF.Exp, [lam], [sc], scale=-1.0)
        E.act(sc[:], sc[:], AF.Ln, [sc], [sc], bias=1.0)
        E.ts('dve', sc[:], sc[:], -8.0, None, ALU.mult, None, [sc], [sc])
        cont = g.cont
        xbp = st.sb([128, 2, HALF + 3], F32, "xbp")
        xc = st.sb([128, 2, HALF], F32, "xc")
        xcb = st.sb([128, T], BF16, "xcb")
        gTr = st.ring([128, T], BF16, 2, name="gT")
        ar = st.ring([128, T], F32, 2, name="a")
        ir = st.ring([128, T], F32, 2, name="i")
        mr = st.ring([128, T], F32, 2, name="m")
        hr = st.ring([128, T], F32, 2, name="hs")
        obr = st.ring([128, T], BF16, 1, name="ob")
        gwsr = st.ring([128, 2, 128], F32, 2, name="gws")
        gwr = st.ring([128, 2, 128], BF16, 2, name="gw")
        inr = st.ring([128, 2], F32, 4, name="init")
        pg = st.ring([128, 512], F32, 4, psum=True)
        E.memset('dve', xbp[:, 0, 0:1], 0.0, [(xbp, 'z')])
        E.memset('dve', xbp[:, 1, HALF + 1:HALF + 3], 0.0, [(xbp, 'z')])
        for n in range(KC):
            for h in range(2):
                S.dma('sp', xbp[:, h, 1:HALF + 1], xb_d[n, :, h * HALF:(h + 1) * HALF], xbp, writes=[(xbp, h)])
            gT = gTr.next()
            S.dma('sp', gT[:], g.gT_d[n], gT, writes=[gT])
            E.ts('dve', xbp[:, 0, HALF + 1:HALF + 3], xbp[:, 1, 1:3], cont[:, 0:1], None, ALU.mult, None,
                 [(xbp, 1), cont], [(xbp, 'h0')])
            E.ts('dve', xbp[:, 1, 0:1], xbp[:, 0, HALF:HALF + 1], cont[:, 0:1], None, ALU.mult, None,
                 [(xbp, 0), cont], [(xbp, 'h1')])
            xk = [(xbp, 0), (xbp, 1), (xbp, 'h0'), (xbp, 'h1'), (xbp, 'z')]
            E.ts('dve', xc[:], xbp[:, :, 0:HALF], cvp[:, n, 0:1], cvp[:, n, 4:5], ALU.mult, ALU.add,
                 xk + [cvp], [xc])
            for j in range(1, 4):
                E.stt(xc[:], xbp[:, :, j:j + HALF], cvp[:, n, j:j + 1], xc[:], ALU.mult, ALU.add, xk + [cvp, xc], [xc])
            xcf = xc[:].rearrange("p h t -> p (h t)")
            E.act(xcb[:], xcf, AF.Copy, [xc], [xcb])
            hs2 = []
            for d in range(2):
                gws, gw = gwsr.next(), gwr.next()
                S.dma('sp', gws[:], gw_in[:, d, :, n, :], gws, writes=[gws])
                E.cp('pool', gw[:], gws[:], [gws], [gw])
                a, it, mt, hs = ar.next(), ir.next(), mr.next(), hr.next()
                for tq in range(8):
                    tsl = slice(tq * 512, (tq + 1) * 512)
                    p_r, p_i = pg.next(), pg.next()
                    E.mm(p_r[:], gw[:, 0, :], xcb[:, tsl], True, True, [gw, xcb], [p_r])
                    E.mm(p_i[:], gw[:, 1, :], xcb[:, tsl], True, True, [gw, xcb], [p_i])
                    E.act(a[:, tsl], p_r[:], AF.Sigmoid, [p_r, gb], [(a, tq)], bias=gb[:, d, 0, n:n + 1])
                    E.act(it[:, tsl], p_i[:], AF.Sigmoid, [p_i, gb], [(it, tq)], bias=gb[:, d, 1, n:n + 1])
                ak = [(a, t) for t in range(8)]
                ik = [(it, t) for t in range(8)]
                E.act(a[:], a[:], AF.Exp, ak + [sc], ak, scale=sc[:, d, n:n + 1])
                E.tt('pool', mt[:], a[:], a[:], ALU.mult, ak, [mt])
                E.act(mt[:], mt[:], AF.Sqrt, [mt], [mt], scale=-1.0, bias=1.0)
                first, mid = (0, HALF) if d == 0 else (T - 1, HALF - 1)
                E.memset('dve', mt[:, first:first + 1], 1.0, [mt])
                E.ts('dve', mt[:, mid:mid + 1], mt[:, mid:mid + 1], cont[:, 0:1], cont[:, 1:2], ALU.mult, ALU.add,
                     [mt, cont], [mt])
                E.tt('pool', it[:], it[:], xcf, ALU.mult, ik + [xc], ik)
                E.tt('dve', it[:], it[:], mt[:], ALU.mult, ik + [mt], ik)
                ini = inr.next()
                if d == 0:
                    E.scan(hs[:, 0:HALF], a[:, 0:HALF], it[:, 0:HALF], 0.0, ak + ik, [(hs, 0)])
                    E.ts('dve', ini[:, 0:1], hs[:, HALF - 1:HALF], cont[:, 0:1], None, ALU.mult, None,
                         [(hs, 0), cont], [ini])
                    E.scan(hs[:, HALF:T], a[:, HALF:T], it[:, HALF:T], ini[:, 0:1], ak + ik + [ini], [(hs, 1)])
                else:
                    E.scan(hs[:, HALF:T][:, ::-1], a[:, HALF:T][:, ::-1], it[:, HALF:T][:, ::-1], 0.0,
                           ak + ik, [(hs, 1)])
                    E.ts('dve', ini[:, 0:1], hs[:, HALF:HALF + 1], cont[:, 0:1], None, ALU.mult, None,
                         [(hs, 1), cont], [ini])
                    E.scan(hs[:, 0:HALF][:, ::-1], a[:, 0:HALF][:, ::-1], it[:, 0:HALF][:, ::-1], ini[:, 0:1],
                           ak + ik + [ini], [(hs, 0)])
                hs2.append(hs)
            hk = [(hs2[0], 0), (hs2[0], 1), (hs2[1], 0), (hs2[1], 1)]
            E.tt('pool', hs2[0][:], hs2[0][:], hs2[1][:], ALU.add, hk, [(hs2[0], 0), (hs2[0], 1)])
            ob = obr.next()
            E.tt('dve', ob[:], hs2[0][:], gT[:], ALU.mult, [(hs2[0], 0), (hs2[0], 1), gT], [ob])
            S.dma('pool', g.oT_d[n], ob[:], ob, reads=[ob])
    return w_out


CH = 64
NCH = T // CH


def rwkv_mixer(g, l):
    S, E = g.S, g.E
    din, dscr = g.din, g.dscr
    mu_in = din("rwkv_muP", [128, 6, KC])
    w_in = din("rwkv_w_in", [4, D, D])
    w0_in = din("rwkv_w0P", [128, 2, KC])
    a0_in = din("rwkv_a0P", [128, 2, KC])
    w1_in = din("rwkv_w1", [2, D, 96])
    w2_in = din("rwkv_w2", [2, 96, D])
    a1_in = din("rwkv_a1", [2, D, 96])
    a2_in = din("rwkv_a2", [2, 96, D])
    kk_in = din("rwkv_kkP", [128, KC])
    ka_in = din("rwkv_kaP", [128, KC])
    rk_in = din("rwkv_rkP", [128, KC])
    lng_in = din("rwkv_lngS", [2, 1024])
    lnb_in = din("rwkv_lnbS", [2, 1024])
    w_out = din("rwkv_w_out_perm", [D, D])
    mask4_in = din("rw_mask4", [2, 128, 512], BF16)
    maskT_in = din("rw_maskT", [2, 128, 128], BF16)
    bdm_in = din("rw_bdmask", [128, 128], BF16)
    lvm_in = din("rw_lvmask", [4, 128, 128], BF16)
    bones_in = din("rw_bones", [128, 128], BF16)
    r_d = dscr("rw_r", [KC, 128, T], F32)
    k_d = dscr("rw_k", [KC, 128, T], F32)
    a_d = dscr("rw_a", [2, KC, 128, T], F32)
    lw_d = dscr("rw_lw", [2, KC, 128, T], F32)
    v_st = dscr("rw_v", [2, T, 1024], BF16)
    sg_st = dscr("rw_sg", [2, T, 1024], BF16)
    X_d = dscr("rw_X", [2, NCH, 128, 16 * 4 * CH], BF16)
    rkr_d = dscr("rw_rkr", [NCH, 128, 16 * CH], BF16)
    o_st = dscr("rw_of", [2, T, 1024], F32)
    of_st = dscr("rw_ofin", [2, T, 1024], BF16)
    wc_d = dscr("rw_wc", [16, 128, KC * 512], BF16)
    w1c_d = dscr("rw_w1c", [4, 128, KC * 96], BF16)
    w2c_d = dscr("rw_w2c", [4, 96, D], BF16)
    cont = g.cont
    pcg = S.sb(g.glob, [128, 2, NCH, 16], F32, "pcg")

    for b in range(8):
      with Stage(S) as st:
        T0 = b * 512
        hx = st.sb([128, KC, 514], BF16, "hx")
        lo, hi = max(T0 - 1, 0), min(T0 + 513, T)
        S.dma('sp', hx[:, :, lo - T0 + 1:hi - T0 + 1], fmv(g.hT_d)[:, :, lo:hi], hx, writes=[hx])
        if b == 0:
            E.memset('dve', hx[:, :, 0:1], 0.0, [hx])
        if b == 7:
            E.memset('dve', hx[:, :, 513:514], 0.0, [hx])
        if b == 4:
            E.ts('dve', hx[:, :, 0:1], hx[:, :, 0:1], cont[:, 0:1], None, ALU.mult, None, [hx, cont], [hx])
        if b == 3:
            E.ts('dve', hx[:, :, 513:514], hx[:, :, 513:514], cont[:, 0:1], None, ALU.mult, None, [hx, cont], [hx])
        mu = st.sb([128, 6, KC], F32, "mu")
        S.dma('sp', mu[:], mu_in[:, :, :], mu, writes=[mu])
        bias0 = st.sb([128, 2, 2, KC], F32, "bias0")
        S.dma('sp', bias0[:, 0], w0_in[:, :, :], bias0, writes=[bias0])
        S.dma('sp', bias0[:, 1], a0_in[:, :, :], bias0, writes=[bias0])
        xx = st.sb([128, KC, 512], F32, "xx")
        tmpr = st.ring([128, 512], F32, 2, name="tmp")
        for kc in range(KC):
            tm = tmpr.next()
            E.tt('dve', tm[:], hx[:, kc, 0:512], hx[:, kc, 2:514], ALU.add, [hx], [tm])
            E.stt(xx[:, kc, :], tm[:], 0.5, hx[:, kc, 1:513], ALU.mult, ALU.subtract, [tm, hx], [(xx, kc)])
        xsr = st.ring([128, KC, 512], BF16, 2, name="xs")
        wl = WLoader(S, E, st, nb=256)
        wq = st.ring([128, KC, 512], BF16, 2, name="wq")
        pz = st.ring([128, 512], F32, 4, psum=True)
        pl = st.ring([128, 512], F32, 2, psum=True)

        def cached(tile, flat, cache_ap, fill):
            if b == 0:
                fill()
                S.dma('pool', cache_ap, flat, tile, reads=[tile])
            else:
                S.dma('sp', flat, cache_ap, tile, writes=[tile])

        fstr = st.ring([128, 4, 512], F32, 2, name="fst")
        vstr = [st.sb([128, 2, 16, 64], BF16, f"vst{i}") for i in range(4)]
        for m in range(6):
            xs = xsr.next()
            for kc in range(KC):
                E.stt(xs[:, kc, :], xx[:, kc, :], mu[:, m, kc:kc + 1], hx[:, kc, 1:513], ALU.mult, ALU.add,
                      [(xx, kc), mu, hx], [(xs, kc)])
            xk = [(xs, kc) for kc in range(KC)]
            if m < 2:
                dst_d = r_d if m == 0 else k_d
                for gq in range(4):
                    w = wq.next()
                    cached(w, w[:].rearrange("p c n -> p (c n)"), wc_d[m * 4 + gq],
                           lambda w=w, gq=gq: wl.load(w, lambda c0, c1, w=w: w[:, :, c0:c1],
                                                      wview(w_in[m])[:, :, gq * 512:(gq + 1) * 512], KC, 512))
                    fs = fstr.next()
                    for fc in range(4):
                        p = pz.next()
                        for kc in range(KC):
                            E.mm(p[:], w[:, kc, fc * 128:(fc + 1) * 128], xs[:, kc, :], kc == 0, kc == KC - 1,
                                 [w, (xs, kc)], [p])
                        E.act(fs[:, fc, :], p[:], AF.Copy, [p], [(fs, fc)])
                    S.dma('pool', fmv(dst_d)[:, gq * 4:gq * 4 + 4, T0:T0 + 512], fs[:], fs,
                          reads=[(fs, f) for f in range(4)])
            elif m < 4:
                dst_d = v_st if m == 2 else sg_st
                for n4 in range(4):
                    w = wq.next()
                    cached(w, w[:].rearrange("p c n -> p (c n)"), wc_d[m * 4 + n4],
                           lambda w=w, n4=n4: wl.load(w, lambda c0, c1, w=w: w[:, :, c0:c1],
                                                      wview(w_in[m])[:, :, n4 * 512:(n4 + 1) * 512], KC, 512))
                    for tt in range(4):
                        p = pz.next()
                        for kc in range(KC):
                            E.mm(p[:], xs[:, kc, tt * 128:(tt + 1) * 128], w[:, kc, :], kc == 0, kc == KC - 1,
                                 [w, (xs, kc)], [p])
                        vs = vstr[tt]
                        E.act(vs[:, :, n4 * 4:n4 * 4 + 4, :].rearrange("p h q i -> p q h i"),
                              p[:].rearrange("p (q h i) -> p q h i", h=2, i=64),
                              AF.Copy if m == 2 else AF.Silu, [p], [(vs, n4)])
                for tt in range(4):
                    for hh in range(2):
                        S.dma('pool', dst_d[hh, T0 + tt * 128:T0 + (tt + 1) * 128, :],
                              vstr[tt][:, hh, :, :].rearrange("p q i -> p (q i)"), vstr[tt],
                              reads=[(vstr[tt], n) for n in range(4)])
            else:
                which = m - 4
                l1_in, l2_in = (w1_in, w2_in) if which == 0 else (a1_in, a2_in)
                dst_d = lw_d if which == 0 else a_d
                for dr in range(2):
                    w1b = st.sb([128, KC, 96], BF16, "w1b") if (m == 4 and dr == 0) else w1b
                    cached(w1b, w1b[:].rearrange("p c n -> p (c n)"), w1c_d[which * 2 + dr],
                           lambda dr=dr: wl.load(w1b, lambda c0, c1: w1b[:, :, c0:c1], wview(l1_in[dr]), KC, 96))
                    p1 = pl.next()
                    for kc in range(KC):
                        E.mm(p1[0:96, :], w1b[:, kc, :], xs[:, kc, :], kc == 0, kc == KC - 1, [w1b, (xs, kc)], [p1])
                    tw = st.sb([96, 512], BF16, "tw") if (m == 4 and dr == 0) else tw
                    E.act(tw[:], p1[0:96, :], AF.Tanh if which == 0 else AF.Copy, [p1], [tw])
                    w2s = st.sb([96, D], F32, "w2s") if (m == 4 and dr == 0) else w2s
                    w2b = st.sb([96, D], BF16, "w2b") if (m == 4 and dr == 0) else w2b
                    def fill2(dr=dr):
                        S.dma('sp', w2s[:], l2_in[dr], w2s, writes=[w2s])
                        E.cp('pool', w2b[:], w2s[:], [w2s], [w2b])
                    cached(w2b, w2b[:], w2c_d[which * 2 + dr], fill2)
                    for gq in range(4):
                        fs = fstr.next()
                        for fc in range(4):
                            oc = gq * 4 + fc
                            p = pz.next()
                            E.mm(p[:], w2b[0:96, oc * 128:(oc + 1) * 128], tw[0:96, :], True, True, [w2b, tw], [p])
                            E.act(fs[:, fc, :], p[:], AF.Sigmoid, [p, bias0], [(fs, fc)],
                                  bias=bias0[:, which, dr, oc:oc + 1])
                            if which == 0:
                                E.ts('pool', fs[:, fc, :], fs[:, fc, :], -math.exp(-0.5), None, ALU.mult, None,
                                     [(fs, fc)], [(fs, fc)])
                        S.dma('pool', fmv(dst_d[dr])[:, gq * 4:gq * 4 + 4, T0:T0 + 512], fs[:], fs,
                              reads=[(fs, f) for f in range(4)])
    if g.stop < 3:
        return w_out

    with Stage(S) as st:
        vecs = st.sb([128, 3, KC], F32, "vecs")
        S.dma('sp', vecs[:, 0], kk_in[:, :], vecs, writes=[vecs])
        S.dma('sp', vecs[:, 1], ka_in[:, :], vecs, writes=[vecs])
        S.dma('sp', vecs[:, 2], rk_in[:, :], vecs, writes=[vecs])
        bones = st.sb([128, 128], BF16, "bones")
        S.dma('sp', bones[:], bones_in[:, :], bones, writes=[bones])
        cm = st.sb([128, 2, 4, 256], F32, "cm")
        E.memset('dve', cm[:], 1.0, [cm])
        for q in range(4):
            E.memset('dve', cm[:, 0, q, 0:256:64], 0.0, [cm])
            E.memset('dve', cm[:, 1, q, 63:256:64], 0.0, [cm])
        Xst = [st.sb([128, 4, 16, 4, CH], BF16, f"Xst{d}") for d in range(2)]
        rkst = st.sb([128, 4, 16, CH], BF16, "rkst")
        inr = st.ring([128, 6, 4, 256], F32, 2, name="inp")
        pn = st.ring([128, 512], F32, 2, psum=True)
        R = lambda nm, dt=F32, n=1: st.ring([128, 4, 256], dt, n, name=nm)
        nkr = R("nkkn")
        kkr, sqr, nrr, kknr, cr, e1r, e2r, e3r, t1r, kdr, kdsr = (R("kk"), R("sq", BF16), R("nr"), R("kkn"), R("c"),
                                                                 R("e1"), R("e2"), R("e3"), R("t1", F32, 3), R("kd"),
                                                                 R("kds"))
        v5 = lambda ap: ap.rearrange("p q (c t) -> p c q t", t=CH)
        for blk in range(16):
            tsl = slice(blk * 256, (blk + 1) * 256)
            for pg in range(4):
                it_ = inr.next()
                srcs = [r_d, k_d, a_d[0], a_d[1], lw_d[0], lw_d[1]]
                for i_, sd_ in enumerate(srcs):
                    S.dma('sp', it_[:, i_], fmv(sd_)[:, pg * 4:pg * 4 + 4, tsl], it_, writes=[(it_, i_)])
                ps4 = slice(pg * 4, pg * 4 + 4)
                vb = lambda j: vecs[:, j, ps4].unsqueeze(2).to_broadcast([128, 4, 256])
                r_, k_ = it_[:, 0], it_[:, 1]
                kk, sq, nr, kkn, kds = kkr.next(), sqr.next(), nrr.next(), kknr.next(), kdsr.next()
                E.tt('dve', kk[:], k_, vb(0), ALU.mult, [(it_, 1), vecs], [kk])
                E.tt('pool', sq[:], kk[:], kk[:], ALU.mult, [kk], [sq])
                for hq in range(2):
                    p = pn.next()
                    E.mm(p[:], bones[:], sq[:, 2 * hq:2 * hq + 2, :].rearrange("p q t -> p (q t)"), True, True,
                         [bones, sq], [p])
                    E.act(nr[:, 2 * hq:2 * hq + 2, :].rearrange("p q t -> p (q t)"), p[:], AF.Sqrt, [p], [(nr, hq)])
                nk = [(nr, 0), (nr, 1)]
                E.ts('dve', nr[:], nr[:], 1e-12, None, ALU.max, None, nk, nk)
                E.recip(nr[:], nr[:], nk, nk)
                E.tt('dve', kkn[:], kk[:], nr[:], ALU.mult, [kk] + nk, [kkn])
                nkkn = nkr.next()
                E.ts('pool', nkkn[:], kkn[:], -1.0, None, ALU.mult, None, [kkn], [nkkn])
                for dr in range(2):
                    a_, lw_ = it_[:, 2 + dr], it_[:, 4 + dr]
                    c, e1, e2, e3, t1, kd = cr.next(), e1r.next(), e2r.next(), e3r.next(), t1r.next(), kdr.next()
                    fl = lambda ap: ap.rearrange("p q t -> p (q t)")
                    if dr == 0:
                        E.scan(fl(c[:]), fl(cm[:, 0]), fl(lw_), 0.0, [cm, (it_, 4)], [c])
                    else:
                        E.scan(fl(c[:])[:, ::-1], fl(cm[:, 1])[:, ::-1], fl(lw_)[:, ::-1], 0.0, [cm, (it_, 5)], [c])
                    E.act(e1[:], c[:], AF.Exp, [c], [e1])
                    E.tt('pool', t1[:], c[:], lw_, ALU.subtract, [c, (it_, 4 + dr)], [t1])
                    E.act(e2[:], t1[:], AF.Exp, [t1], [e2])
                    E.act(e3[:], c[:], AF.Exp, [c], [e3], scale=-1.0)
                    X = Xst[dr]
                    E.tt('dve', X[:, :, ps4, 1, :], v5(r_), v5(e1[:]), ALU.mult, [(it_, 0), e1], [(X, pg, 1)])
                    E.tt('dve', X[:, :, ps4, 0, :], v5(nkkn[:]), v5(e2[:]), ALU.mult, [nkkn, e2], [(X, pg, 0)])
                    t2 = t1r.next()
                    E.tt('dve', t2[:], kkn[:], a_, ALU.mult, [kkn, (it_, 2 + dr)], [t2])
                    E.tt('dve', X[:, :, ps4, 2, :], v5(t2[:]), v5(e3[:]), ALU.mult, [t2, e3], [(X, pg, 2)])
                    t3 = t1r.next()
                    E.ts('dve', t3[:], a_, -1.0, None, ALU.add, None, [(it_, 2 + dr)], [t3])
                    E.tt('dve', t3[:], t3[:], vb(1), ALU.mult, [t3, vecs], [t3])
                    E.stt(kd[:], t3[:], 1.0, k_, ALU.add, ALU.mult, [t3, (it_, 1)], [kd])
                    E.tt('dve', X[:, :, ps4, 3, :], v5(kd[:]), v5(e3[:]), ALU.mult, [kd, e3], [(X, pg, 3)])
                    col = 63 if dr == 0 else 0
                    E.cp('pool', pcg[:, dr, blk * 4:blk * 4 + 4, ps4], e1[:, :, col:256:64].rearrange("p q c -> p c q"),
                         [e1], [(pcg, dr, blk, pg)])
                    if dr == 0:
                        E.cp('pool', kds[:], kd[:], [kd], [kds])
                    else:
                        E.tt('pool', kds[:], kds[:], kd[:], ALU.add, [kds, kd], [kds])
                E.tt('dve', kds[:], kds[:], vb(2), ALU.mult, [kds, vecs], [kds])
                E.tt('dve', rkst[:, :, ps4, :], v5(kds[:]), v5(r_), ALU.mult, [kds, (it_, 0)], [(rkst, pg)])
            for dr in range(2):
                X = Xst[dr]
                S.dma('pool', X_d[dr, blk * 4:blk * 4 + 4].rearrange("c p x -> p c x"),
                      X[:].rearrange("p c q x t -> p c (q x t)"), X,
                      reads=[(X, pg_, x) for pg_ in range(4) for x in range(4)])
            S.dma('pool', rkr_d[blk * 4:blk * 4 + 4].rearrange("c p x -> p c x"),
                  rkst[:].rearrange("p c q t -> p c (q t)"), rkst, reads=[(rkst, pg_) for pg_ in range(4)])

    import os
    if os.environ.get("RW_STOP") == "C1":
        return w_out
    for dr in range(1 if os.environ.get("RW_STOP") == "C2f" else 2):
      with Stage(S) as st:
        mask4 = st.sb([128, 512], BF16, "mask4")
        maskT = st.sb([128, 128], BF16, "maskT")
        bdm = st.sb([128, 2, 64], BF16, "bdm")
        S.dma('sp', mask4[:], mask4_in[dr], mask4, writes=[mask4])
        S.dma('sp', maskT[:], maskT_in[dr], maskT, writes=[maskT])
        S.dma('sp', bdm[:].rearrange("p h t -> p (h t)"), bdm_in[:, :], bdm, writes=[bdm])
        S32 = st.sb([128, 16, 64], F32, "S32")
        Sb = st.sb([128, 16, 64], BF16, "Sb")
        E.memset('dve', S32[:], 0.0, [(S32, 0), (S32, 1)])
        E.memset('dve', Sb[:], 0.0, [(Sb, 0), (Sb, 1)])
        Xr = st.ring([128, 16, 4, CH], BF16, 2, name="X")
        Vr = st.ring([128, 16, 64], BF16, 3, name="V")
        Ostr = st.ring([128, 16, 64], F32, 2, name="Ost")
        tmpr = st.ring([128, 8, 64], F32, 2, name="stmp")
        two = lambda shape, nm, dt=BF16: [st.sb(shape, dt, nm)] * 2
        ARs, Bbs, Kbs = two([128, 16, 256], "AR"), two([128, 16, 128], "Bb"), two([128, 16, 128], "Kb")
        G1s = two([128, 16, 512], "G1")
        sq16 = lambda nm: st.sb([128, 16, 128], BF16, nm)
        NTt, Nd, NTd, No, NTo, Mt, MTt, Tt, TTt, Yt, Zt = [sq16(n_) for n_ in
                                                          ("NT", "Nd", "NTd", "No", "NTo", "M", "MT", "T", "TT", "Y", "Z")]
        lvm = st.sb([128, 4, 128], BF16, "lvm")
        for i_ in range(4):
            S.dma('sp', lvm[:, i_, :], lvm_in[i_], lvm, writes=[lvm])
        Tbs = two([128, 16, 128], "Tb")
        BKTs, XTs, UTs = two([128, 16, 2, 128], "BKT"), two([128, 16, 64], "XT"), two([128, 16, 64], "UT")
        P1r = st.ring([128, 512], F32, 2, psum=True)
        Qr = st.ring([128, 512], F32, 4, psum=True)
        Sr = st.ring([128, 512], F32, 2, psum=True)
        if dr == 1:
            lng = st.sb([128, 16, 64], F32, "lng")
            lnb = st.sb([128, 16, 64], F32, "lnb")
            for hh in range(2):
                S.dma('sp', lng[hh * 64:(hh + 1) * 64].rearrange("p q i -> p (q i)"),
                      lng_in[hh:hh + 1, :].partition_broadcast(64), lng, writes=[lng])
                S.dma('sp', lnb[hh * 64:(hh + 1) * 64].rearrange("p q i -> p (q i)"),
                      lnb_in[hh:hh + 1, :].partition_broadcast(64), lnb, writes=[lnb])
            Ofr = st.ring([128, 16, 64], F32, 1, name="Of")
            sgr = st.ring([128, 16, 64], BF16, 2, name="sg")
            rkrr = st.ring([128, 16, CH], BF16, 2, name="rkr")
            rkbd = st.ring([128, 16, 2, 64], BF16, 1, name="rkbd")
            o2r = st.ring([128, 16, 64], F32, 1, name="o2")
            o3r = st.ring([128, 16, 64], F32, 1, name="o3")
            ofr = st.ring([128, 16, 64], BF16, 2, name="ofin")
            stt_r = st.ring([128, 8, 16], F32, 2, name="gnst")
        order = list(range(NCH)) if dr == 0 else list(range(NCH - 1, -1, -1))
        bd4 = bdm[:].unsqueeze(1).to_broadcast([128, 16, 2, 64])
        G4 = [list(range(4 * g_, 4 * g_ + 4)) for g_ in range(4)]
        q4 = lambda bank: bank[:].rearrange("p (j t) -> p j t", t=128)
        q8 = lambda bank: bank[:].rearrange("p (j t) -> p j t", t=64)

        def phaseG(ci, c):
            par = ci % 2
            AR, Bb, Kb, G1, BKT, Tb = ARs[par], Bbs[par], Kbs[par], G1s[par], BKTs[par], Tbs[par]
            X, V = Xr.next(), Vr.next()
            S.dma('sp', X[:].rearrange("p q x t -> p (q x t)"), X_d[dr, c], X, writes=[X])
            for hh in range(2):
                S.dma('sp', V[hh * 64:(hh + 1) * 64].rearrange("p q i -> p (q i)"),
                      v_st[hh, c * CH:(c + 1) * CH, :], V, writes=[V])
            x4 = lambda x: X[:, :, x, :].unsqueeze(2).to_broadcast([128, 16, 2, 64])
            ARv = AR[:].rearrange("p q (x h t) -> p q x h t", x=2, h=2)
            E.tt('dve', ARv[:, :, 0], x4(0), bd4, ALU.mult, [X, bdm], [(AR, 0)])
            E.tt('pool', ARv[:, :, 1], x4(1), bd4, ALU.mult, [X, bdm], [(AR, 1)])
            E.tt('dve', Bb[:].rearrange("p q (h t) -> p q h t", h=2), x4(2), bd4, ALU.mult, [X, bdm], [Bb])
            E.tt('pool', Kb[:].rearrange("p q (h t) -> p q h t", h=2), x4(3), bd4, ALU.mult, [X, bdm], [Kb])
            for q in range(16):
                p1 = P1r.next()
                E.mm(p1[:, 0:256], Bb[:, q, :], AR[:, q, :], True, True, [Bb, (AR, 0), (AR, 1)], [p1])
                E.mm(p1[:, 256:512], Kb[:, q, :], AR[:, q, :], True, True, [Kb, (AR, 0), (AR, 1)], [p1])
                E.tt('dve', G1[:, q, :], p1[:], mask4[:], ALU.mult, [p1, mask4], [(G1, q // 4)])
            m4 = lambda i_: lvm[:, i_, :].unsqueeze(1).to_broadcast([128, 4, 128])
            idb = g.ident[:].unsqueeze(1).to_broadcast([128, 4, 128])
            def chain(g_):
                gs_ = slice(4 * g_, 4 * g_ + 4)
                K_ = lambda t_: (t_, g_)

                def mm4(lhs, rhs, rk):
                    bank = Qr.next()
                    for j, q in enumerate(G4[g_]):
                        E.mm(bank[:, j * 128:(j + 1) * 128], lhs[:, q, :], rhs[:, q, :], True, True, rk, [bank])
                    return bank

                qb = Qr.next()
                for j, q in enumerate(G4[g_]):
                    E.mm(qb[:, j * 128:(j + 1) * 128], AR[:, q, 0:128], Bb[:, q, :], True, True, [(AR, 0), Bb], [qb])
                E.tt('dve', NTt[:, gs_, :], q4(qb), maskT[:].unsqueeze(1).to_broadcast([128, 4, 128]),
                     ALU.mult, [qb, maskT], [K_(NTt)])
                yield
                qt = Qr.next()
                qtb = qt[:].bitcast(BF16)
                for j, q in enumerate(G4[g_]):
                    E.tr(qtb[:, (2 * j) * 128:(2 * j + 1) * 128], Bb[:, q, :], g.ident[:], [Bb, g.ident], [qt])
                    E.tr(qtb[:, (2 * j + 1) * 128:(2 * j + 2) * 128], Kb[:, q, :], g.ident[:], [Kb, g.ident], [qt])
                E.act(BKT[:, gs_, :, :], qtb.rearrange("p (j w t) -> p j w t", w=2, t=128), AF.Copy, [qt], [K_(BKT)])
                Nn = G1[:, gs_, 0:128]
                E.tt('dve', Nd[:, gs_, :], Nn, m4(0), ALU.mult, [K_(G1), lvm], [K_(Nd)])
                E.tt('dve', NTd[:, gs_, :], NTt[:, gs_, :], m4(0), ALU.mult, [K_(NTt), lvm], [K_(NTd)])
                E.tt('dve', Tt[:, gs_, :], Nd[:, gs_, :], idb, ALU.add, [K_(Nd), g.ident], [K_(Tt)])
                E.tt('dve', TTt[:, gs_, :], NTd[:, gs_, :], idb, ALU.add, [K_(NTd), g.ident], [K_(TTt)])
                yield
                Mc, MTc = Nd, NTd
                for lvl in range(2):
                    ba = mm4(MTc, Mc, [K_(MTc), K_(Mc)])
                    bb = mm4(Mc, MTc, [K_(MTc), K_(Mc)])
                    E.act(Mt[:, gs_, :], q4(ba), AF.Copy, [ba], [K_(Mt)])
                    E.act(MTt[:, gs_, :], q4(bb), AF.Copy, [bb], [K_(MTt)])
                    yield
                    Mc, MTc = Mt, MTt
                    bc_ = mm4(MTc, Tt, [K_(MTc), K_(Tt)])
                    bd_ = mm4(Mc, TTt, [K_(Mc), K_(TTt)])
                    E.tt('dve', Tt[:, gs_, :], q4(bc_), Tt[:, gs_, :], ALU.add, [bc_, K_(Tt)], [K_(Tt)])
                    E.tt('dve', TTt[:, gs_, :], q4(bd_), TTt[:, gs_, :], ALU.add, [bd_, K_(TTt)], [K_(TTt)])
                    yield
                for mi in range(1, 4):
                    lastm = mi == 3
                    E.tt('dve', NTo[:, gs_, :], NTt[:, gs_, :], m4(mi), ALU.mult, [K_(NTt), lvm], [K_(NTo)])
                    by = mm4(NTo, Tt, [K_(NTo), K_(Tt)])
                    E.act(Yt[:, gs_, :], q4(by), AF.Copy, [by], [K_(Yt)])
                    if not lastm:
                        E.tt('dve', No[:, gs_, :], Nn, m4(mi), ALU.mult, [K_(G1), lvm], [K_(No)])
                        bz = mm4(No, TTt, [K_(No), K_(TTt)])
                        E.act(Zt[:, gs_, :], q4(bz), AF.Copy, [bz], [K_(Zt)])
                    yield
                    bc_ = mm4(TTt, Yt, [K_(TTt), K_(Yt)])
                    if not lastm:
                        bd_ = mm4(Tt, Zt, [K_(Tt), K_(Zt)])
                        E.tt('dve', Tt[:, gs_, :], q4(bc_), Tt[:, gs_, :], ALU.add, [bc_, K_(Tt)], [K_(Tt)])
                        E.tt('dve', TTt[:, gs_, :], q4(bd_), TTt[:, gs_, :], ALU.add, [bd_, K_(TTt)], [K_(TTt)])
                    else:
                        E.tt('dve', Tb[:, gs_, :], q4(bc_), Tt[:, gs_, :], ALU.add, [bc_, K_(Tt)], [K_(Tb)])
                    yield

            alive = [chain(g_) for g_ in range(4)]
            while alive:
                for gen in list(alive):
                    try:
                        next(gen)
                    except StopIteration:
                        alive.remove(gen)
            return V

        def phaseS(ci, c, V):
            par = ci % 2
            AR, G1, Tm, BKT, XT, UT = ARs[par], G1s[par], Tbs[par], BKTs[par], XTs[par], UTs[par]
            Ost = Ostr.next()
            H8 = [list(range(8 * h_, 8 * h_ + 8)) for h_ in range(2)]
            sk = lambda t_, h_: [(t_, 2 * h_), (t_, 2 * h_ + 1)]
            for h_ in range(2):
                sb = Sr.next()
                for j, q in enumerate(H8[h_]):
                    sl = sb[:, j * 64:(j + 1) * 64]
                    E.mm(sl, AR[:, q, 0:128], Sb[:, q, :], True, False, [(AR, 0), (Sb, h_)], [sb])
                    E.mm(sl, G1[:, q, 256:384], V[:, q, :], False, True, sk(G1, h_) + [V], [sb])
                E.act(XT[:, 8 * h_:8 * h_ + 8, :], q8(sb), AF.Copy, [sb], [(XT, h_)])
            for h_ in range(2):
                sb = Sr.next()
                for j, q in enumerate(H8[h_]):
                    E.mm(sb[:, j * 64:(j + 1) * 64], Tm[:, q, :], XT[:, q, :], True, True, sk(Tm, h_) + [(XT, h_)], [sb])
                E.act(UT[:, 8 * h_:8 * h_ + 8, :], q8(sb), AF.Copy, [sb], [(UT, h_)])
            for h_ in range(2):
                sb = Sr.next()
                for j, q in enumerate(H8[h_]):
                    sl = sb[:, j * 64:(j + 1) * 64]
                    E.mm(sl, AR[:, q, 128:256], Sb[:, q, :], True, False, [(AR, 1), (Sb, h_)], [sb])
                    E.mm(sl, G1[:, q, 128:256], UT[:, q, :], False, False, sk(G1, h_) + [(UT, h_)], [sb])
                    E.mm(sl, G1[:, q, 384:512], V[:, q, :], False, True, sk(G1, h_) + [V], [sb])
                E.cp('dve', Ost[:, 8 * h_:8 * h_ + 8, :], q8(sb), [sb], [(Ost, h_)])
            for h_ in range(2):
                sb = Sr.next()
                for j, q in enumerate(H8[h_]):
                    sl = sb[:, j * 64:(j + 1) * 64]
                    E.mm(sl, BKT[:, q, 0, :], UT[:, q, :], True, False, sk(BKT, h_) + [(UT, h_)], [sb])
                    E.mm(sl, BKT[:, q, 1, :], V[:, q, :], False, True, sk(BKT, h_) + [V], [sb])
                pcb = pcg[:, dr, c, 8 * h_:8 * h_ + 8].unsqueeze(2).to_broadcast([128, 8, 64])
                pk = [(pcg, dr, c // 4, q // 4) for q in H8[h_]]
                S8 = S32[:, 8 * h_:8 * h_ + 8, :]
                tm = tmpr.next()
                E.tt('pool', S8, S8, pcb, ALU.mult, [(S32, h_)] + pk, [(S32, h_)])
                E.tt('dve', tm[:], q8(sb), pcb, ALU.mult, [sb] + pk, [tm])
                E.tt('pool', S8, S8, tm[:], ALU.add, [(S32, h_), tm], [(S32, h_)])
                E.act(Sb[:, 8 * h_:8 * h_ + 8, :], S8, AF.Copy, [(S32, h_)], [(Sb, h_)])
            return Ost

        def reset_state():
            allk = [(S32, 0), (S32, 1)]
            E.ts('pool', S32[:], S32[:], cont[:, 0:1], None, ALU.mult, None, allk + [cont], allk)
            E.act(Sb[:], S32[:], AF.Copy, allk, [(Sb, 0), (Sb, 1)])

        def finalize(c, Ost, V):
            ok = [(Ost, 0), (Ost, 1)]
            if dr == 0:
                for hh in range(2):
                    S.dma('pool', o_st[hh, c * CH:(c + 1) * CH, :],
                          Ost[hh * 64:(hh + 1) * 64].rearrange("p q i -> p (q i)"), Ost, reads=ok)
                return
            Of, sg, rk, o2, o3, ofin, gs, rb = (Ofr.next(), sgr.next(), rkrr.next(), o2r.next(), o3r.next(), ofr.next(),
                                                stt_r.next(), rkbd.next())
            for hh in range(2):
                S.dma('sp', Of[hh * 64:(hh + 1) * 64].rearrange("p q i -> p (q i)"),
                      o_st[hh, c * CH:(c + 1) * CH, :], Of, writes=[Of])
                S.dma('sp', sg[hh * 64:(hh + 1) * 64].rearrange("p q i -> p (q i)"),
                      sg_st[hh, c * CH:(c + 1) * CH, :], sg, writes=[sg])
            S.dma('sp', rk[:].rearrange("p q t -> p (q t)"), rkr_d[c], rk, writes=[rk])
            E.tt('pool', rb[:], rk[:].unsqueeze(2).to_broadcast([128, 16, 2, 64]), bd4, ALU.mult, [rk, bdm], [rb])
            sb = Sr.next()
            for q in range(16):
                E.mm(sb[:, q:q + 1], rb[:, q].rearrange("p h t -> p (h t)"), g.ones_b[:, 0:1], True, True,
                     [rb, g.ones_b], [sb])
            E.cp('dve', gs[:, 7, :], sb[:, 0:16], [sb], [(gs, 7)])
            E.tt('pool', o2[:], Ost[:], Of[:], ALU.add, ok + [Of], [o2])
            bc = lambda j: gs[:, j, :].unsqueeze(2).to_broadcast([128, 16, 64])
            E.red(gs[:, 0, :], o2[:], ALU.add, [o2], [(gs, 0)])
            E.ts('dve', gs[:, 2, :], gs[:, 0, :], 1.0 / 64, None, ALU.mult, None, [(gs, 0)], [(gs, 2)])
            E.tt('dve', o2[:], o2[:], bc(2), ALU.subtract, [o2, (gs, 2)], [o2])
            E.tt('pool', o3[:], o2[:], o2[:], ALU.mult, [o2], [o3])
            E.red(gs[:, 1, :], o3[:], ALU.add, [o3], [(gs, 1)])
            E.act(gs[:, 5, :], gs[:, 1, :], AF.Sqrt, [(gs, 1)], [(gs, 5)], scale=1.0 / 64, bias=64e-5)
            E.recip(gs[:, 6, :], gs[:, 5, :], [(gs, 5)], [(gs, 6)])
            E.tt('dve', o2[:], o2[:], bc(6), ALU.mult, [o2, (gs, 6)], [o2])
            E.tt('pool', o2[:], o2[:], lng[:], ALU.mult, [o2, lng], [o2])
            E.tt('pool', o2[:], o2[:], lnb[:], ALU.add, [o2, lnb], [o2])
            E.tt('dve', o3[:], V[:], bc(7), ALU.mult, [V, (gs, 7)], [o3])
            E.tt('pool', o2[:], o2[:], o3[:], ALU.add, [o2, o3], [o2])
            E.tt('dve', ofin[:], o2[:], sg[:], ALU.mult, [o2, sg], [ofin])
            for hh in range(2):
                S.dma('pool', of_st[hh, c * CH:(c + 1) * CH, :],
                      ofin[hh * 64:(hh + 1) * 64].rearrange("p q i -> p (q i)"), ofin, reads=[ofin])

        for ci, c in enumerate(order):
            V = phaseG(ci, c)
            if ci == NCH // 2:
                reset_state()
            Ost = phaseS(ci, c, V)
            finalize(c, Ost, V)

    with Stage(S) as st:
        inr = st.ring([128, 1024], BF16, 3, name="oin")
        ptr = st.ring([128, 8, 128], BF16, 2, psum=True)
        ostr = [st.ring([128, 8, 512], BF16, 2, name=f"ost{hh}") for hh in range(2)]
        for grp in range(8):
            os_ = [ostr[hh].next() for hh in range(2)]
            for sub in range(4):
                tt = grp * 4 + sub
                for hh in range(2):
                    it_ = inr.next()
                    S.dma('sp', it_[:], of_st[hh, tt * 128:(tt + 1) * 128, :], it_, writes=[it_])
                    pt = ptr.next()
                    for q in range(8):
                        E.tr(pt[:, q, :], it_[:, q * 128:(q + 1) * 128], g.ident[:], [it_, g.ident], [pt])
                    if hh == 0:
                        E.cp('dve', os_[hh][:, :, sub * 128:(sub + 1) * 128], pt[:], [pt], [(os_[hh], sub)])
                    else:
                        E.act(os_[hh][:, :, sub * 128:(sub + 1) * 128], pt[:], AF.Copy, [pt], [(os_[hh], sub)])
            for hh in range(2):
                S.dma('pool', fmv(g.oT_d)[:, hh * 8:hh * 8 + 8, grp * 512:(grp + 1) * 512], os_[hh][:], os_[hh],
                      reads=[(os_[hh], s_) for s_ in range(4)])
    return w_out


def core_inputs_more(n, inp, pos, cont):
    if n == 'diff_w_in':
        return inp['diff_w_in'][0]
    if n == 'diff_lam':
        return inp['diff_lambda'][0].reshape(1, 256)
    if n == 'diff_gsubP':
        return _P(inp['diff_subln_g'][0])
    if n == 'diff_w_out':
        return inp['diff_w_out'][0]
    if n in ('diff_cos', 'diff_sin'):
        cs, sn = _rope_tables(pos, 16, 500000.0)
        t = (cs if n == 'diff_cos' else sn)[:, 0:8]
        return np.ascontiguousarray(t.reshape(NT, 128, 8).transpose(1, 0, 2))
    if n == 'lru_w_in':
        return inp['lru_w_in'][0]
    if n == 'lru_w_out':
        return inp['lru_w_out'][0]
    if n == 'lru_convP':
        cw = inp['lru_conv_w'][0]
        cb = inp['lru_conv_b'][0]
        a = np.concatenate([cw, cb[None, :]], axis=0)
        return np.ascontiguousarray(a.reshape(5, KC, 128).transpose(2, 1, 0))
    if n == 'lru_gate_w':
        return np.ascontiguousarray(inp['lru_gate_w'][0].transpose(3, 0, 1, 2, 4))
    if n == 'lru_gate_bP':
        return np.ascontiguousarray(inp['lru_gate_b'][0].reshape(2, 2, KC, 128).transpose(3, 0, 1, 2))
    if n == 'lru_lamP':
        return np.ascontiguousarray(inp['lru_lambda'][0].reshape(2, KC, 128).transpose(2, 0, 1))
    return core_inputs_rwkv(n, inp, pos, cont)


def _tri_masks():
    p = np.arange(128)
    same = (p[:, None] // 64) == (p[None, :] // 64)
    s_, t_ = p[:, None] % 64, p[None, :] % 64
    out4, outT = [], []
    for dr in range(2):
        strict = same & ((s_ < t_) if dr == 0 else (s_ > t_))
        incl = same & ((s_ <= t_) if dr == 0 else (s_ >= t_))
        out4.append(np.concatenate([strict, incl, strict, incl], axis=1))
        outT.append(strict.T)
    return np.stack(out4).astype(np.float32), np.stack(outT).astype(np.float32), same.astype(np.float32)


def core_inputs_rwkv(n, inp, pos, cont):
    if n == 'rwkv_muP':
        return np.ascontiguousarray(inp['rwkv_mu'][0].reshape(6, KC, 128).transpose(2, 0, 1))
    if n == 'rwkv_w_in':
        return inp['rwkv_w_in'][0]
    if n in ('rwkv_w0P', 'rwkv_a0P'):
        k = 'rwkv_w0' if n == 'rwkv_w0P' else 'rwkv_a0'
        return np.ascontiguousarray(inp[k][0].reshape(2, KC, 128).transpose(2, 0, 1))
    if n in ('rwkv_w1', 'rwkv_w2', 'rwkv_a1', 'rwkv_a2'):
        return inp[n][0]
    if n in ('rwkv_kkP', 'rwkv_kaP', 'rwkv_rkP'):
        k = {'rwkv_kkP': 'rwkv_k_k', 'rwkv_kaP': 'rwkv_k_a', 'rwkv_rkP': 'rwkv_r_k'}[n]
        return _P(inp[k][0].reshape(-1))
    if n in ('rwkv_lngS', 'rwkv_lnbS'):
        v = inp['rwkv_ln_g' if n == 'rwkv_lngS' else 'rwkv_ln_b'][0]
        return np.ascontiguousarray(v.reshape(16, 2, 64).transpose(1, 0, 2).reshape(2, 1024))
    if n == 'rwkv_w_out_perm':
        cp, e, i = np.meshgrid(np.arange(16), np.arange(2), np.arange(64), indexing='ij')
        hh, q = cp // 8, cp % 8
        perm = ((2 * q + e) * 128 + hh * 64 + i).reshape(-1)
        return np.ascontiguousarray(inp['rwkv_w_out'][0][perm, :])
    if n == 'rw_mask4':
        return _bf(_tri_masks()[0])
    if n == 'rw_maskT':
        return _bf(_tri_masks()[1])
    if n == 'rw_lvmask':
        p = np.arange(128)
        blk = lambda b: ((p[:, None] // b) == (p[None, :] // b)).astype(np.float32)
        return _bf(np.stack([blk(8), blk(16) - blk(8), blk(32) - blk(16), blk(64) - blk(32)]))
    if n in ('rw_bdmask', 'rw_bones'):
        return _bf(_tri_masks()[2])
    raise KeyError(n)


def _P(v):
    return np.ascontiguousarray(np.asarray(v, np.float32).reshape(-1, 128).T)


def _bf(a):
    return np.asarray(a, np.float32).astype(ml_dtypes.bfloat16)


def _rope_tables(pos, dim, theta):
    inv = (1.0 / (np.float32(theta) ** (np.arange(0, dim, 2, dtype=np.float32) / np.float32(dim)))).astype(np.float32)
    ang = pos.astype(np.float32)[:, None] * inv[None, :]
    ang = np.concatenate([ang, ang], axis=-1).astype(np.float32)
    return np.cos(ang).astype(np.float32), np.sin(ang).astype(np.float32)


def core_inputs(i, inp, names):
    if i < 4:
        x = inp['x_sample'][i]
        c2 = np.stack([inp['c_sample'][i], inp['c_sample'][i]])
        cont = 1.0
        pos = np.arange(T)
    else:
        j = i - 4
        x = np.concatenate([inp['x_prompt'][2 * j], inp['x_prompt'][2 * j + 1]], axis=0)
        c2 = np.stack([inp['c_prompt'][2 * j], inp['c_prompt'][2 * j + 1]])
        cont = 0.0
        pos = np.arange(T) % HALF
    hq = (np.arange(T) >= HALF).astype(np.float32)
    d = {}
    for n in names:
        if n == 'x':
            v = np.ascontiguousarray(x, np.float32)
        elif n == 'cT':
            v = np.ascontiguousarray(c2.reshape(2, KC, 128).transpose(2, 1, 0), np.float32)
        elif n == 'cont':
            v = np.tile(np.array([[cont, 1.0 - cont]], np.float32), (128, 1))
        elif n == 'seg_q':
            v = _bf(np.stack([BIG * hq, BIG * (1 - hq)]) * (1.0 - cont))
        elif n == 'seg_k':
            v = _bf(np.stack([-(1 - hq), -hq]))
        elif n == 'ident':
            v = _bf(np.eye(128))
        elif n == 'identf':
            v = np.eye(128, dtype=np.float32)
        elif n == 'ada_w':
            v = inp['ada_w']
        elif n == 'ada_bP':
            v = np.ascontiguousarray(np.stack([_P(inp['ada_b'][l]) for l in range(4)], axis=1))
        elif n == 'pre_gP':
            v = np.ascontiguousarray(np.stack([_P(inp['norm_pre_g'][l]) for l in range(4)], axis=1))
        elif n == 'post_g':
            v = inp['norm_post_g']
        elif n == 'mla_w_in':
            v = inp['mla_w_in'][0]
        elif n == 'mla_qgP':
            v = _P(inp['mla_q_norm_g'][0])
        elif n == 'mla_kvgP':
            v = _P(inp['mla_kv_norm_g'][0])
        elif n == 'mla_w_q_up':
            v = inp['mla_w_q_up'][0]
        elif n == 'mla_w_kv_up':
            v = inp['mla_w_kv_up'][0]
        elif n == 'mla_w_out':
            v = inp['mla_w_out'][0]
        elif n in ('mla_cosT', 'mla_sinT'):
            cs, sn = _rope_tables(pos, 64, 10000.0)
            v = np.ascontiguousarray((cs if n == 'mla_cosT' else sn).T)
        else:
            v = core_inputs_more(n, inp, pos, cont)
        d[n] = np.ascontiguousarray(v)
    return d


_PROG = {}


def run_layers(inputs, NL=4):
    if NL not in _PROG:
        _PROG[NL] = build(NL)
    g = _PROG[NL]
    inp = {k: np.asarray(v) for k, v in inputs.items()}
    in_maps = [core_inputs(i, inp, g.in_names) for i in range(8)]
    res = run_bass_kernel_spmd(g.nc, in_maps, core_ids=list(range(8)))
    return [r["y"] for r in res.results]


def kernel(**inputs):
    ys = run_layers(inputs, 4)
    y_sample = np.stack([np.asarray(ys[i], np.float32) for i in range(4)])
    y_prompt = np.stack([np.asarray(ys[4 + j // 2], np.float32)[(j % 2) * HALF:(j % 2 + 1) * HALF] for j in range(8)])
    return (y_prompt, y_sample)
```

```python
import contextlib
import math
import numpy as np
import ml_dtypes
import concourse.bass as bass
import concourse.mybir as mybir
from concourse.bass_utils import run_bass_kernel_spmd

F32 = mybir.dt.float32
BF16 = mybir.dt.bfloat16
AF = mybir.ActivationFunctionType
ALU = mybir.AluOpType
AX = mybir.AxisListType

EPOCH = 60000
DEPOCH = 3500
COMPUTE = ('pe', 'act', 'dve', 'pool')
QUEUES = COMPUTE + ('sp',)

T = 4096
D = 2048
KC = 16
NT = 32
HALF = 2048
BIG = 30000.0


class DSem:
    def __init__(self, S, name):
        self.S = S
        self.name = name
        self.handles = []
        self.counts = []
        self._new()

    def _new(self):
        self.handles.append(self.S._sem(f"d_{self.name}_{len(self.handles)}"))
        self.counts.append(0)


class Sched:
    def __init__(self, nc):
        self.nc = nc
        self.stack = contextlib.ExitStack()
        self.ops = {e: [] for e in QUEUES}
        self.csem = {e: [] for e in COMPUTE}
        self.ccnt = {e: [] for e in COMPUTE}
        self.last_w = {}
        self.readers = {}
        self.dsems = []
        self.free_ds = {}
        self.tile_ds = {}
        self.nsem = 0
        self.uid = 0
        self.waited = {e: {} for e in QUEUES}
        self.psum = {}
        self.last_acc = {}
        for e in COMPUTE:
            self._new_epoch(e)

    def _sem(self, name):
        self.nsem += 1
        return self.stack.enter_context(self.nc.semaphore(name))

    def _new_epoch(self, e):
        self.csem[e].append(self._sem(f"c_{e}_{len(self.csem[e])}"))
        self.ccnt[e].append(0)

    def get_ds(self, q):
        fl = self.free_ds.setdefault(q, [])
        if fl:
            return fl.pop()
        d = DSem(self, f"{q}{len(self.dsems)}")
        d.q = q
        self.dsems.append(d)
        return d

    def ds_of(self, tile, q):
        k = (id(tile), q)
        if k not in self.tile_ds:
            self.tile_ds[k] = self.get_ds(q)
        return self.tile_ds[k]

    def sb(self, sc, shape, dtype, name=None):
        self.uid += 1
        return sc.enter_context(self.nc.sbuf_tensor(f"{name or 't'}_{self.uid}", list(shape), dtype))

    def ps(self, sc, shape, dtype, name=None):
        self.uid += 1
        t = sc.enter_context(self.nc.psum_tensor(f"{name or 'p'}_{self.uid}", list(shape), dtype))
        nbytes = int(np.prod(shape[1:])) * (2 if dtype == BF16 else 4)
        assert nbytes % 2048 == 0, "PSUM tiles must be whole banks"
        self.psum[id(t)] = nbytes // 2048
        return t

    def _split(self, keys):
        norm, banks = [], []
        for k in keys:
            base = k[0] if isinstance(k, tuple) else k
            nb = self.psum.get(id(base)) if not isinstance(base, (str, int)) else None
            if nb is None:
                norm.append(self._key(k))
            elif nb == 1:
                banks.append(('B', id(base)))
            else:
                banks.append(('B', id(base), k[1]))
        return norm, banks

    @staticmethod
    def _key(k):
        if isinstance(k, tuple):
            return tuple(Sched._key(x) for x in k)
        if isinstance(k, (str, int)):
            return k
        return id(k)

    def _waits(self, eng, toks):
        w = {}
        for t in toks:
            if t[0] == 'c':
                _, e, ep, idx = t
                if e == eng and e == 'pe':
                    continue
                h = self.csem[e][ep]
                v = idx
            else:
                _, d, ep, cnt = t
                h = d.handles[ep]
                v = d.counts[ep]
            k = id(h)
            if k not in w or w[k][1] < v:
                w[k] = (h, v)
        return list(w.values())

    def _deps(self, reads, writes):
        toks = []
        for k in reads + writes:
            t = self.last_w.get(k)
            if t is not None:
                toks.append(t)
        for k in writes:
            r = self.readers.get(k)
            if r:
                toks.extend(r.values())
        return toks

    def _update(self, tok, reads, writes):
        for k in writes:
            self.last_w[k] = tok
            self.readers[k] = {}
        for k in reads:
            r = self.readers.setdefault(k, {})
            if tok[0] == 'c':
                r[(tok[1], tok[2])] = tok
            else:
                r[(id(tok[1]), tok[2])] = tok

    def op(self, eng, fn, reads=(), writes=()):
        reads, b1 = self._split(reads)
        writes, b2 = self._split(writes)
        toks = self._deps(reads, writes)
        for bk in b1 + b2:
            for e2, t in self.last_acc.get(bk, {}).items():
                if e2 != eng:
                    toks.append(t)
        waits = self._waits(eng, toks)
        if self.ccnt[eng][-1] >= EPOCH:
            self._new_epoch(eng)
        ep = len(self.ccnt[eng]) - 1
        self.ccnt[eng][ep] += 1
        tok = ('c', eng, ep, self.ccnt[eng][ep])
        self.ops[eng].append((waits, fn, (self.csem[eng][ep], 1)))
        self._update(tok, reads, writes)
        for bk in b1 + b2:
            self.last_acc.setdefault(bk, {})[eng] = tok
        return tok

    def dma(self, q, out, in_, tile, reads=(), writes=()):
        ds = self.ds_of(tile, q)
        reads = [self._key(k) for k in reads]
        writes = [self._key(k) for k in writes]
        waits = self._waits(q, self._deps(reads, writes))
        if ds.counts[-1] >= DEPOCH * 16:
            ds._new()
        ep = len(ds.counts) - 1
        ds.counts[ep] += 16
        tok = ('d', ds, ep, ds.counts[ep])
        self.ops[q].append((waits, (lambda e, o=out, i=in_: e.dma_start(out=o, in_=i)), (ds.handles[ep], 16)))
        self._update(tok, reads, writes)
        return tok

    def barrier(self):
        toks = []
        for e in COMPUTE:
            for ep in range(len(self.ccnt[e])):
                if self.ccnt[e][ep] > 0:
                    toks.append(('c', e, ep, self.ccnt[e][ep]))
        for d in self.dsems:
            for ep in range(len(d.counts)):
                if d.counts[ep] > 0:
                    toks.append(('d', d, ep, d.counts[ep]))
        for e in QUEUES:
            tk = [t for t in toks if not (t[0] == 'c' and t[1] == e)]
            self.ops[e].append((self._waits(e, tk), None, None))

    def release(self, tiles):
        for t in tiles:
            for q in QUEUES:
                k = (id(t), q)
                if k in self.tile_ds:
                    self.free_ds.setdefault(q, []).append(self.tile_ds.pop(k))

    def emit(self):
        self.barrier()
        nc = self.nc
        ops = self.ops
        self.ops = {e: [] for e in QUEUES}
        with nc.Block() as block:
            def run(e, name):
                waited = self.waited[name]
                for waits, fn, inc in ops[name]:
                    for h, v in waits:
                        k = id(h)
                        if waited.get(k, 0) < v:
                            e.wait_ge(h, v)
                            waited[k] = v
                    if fn is not None:
                        fn(e).then_inc(inc[0], inc[1])

            @block.tensor
            def _(e):
                run(e, 'pe')

            @block.scalar
            def _(e):
                run(e, 'act')

            @block.vector
            def _(e):
                run(e, 'dve')

            @block.gpsimd
            def _(e):
                run(e, 'pool')

            @block.sync
            def _(e):
                run(e, 'sp')


class Ring:
    def __init__(self, S, sc, shape, dtype, n, psum=False, name=None):
        self.tiles = [(S.ps if psum else S.sb)(sc, shape, dtype, name) for _ in range(n)]
        self.i = 0

    def next(self):
        t = self.tiles[self.i % len(self.tiles)]
        self.i += 1
        return t


class Stage:
    def __init__(self, S):
        self.S = S
        self.sc = contextlib.ExitStack()
        self.tiles = []

    def __enter__(self):
        self.sc.__enter__()
        return self

    def sb(self, shape, dtype, name=None):
        t = self.S.sb(self.sc, shape, dtype, name)
        self.tiles.append(t)
        return t

    def ps(self, shape, dtype, name=None):
        return self.S.ps(self.sc, shape, dtype, name)

    def ring(self, shape, dtype, n, psum=False, name=None):
        r = Ring(self.S, self.sc, shape, dtype, n, psum, name)
        if not psum:
            self.tiles.extend(r.tiles)
        return r

    def __exit__(self, *a):
        self.S.emit()
        self.S.release(self.tiles)
        return self.sc.__exit__(*a)


class K:
    pass


class Em:
    def __init__(self, S):
        self.S = S

    def mm(self, out, lhsT, rhs, start, stop, r, w):
        return self.S.op('pe', lambda e: e.matmul(out, lhsT=lhsT, rhs=rhs, start=start, stop=stop), r, w)

    def tr(self, out, in_, ident, r, w):
        return self.S.op('pe', lambda e: e.transpose(out=out, in_=in_, identity=ident), r, w)

    def act(self, out, in_, func, r, w, scale=1.0, bias=0.0, accum=None, eng='act'):
        if accum is None:
            return self.S.op(eng, lambda e: e.activation(out=out, in_=in_, func=func, scale=scale, bias=bias), r, w)
        return self.S.op(eng, lambda e: e.activation(out=out, in_=in_, func=func, scale=scale, bias=bias,
                                                     accum_out=accum), r, w)

    def tt(self, eng, out, in0, in1, op, r, w):
        return self.S.op(eng, lambda e: e.tensor_tensor(out=out, in0=in0, in1=in1, op=op), r, w)

    def ts(self, eng, out, in0, s1, s2, op0, op1, r, w):
        if s2 is None:
            return self.S.op(eng, lambda e: e.tensor_scalar(out=out, in0=in0, scalar1=s1, scalar2=None, op0=op0), r, w)
        return self.S.op(eng, lambda e: e.tensor_scalar(out=out, in0=in0, scalar1=s1, scalar2=s2, op0=op0, op1=op1), r, w)

    def stt(self, out, in0, scalar, in1, op0, op1, r, w):
        return self.S.op('dve', lambda e: e.scalar_tensor_tensor(out=out, in0=in0, scalar=scalar, in1=in1,
                                                                 op0=op0, op1=op1), r, w)

    def cp(self, eng, out, in_, r, w):
        return self.S.op(eng, lambda e: e.tensor_copy(out=out, in_=in_), r, w)

    def recip(self, out, in_, r, w):
        return self.S.op('dve', lambda e: e.reciprocal(out=out, in_=in_), r, w)

    def red(self, out, in_, op, r, w):
        return self.S.op('dve', lambda e: e.tensor_reduce(out=out, in_=in_, axis=AX.X, op=op), r, w)

    def memset(self, eng, ap, val, w):
        return self.S.op(eng, lambda e: e.memset(ap, val), (), w)

    def scan(self, out, d0, d1, init, r, w):
        return self.S.op('dve', lambda e: e.tensor_tensor_scan(out=out, data0=d0, data1=d1, initial=init,
                                                               op0=ALU.mult, op1=ALU.add), r, w)


def wview(w_ap):
    return w_ap.rearrange("(c p) n -> p c n", p=128)


def fmv(d_ap):
    return d_ap.rearrange("c p t -> p c t")


class WLoader:
    def __init__(self, S, E, st, kc=KC, nb=256):
        self.S, self.E = S, E
        self.ring = st.ring([128, kc, nb], F32, 2, name="wst")
        self.nb = nb

    def load(self, dst_tile, dst_fn, src_view, kc, ncols, scale=None):
        for c0 in range(0, ncols, self.nb):
            c1 = min(ncols, c0 + self.nb)
            w = self.ring.next()
            self.S.dma('sp', w[:, 0:kc, 0:c1 - c0], src_view[:, :, c0:c1], w, writes=[w])
            if scale is None:
                self.E.cp('pool', dst_fn(c0, c1), w[:, 0:kc, 0:c1 - c0], [w], [dst_tile])
            else:
                self.E.ts('pool', dst_fn(c0, c1), w[:, 0:kc, 0:c1 - c0], scale, None, ALU.mult, None, [w], [dst_tile])


class G:
    pass


def build(NL=4):
    nc = bass.Bass("TRN2", target_bir_lowering=False)
    S = Sched(nc)
    E = Em(S)
    g = G()
    g.nc, g.S, g.E = nc, S, E

    g.in_names = []

    def din(name, shape, dt=F32):
        g.in_names.append(name)
        return nc.dram_tensor(name, list(shape), dt, kind="ExternalInput").ap()

    def dscr(name, shape, dt):
        return nc.dram_tensor(name, list(shape), dt).ap()

    g.din, g.dscr = din, dscr
    g.x_in = din("x", [T, D])
    g.cT_in = din("cT", [128, KC, 2])
    g.cont_in = din("cont", [128, 2])
    g.seg_q = din("seg_q", [2, T], BF16)
    g.seg_k = din("seg_k", [2, T], BF16)
    g.ident_in = din("ident", [128, 128], BF16)
    g.identf_in = din("identf", [128, 128], F32)
    g.ada_w = din("ada_w", [4, D, 3 * D])
    g.ada_bP = din("ada_bP", [128, 4, 48])
    g.pre_gP = din("pre_gP", [128, 4, KC])
    g.post_g = din("post_g", [4, D])
    g.y_out = nc.dram_tensor("y", [T, D], F32, kind="ExternalOutput").ap()
    g.hT_d = dscr("hT_d", [KC, 128, T], BF16)
    g.oT_d = dscr("oT_d", [KC, 128, T], BF16)
    g.gT_d = dscr("gT_d", [KC, 128, T], BF16)
    g.gg_d = dscr("gg_d", [4, 2, D], F32)

    glob = contextlib.ExitStack()
    g.glob = glob
    g.ident = S.sb(glob, [128, 128], BF16, "ident")
    g.identf = S.sb(glob, [128, 128], F32, "identf")
    g.ones_b = S.sb(glob, [128, 128], BF16, "ones")
    g.cont = S.sb(glob, [128, 2], F32, "cont")
    g.preA = S.sb(glob, [128, 4, 2, KC], F32, "preA")
    g.preB = S.sb(glob, [128, 4, 2, KC], F32, "preB")
    S.dma('sp', g.ident[:], g.ident_in[:, :], g.ident, writes=[g.ident])
    S.dma('sp', g.identf[:], g.identf_in[:, :], g.identf, writes=[g.identf])
    S.dma('sp', g.cont[:], g.cont_in[:, :], g.cont, writes=[g.cont])
    E.memset('dve', g.ones_b[:], 1.0, [g.ones_b])

    import os
    g.stop = int(os.environ.get("KSTOP", "99"))
    prologue(g, NL)
    mixers = [mla_mixer, diff_mixer, lru_mixer, rwkv_mixer]
    for l in range(NL):
        if g.stop < 1:
            break
        stage_pre(g, l, g.x_in if l == 0 else g.y_out)
        if g.stop < 2:
            break
        w_out = mixers[l % 4](g, l)
        if g.stop < 4:
            break
        stage_post(g, l, w_out, g.x_in if l == 0 else g.y_out)
    if NL == 0:
        raise ValueError
    S.emit()
    return g


def prologue(g, NL):
    S, E = g.S, g.E
    with Stage(S) as st:
        cT = st.sb([128, KC, 2], F32)
        csT = st.sb([128, KC, 2], F32)
        adab = st.sb([128, 4, 48], F32)
        preg = st.sb([128, 4, KC], F32)
        modT = st.sb([128, 4, 2, 48], F32)
        gsb = st.sb([128, 2, 16], F32)
        ggs = st.sb([32, 128], F32)
        S.dma('sp', cT[:], g.cT_in[:, :, :], cT, writes=[cT])
        S.dma('sp', adab[:], g.ada_bP[:, :, :], adab, writes=[adab])
        S.dma('sp', preg[:], g.pre_gP[:, :, :], preg, writes=[preg])
        E.act(csT[:], cT[:], AF.Silu, [cT], [csT])
        wring = st.ring([128, KC, 512], F32, 2, name="adaw")
        pm = st.ring([128, 512], F32, 2, psum=True)
        for l in range(NL):
            pmod = pm.next()
            for blk in range(12):
                w = wring.next()
                S.dma('sp', w[:], wview(g.ada_w[l])[:, :, blk * 512:(blk + 1) * 512], w, writes=[w])
                for fc in range(4):
                    ch = blk * 4 + fc
                    for kc in range(KC):
                        E.mm(pmod[:, ch * 2:ch * 2 + 2], w[:, kc, fc * 128:(fc + 1) * 128], csT[:, kc, :],
                             kc == 0, kc == KC - 1, [w, csT], [pmod])
            E.tt('dve', modT[:, l, :, :], pmod[:, 0:96].rearrange("p (c h) -> p h c", h=2),
                 adab[:, l, :].unsqueeze(1).to_broadcast([128, 2, 48]), ALU.add, [pmod, adab], [modT])
            E.ts('dve', g.preA[:, l, :, :], modT[:, l, :, 16:32], 1.0, None, ALU.add, None, [modT], [g.preA])
            E.tt('dve', g.preA[:, l, :, :], g.preA[:, l, :, :],
                 preg[:, l, :].unsqueeze(1).to_broadcast([128, 2, KC]), ALU.mult, [g.preA, preg], [g.preA])
            E.cp('dve', g.preB[:, l, :, :], modT[:, l, :, 0:16], [modT], [g.preB])
            E.cp('dve', gsb[:], modT[:, l, :, 32:48], [modT], [gsb])
            pt = pm.next()
            E.tr(pt[0:32, 0:128], gsb[:].rearrange("p h c -> p (h c)"), g.identf[:], [gsb, g.identf], [pt])
            E.cp('dve', ggs[:], pt[0:32, 0:128], [pt], [ggs])
            S.dma('pool', g.gg_d[l].rearrange("h (c p) -> (h c) p", p=128), ggs[:], ggs, reads=[ggs])


def stage_pre(g, l, x_src):
    S, E = g.S, g.E
    with Stage(S) as st:
        xr = st.ring([128, D], F32, 3, name="x")
        junkr = st.ring([128, D], BF16, 2, name="junk")
        xs = st.ring([128, D], BF16, 2, name="xs")
        ssr = st.ring([128, 4], F32, 4, name="ss")
        ptr = st.ring([128, 1024], BF16, 4, psum=True)
        tmpr = st.ring([128, 8, 128], F32, 3, name="tmp")
        hst = st.ring([128, KC, 512], BF16, 2, name="hst")
        for grp in range(8):
            hs = hst.next()
            for sub in range(4):
                tt = grp * 4 + sub
                half = tt // 16
                x = xr.next()
                S.dma('sp', x[:], x_src[tt * 128:(tt + 1) * 128, :], x, writes=[x])
                ss = ssr.next()
                junk = junkr.next()
                E.act(junk[:], x[:], AF.Square, [x], [ss, junk], accum=ss[:, 0:1])
                E.act(ss[:, 1:2], ss[:, 0:1], AF.Sqrt, [ss], [ss], scale=1.0 / D, bias=1e-6)
                E.recip(ss[:, 2:3], ss[:, 1:2], [ss], [ss])
                xb = xs.next()
                E.ts('dve', xb[:], x[:], ss[:, 2:3], None, ALU.mult, None, [x, ss], [xb])
                for hc in range(2):
                    pt = ptr.next()
                    for c8 in range(8):
                        c = hc * 8 + c8
                        E.tr(pt[:, c8 * 128:(c8 + 1) * 128], xb[:, c * 128:(c + 1) * 128], g.ident[:],
                             [xb, g.ident], [pt])
                    tm = tmpr.next()
                    E.tt('dve', tm[:], pt[:].rearrange("p (c t) -> p c t", t=128),
                         g.preA[:, l, half, hc * 8:hc * 8 + 8].unsqueeze(2).to_broadcast([128, 8, 128]),
                         ALU.mult, [pt, g.preA], [tm])
                    E.tt('dve', hs[:, hc * 8:hc * 8 + 8, sub * 128:(sub + 1) * 128], tm[:],
                         g.preB[:, l, half, hc * 8:hc * 8 + 8].unsqueeze(2).to_broadcast([128, 8, 128]),
                         ALU.add, [tm, g.preB], [hs])
            S.dma('pool', fmv(g.hT_d)[:, :, grp * 512:(grp + 1) * 512], hs[:], hs, reads=[hs])


def stage_post(g, l, w_out, x_src):
    S, E = g.S, g.E
    with Stage(S) as st:
        wl = WLoader(S, E, st)
        wo = st.sb([128, KC, D], BF16, "wo")
        wl.load(wo, lambda c0, c1: wo[:, :, c0:c1], wview(w_out), KC, D)
        gg = st.sb([128, 2, D], F32, "gg")
        pg = st.sb([128, D], F32, "pg")
        for h in range(2):
            S.dma('sp', gg[:, h, :], g.gg_d[l, h:h + 1, :].partition_broadcast(128), gg, writes=[gg])
        S.dma('sp', pg[:], g.post_g[l:l + 1, :].partition_broadcast(128), pg, writes=[pg])
        E.tt('dve', gg[:], gg[:], pg[:].unsqueeze(1).to_broadcast([128, 2, D]), ALU.mult, [gg, pg], [gg])
        oring = st.ring([128, KC, 512], BF16, 2, name="o")
        xr = st.ring([128, D], F32, 2, name="x")
        tmp = st.ring([128, D], F32, 2, name="t")
        pyr = st.ring([128, D], F32, 2, psum=True)
        junkr = st.ring([128, 512], BF16, 4, name="junk")
        ssr = st.ring([128, 8], F32, 4, name="ss")
        for grp in range(8):
            o = oring.next()
            S.dma('sp', o[:], fmv(g.oT_d)[:, :, grp * 512:(grp + 1) * 512], o, writes=[o])
            for sub in range(4):
                tt = grp * 4 + sub
                half = tt // 16
                y = pyr.next()
                for n4 in range(4):
                    for kc in range(KC):
                        E.mm(y[:, n4 * 512:(n4 + 1) * 512], o[:, kc, sub * 128:(sub + 1) * 128],
                             wo[:, kc, n4 * 512:(n4 + 1) * 512], kc == 0, kc == KC - 1, [o, wo], [(y, n4)])
                ss = ssr.next()
                for n4 in range(4):
                    junk = junkr.next()
                    E.act(junk[:], y[:, n4 * 512:(n4 + 1) * 512], AF.Square, [(y, n4)], [(ss, n4), junk],
                          accum=ss[:, n4:n4 + 1])
                E.red(ss[:, 4:5], ss[:, 0:4], ALU.add, [(ss, 0), (ss, 1), (ss, 2), (ss, 3)], [(ss, 4)])
                E.act(ss[:, 5:6], ss[:, 4:5], AF.Sqrt, [(ss, 4)], [(ss, 5)], scale=1.0 / D, bias=1e-6)
                E.recip(ss[:, 6:7], ss[:, 5:6], [(ss, 5)], [(ss, 6)])
                x = xr.next()
                S.dma('sp', x[:], x_src[tt * 128:(tt + 1) * 128, :], x, writes=[x])
                t = tmp.next()
                for n4 in range(4):
                    sl = slice(n4 * 512, (n4 + 1) * 512)
                    E.stt(t[:, sl], y[:, sl], ss[:, 6:7], gg[:, half, sl], ALU.mult, ALU.mult,
                          [(y, n4), (ss, 6), gg], [(t, n4)])
                    E.tt('dve', t[:, sl], t[:, sl], x[:, sl], ALU.add, [(t, n4), x], [(t, n4)])
                S.dma('pool', g.y_out[tt * 128:(tt + 1) * 128, :], t[:], t,
                      reads=[(t, 0), (t, 1), (t, 2), (t, 3)])


def mla_mixer(g, l):
    S, E = g.S, g.E
    din, dscr = g.din, g.dscr
    w_in = din("mla_w_in", [D, 3136])
    qgP = din("mla_qgP", [128, 4])
    kvgP = din("mla_kvgP", [128, 4])
    w_q = din("mla_w_q_up", [512, 3072])
    w_kv = din("mla_w_kv_up", [512, 4096])
    w_out = din("mla_w_out", [D, D])
    cos_in = din("mla_cosT", [64, T])
    sin_in = din("mla_sinT", [64, T])
    lat_d = dscr("lat_d", [8, 128, T], BF16)
    kpe_d = dscr("kpe_d", [64, T], BF16)
    scale = 192 ** -0.5

    for hf in range(2):
      with Stage(S) as st:
        T0 = hf * HALF
        hT = st.sb([128, KC, HALF], BF16, "hT")
        for c in range(KC):
            S.dma('sp', hT[:, c, :], g.hT_d[c, :, T0:T0 + HALF], hT, writes=[(hT, c)])
        wl = WLoader(S, E, st, nb=128)
        wq = st.ring([128, KC, 512], BF16, 2, name="wq")
        pz = st.ring([128, 512], F32, 4, psum=True)
        pss = st.ring([128, 512], F32, 2, psum=True)
        zr = st.ring([128, 512], F32, 6, name="z")
        sqr = st.ring([128, 512], BF16, 2, name="sq")
        sdr = st.ring([128, 512], F32, 2, name="sd")
        ostr = st.ring([128, 4, 512], BF16, 2, name="ost")
        gt = st.sb([128, 8], F32, "gt")
        S.dma('sp', gt[:, 0:4], qgP[:, :], gt, writes=[gt])
        S.dma('sp', gt[:, 4:8], kvgP[:, :], gt, writes=[gt])
        for gi in range(2):
            w = wq.next()
            wl.load(w, lambda c0, c1, w=w: w[:, :, c0:c1], wview(w_in)[:, :, gi * 512:(gi + 1) * 512], KC, 512)
            for tq in range(4):
                tsl = slice(tq * 512, (tq + 1) * 512)
                gsl = slice(T0 + tq * 512, T0 + (tq + 1) * 512)
                zc = [zr.next() for _ in range(4)]
                psum_ss = pss.next()
                for fc in range(4):
                    p = pz.next()
                    for kc in range(KC):
                        E.mm(p[:], w[:, kc, fc * 128:(fc + 1) * 128], hT[:, kc, tsl], kc == 0, kc == KC - 1,
                             [w, (hT, kc)], [p])
                    sq = sqr.next()
                    E.act(sq[:], p[:], AF.Square, [p], [sq])
                    E.cp('dve', zc[fc][:], p[:], [p], [zc[fc]])
                    E.mm(psum_ss[:], g.ones_b[:], sq[:], fc == 0, fc == 3, [g.ones_b, sq], [psum_ss])
                sd = sdr.next()
                E.act(sd[:], psum_ss[:], AF.Sqrt, [psum_ss], [sd], scale=1.0 / 512, bias=1e-6)
                E.recip(sd[:], sd[:], [sd], [sd])
                ost = ostr.next()
                for fc in range(4):
                    E.stt(ost[:, fc, :], zc[fc][:], gt[:, gi * 4 + fc:gi * 4 + fc + 1], sd[:], ALU.mult, ALU.mult,
                          [zc[fc], gt, sd], [ost])
                S.dma('pool', fmv(lat_d)[:, gi * 4:gi * 4 + 4, gsl], ost[:], ost, reads=[ost])
        wk = st.sb([128, KC, 128], BF16, "wk")
        wv = wview(w_in)
        wl.load(wk, lambda c0, c1: wk[:, :, c0:c1], wv[:, :, 1024:1088], KC, 64)
        wl.load(wk, lambda c0, c1: wk[:, :, 64 + c0:64 + c1], wv[:, :, 1056:1088], KC, 32, scale=-1.0)
        wl.load(wk, lambda c0, c1: wk[:, :, 96 + c0:96 + c1], wv[:, :, 1024:1056], KC, 32)
        cosr = st.ring([64, 512], F32, 2, name="cos")
        sinr = st.ring([64, 512], F32, 2, name="sin")
        kstr = st.ring([64, 512], BF16, 2, name="kst")
        t1r = st.ring([64, 512], F32, 2, name="t1")
        t2r = st.ring([64, 512], F32, 2, name="t2")
        for tq in range(4):
            tsl = slice(tq * 512, (tq + 1) * 512)
            gsl = slice(T0 + tq * 512, T0 + (tq + 1) * 512)
            cos, sin = cosr.next(), sinr.next()
            S.dma('sp', cos[:], cos_in[:, gsl], cos, writes=[cos])
            S.dma('sp', sin[:], sin_in[:, gsl], sin, writes=[sin])
            pa, pb = pz.next(), pz.next()
            for kc in range(KC):
                E.mm(pa[0:64, :], wk[:, kc, 0:64], hT[:, kc, tsl], kc == 0, kc == KC - 1, [wk, (hT, kc)], [pa])
            for kc in range(KC):
                E.mm(pb[0:64, :], wk[:, kc, 64:128], hT[:, kc, tsl], kc == 0, kc == KC - 1, [wk, (hT, kc)], [pb])
            t1, t2, kst = t1r.next(), t2r.next(), kstr.next()
            E.tt('dve', t1[:], pa[0:64, :], cos[:], ALU.mult, [pa, cos], [t1])
            E.tt('dve', t2[:], pb[0:64, :], sin[:], ALU.mult, [pb, sin], [t2])
            E.tt('pool', kst[:], t1[:], t2[:], ALU.add, [t1, t2], [kst])
            S.dma('pool', kpe_d[:, gsl], kst[:], kst, reads=[kst])
        gstr = st.ring([128, 4, 512], BF16, 2, name="gst")
        for gq in range(4):
            w = wq.next()
            wl.load(w, lambda c0, c1, w=w: w[:, :, c0:c1], wv[:, :, 1088 + gq * 512:1088 + (gq + 1) * 512], KC, 512)
            for tq in range(4):
                tsl = slice(tq * 512, (tq + 1) * 512)
                gsl = slice(T0 + tq * 512, T0 + (tq + 1) * 512)
                gs = gstr.next()
                for fc in range(4):
                    p = pz.next()
                    for kc in range(KC):
                        E.mm(p[:], w[:, kc, fc * 128:(fc + 1) * 128], hT[:, kc, tsl], kc == 0, kc == KC - 1,
                             [w, (hT, kc)], [p])
                    E.act(gs[:, fc, :], p[:], AF.Silu, [p], [gs])
                S.dma('pool', fmv(g.gT_d)[:, gq * 4:gq * 4 + 4, gsl], gs[:], gs, reads=[gs])

    if g.stop < 3:
        return w_out
    with Stage(S) as st:
        qn = st.sb([128, 4, T], BF16, "qn")
        kvn = st.sb([128, 4, T], BF16, "kvn")
        for c in range(4):
            S.dma('sp', qn[:, c, :], lat_d[c], qn, writes=[qn])
            S.dma('sp', kvn[:, c, :], lat_d[4 + c], kvn, writes=[kvn])
        kaug = st.sb([66, T], BF16, "kaug")
        S.dma('sp', kaug[0:64, :], kpe_d[:, :], kaug, writes=[kaug])
        S.dma('sp', kaug[64:66, :], g.seg_k[:, :], kaug, writes=[kaug])
        qaugr = st.ring([66, T], BF16, 2, name="qaug")
        for qa in qaugr.tiles:
            S.dma('sp', qa[64:66, :], g.seg_q[:, :], qa, writes=[(qa, 'seg')])
        cosr = st.ring([64, 512], F32, 2, name="cos")
        sinr = st.ring([64, 512], F32, 2, name="sin")
        wl = WLoader(S, E, st, kc=4, nb=256)
        wqr = st.ring([128, 4, 256], BF16, 2, name="wqh")
        wkvr = st.ring([128, 4, 256], BF16, 2, name="wkvh")
        qnpr = st.ring([128, T], BF16, 2, name="qnp")
        knpr = st.ring([128, T], BF16, 2, name="knp")
        Vr = st.ring([128, NT, 128], BF16, 1, name="V")
        ps_s = st.ring([128, 512], F32, 3, psum=True)
        ps_o = st.ring([128, 512], F32, 1, psum=True)
        ps_m = st.ring([128, 512], F32, 1, psum=True)
        ps_x = st.ring([128, 512], F32, 3, psum=True)
        sqr = st.ring([128, 512], BF16, 4, name="sq")
        t1r = st.ring([64, 512], F32, 2, name="t1")
        t2r = st.ring([64, 512], F32, 2, name="t2")
        pTr = st.ring([128, 512], BF16, 8, name="pT")
        sgr = st.ring([128, 512], BF16, 4, name="sg")
        rsr = st.ring([128, 512], F32, 2, name="rs")
        ofr = st.ring([128, 512], F32, 2, name="of")
        obr = st.ring([128, 512], BF16, 2, name="ob")
        glr = st.ring([128, 512], BF16, 2, name="gl")
        statr = st.ring([128, 24], F32, 2, name="stat")
        wqv = w_q.rearrange("(c p) n -> p c n", p=128)
        wkvv = w_kv.rearrange("(c p) n -> p c n", p=128)
        for h in range(16):
            wqh, wkvh = wqr.next(), wkvr.next()
            wl.load(wqh, lambda c0, c1, t=wqh: t[:, :, c0:c1], wqv[:, :, 192 * h:192 * h + 192], 4, 192)
            wl.load(wqh, lambda c0, c1, t=wqh: t[:, :, 192 + c0:192 + c1], wqv[:, :, 192 * h + 160:192 * h + 192],
                    4, 32, scale=-1.0)
            wl.load(wqh, lambda c0, c1, t=wqh: t[:, :, 224 + c0:224 + c1], wqv[:, :, 192 * h + 128:192 * h + 160],
                    4, 32)
            wl.load(wkvh, lambda c0, c1, t=wkvh: t[:, :, c0:c1], wkvv[:, :, 256 * h:256 * h + 256], 4, 256)
            qnp, knp, V, qaug, stat = qnpr.next(), knpr.next(), Vr.next(), qaugr.next(), statr.next()
            for tq in range(8):
                tsl = slice(tq * 512, (tq + 1) * 512)
                p = ps_x.next()
                for kc in range(4):
                    E.mm(p[:], wqh[:, kc, 0:128], qn[:, kc, tsl], kc == 0, kc == 3, [wqh, qn], [p])
                E.act(qnp[:, tsl], p[:], AF.Copy, [p], [(qnp, tq)])
                sq1 = sqr.next()
                E.cp('dve', sq1[:], p[:], [p], [sq1])
                E.tt('pool', sq1[:], sq1[:], sq1[:], ALU.mult, [sq1], [sq1])
                pa, pb = ps_x.next(), ps_x.next()
                for kc in range(4):
                    E.mm(pa[0:64, :], wqh[:, kc, 128:192], qn[:, kc, tsl], kc == 0, kc == 3, [wqh, qn], [pa])
                for kc in range(4):
                    E.mm(pb[0:64, :], wqh[:, kc, 192:256], qn[:, kc, tsl], kc == 0, kc == 3, [wqh, qn], [pb])
                t1, t2 = t1r.next(), t2r.next()
                cos, sin = cosr.next(), sinr.next()
                S.dma('sp', cos[:], cos_in[:, tsl], cos, writes=[cos])
                S.dma('sp', sin[:], sin_in[:, tsl], sin, writes=[sin])
                E.tt('dve', t1[:], pa[0:64, :], cos[:], ALU.mult, [pa, cos], [t1])
                E.tt('dve', t2[:], pb[0:64, :], sin[:], ALU.mult, [pb, sin], [t2])
                E.tt('pool', qaug[0:64, tsl], t1[:], t2[:], ALU.add, [t1, t2], [(qaug, tq)])
                sq2 = sqr.next()
                E.tt('pool', sq2[0:64, :], qaug[0:64, tsl], qaug[0:64, tsl], ALU.mult, [(qaug, tq)], [sq2])
                pn = ps_x.next()
                E.mm(pn[:], g.ones_b[:], sq1[:], True, False, [g.ones_b, sq1], [pn])
                E.mm(pn[:], g.ones_b[0:64, :], sq2[0:64, :], False, True, [g.ones_b, sq2], [pn])
                E.red(stat[:, tq:tq + 1], pn[:], ALU.max, [pn], [(stat, 'q', tq)])
                p = ps_x.next()
                for kc in range(4):
                    E.mm(p[:], wkvh[:, kc, 0:128], kvn[:, kc, tsl], kc == 0, kc == 3, [wkvh, kvn], [p])
                E.act(knp[:, tsl], p[:], AF.Copy, [p], [(knp, tq)])
                sq3 = sqr.next()
                E.cp('dve', sq3[:], p[:], [p], [sq3])
                E.tt('pool', sq3[:], sq3[:], sq3[:], ALU.mult, [sq3], [sq3])
                sq4 = sqr.next()
                E.tt('pool', sq4[0:64, :], kaug[0:64, tsl], kaug[0:64, tsl], ALU.mult, [kaug], [sq4])
                pn = ps_x.next()
                E.mm(pn[:], g.ones_b[:], sq3[:], True, False, [g.ones_b, sq3], [pn])
                E.mm(pn[:], g.ones_b[0:64, :], sq4[0:64, :], False, True, [g.ones_b, sq4], [pn])
                E.red(stat[:, 8 + tq:9 + tq], pn[:], ALU.max, [pn], [(stat, 'k', tq)])
                p = ps_x.next()
                for j in range(4):
                    tt = tq * 4 + j
                    for kc in range(4):
                        E.mm(p[:, j * 128:(j + 1) * 128], kvn[:, kc, tt * 128:(tt + 1) * 128], wkvh[:, kc, 128:256],
                             kc == 0, kc == 3, [wkvh, kvn], [p])
                E.act(V[:, tq * 4:tq * 4 + 4, :], p[:].rearrange("p (j d) -> p j d", d=128), AF.Copy, [p], [(V, tq)])
            qk = [(stat, 'q', t) for t in range(8)]
            kk = [(stat, 'k', t) for t in range(8)]
            E.red(stat[:, 16:17], stat[:, 0:8], ALU.max, qk, [(stat, 16)])
            E.red(stat[:, 17:18], stat[:, 8:16], ALU.max, kk, [(stat, 17)])
            E.tt('dve', stat[:, 18:19], stat[:, 16:17], stat[:, 17:18], ALU.mult, [(stat, 16), (stat, 17)], [(stat, 18)])
            E.act(stat[:, 19:20], stat[:, 18:19], AF.Sqrt, [(stat, 18)], [(stat, 19)])
            E.ts('dve', stat[:, 20:21], stat[:, 19:20], -scale, None, ALU.mult, None, [(stat, 19)], [(stat, 20)])
            negB = stat[:, 20:21]
            qkeys = [(qnp, t) for t in range(8)] + [(qaug, t) for t in range(8)] + [(qaug, 'seg')]
            kkeys = [(knp, t) for t in range(8)] + [kaug]
            vkeys = [(V, t) for t in range(8)]
            units = [(tq, kb) for tq in range(8) for kb in range(NT)]

            def qk_mm(u):
                tq, kb = units[u]
                ps = ps_s.next()
                tsl = slice(tq * 512, (tq + 1) * 512)
                ksl = slice(kb * 128, (kb + 1) * 128)
                E.mm(ps[:], knp[:, ksl], qnp[:, tsl], True, False, [(knp, kb // 4), (qnp, tq)], [ps])
                E.mm(ps[:], kaug[0:66, ksl], qaug[0:66, tsl], False, True, [kaug, (qaug, tq), (qaug, 'seg')], [ps])
                return ps

            pend = [qk_mm(0), qk_mm(1)]
            po = psm = None
            for u, (tq, kb) in enumerate(units):
                ps = pend.pop(0)
                if u + 2 < len(units):
                    pend.append(qk_mm(u + 2))
                if kb == 0:
                    po, psm = ps_o.next(), ps_m.next()
                pT = pTr.next()
                E.act(pT[:], ps[:], AF.Exp, [ps, (stat, 20)], [pT], scale=scale, bias=negB)
                E.mm(po[:], V[:, kb, :], pT[:], kb == 0, kb == NT - 1, [(V, kb // 4), pT], [po])
                E.mm(psm[:], g.ones_b[:], pT[:], kb == 0, kb == NT - 1, [g.ones_b, pT], [psm])
                if kb == NT - 1:
                    tsl = slice(tq * 512, (tq + 1) * 512)
                    rs, of, ob, gl = rsr.next(), ofr.next(), obr.next(), glr.next()
                    S.dma('sp', gl[:], g.gT_d[h, :, tsl], gl, writes=[gl])
                    E.recip(rs[:], psm[:], [psm], [rs])
                    E.tt('dve', of[:], po[:], rs[:], ALU.mult, [po, rs], [of])
                    E.tt('pool', ob[:], of[:], gl[:], ALU.mult, [of, gl], [ob])
                    S.dma('pool', g.oT_d[h, :, tsl], ob[:], ob, reads=[ob])
    return w_out


def diff_mixer(g, l):
    S, E = g.S, g.E
    din, dscr = g.din, g.dscr
    w_in = din("diff_w_in", [D, 8192])
    lam_in = din("diff_lam", [1, 256])
    gsub_in = din("diff_gsubP", [128, 1])
    w_out = din("diff_w_out", [D, D])
    cos_in = din("diff_cos", [128, NT, 8])
    sin_in = din("diff_sin", [128, NT, 8])
    qk_d = dscr("dqk_d", [2, KC, 128, T], BF16)
    v_d = dscr("dv_d", [T, D], BF16)
    scale = 64 ** -0.5
    lam_init = 0.8 - 0.6 * math.exp(-0.3 * l)
    wv = wview(w_in)

    for hf in range(2):
      with Stage(S) as st:
        T0 = hf * HALF
        hT = st.sb([128, KC, HALF], BF16, "hT")
        for c in range(KC):
            S.dma('sp', hT[:, c, :], g.hT_d[c, :, T0:T0 + HALF], hT, writes=[(hT, c)])
        wl = WLoader(S, E, st, nb=128)
        wq = st.ring([128, KC, 512], BF16, 2, name="wq")
        pz = st.ring([128, 512], F32, 4, psum=True)
        ptr = st.ring([128, 8, 128], BF16, 2, psum=True)
        ctab = st.sb([128, NT, 8], F32, "ctab")
        stab = st.sb([128, NT, 8], F32, "stab")
        S.dma('sp', ctab[:], cos_in[:, :, :], ctab, writes=[ctab])
        S.dma('sp', stab[:], sin_in[:, :, :], stab, writes=[stab])
        qtmr = st.ring([128, 512], BF16, 3, name="qtm")
        rr = st.ring([128, 4, 8, 8], F32, 3, name="rope")
        qTr = st.ring([128, 4, HALF], BF16, 2, name="qTst")
        for sel in range(2):
            for cb in range(4):
                w = wq.next()
                col0 = sel * 2048 + cb * 512
                wl.load(w, lambda c0, c1, w=w: w[:, :, c0:c1], wv[:, :, col0:col0 + 512], KC, 512)
                qT = qTr.next()
                for tt in range(16):
                    gtt = hf * 16 + tt
                    p = pz.next()
                    for kc in range(KC):
                        E.mm(p[:], hT[:, kc, tt * 128:(tt + 1) * 128], w[:, kc, :], kc == 0, kc == KC - 1,
                             [w, (hT, kc)], [p])
                    qtm = qtmr.next()
                    E.act(qtm[:], p[:], AF.Copy, [p], [qtm])
                    p3 = p[:].rearrange("p (h d) -> p h d", d=64)
                    q3 = qtm[:].rearrange("p (h d) -> p h d", d=64)
                    cs = ctab[:, gtt, :].unsqueeze(1).to_broadcast([128, 8, 8])
                    sn = stab[:, gtt, :].unsqueeze(1).to_broadcast([128, 8, 8])
                    r = rr.next()
                    E.tt('dve', r[:, 0], p3[:, :, 0:8], cs, ALU.mult, [p, ctab], [(r, 0)])
                    E.tt('dve', r[:, 1], p3[:, :, 8:16], sn, ALU.mult, [p, stab], [(r, 1)])
                    E.tt('dve', r[:, 2], p3[:, :, 8:16], cs, ALU.mult, [p, ctab], [(r, 2)])
                    E.tt('dve', r[:, 3], p3[:, :, 0:8], sn, ALU.mult, [p, stab], [(r, 3)])
                    E.tt('pool', q3[:, :, 0:8], r[:, 0], r[:, 1], ALU.subtract, [(r, 0), (r, 1), qtm], [qtm])
                    E.tt('pool', q3[:, :, 8:16], r[:, 2], r[:, 3], ALU.add, [(r, 2), (r, 3), qtm], [qtm])
                    pt = ptr.next()
                    for j in range(4):
                        E.tr(pt[:, j, :], qtm[:, j * 128:(j + 1) * 128], g.ident[:], [qtm, g.ident], [pt])
                    E.cp('dve', qT[:, :, tt * 128:(tt + 1) * 128], pt[:, 0:4, :], [pt], [(qT, tt)])
                S.dma('pool', fmv(qk_d[sel])[:, cb * 4:cb * 4 + 4, T0:T0 + HALF], qT[:], qT,
                      reads=[(qT, t) for t in range(16)])
        vstr = st.ring([128, 512], BF16, 3, name="vst")
        for cb in range(4):
            w = wq.next()
            col0 = 4096 + cb * 512
            wl.load(w, lambda c0, c1, w=w: w[:, :, c0:c1], wv[:, :, col0:col0 + 512], KC, 512)
            for tt in range(16):
                p = pz.next()
                for kc in range(KC):
                    E.mm(p[:], hT[:, kc, tt * 128:(tt + 1) * 128], w[:, kc, :], kc == 0, kc == KC - 1,
                         [w, (hT, kc)], [p])
                vs = vstr.next()
                E.act(vs[:], p[:], AF.Copy, [p], [vs])
                S.dma('pool', v_d[T0 + tt * 128:T0 + (tt + 1) * 128, cb * 512:(cb + 1) * 512], vs[:], vs, reads=[vs])
        gstr = st.ring([128, 4, 512], BF16, 2, name="gst")
        for gq in range(4):
            w = wq.next()
            wl.load(w, lambda c0, c1, w=w: w[:, :, c0:c1], wv[:, :, 6144 + gq * 512:6144 + (gq + 1) * 512], KC, 512)
            for tq in range(4):
                tsl = slice(tq * 512, (tq + 1) * 512)
                gsl = slice(T0 + tq * 512, T0 + (tq + 1) * 512)
                gs = gstr.next()
                for fc in range(4):
                    p = pz.next()
                    for kc in range(KC):
                        E.mm(p[:], w[:, kc, fc * 128:(fc + 1) * 128], hT[:, kc, tsl], kc == 0, kc == KC - 1,
                             [w, (hT, kc)], [p])
                    E.act(gs[:, fc, :], p[:], AF.Silu, [p], [gs])
                S.dma('pool', fmv(g.gT_d)[:, gq * 4:gq * 4 + 4, gsl], gs[:], gs, reads=[gs])
    if g.stop < 3:
        return w_out

    with Stage(S) as st:
        lam = st.sb([128, 256], F32, "lam")
        S.dma('sp', lam[:], lam_in[0:1, :].partition_broadcast(128), lam, writes=[lam])
        lt = st.sb([128, 128], F32, "lt")
        ls = st.sb([128, 8], F32, "ls")
        E.tt('dve', lt[:, 0:64], lam[:, 0:64], lam[:, 64:128], ALU.mult, [lam], [lt])
        E.tt('dve', lt[:, 64:128], lam[:, 128:192], lam[:, 192:256], ALU.mult, [lam], [lt])
        E.red(ls[:, 0:1], lt[:, 0:64], ALU.add, [lt], [(ls, 0)])
        E.red(ls[:, 1:2], lt[:, 64:128], ALU.add, [lt], [(ls, 1)])
        E.act(ls[:, 2:4], ls[:, 0:2], AF.Exp, [(ls, 0), (ls, 1)], [(ls, 2)])
        E.tt('dve', ls[:, 4:5], ls[:, 3:4], ls[:, 2:3], ALU.subtract, [(ls, 2)], [(ls, 4)])
        E.ts('dve', ls[:, 5:6], ls[:, 4:5], -lam_init, None, ALU.add, None, [(ls, 4)], [(ls, 5)])
        neglam = ls[:, 5:6]
        gsub = st.sb([128, 2], F32, "gsub")
        S.dma('sp', gsub[:, 0:1], gsub_in[:, :], gsub, writes=[gsub])
        E.ts('dve', gsub[:, 1:2], gsub[:, 0:1], 1.0 - lam_init, None, ALU.mult, None, [gsub], [(gsub, 1)])
        qr = [st.ring([66, T], BF16, 2, name=f"q{c}") for c in range(2)]
        kr = [st.ring([66, T], BF16, 2, name=f"k{c}") for c in range(2)]
        for c in range(2):
            for t_ in qr[c].tiles:
                S.dma('sp', t_[64:66, :], g.seg_q[:, :], t_, writes=[(t_, 'seg')])
            for t_ in kr[c].tiles:
                S.dma('sp', t_[64:66, :], g.seg_k[:, :], t_, writes=[(t_, 'seg')])
        Vr = st.ring([128, NT, 128], BF16, 2, name="V")
        ps_s = st.ring([128, 512], F32, 3, psum=True)
        ps_o = [st.ring([128, 512], F32, 1, psum=True) for _ in range(2)]
        ps_m = [st.ring([128, 512], F32, 1, psum=True) for _ in range(2)]
        ps_x = st.ring([128, 512], F32, 1, psum=True)
        sqr = st.ring([128, 512], BF16, 3, name="sq")
        pTr = st.ring([128, 512], BF16, 14, name="pT")
        sgr = st.ring([128, 512], BF16, 6, name="sg")
        rsr = st.ring([128, 512], F32, 2, name="rs")
        ofr = st.ring([128, 512], F32, 4, name="of")
        obr = st.ring([128, 512], BF16, 2, name="ob")
        glr = st.ring([128, 512], BF16, 2, name="gl")
        statr = st.ring([128, 48], F32, 2, name="stat")
        vview = v_d.rearrange("(n p) f -> p n f", p=128)
        for h in range(16):
            q = [qr[c].next() for c in range(2)]
            k = [kr[c].next() for c in range(2)]
            V, stat = Vr.next(), statr.next()
            for c in range(2):
                S.dma('sp', q[c][0:64, :], qk_d[0, h, 64 * c:64 * c + 64, :], q[c], writes=[(q[c], 'd')])
                S.dma('sp', k[c][0:64, :], qk_d[1, h, 64 * c:64 * c + 64, :], k[c], writes=[(k[c], 'd')])
            for j in range(4):
                S.dma('sp', V[:, j * 8:j * 8 + 8, :], vview[:, j * 8:j * 8 + 8, h * 128:(h + 1) * 128], V, writes=[V])
            for ti, tl in enumerate((q[0], q[1], k[0], k[1])):
                for tq in range(8):
                    tsl = slice(tq * 512, (tq + 1) * 512)
                    sq = sqr.next()
                    E.tt('pool', sq[0:64, :], tl[0:64, tsl], tl[0:64, tsl], ALU.mult, [(tl, 'd')], [sq])
                    pn = ps_x.next()
                    E.mm(pn[:], g.ones_b[0:64, :], sq[0:64, :], True, True, [g.ones_b, sq], [pn])
                    E.red(stat[:, ti * 8 + tq:ti * 8 + tq + 1], pn[:], ALU.max, [pn], [(stat, ti, tq)])
                E.red(stat[:, 32 + ti:33 + ti], stat[:, ti * 8:ti * 8 + 8], ALU.max,
                      [(stat, ti, t) for t in range(8)], [(stat, 32 + ti)])
            negB = []
            for c in range(2):
                E.tt('dve', stat[:, 36 + c:37 + c], stat[:, 32 + c:33 + c], stat[:, 34 + c:35 + c], ALU.mult,
                     [(stat, 32 + c), (stat, 34 + c)], [(stat, 36 + c)])
                E.act(stat[:, 38 + c:39 + c], stat[:, 36 + c:37 + c], AF.Sqrt, [(stat, 36 + c)], [(stat, 38 + c)])
                E.ts('dve', stat[:, 40 + c:41 + c], stat[:, 38 + c:39 + c], -scale, None, ALU.mult, None,
                     [(stat, 38 + c)], [(stat, 40 + c)])
                negB.append(stat[:, 40 + c:41 + c])
            units = [(tq, kb, c) for tq in range(8) for kb in range(NT) for c in range(2)]

            def qk_mm(u):
                tq, kb, c = units[u]
                ps = ps_s.next()
                E.mm(ps[:], k[c][0:66, kb * 128:(kb + 1) * 128], q[c][0:66, tq * 512:(tq + 1) * 512], True, True,
                     [(k[c], 'd'), (k[c], 'seg'), (q[c], 'd'), (q[c], 'seg')], [ps])
                return ps

            pend = [qk_mm(0), qk_mm(1)]
            po = [None, None]
            psm = [None, None]
            grp = [[], []]
            for u, (tq, kb, c) in enumerate(units):
                ps = pend.pop(0)
                if u + 2 < len(units):
                    pend.append(qk_mm(u + 2))
                if kb == 0:
                    po[c], psm[c] = ps_o[c].next(), ps_m[c].next()
                pT = pTr.next()
                E.act(pT[:], ps[:], AF.Exp, [ps, (stat, 40 + c)], [pT], scale=scale, bias=negB[c])
                E.mm(po[c][:], V[:, kb, :], pT[:], kb == 0, kb == NT - 1, [V, pT], [po[c]])
                E.mm(psm[c][:], g.ones_b[:], pT[:], kb == 0, kb == NT - 1, [g.ones_b, pT], [psm[c]])
                if kb == NT - 1 and c == 1:
                    tsl = slice(tq * 512, (tq + 1) * 512)
                    gl = glr.next()
                    S.dma('sp', gl[:], g.gT_d[h, :, tsl], gl, writes=[gl])
                    o = []
                    for cc in range(2):
                        rs, of = rsr.next(), ofr.next()
                        E.recip(rs[:], psm[cc][:], [psm[cc]], [rs])
                        E.tt('dve', of[:], po[cc][:], rs[:], ALU.mult, [po[cc], rs], [of])
                        o.append(of)
                    od = ofr.next()
                    E.stt(od[:], o[1][:], neglam, o[0][:], ALU.mult, ALU.add, [o[0], o[1], (ls, 5)], [od])
                    sq = sqr.next()
                    E.tt('pool', sq[:], od[:], od[:], ALU.mult, [od], [sq])
                    pn = ps_x.next()
                    E.mm(pn[:], g.ones_b[:], sq[:], True, True, [g.ones_b, sq], [pn])
                    rs = rsr.next()
                    E.act(rs[:], pn[:], AF.Sqrt, [pn], [rs], scale=1.0 / 128, bias=1e-5)
                    E.recip(rs[:], rs[:], [rs], [rs])
                    on = ofr.next()
                    E.stt(on[:], od[:], gsub[:, 1:2], rs[:], ALU.mult, ALU.mult, [od, (gsub, 1), rs], [on])
                    ob = obr.next()
                    E.tt('pool', ob[:], on[:], gl[:], ALU.mult, [on, gl], [ob])
                    S.dma('pool', g.oT_d[h, :, tsl], ob[:], ob, reads=[ob])
    return w_out


def lru_mixer(g, l):
    S, E = g.S, g.E
    din, dscr = g.din, g.dscr
    w_in = din("lru_w_in", [D, 2 * D])
    w_out = din("lru_w_out", [D, D])
    conv_in = din("lru_convP", [128, KC, 5])
    gw_in = din("lru_gate_w", [128, 2, 2, KC, 128])
    gb_in = din("lru_gate_bP", [128, 2, 2, KC])
    lam_in = din("lru_lamP", [128, 2, KC])
    xb_d = dscr("xb_d", [KC, 128, T], F32)
    wv = wview(w_in)

    for hf in range(2):
      with Stage(S) as st:
        T0 = hf * HALF
        hT = st.sb([128, KC, HALF], BF16, "hT")
        for c in range(KC):
            S.dma('sp', hT[:, c, :], g.hT_d[c, :, T0:T0 + HALF], hT, writes=[(hT, c)])
        wl = WLoader(S, E, st, nb=128)
        wq = st.ring([128, KC, 512], BF16, 2, name="wq")
        pz = st.ring([128, 512], F32, 4, psum=True)
        xstr = st.ring([128, 4, 512], F32, 2, name="xst")
        gstr = st.ring([128, 4, 512], BF16, 2, name="gst")
        for part in range(2):
            for gq in range(4):
                w = wq.next()
                col0 = part * 2048 + gq * 512
                wl.load(w, lambda c0, c1, w=w: w[:, :, c0:c1], wv[:, :, col0:col0 + 512], KC, 512)
                for tq in range(4):
                    tsl = slice(tq * 512, (tq + 1) * 512)
                    gsl = slice(T0 + tq * 512, T0 + (tq + 1) * 512)
                    stt_ = (xstr if part == 0 else gstr).next()
                    for fc in range(4):
                        p = pz.next()
                        for kc in range(KC):
                            E.mm(p[:], w[:, kc, fc * 128:(fc + 1) * 128], hT[:, kc, tsl], kc == 0, kc == KC - 1,
                                 [w, (hT, kc)], [p])
                        E.act(stt_[:, fc, :], p[:], AF.Copy if part == 0 else AF.Silu, [p], [stt_])
                    dst = fmv(xb_d if part == 0 else g.gT_d)[:, gq * 4:gq * 4 + 4, gsl]
                    S.dma('pool', dst, stt_[:], stt_, reads=[stt_])
    if g.stop < 3:
        return w_out

    with Stage(S) as st:
        cvp = st.sb([128, KC, 5], F32, "cvp")
        gb = st.sb([128, 2, 2, KC], F32, "gb")
        lam = st.sb([128, 2, KC], F32, "lam")
        sc = st.sb([128, 2, KC], F32, "sc")
        S.dma('sp', cvp[:], conv_in[:, :, :], cvp, writes=[cvp])
        S.dma('sp', gb[:], gb_in[:, :, :, :], gb, writes=[gb])
        S.dma('sp', lam[:], lam_in[:, :, :], lam, writes=[lam])
        E.act(sc[:], lam[:], AF.Exp, [lam], [sc], scale=-1.0)
        E.act(sc[:], sc[:], AF.Ln, [sc], [sc], bias=1.0)
        E.ts('dve', sc[:], sc[:], -8.0, None, ALU.mult, None, [sc], [sc])
        cont = g.cont
        xbp = st.sb([128, 2, HALF + 3], F32, "xbp")
        xc = st.sb([128, 2, HALF], F32, "xc")
        xcb = st.sb([128, T], BF16, "xcb")
        gTr = st.ring([128, T], BF16, 2, name="gT")
        ar = st.ring([128, T], F32, 2, name="a")
        ir = st.ring([128, T], F32, 2, name="i")
        mr = st.ring([128, T], F32, 2, name="m")
        hr = st.ring([128, T], F32, 2, name="hs")
        obr = st.ring([128, T], BF16, 1, name="ob")
        gwsr = st.ring([128, 2, 128], F32, 2, name="gws")
        gwr = st.ring([128, 2, 128], BF16, 2, name="gw")
        inr = st.ring([128, 2], F32, 4, name="init")
        pg = st.ring([128, 512], F32, 4, psum=True)
        E.memset('dve', xbp[:, 0, 0:1], 0.0, [(xbp, 'z')])
        E.memset('dve', xbp[:, 1, HALF + 1:HALF + 3], 0.0, [(xbp, 'z')])
        for n in range(KC):
            for h in range(2):
                S.dma('sp', xbp[:, h, 1:HALF + 1], xb_d[n, :, h * HALF:(h + 1) * HALF], xbp, writes=[(xbp, h)])
            gT = gTr.next()
            S.dma('sp', gT[:], g.gT_d[n], gT, writes=[gT])
            E.ts('dve', xbp[:, 0, HALF + 1:HALF + 3], xbp[:, 1, 1:3], cont[:, 0:1], None, ALU.mult, None,
                 [(xbp, 1), cont], [(xbp, 'h0')])
            E.ts('dve', xbp[:, 1, 0:1], xbp[:, 0, HALF:HALF + 1], cont[:, 0:1], None, ALU.mult, None,
                 [(xbp, 0), cont], [(xbp, 'h1')])
            xk = [(xbp, 0), (xbp, 1), (xbp, 'h0'), (xbp, 'h1'), (xbp, 'z')]
            E.ts('dve', xc[:], xbp[:, :, 0:HALF], cvp[:, n, 0:1], cvp[:, n, 4:5], ALU.mult, ALU.add,
                 xk + [cvp], [xc])
            for j in range(1, 4):
                E.stt(xc[:], xbp[:, :, j:j + HALF], cvp[:, n, j:j + 1], xc[:], ALU.mult, ALU.add, xk + [cvp, xc], [xc])
            xcf = xc[:].rearrange("p h t -> p (h t)")
            E.act(xcb[:], xcf, AF.Copy, [xc], [xcb])
            hs2 = []
            for d in range(2):
                gws, gw = gwsr.next(), gwr.next()
                S.dma('sp', gws[:], gw_in[:, d, :, n, :], gws, writes=[gws])
                E.cp('pool', gw[:], gws[:], [gws], [gw])
                a, it, mt, hs = ar.next(), ir.next(), mr.next(), hr.next()
                for tq in range(8):
                    tsl = slice(tq * 512, (tq + 1) * 512)
                    p_r, p_i = pg.next(), pg.next()
                    E.mm(p_r[:], gw[:, 0, :], xcb[:, tsl], True, True, [gw, xcb], [p_r])
                    E.mm(p_i[:], gw[:, 1, :], xcb[:, tsl], True, True, [gw, xcb], [p_i])
                    E.act(a[:, tsl], p_r[:], AF.Sigmoid, [p_r, gb], [(a, tq)], bias=gb[:, d, 0, n:n + 1])
                    E.act(it[:, tsl], p_i[:], AF.Sigmoid, [p_i, gb], [(it, tq)], bias=gb[:, d, 1, n:n + 1])
                ak = [(a, t) for t in range(8)]
                ik = [(it, t) for t in range(8)]
                E.act(a[:], a[:], AF.Exp, ak + [sc], ak, scale=sc[:, d, n:n + 1])
                E.act(mt[:], a[:], AF.Square, ak, [mt])
                E.act(mt[:], mt[:], AF.Sqrt, [mt], [mt], scale=-1.0, bias=1.0)
                first, mid = (0, HALF) if d == 0 else (T - 1, HALF - 1)
                E.memset('dve', mt[:, first:first + 1], 1.0, [mt])
                E.ts('dve', mt[:, mid:mid + 1], mt[:, mid:mid + 1], cont[:, 0:1], cont[:, 1:2], ALU.mult, ALU.add,
                     [mt, cont], [mt])
                E.tt('dve', it[:], it[:], xcf, ALU.mult, ik + [xc], ik)
                E.tt('dve', it[:], it[:], mt[:], ALU.mult, ik + [mt], ik)
                ini = inr.next()
                if d == 0:
                    E.scan(hs[:, 0:HALF], a[:, 0:HALF], it[:, 0:HALF], 0.0, ak + ik, [(hs, 0)])
                    E.ts('dve', ini[:, 0:1], hs[:, HALF - 1:HALF], cont[:, 0:1], None, ALU.mult, None,
                         [(hs, 0), cont], [ini])
                    E.scan(hs[:, HALF:T], a[:, HALF:T], it[:, HALF:T], ini[:, 0:1], ak + ik + [ini], [(hs, 1)])
                else:
                    E.scan(hs[:, HALF:T][:, ::-1], a[:, HALF:T][:, ::-1], it[:, HALF:T][:, ::-1], 0.0,
                           ak + ik, [(hs, 1)])
                    E.ts('dve', ini[:, 0:1], hs[:, HALF:HALF + 1], cont[:, 0:1], None, ALU.mult, None,
                         [(hs, 1), cont], [ini])
                    E.scan(hs[:, 0:HALF][:, ::-1], a[:, 0:HALF][:, ::-1], it[:, 0:HALF][:, ::-1], ini[:, 0:1],
                           ak + ik + [ini], [(hs, 0)])
                hs2.append(hs)
            hk = [(hs2[0], 0), (hs2[0], 1), (hs2[1], 0), (hs2[1], 1)]
            E.tt('dve', hs2[0][:], hs2[0][:], hs2[1][:], ALU.add, hk, [(hs2[0], 0), (hs2[0], 1)])
            ob = obr.next()
            E.tt('dve', ob[:], hs2[0][:], gT[:], ALU.mult, [(hs2[0], 0), (hs2[0], 1), gT], [ob])
            S.dma('pool', g.oT_d[n], ob[:], ob, reads=[ob])
    return w_out


CH = 64
NCH = T // CH


def rwkv_mixer(g, l):
    S, E = g.S, g.E
    din, dscr = g.din, g.dscr
    mu_in = din("rwkv_muP", [128, 6, KC])
    w_in = din("rwkv_w_in", [4, D, D])
    w0_in = din("rwkv_w0P", [128, 2, KC])
    a0_in = din("rwkv_a0P", [128, 2, KC])
    w1_in = din("rwkv_w1", [2, D, 96])
    w2_in = din("rwkv_w2", [2, 96, D])
    a1_in = din("rwkv_a1", [2, D, 96])
    a2_in = din("rwkv_a2", [2, 96, D])
    kk_in = din("rwkv_kkP", [128, KC])
    ka_in = din("rwkv_kaP", [128, KC])
    rk_in = din("rwkv_rkP", [128, KC])
    lng_in = din("rwkv_lngS", [2, 1024])
    lnb_in = din("rwkv_lnbS", [2, 1024])
    w_out = din("rwkv_w_out_perm", [D, D])
    mask4_in = din("rw_mask4", [2, 128, 512], BF16)
    maskT_in = din("rw_maskT", [2, 128, 128], BF16)
    bdm_in = din("rw_bdmask", [128, 128], BF16)
    lvm_in = din("rw_lvmask", [4, 128, 128], BF16)
    bones_in = din("rw_bones", [128, 128], BF16)
    r_d = dscr("rw_r", [KC, 128, T], F32)
    k_d = dscr("rw_k", [KC, 128, T], F32)
    a_d = dscr("rw_a", [2, KC, 128, T], F32)
    lw_d = dscr("rw_lw", [2, KC, 128, T], F32)
    v_st = dscr("rw_v", [2, T, 1024], BF16)
    sg_st = dscr("rw_sg", [2, T, 1024], BF16)
    X_d = dscr("rw_X", [2, NCH, 128, 16 * 4 * CH], BF16)
    rkr_d = dscr("rw_rkr", [NCH, 128, 16 * CH], BF16)
    o_st = dscr("rw_of", [2, T, 1024], F32)
    of_st = dscr("rw_ofin", [2, T, 1024], BF16)
    wc_d = dscr("rw_wc", [16, 128, KC * 512], BF16)
    w1c_d = dscr("rw_w1c", [4, 128, KC * 96], BF16)
    w2c_d = dscr("rw_w2c", [4, 96, D], BF16)
    cont = g.cont
    pcg = S.sb(g.glob, [128, 2, NCH, 16], F32, "pcg")

    for b in range(8):
      with Stage(S) as st:
        T0 = b * 512
        hx = st.sb([128, KC, 514], BF16, "hx")
        lo, hi = max(T0 - 1, 0), min(T0 + 513, T)
        S.dma('sp', hx[:, :, lo - T0 + 1:hi - T0 + 1], fmv(g.hT_d)[:, :, lo:hi], hx, writes=[hx])
        if b == 0:
            E.memset('dve', hx[:, :, 0:1], 0.0, [hx])
        if b == 7:
            E.memset('dve', hx[:, :, 513:514], 0.0, [hx])
        if b == 4:
            E.ts('dve', hx[:, :, 0:1], hx[:, :, 0:1], cont[:, 0:1], None, ALU.mult, None, [hx, cont], [hx])
        if b == 3:
            E.ts('dve', hx[:, :, 513:514], hx[:, :, 513:514], cont[:, 0:1], None, ALU.mult, None, [hx, cont], [hx])
        mu = st.sb([128, 6, KC], F32, "mu")
        S.dma('sp', mu[:], mu_in[:, :, :], mu, writes=[mu])
        bias0 = st.sb([128, 2, 2, KC], F32, "bias0")
        S.dma('sp', bias0[:, 0], w0_in[:, :, :], bias0, writes=[bias0])
        S.dma('sp', bias0[:, 1], a0_in[:, :, :], bias0, writes=[bias0])
        xx = st.sb([128, KC, 512], F32, "xx")
        tmpr = st.ring([128, 512], F32, 2, name="tmp")
        for kc in range(KC):
            tm = tmpr.next()
            E.tt('dve', tm[:], hx[:, kc, 0:512], hx[:, kc, 2:514], ALU.add, [hx], [tm])
            E.stt(xx[:, kc, :], tm[:], 0.5, hx[:, kc, 1:513], ALU.mult, ALU.subtract, [tm, hx], [(xx, kc)])
        xsr = st.ring([128, KC, 512], BF16, 2, name="xs")
        wl = WLoader(S, E, st, nb=256)
        wq = st.ring([128, KC, 512], BF16, 2, name="wq")
        pz = st.ring([128, 512], F32, 4, psum=True)
        pl = st.ring([128, 512], F32, 2, psum=True)

        def cached(tile, flat, cache_ap, fill):
            if b == 0:
                fill()
                S.dma('pool', cache_ap, flat, tile, reads=[tile])
            else:
                S.dma('sp', flat, cache_ap, tile, writes=[tile])

        fstr = st.ring([128, 4, 512], F32, 2, name="fst")
        vstr = [st.sb([128, 2, 16, 64], BF16, f"vst{i}") for i in range(4)]
        for m in range(6):
            xs = xsr.next()
            for kc in range(KC):
                E.stt(xs[:, kc, :], xx[:, kc, :], mu[:, m, kc:kc + 1], hx[:, kc, 1:513], ALU.mult, ALU.add,
                      [(xx, kc), mu, hx], [(xs, kc)])
            xk = [(xs, kc) for kc in range(KC)]
            if m < 2:
                dst_d = r_d if m == 0 else k_d
                for gq in range(4):
                    w = wq.next()
                    cached(w, w[:].rearrange("p c n -> p (c n)"), wc_d[m * 4 + gq],
                           lambda w=w, gq=gq: wl.load(w, lambda c0, c1, w=w: w[:, :, c0:c1],
                                                      wview(w_in[m])[:, :, gq * 512:(gq + 1) * 512], KC, 512))
                    fs = fstr.next()
                    for fc in range(4):
                        p = pz.next()
                        for kc in range(KC):
                            E.mm(p[:], w[:, kc, fc * 128:(fc + 1) * 128], xs[:, kc, :], kc == 0, kc == KC - 1,
                                 [w, (xs, kc)], [p])
                        E.act(fs[:, fc, :], p[:], AF.Copy, [p], [(fs, fc)])
                    S.dma('pool', fmv(dst_d)[:, gq * 4:gq * 4 + 4, T0:T0 + 512], fs[:], fs,
                          reads=[(fs, f) for f in range(4)])
            elif m < 4:
                dst_d = v_st if m == 2 else sg_st
                for n4 in range(4):
                    w = wq.next()
                    cached(w, w[:].rearrange("p c n -> p (c n)"), wc_d[m * 4 + n4],
                           lambda w=w, n4=n4: wl.load(w, lambda c0, c1, w=w: w[:, :, c0:c1],
                                                      wview(w_in[m])[:, :, n4 * 512:(n4 + 1) * 512], KC, 512))
                    for tt in range(4):
                        p = pz.next()
                        for kc in range(KC):
                            E.mm(p[:], xs[:, kc, tt * 128:(tt + 1) * 128], w[:, kc, :], kc == 0, kc == KC - 1,
                                 [w, (xs, kc)], [p])
                        vs = vstr[tt]
                        E.act(vs[:, :, n4 * 4:n4 * 4 + 4, :].rearrange("p h q i -> p q h i"),
                              p[:].rearrange("p (q h i) -> p q h i", h=2, i=64),
                              AF.Copy if m == 2 else AF.Silu, [p], [(vs, n4)])
                for tt in range(4):
                    for hh in range(2):
                        S.dma('pool', dst_d[hh, T0 + tt * 128:T0 + (tt + 1) * 128, :],
                              vstr[tt][:, hh, :, :].rearrange("p q i -> p (q i)"), vstr[tt],
                              reads=[(vstr[tt], n) for n in range(4)])
            else:
                which = m - 4
                l1_in, l2_in = (w1_in, w2_in) if which == 0 else (a1_in, a2_in)
                dst_d = lw_d if which == 0 else a_d
                for dr in range(2):
                    w1b = st.sb([128, KC, 96], BF16, "w1b") if (m == 4 and dr == 0) else w1b
                    cached(w1b, w1b[:].rearrange("p c n -> p (c n)"), w1c_d[which * 2 + dr],
                           lambda dr=dr: wl.load(w1b, lambda c0, c1: w1b[:, :, c0:c1], wview(l1_in[dr]), KC, 96))
                    p1 = pl.next()
                    for kc in range(KC):
                        E.mm(p1[0:96, :], w1b[:, kc, :], xs[:, kc, :], kc == 0, kc == KC - 1, [w1b, (xs, kc)], [p1])
                    tw = st.sb([96, 512], BF16, "tw") if (m == 4 and dr == 0) else tw
                    E.act(tw[:], p1[0:96, :], AF.Tanh if which == 0 else AF.Copy, [p1], [tw])
                    w2s = st.sb([96, D], F32, "w2s") if (m == 4 and dr == 0) else w2s
                    w2b = st.sb([96, D], BF16, "w2b") if (m == 4 and dr == 0) else w2b
                    def fill2(dr=dr):
                        S.dma('sp', w2s[:], l2_in[dr], w2s, writes=[w2s])
                        E.cp('pool', w2b[:], w2s[:], [w2s], [w2b])
                    cached(w2b, w2b[:], w2c_d[which * 2 + dr], fill2)
                    for gq in range(4):
                        fs = fstr.next()
                        for fc in range(4):
                            oc = gq * 4 + fc
                            p = pz.next()
                            E.mm(p[:], w2b[0:96, oc * 128:(oc + 1) * 128], tw[0:96, :], True, True, [w2b, tw], [p])
                            E.act(fs[:, fc, :], p[:], AF.Sigmoid, [p, bias0], [(fs, fc)],
                                  bias=bias0[:, which, dr, oc:oc + 1])
                            if which == 0:
                                E.ts('pool', fs[:, fc, :], fs[:, fc, :], -math.exp(-0.5), None, ALU.mult, None,
                                     [(fs, fc)], [(fs, fc)])
                        S.dma('pool', fmv(dst_d[dr])[:, gq * 4:gq * 4 + 4, T0:T0 + 512], fs[:], fs,
                              reads=[(fs, f) for f in range(4)])
    if g.stop < 3:
        return w_out

    with Stage(S) as st:
        vecs = st.sb([128, 3, KC], F32, "vecs")
        S.dma('sp', vecs[:, 0], kk_in[:, :], vecs, writes=[vecs])
        S.dma('sp', vecs[:, 1], ka_in[:, :], vecs, writes=[vecs])
        S.dma('sp', vecs[:, 2], rk_in[:, :], vecs, writes=[vecs])
        bones = st.sb([128, 128], BF16, "bones")
        S.dma('sp', bones[:], bones_in[:, :], bones, writes=[bones])
        cm = st.sb([128, 2, 4, 256], F32, "cm")
        E.memset('dve', cm[:], 1.0, [cm])
        for q in range(4):
            E.memset('dve', cm[:, 0, q, 0:256:64], 0.0, [cm])
            E.memset('dve', cm[:, 1, q, 63:256:64], 0.0, [cm])
        Xst = [st.sb([128, 4, 16, 4, CH], BF16, f"Xst{d}") for d in range(2)]
        rkst = st.sb([128, 4, 16, CH], BF16, "rkst")
        inr = st.ring([128, 6, 4, 256], F32, 2, name="inp")
        pn = st.ring([128, 512], F32, 2, psum=True)
        R = lambda nm, dt=F32, n=1: st.ring([128, 4, 256], dt, n, name=nm)
        nkr = R("nkkn")
        kkr, sqr, nrr, kknr, cr, e1r, e2r, e3r, t1r, kdr, kdsr = (R("kk"), R("sq", BF16), R("nr"), R("kkn"), R("c"),
                                                                 R("e1"), R("e2"), R("e3"), R("t1", F32, 3), R("kd"),
                                                                 R("kds"))
        v5 = lambda ap: ap.rearrange("p q (c t) -> p c q t", t=CH)
        for blk in range(16):
            tsl = slice(blk * 256, (blk + 1) * 256)
            for pg in range(4):
                it_ = inr.next()
                srcs = [r_d, k_d, a_d[0], a_d[1], lw_d[0], lw_d[1]]
                for i_, sd_ in enumerate(srcs):
                    S.dma('sp', it_[:, i_], fmv(sd_)[:, pg * 4:pg * 4 + 4, tsl], it_, writes=[(it_, i_)])
                ps4 = slice(pg * 4, pg * 4 + 4)
                vb = lambda j: vecs[:, j, ps4].unsqueeze(2).to_broadcast([128, 4, 256])
                r_, k_ = it_[:, 0], it_[:, 1]
                kk, sq, nr, kkn, kds = kkr.next(), sqr.next(), nrr.next(), kknr.next(), kdsr.next()
                E.tt('dve', kk[:], k_, vb(0), ALU.mult, [(it_, 1), vecs], [kk])
                E.tt('pool', sq[:], kk[:], kk[:], ALU.mult, [kk], [sq])
                for hq in range(2):
                    p = pn.next()
                    E.mm(p[:], bones[:], sq[:, 2 * hq:2 * hq + 2, :].rearrange("p q t -> p (q t)"), True, True,
                         [bones, sq], [p])
                    E.act(nr[:, 2 * hq:2 * hq + 2, :].rearrange("p q t -> p (q t)"), p[:], AF.Sqrt, [p], [(nr, hq)])
                nk = [(nr, 0), (nr, 1)]
                E.ts('dve', nr[:], nr[:], 1e-12, None, ALU.max, None, nk, nk)
                E.recip(nr[:], nr[:], nk, nk)
                E.tt('dve', kkn[:], kk[:], nr[:], ALU.mult, [kk] + nk, [kkn])
                nkkn = nkr.next()
                E.ts('pool', nkkn[:], kkn[:], -1.0, None, ALU.mult, None, [kkn], [nkkn])
                for dr in range(2):
                    a_, lw_ = it_[:, 2 + dr], it_[:, 4 + dr]
                    c, e1, e2, e3, t1, kd = cr.next(), e1r.next(), e2r.next(), e3r.next(), t1r.next(), kdr.next()
                    fl = lambda ap: ap.rearrange("p q t -> p (q t)")
                    if dr == 0:
                        E.scan(fl(c[:]), fl(cm[:, 0]), fl(lw_), 0.0, [cm, (it_, 4)], [c])
                    else:
                        E.scan(fl(c[:])[:, ::-1], fl(cm[:, 1])[:, ::-1], fl(lw_)[:, ::-1], 0.0, [cm, (it_, 5)], [c])
                    E.act(e1[:], c[:], AF.Exp, [c], [e1])
                    E.tt('pool', t1[:], c[:], lw_, ALU.subtract, [c, (it_, 4 + dr)], [t1])
                    E.act(e2[:], t1[:], AF.Exp, [t1], [e2])
                    E.act(e3[:], c[:], AF.Exp, [c], [e3], scale=-1.0)
                    X = Xst[dr]
                    E.tt('dve', X[:, :, ps4, 1, :], v5(r_), v5(e1[:]), ALU.mult, [(it_, 0), e1], [(X, pg, 1)])
                    E.tt('dve', X[:, :, ps4, 0, :], v5(nkkn[:]), v5(e2[:]), ALU.mult, [nkkn, e2], [(X, pg, 0)])
                    t2 = t1r.next()
                    E.tt('dve', t2[:], kkn[:], a_, ALU.mult, [kkn, (it_, 2 + dr)], [t2])
                    E.tt('dve', X[:, :, ps4, 2, :], v5(t2[:]), v5(e3[:]), ALU.mult, [t2, e3], [(X, pg, 2)])
                    t3 = t1r.next()
                    E.ts('dve', t3[:], a_, -1.0, None, ALU.add, None, [(it_, 2 + dr)], [t3])
                    E.tt('dve', t3[:], t3[:], vb(1), ALU.mult, [t3, vecs], [t3])
                    E.stt(kd[:], t3[:], 1.0, k_, ALU.add, ALU.mult, [t3, (it_, 1)], [kd])
                    E.tt('dve', X[:, :, ps4, 3, :], v5(kd[:]), v5(e3[:]), ALU.mult, [kd, e3], [(X, pg, 3)])
                    col = 63 if dr == 0 else 0
                    E.cp('pool', pcg[:, dr, blk * 4:blk * 4 + 4, ps4], e1[:, :, col:256:64].rearrange("p q c -> p c q"),
                         [e1], [(pcg, dr, blk, pg)])
                    if dr == 0:
                        E.cp('pool', kds[:], kd[:], [kd], [kds])
                    else:
                        E.tt('pool', kds[:], kds[:], kd[:], ALU.add, [kds, kd], [kds])
                E.tt('dve', kds[:], kds[:], vb(2), ALU.mult, [kds, vecs], [kds])
                E.tt('dve', rkst[:, :, ps4, :], v5(kds[:]), v5(r_), ALU.mult, [kds, (it_, 0)], [(rkst, pg)])
            for dr in range(2):
                X = Xst[dr]
                S.dma('pool', X_d[dr, blk * 4:blk * 4 + 4].rearrange("c p x -> p c x"),
                      X[:].rearrange("p c q x t -> p c (q x t)"), X,
                      reads=[(X, pg_, x) for pg_ in range(4) for x in range(4)])
            S.dma('pool', rkr_d[blk * 4:blk * 4 + 4].rearrange("c p x -> p c x"),
                  rkst[:].rearrange("p c q t -> p c (q t)"), rkst, reads=[(rkst, pg_) for pg_ in range(4)])

    import os
    if os.environ.get("RW_STOP") == "C1":
        return w_out
    for dr in range(1 if os.environ.get("RW_STOP") == "C2f" else 2):
      with Stage(S) as st:
        mask4 = st.sb([128, 512], BF16, "mask4")
        maskT = st.sb([128, 128], BF16, "maskT")
        bdm = st.sb([128, 2, 64], BF16, "bdm")
        S.dma('sp', mask4[:], mask4_in[dr], mask4, writes=[mask4])
        S.dma('sp', maskT[:], maskT_in[dr], maskT, writes=[maskT])
        S.dma('sp', bdm[:].rearrange("p h t -> p (h t)"), bdm_in[:, :], bdm, writes=[bdm])
        S32 = st.sb([128, 16, 64], F32, "S32")
        Sb = st.sb([128, 16, 64], BF16, "Sb")
        E.memset('dve', S32[:], 0.0, [(S32, 0), (S32, 1)])
        E.memset('dve', Sb[:], 0.0, [(Sb, 0), (Sb, 1)])
        Xr = st.ring([128, 16, 4, CH], BF16, 2, name="X")
        Vr = st.ring([128, 16, 64], BF16, 3, name="V")
        Ostr = st.ring([128, 16, 64], F32, 2, name="Ost")
        tmpr = st.ring([128, 8, 64], F32, 2, name="stmp")
        two = lambda shape, nm, dt=BF16: [st.sb(shape, dt, nm)] * 2
        ARs, Bbs, Kbs = two([128, 16, 256], "AR"), two([128, 16, 128], "Bb"), two([128, 16, 128], "Kb")
        G1s = two([128, 16, 512], "G1")
        sq16 = lambda nm: st.sb([128, 16, 128], BF16, nm)
        NTt, Nd, NTd, No, NTo, Mt, MTt, Tt, TTt, Yt, Zt = [sq16(n_) for n_ in
                                                          ("NT", "Nd", "NTd", "No", "NTo", "M", "MT", "T", "TT", "Y", "Z")]
        lvm = st.sb([128, 4, 128], BF16, "lvm")
        for i_ in range(4):
            S.dma('sp', lvm[:, i_, :], lvm_in[i_], lvm, writes=[lvm])
        Tbs = two([128, 16, 128], "Tb")
        BKTs, XTs, UTs = two([128, 16, 2, 128], "BKT"), two([128, 16, 64], "XT"), two([128, 16, 64], "UT")
        P1r = st.ring([128, 512], F32, 2, psum=True)
        Qr = st.ring([128, 512], F32, 4, psum=True)
        Sr = st.ring([128, 512], F32, 2, psum=True)
        if dr == 1:
            lng = st.sb([128, 16, 64], F32, "lng")
            lnb = st.sb([128, 16, 64], F32, "lnb")
            for hh in range(2):
                S.dma('sp', lng[hh * 64:(hh + 1) * 64].rearrange("p q i -> p (q i)"),
                      lng_in[hh:hh + 1, :].partition_broadcast(64), lng, writes=[lng])
                S.dma('sp', lnb[hh * 64:(hh + 1) * 64].rearrange("p q i -> p (q i)"),
                      lnb_in[hh:hh + 1, :].partition_broadcast(64), lnb, writes=[lnb])
            Ofr = st.ring([128, 16, 64], F32, 1, name="Of")
            sgr = st.ring([128, 16, 64], BF16, 2, name="sg")
            rkrr = st.ring([128, 16, CH], BF16, 2, name="rkr")
            rkbd = st.ring([128, 16, 2, 64], BF16, 1, name="rkbd")
            o2r = st.ring([128, 16, 64], F32, 1, name="o2")
            o3r = st.ring([128, 16, 64], F32, 1, name="o3")
            ofr = st.ring([128, 16, 64], BF16, 2, name="ofin")
            stt_r = st.ring([128, 8, 16], F32, 2, name="gnst")
        order = list(range(NCH)) if dr == 0 else list(range(NCH - 1, -1, -1))
        bd4 = bdm[:].unsqueeze(1).to_broadcast([128, 16, 2, 64])
        G4 = [list(range(4 * g_, 4 * g_ + 4)) for g_ in range(4)]
        q4 = lambda bank: bank[:].rearrange("p (j t) -> p j t", t=128)
        q8 = lambda bank: bank[:].rearrange("p (j t) -> p j t", t=64)

        def phaseG(ci, c):
            par = ci % 2
            AR, Bb, Kb, G1, BKT, Tb = ARs[par], Bbs[par], Kbs[par], G1s[par], BKTs[par], Tbs[par]
            X, V = Xr.next(), Vr.next()
            S.dma('sp', X[:].rearrange("p q x t -> p (q x t)"), X_d[dr, c], X, writes=[X])
            for hh in range(2):
                S.dma('sp', V[hh * 64:(hh + 1) * 64].rearrange("p q i -> p (q i)"),
                      v_st[hh, c * CH:(c + 1) * CH, :], V, writes=[V])
            x4 = lambda x: X[:, :, x, :].unsqueeze(2).to_broadcast([128, 16, 2, 64])
            ARv = AR[:].rearrange("p q (x h t) -> p q x h t", x=2, h=2)
            E.tt('dve', ARv[:, :, 0], x4(0), bd4, ALU.mult, [X, bdm], [(AR, 0)])
            E.tt('pool', ARv[:, :, 1], x4(1), bd4, ALU.mult, [X, bdm], [(AR, 1)])
            E.tt('dve', Bb[:].rearrange("p q (h t) -> p q h t", h=2), x4(2), bd4, ALU.mult, [X, bdm], [Bb])
            E.tt('pool', Kb[:].rearrange("p q (h t) -> p q h t", h=2), x4(3), bd4, ALU.mult, [X, bdm], [Kb])
            for q in range(16):
                p1 = P1r.next()
                E.mm(p1[:, 0:256], Bb[:, q, :], AR[:, q, :], True, True, [Bb, (AR, 0), (AR, 1)], [p1])
                E.mm(p1[:, 256:512], Kb[:, q, :], AR[:, q, :], True, True, [Kb, (AR, 0), (AR, 1)], [p1])
                E.tt('dve', G1[:, q, :], p1[:], mask4[:], ALU.mult, [p1, mask4], [(G1, q // 4)])
            m4 = lambda i_: lvm[:, i_, :].unsqueeze(1).to_broadcast([128, 4, 128])
            idb = g.ident[:].unsqueeze(1).to_broadcast([128, 4, 128])
            def chain(g_):
                gs_ = slice(4 * g_, 4 * g_ + 4)
                K_ = lambda t_: (t_, g_)

                def mm4(lhs, rhs, rk):
                    bank = Qr.next()
                    for j, q in enumerate(G4[g_]):
                        E.mm(bank[:, j * 128:(j + 1) * 128], lhs[:, q, :], rhs[:, q, :], True, True, rk, [bank])
                    return bank

                qb = Qr.next()
                for j, q in enumerate(G4[g_]):
                    E.mm(qb[:, j * 128:(j + 1) * 128], AR[:, q, 0:128], Bb[:, q, :], True, True, [(AR, 0), Bb], [qb])
                E.tt('dve', NTt[:, gs_, :], q4(qb), maskT[:].unsqueeze(1).to_broadcast([128, 4, 128]),
                     ALU.mult, [qb, maskT], [K_(NTt)])
                yield
                qt = Qr.next()
                qtb = qt[:].bitcast(BF16)
                for j, q in enumerate(G4[g_]):
                    E.tr(qtb[:, (2 * j) * 128:(2 * j + 1) * 128], Bb[:, q, :], g.ident[:], [Bb, g.ident], [qt])
                    E.tr(qtb[:, (2 * j + 1) * 128:(2 * j + 2) * 128], Kb[:, q, :], g.ident[:], [Kb, g.ident], [qt])
                E.act(BKT[:, gs_, :, :], qtb.rearrange("p (j w t) -> p j w t", w=2, t=128), AF.Copy, [qt], [K_(BKT)])
                Nn = G1[:, gs_, 0:128]
                E.tt('dve', Nd[:, gs_, :], Nn, m4(0), ALU.mult, [K_(G1), lvm], [K_(Nd)])
                E.tt('dve', NTd[:, gs_, :], NTt[:, gs_, :], m4(0), ALU.mult, [K_(NTt), lvm], [K_(NTd)])
                E.tt('dve', Tt[:, gs_, :], Nd[:, gs_, :], idb, ALU.add, [K_(Nd), g.ident], [K_(Tt)])
                E.tt('dve', TTt[:, gs_, :], NTd[:, gs_, :], idb, ALU.add, [K_(NTd), g.ident], [K_(TTt)])
                yield
                Mc, MTc = Nd, NTd
                for lvl in range(2):
                    ba = mm4(MTc, Mc, [K_(MTc), K_(Mc)])
                    bb = mm4(Mc, MTc, [K_(MTc), K_(Mc)])
                    E.act(Mt[:, gs_, :], q4(ba), AF.Copy, [ba], [K_(Mt)])
                    E.act(MTt[:, gs_, :], q4(bb), AF.Copy, [bb], [K_(MTt)])
                    yield
                    Mc, MTc = Mt, MTt
                    bc_ = mm4(MTc, Tt, [K_(MTc), K_(Tt)])
                    bd_ = mm4(Mc, TTt, [K_(Mc), K_(TTt)])
                    E.tt('dve', Tt[:, gs_, :], q4(bc_), Tt[:, gs_, :], ALU.add, [bc_, K_(Tt)], [K_(Tt)])
                    E.tt('dve', TTt[:, gs_, :], q4(bd_), TTt[:, gs_, :], ALU.add, [bd_, K_(TTt)], [K_(TTt)])
                    yield
                for mi in range(1, 4):
                    lastm = mi == 3
                    E.tt('dve', NTo[:, gs_, :], NTt[:, gs_, :], m4(mi), ALU.mult, [K_(NTt), lvm], [K_(NTo)])
                    by = mm4(NTo, Tt, [K_(NTo), K_(Tt)])
                    E.act(Yt[:, gs_, :], q4(by), AF.Copy, [by], [K_(Yt)])
                    if not lastm:
                        E.tt('dve', No[:, gs_, :], Nn, m4(mi), ALU.mult, [K_(G1), lvm], [K_(No)])
                        bz = mm4(No, TTt, [K_(No), K_(TTt)])
                        E.act(Zt[:, gs_, :], q4(bz), AF.Copy, [bz], [K_(Zt)])
                    yield
                    bc_ = mm4(TTt, Yt, [K_(TTt), K_(Yt)])
                    if not lastm:
                        bd_ = mm4(Tt, Zt, [K_(Tt), K_(Zt)])
                        E.tt('dve', Tt[:, gs_, :], q4(bc_), Tt[:, gs_, :], ALU.add, [bc_, K_(Tt)], [K_(Tt)])
                        E.tt('dve', TTt[:, gs_, :], q4(bd_), TTt[:, gs_, :], ALU.add, [bd_, K_(TTt)], [K_(TTt)])
                    else:
                        E.tt('dve', Tb[:, gs_, :], q4(bc_), Tt[:, gs_, :], ALU.add, [bc_, K_(Tt)], [K_(Tb)])
                    yield

            alive = [chain(g_) for g_ in range(4)]
            while alive:
                for gen in list(alive):
                    try:
                        next(gen)
                    except StopIteration:
                        alive.remove(gen)
            return V

        def phaseS(ci, c, V):
            par = ci % 2
            AR, G1, Tm, BKT, XT, UT = ARs[par], G1s[par], Tbs[par], BKTs[par], XTs[par], UTs[par]
            Ost = Ostr.next()
            H8 = [list(range(8 * h_, 8 * h_ + 8)) for h_ in range(2)]
            sk = lambda t_, h_: [(t_, 2 * h_), (t_, 2 * h_ + 1)]
            for h_ in range(2):
                sb = Sr.next()
                for j, q in enumerate(H8[h_]):
                    sl = sb[:, j * 64:(j + 1) * 64]
                    E.mm(sl, AR[:, q, 0:128], Sb[:, q, :], True, False, [(AR, 0), (Sb, h_)], [sb])
                    E.mm(sl, G1[:, q, 256:384], V[:, q, :], False, True, sk(G1, h_) + [V], [sb])
                E.act(XT[:, 8 * h_:8 * h_ + 8, :], q8(sb), AF.Copy, [sb], [(XT, h_)])
            for h_ in range(2):
                sb = Sr.next()
                for j, q in enumerate(H8[h_]):
                    E.mm(sb[:, j * 64:(j + 1) * 64], Tm[:, q, :], XT[:, q, :], True, True, sk(Tm, h_) + [(XT, h_)], [sb])
                E.act(UT[:, 8 * h_:8 * h_ + 8, :], q8(sb), AF.Copy, [sb], [(UT, h_)])
            for h_ in range(2):
                sb = Sr.next()
                for j, q in enumerate(H8[h_]):
                    sl = sb[:, j * 64:(j + 1) * 64]
                    E.mm(sl, AR[:, q, 128:256], Sb[:, q, :], True, False, [(AR, 1), (Sb, h_)], [sb])
                    E.mm(sl, G1[:, q, 128:256], UT[:, q, :], False, False, sk(G1, h_) + [(UT, h_)], [sb])
                    E.mm(sl, G1[:, q, 384:512], V[:, q, :], False, True, sk(G1, h_) + [V], [sb])
                E.cp('dve', Ost[:, 8 * h_:8 * h_ + 8, :], q8(sb), [sb], [(Ost, h_)])
            for h_ in range(2):
                sb = Sr.next()
                for j, q in enumerate(H8[h_]):
                    sl = sb[:, j * 64:(j + 1) * 64]
                    E.mm(sl, BKT[:, q, 0, :], UT[:, q, :], True, False, sk(BKT, h_) + [(UT, h_)], [sb])
                    E.mm(sl, BKT[:, q, 1, :], V[:, q, :], False, True, sk(BKT, h_) + [V], [sb])
                pcb = pcg[:, dr, c, 8 * h_:8 * h_ + 8].unsqueeze(2).to_broadcast([128, 8, 64])
                pk = [(pcg, dr, c // 4, q // 4) for q in H8[h_]]
                S8 = S32[:, 8 * h_:8 * h_ + 8, :]
                tm = tmpr.next()
                E.tt('pool', S8, S8, pcb, ALU.mult, [(S32, h_)] + pk, [(S32, h_)])
                E.tt('dve', tm[:], q8(sb), pcb, ALU.mult, [sb] + pk, [tm])
                E.tt('pool', S8, S8, tm[:], ALU.add, [(S32, h_), tm], [(S32, h_)])
                E.act(Sb[:, 8 * h_:8 * h_ + 8, :], S8, AF.Copy, [(S32, h_)], [(Sb, h_)])
            return Ost

        def reset_state():
            allk = [(S32, 0), (S32, 1)]
            E.ts('pool', S32[:], S32[:], cont[:, 0:1], None, ALU.mult, None, allk + [cont], allk)
            E.act(Sb[:], S32[:], AF.Copy, allk, [(Sb, 0), (Sb, 1)])

        def finalize(c, Ost, V):
            ok = [(Ost, 0), (Ost, 1)]
            if dr == 0:
                for hh in range(2):
                    S.dma('pool', o_st[hh, c * CH:(c + 1) * CH, :],
                          Ost[hh * 64:(hh + 1) * 64].rearrange("p q i -> p (q i)"), Ost, reads=ok)
                return
            Of, sg, rk, o2, o3, ofin, gs, rb = (Ofr.next(), sgr.next(), rkrr.next(), o2r.next(), o3r.next(), ofr.next(),
                                                stt_r.next(), rkbd.next())
            for hh in range(2):
                S.dma('sp', Of[hh * 64:(hh + 1) * 64].rearrange("p q i -> p (q i)"),
                      o_st[hh, c * CH:(c + 1) * CH, :], Of, writes=[Of])
                S.dma('sp', sg[hh * 64:(hh + 1) * 64].rearrange("p q i -> p (q i)"),
                      sg_st[hh, c * CH:(c + 1) * CH, :], sg, writes=[sg])
            S.dma('sp', rk[:].rearrange("p q t -> p (q t)"), rkr_d[c], rk, writes=[rk])
            E.tt('pool', rb[:], rk[:].unsqueeze(2).to_broadcast([128, 16, 2, 64]), bd4, ALU.mult, [rk, bdm], [rb])
            sb = Sr.next()
            for q in range(16):
                E.mm(sb[:, q:q + 1], rb[:, q].rearrange("p h t -> p (h t)"), g.ones_b[:, 0:1], True, True,
                     [rb, g.ones_b], [sb])
            E.cp('dve', gs[:, 7, :], sb[:, 0:16], [sb], [(gs, 7)])
            E.tt('pool', o2[:], Ost[:], Of[:], ALU.add, ok + [Of], [o2])
            bc = lambda j: gs[:, j, :].unsqueeze(2).to_broadcast([128, 16, 64])
            E.red(gs[:, 0, :], o2[:], ALU.add, [o2], [(gs, 0)])
            E.ts('dve', gs[:, 2, :], gs[:, 0, :], 1.0 / 64, None, ALU.mult, None, [(gs, 0)], [(gs, 2)])
            E.tt('dve', o2[:], o2[:], bc(2), ALU.subtract, [o2, (gs, 2)], [o2])
            E.tt('pool', o3[:], o2[:], o2[:], ALU.mult, [o2], [o3])
            E.red(gs[:, 1, :], o3[:], ALU.add, [o3], [(gs, 1)])
            E.act(gs[:, 5, :], gs[:, 1, :], AF.Sqrt, [(gs, 1)], [(gs, 5)], scale=1.0 / 64, bias=64e-5)
            E.recip(gs[:, 6, :], gs[:, 5, :], [(gs, 5)], [(gs, 6)])
            E.tt('dve', o2[:], o2[:], bc(6), ALU.mult, [o2, (gs, 6)], [o2])
            E.tt('pool', o2[:], o2[:], lng[:], ALU.mult, [o2, lng], [o2])
            E.tt('pool', o2[:], o2[:], lnb[:], ALU.add, [o2, lnb], [o2])
            E.tt('dve', o3[:], V[:], bc(7), ALU.mult, [V, (gs, 7)], [o3])
            E.tt('pool', o2[:], o2[:], o3[:], ALU.add, [o2, o3], [o2])
            E.tt('dve', ofin[:], o2[:], sg[:], ALU.mult, [o2, sg], [ofin])
            for hh in range(2):
                S.dma('pool', of_st[hh, c * CH:(c + 1) * CH, :],
                      ofin[hh * 64:(hh + 1) * 64].rearrange("p q i -> p (q i)"), ofin, reads=[ofin])

        for ci, c in enumerate(order):
            V = phaseG(ci, c)
            if ci == NCH // 2:
                reset_state()
            Ost = phaseS(ci, c, V)
            finalize(c, Ost, V)

    with Stage(S) as st:
        inr = st.ring([128, 1024], BF16, 3, name="oin")
        ptr = st.ring([128, 8, 128], BF16, 2, psum=True)
        ostr = [st.ring([128, 8, 512], BF16, 2, name=f"ost{hh}") for hh in range(2)]
        for grp in range(8):
            os_ = [ostr[hh].next() for hh in range(2)]
            for sub in range(4):
                tt = grp * 4 + sub
                for hh in range(2):
                    it_ = inr.next()
                    S.dma('sp', it_[:], of_st[hh, tt * 128:(tt + 1) * 128, :], it_, writes=[it_])
                    pt = ptr.next()
                    for q in range(8):
                        E.tr(pt[:, q, :], it_[:, q * 128:(q + 1) * 128], g.ident[:], [it_, g.ident], [pt])
                    if hh == 0:
                        E.cp('dve', os_[hh][:, :, sub * 128:(sub + 1) * 128], pt[:], [pt], [(os_[hh], sub)])
                    else:
                        E.act(os_[hh][:, :, sub * 128:(sub + 1) * 128], pt[:], AF.Copy, [pt], [(os_[hh], sub)])
            for hh in range(2):
                S.dma('pool', fmv(g.oT_d)[:, hh * 8:hh * 8 + 8, grp * 512:(grp + 1) * 512], os_[hh][:], os_[hh],
                      reads=[(os_[hh], s_) for s_ in range(4)])
    return w_out


def core_inputs_more(n, inp, pos, cont):
    if n == 'diff_w_in':
        return inp['diff_w_in'][0]
    if n == 'diff_lam':
        return inp['diff_lambda'][0].reshape(1, 256)
    if n == 'diff_gsubP':
        return _P(inp['diff_subln_g'][0])
    if n == 'diff_w_out':
        return inp['diff_w_out'][0]
    if n in ('diff_cos', 'diff_sin'):
        cs, sn = _rope_tables(pos, 16, 500000.0)
        t = (cs if n == 'diff_cos' else sn)[:, 0:8]
        return np.ascontiguousarray(t.reshape(NT, 128, 8).transpose(1, 0, 2))
    if n == 'lru_w_in':
        return inp['lru_w_in'][0]
    if n == 'lru_w_out':
        return inp['lru_w_out'][0]
    if n == 'lru_convP':
        cw = inp['lru_conv_w'][0]
        cb = inp['lru_conv_b'][0]
        a = np.concatenate([cw, cb[None, :]], axis=0)
        return np.ascontiguousarray(a.reshape(5, KC, 128).transpose(2, 1, 0))
    if n == 'lru_gate_w':
        return np.ascontiguousarray(inp['lru_gate_w'][0].transpose(3, 0, 1, 2, 4))
    if n == 'lru_gate_bP':
        return np.ascontiguousarray(inp['lru_gate_b'][0].reshape(2, 2, KC, 128).transpose(3, 0, 1, 2))
    if n == 'lru_lamP':
        return np.ascontiguousarray(inp['lru_lambda'][0].reshape(2, KC, 128).transpose(2, 0, 1))
    return core_inputs_rwkv(n, inp, pos, cont)


def _tri_masks():
    p = np.arange(128)
    same = (p[:, None] // 64) == (p[None, :] // 64)
    s_, t_ = p[:, None] % 64, p[None, :] % 64
    out4, outT = [], []
    for dr in range(2):
        strict = same & ((s_ < t_) if dr == 0 else (s_ > t_))
        incl = same & ((s_ <= t_) if dr == 0 else (s_ >= t_))
        out4.append(np.concatenate([strict, incl, strict, incl], axis=1))
        outT.append(strict.T)
    return np.stack(out4).astype(np.float32), np.stack(outT).astype(np.float32), same.astype(np.float32)


def core_inputs_rwkv(n, inp, pos, cont):
    if n == 'rwkv_muP':
        return np.ascontiguousarray(inp['rwkv_mu'][0].reshape(6, KC, 128).transpose(2, 0, 1))
    if n == 'rwkv_w_in':
        return inp['rwkv_w_in'][0]
    if n in ('rwkv_w0P', 'rwkv_a0P'):
        k = 'rwkv_w0' if n == 'rwkv_w0P' else 'rwkv_a0'
        return np.ascontiguousarray(inp[k][0].reshape(2, KC, 128).transpose(2, 0, 1))
    if n in ('rwkv_w1', 'rwkv_w2', 'rwkv_a1', 'rwkv_a2'):
        return inp[n][0]
    if n in ('rwkv_kkP', 'rwkv_kaP', 'rwkv_rkP'):
        k = {'rwkv_kkP': 'rwkv_k_k', 'rwkv_kaP': 'rwkv_k_a', 'rwkv_rkP': 'rwkv_r_k'}[n]
        return _P(inp[k][0].reshape(-1))
    if n in ('rwkv_lngS', 'rwkv_lnbS'):
        v = inp['rwkv_ln_g' if n == 'rwkv_lngS' else 'rwkv_ln_b'][0]
        return np.ascontiguousarray(v.reshape(16, 2, 64).transpose(1, 0, 2).reshape(2, 1024))
    if n == 'rwkv_w_out_perm':
        cp, e, i = np.meshgrid(np.arange(16), np.arange(2), np.arange(64), indexing='ij')
        hh, q = cp // 8, cp % 8
        perm = ((2 * q + e) * 128 + hh * 64 + i).reshape(-1)
        return np.ascontiguousarray(inp['rwkv_w_out'][0][perm, :])
    if n == 'rw_mask4':
        return _bf(_tri_masks()[0])
    if n == 'rw_maskT':
        return _bf(_tri_masks()[1])
    if n == 'rw_lvmask':
        p = np.arange(128)
        blk = lambda b: ((p[:, None] // b) == (p[None, :] // b)).astype(np.float32)
        return _bf(np.stack([blk(8), blk(16) - blk(8), blk(32) - blk(16), blk(64) - blk(32)]))
    if n in ('rw_bdmask', 'rw_bones'):
        return _bf(_tri_masks()[2])
    raise KeyError(n)


def _P(v):
    return np.ascontiguousarray(np.asarray(v, np.float32).reshape(-1, 128).T)


def _bf(a):
    return np.asarray(a, np.float32).astype(ml_dtypes.bfloat16)


def _rope_tables(pos, dim, theta):
    inv = (1.0 / (np.float32(theta) ** (np.arange(0, dim, 2, dtype=np.float32) / np.float32(dim)))).astype(np.float32)
    ang = pos.astype(np.float32)[:, None] * inv[None, :]
    ang = np.concatenate([ang, ang], axis=-1).astype(np.float32)
    return np.cos(ang).astype(np.float32), np.sin(ang).astype(np.float32)


def core_inputs(i, inp, names):
    if i < 4:
        x = inp['x_sample'][i]
        c2 = np.stack([inp['c_sample'][i], inp['c_sample'][i]])
        cont = 1.0
        pos = np.arange(T)
    else:
        j = i - 4
        x = np.concatenate([inp['x_prompt'][2 * j], inp['x_prompt'][2 * j + 1]], axis=0)
        c2 = np.stack([inp['c_prompt'][2 * j], inp['c_prompt'][2 * j + 1]])
        cont = 0.0
        pos = np.arange(T) % HALF
    hq = (np.arange(T) >= HALF).astype(np.float32)
    d = {}
    for n in names:
        if n == 'x':
            v = np.ascontiguousarray(x, np.float32)
        elif n == 'cT':
            v = np.ascontiguousarray(c2.reshape(2, KC, 128).transpose(2, 1, 0), np.float32)
        elif n == 'cont':
            v = np.tile(np.array([[cont, 1.0 - cont]], np.float32), (128, 1))
        elif n == 'seg_q':
            v = _bf(np.stack([BIG * hq, BIG * (1 - hq)]) * (1.0 - cont))
        elif n == 'seg_k':
            v = _bf(np.stack([-(1 - hq), -hq]))
        elif n == 'ident':
            v = _bf(np.eye(128))
        elif n == 'identf':
            v = np.eye(128, dtype=np.float32)
        elif n == 'ada_w':
            v = inp['ada_w']
        elif n == 'ada_bP':
            v = np.ascontiguousarray(np.stack([_P(inp['ada_b'][l]) for l in range(4)], axis=1))
        elif n == 'pre_gP':
            v = np.ascontiguousarray(np.stack([_P(inp['norm_pre_g'][l]) for l in range(4)], axis=1))
        elif n == 'post_g':
            v = inp['norm_post_g']
        elif n == 'mla_w_in':
            v = inp['mla_w_in'][0]
        elif n == 'mla_qgP':
            v = _P(inp['mla_q_norm_g'][0])
        elif n == 'mla_kvgP':
            v = _P(inp['mla_kv_norm_g'][0])
        elif n == 'mla_w_q_up':
            v = inp['mla_w_q_up'][0]
        elif n == 'mla_w_kv_up':
            v = inp['mla_w_kv_up'][0]
        elif n == 'mla_w_out':
            v = inp['mla_w_out'][0]
        elif n in ('mla_cosT', 'mla_sinT'):
            cs, sn = _rope_tables(pos, 64, 10000.0)
            v = np.ascontiguousarray((cs if n == 'mla_cosT' else sn).T)
        else:
            v = core_inputs_more(n, inp, pos, cont)
        d[n] = np.ascontiguousarray(v)
    return d


_PROG = {}


def run_layers(inputs, NL=4):
    if NL not in _PROG:
        _PROG[NL] = build(NL)
    g = _PROG[NL]
    inp = {k: np.asarray(v) for k, v in inputs.items()}
    in_maps = [core_inputs(i, inp, g.in_names) for i in range(8)]
    res = run_bass_kernel_spmd(g.nc, in_maps, core_ids=list(range(8)))
    return [r["y"] for r in res.results]


def kernel(**inputs):
    ys = run_layers(inputs, 4)
    y_sample = np.stack([np.asarray(ys[i], np.float32) for i in range(4)])
    y_prompt = np.stack([np.asarray(ys[4 + j // 2], np.float32)[(j % 2) * HALF:(j % 2 + 1) * HALF] for j in range(8)])
    return (y_prompt, y_sample)
```

```python
import contextlib
import math
import numpy as np
import ml_dtypes
import concourse.bass as bass
import concourse.mybir as mybir
from concourse.bass_utils import run_bass_kernel_spmd

F32 = mybir.dt.float32
BF16 = mybir.dt.bfloat16
AF = mybir.ActivationFunctionType
ALU = mybir.AluOpType
AX = mybir.AxisListType

EPOCH = 60000
DEPOCH = 3500
COMPUTE = ('pe', 'act', 'dve', 'pool')
QUEUES = COMPUTE + ('sp',)

T = 4096
D = 2048
KC = 16
NT = 32
HALF = 2048
BIG = 30000.0


class DSem:
    def __init__(self, S, name):
        self.S = S
        self.name = name
        self.handles = []
        self.counts = []
        self._new()

    def _new(self):
        self.handles.append(self.S._sem(f"d_{self.name}_{len(self.handles)}"))
        self.counts.append(0)


class Sched:
    def __init__(self, nc):
        self.nc = nc
        self.stack = contextlib.ExitStack()
        self.ops = {e: [] for e in QUEUES}
        self.csem = {e: [] for e in COMPUTE}
        self.ccnt = {e: [] for e in COMPUTE}
        self.last_w = {}
        self.readers = {}
        self.dsems = []
        self.free_ds = {}
        self.tile_ds = {}
        self.nsem = 0
        self.uid = 0
        self.waited = {e: {} for e in QUEUES}
        self.psum = {}
        self.last_acc = {}
        for e in COMPUTE:
            self._new_epoch(e)

    def _sem(self, name):
        self.nsem += 1
        return self.stack.enter_context(self.nc.semaphore(name))

    def _new_epoch(self, e):
        self.csem[e].append(self._sem(f"c_{e}_{len(self.csem[e])}"))
        self.ccnt[e].append(0)

    def get_ds(self, q):
        fl = self.free_ds.setdefault(q, [])
        if fl:
            return fl.pop()
        d = DSem(self, f"{q}{len(self.dsems)}")
        d.q = q
        self.dsems.append(d)
        return d

    def ds_of(self, tile, q):
        k = (id(tile), q)
        if k not in self.tile_ds:
            self.tile_ds[k] = self.get_ds(q)
        return self.tile_ds[k]

    def sb(self, sc, shape, dtype, name=None):
        self.uid += 1
        return sc.enter_context(self.nc.sbuf_tensor(f"{name or 't'}_{self.uid}", list(shape), dtype))

    def ps(self, sc, shape, dtype, name=None):
        self.uid += 1
        t = sc.enter_context(self.nc.psum_tensor(f"{name or 'p'}_{self.uid}", list(shape), dtype))
        nbytes = int(np.prod(shape[1:])) * (2 if dtype == BF16 else 4)
        assert nbytes % 2048 == 0, "PSUM tiles must be whole banks"
        self.psum[id(t)] = nbytes // 2048
        return t

    def _split(self, keys):
        norm, banks = [], []
        for k in keys:
            base = k[0] if isinstance(k, tuple) else k
            nb = self.psum.get(id(base)) if not isinstance(base, (str, int)) else None
            if nb is None:
                norm.append(self._key(k))
            elif nb == 1:
                banks.append(('B', id(base)))
            else:
                banks.append(('B', id(base), k[1]))
        return norm, banks

    @staticmethod
    def _key(k):
        if isinstance(k, tuple):
            return tuple(Sched._key(x) for x in k)
        if isinstance(k, (str, int)):
            return k
        return id(k)

    def _waits(self, eng, toks):
        w = {}
        for t in toks:
            if t[0] == 'c':
                _, e, ep, idx = t
                if e == eng and e == 'pe':
                    continue
                h = self.csem[e][ep]
                v = idx
            else:
                _, d, ep, cnt = t
                h = d.handles[ep]
                v = d.counts[ep]
            k = id(h)
            if k not in w or w[k][1] < v:
                w[k] = (h, v)
        return list(w.values())

    def _deps(self, reads, writes):
        toks = []
        for k in reads + writes:
            t = self.last_w.get(k)
            if t is not None:
                toks.append(t)
        for k in writes:
            r = self.readers.get(k)
            if r:
                toks.extend(r.values())
        return toks

    def _update(self, tok, reads, writes):
        for k in writes:
            self.last_w[k] = tok
            self.readers[k] = {}
        for k in reads:
            r = self.readers.setdefault(k, {})
            if tok[0] == 'c':
                r[(tok[1], tok[2])] = tok
            else:
                r[(id(tok[1]), tok[2])] = tok

    def op(self, eng, fn, reads=(), writes=()):
        reads, b1 = self._split(reads)
        writes, b2 = self._split(writes)
        toks = self._deps(reads, writes)
        for bk in b1 + b2:
            for e2, t in self.last_acc.get(bk, {}).items():
                if e2 != eng:
                    toks.append(t)
        waits = self._waits(eng, toks)
        if self.ccnt[eng][-1] >= EPOCH:
            self._new_epoch(eng)
        ep = len(self.ccnt[eng]) - 1
        self.ccnt[eng][ep] += 1
        tok = ('c', eng, ep, self.ccnt[eng][ep])
        self.ops[eng].append((waits, fn, (self.csem[eng][ep], 1)))
        self._update(tok, reads, writes)
        for bk in b1 + b2:
            self.last_acc.setdefault(bk, {})[eng] = tok
        return tok

    def dma(self, q, out, in_, tile, reads=(), writes=()):
        ds = self.ds_of(tile, q)
        reads = [self._key(k) for k in reads]
        writes = [self._key(k) for k in writes]
        waits = self._waits(q, self._deps(reads, writes))
        if ds.counts[-1] >= DEPOCH * 16:
            ds._new()
        ep = len(ds.counts) - 1
        ds.counts[ep] += 16
        tok = ('d', ds, ep, ds.counts[ep])
        self.ops[q].append((waits, (lambda e, o=out, i=in_: e.dma_start(out=o, in_=i)), (ds.handles[ep], 16)))
        self._update(tok, reads, writes)
        return tok

    def barrier(self):
        toks = []
        for e in COMPUTE:
            for ep in range(len(self.ccnt[e])):
                if self.ccnt[e][ep] > 0:
                    toks.append(('c', e, ep, self.ccnt[e][ep]))
        for d in self.dsems:
            for ep in range(len(d.counts)):
                if d.counts[ep] > 0:
                    toks.append(('d', d, ep, d.counts[ep]))
        for e in QUEUES:
            tk = [t for t in toks if not (t[0] == 'c' and t[1] == e)]
            self.ops[e].append((self._waits(e, tk), None, None))

    def release(self, tiles):
        for t in tiles:
            for q in QUEUES:
                k = (id(t), q)
                if k in self.tile_ds:
                    self.free_ds.setdefault(q, []).append(self.tile_ds.pop(k))

    def emit(self):
        self.barrier()
        nc = self.nc
        ops = self.ops
        self.ops = {e: [] for e in QUEUES}
        with nc.Block() as block:
            def run(e, name):
                waited = self.waited[name]
                for waits, fn, inc in ops[name]:
                    for h, v in waits:
                        k = id(h)
                        if waited.get(k, 0) < v:
                            e.wait_ge(h, v)
                            waited[k] = v
                    if fn is not None:
                        fn(e).then_inc(inc[0], inc[1])

            @block.tensor
            def _(e):
                run(e, 'pe')

            @block.scalar
            def _(e):
                run(e, 'act')

            @block.vector
            def _(e):
                run(e, 'dve')

            @block.gpsimd
            def _(e):
                run(e, 'pool')

            @block.sync
            def _(e):
                run(e, 'sp')


class Ring:
    def __init__(self, S, sc, shape, dtype, n, psum=False, name=None):
        self.tiles = [(S.ps if psum else S.sb)(sc, shape, dtype, name) for _ in range(n)]
        self.i = 0

    def next(self):
        t = self.tiles[self.i % len(self.tiles)]
        self.i += 1
        return t


class Stage:
    def __init__(self, S):
        self.S = S
        self.sc = contextlib.ExitStack()
        self.tiles = []

    def __enter__(self):
        self.sc.__enter__()
        return self

    def sb(self, shape, dtype, name=None):
        t = self.S.sb(self.sc, shape, dtype, name)
        self.tiles.append(t)
        return t

    def ps(self, shape, dtype, name=None):
        return self.S.ps(self.sc, shape, dtype, name)

    def ring(self, shape, dtype, n, psum=False, name=None):
        r = Ring(self.S, self.sc, shape, dtype, n, psum, name)
        if not psum:
            self.tiles.extend(r.tiles)
        return r

    def __exit__(self, *a):
        self.S.emit()
        self.S.release(self.tiles)
        return self.sc.__exit__(*a)


class K:
    pass


class Em:
    def __init__(self, S):
        self.S = S

    def mm(self, out, lhsT, rhs, start, stop, r, w):
        return self.S.op('pe', lambda e: e.matmul(out, lhsT=lhsT, rhs=rhs, start=start, stop=stop), r, w)

    def tr(self, out, in_, ident, r, w):
        return self.S.op('pe', lambda e: e.transpose(out=out, in_=in_, identity=ident), r, w)

    def act(self, out, in_, func, r, w, scale=1.0, bias=0.0, accum=None, eng='act'):
        if accum is None:
            return self.S.op(eng, lambda e: e.activation(out=out, in_=in_, func=func, scale=scale, bias=bias), r, w)
        return self.S.op(eng, lambda e: e.activation(out=out, in_=in_, func=func, scale=scale, bias=bias,
                                                     accum_out=accum), r, w)

    def tt(self, eng, out, in0, in1, op, r, w):
        return self.S.op(eng, lambda e: e.tensor_tensor(out=out, in0=in0, in1=in1, op=op), r, w)

    def ts(self, eng, out, in0, s1, s2, op0, op1, r, w):
        if s2 is None:
            return self.S.op(eng, lambda e: e.tensor_scalar(out=out, in0=in0, scalar1=s1, scalar2=None, op0=op0), r, w)
        return self.S.op(eng, lambda e: e.tensor_scalar(out=out, in0=in0, scalar1=s1, scalar2=s2, op0=op0, op1=op1), r, w)

    def stt(self, out, in0, scalar, in1, op0, op1, r, w):
        return self.S.op('dve', lambda e: e.scalar_tensor_tensor(out=out, in0=in0, scalar=scalar, in1=in1,
                                                                 op0=op0, op1=op1), r, w)

    def cp(self, eng, out, in_, r, w):
        return self.S.op(eng, lambda e: e.tensor_copy(out=out, in_=in_), r, w)

    def recip(self, out, in_, r, w):
        return self.S.op('dve', lambda e: e.reciprocal(out=out, in_=in_), r, w)

    def red(self, out, in_, op, r, w):
        return self.S.op('dve', lambda e: e.tensor_reduce(out=out, in_=in_, axis=AX.X, op=op), r, w)

    def memset(self, eng, ap, val, w):
        return self.S.op(eng, lambda e: e.memset(ap, val), (), w)

    def scan(self, out, d0, d1, init, r, w):
        return self.S.op('dve', lambda e: e.tensor_tensor_scan(out=out, data0=d0, data1=d1, initial=init,
                                                               op0=ALU.mult, op1=ALU.add), r, w)


def wview(w_ap):
    return w_ap.rearrange("(c p) n -> p c n", p=128)


def fmv(d_ap):
    return d_ap.rearrange("c p t -> p c t")


class WLoader:
    def __init__(self, S, E, st, kc=KC, nb=256):
        self.S, self.E = S, E
        self.ring = st.ring([128, kc, nb], F32, 2, name="wst")
        self.nb = nb

    def load(self, dst_tile, dst_fn, src_view, kc, ncols, scale=None):
        for c0 in range(0, ncols, self.nb):
            c1 = min(ncols, c0 + self.nb)
            w = self.ring.next()
            self.S.dma('sp', w[:, 0:kc, 0:c1 - c0], src_view[:, :, c0:c1], w, writes=[w])
            if scale is None:
                self.E.cp('pool', dst_fn(c0, c1), w[:, 0:kc, 0:c1 - c0], [w], [dst_tile])
            else:
                self.E.ts('pool', dst_fn(c0, c1), w[:, 0:kc, 0:c1 - c0], scale, None, ALU.mult, None, [w], [dst_tile])


class G:
    pass


def build(NL=4):
    nc = bass.Bass("TRN2", target_bir_lowering=False)
    S = Sched(nc)
    E = Em(S)
    g = G()
    g.nc, g.S, g.E = nc, S, E

    g.in_names = []

    def din(name, shape, dt=F32):
        g.in_names.append(name)
        return nc.dram_tensor(name, list(shape), dt, kind="ExternalInput").ap()

    def dscr(name, shape, dt):
        return nc.dram_tensor(name, list(shape), dt).ap()

    g.din, g.dscr = din, dscr
    g.x_in = din("x", [T, D])
    g.cT_in = din("cT", [128, KC, 2])
    g.cont_in = din("cont", [128, 2])
    g.seg_q = din("seg_q", [2, T], BF16)
    g.seg_k = din("seg_k", [2, T], BF16)
    g.ident_in = din("ident", [128, 128], BF16)
    g.identf_in = din("identf", [128, 128], F32)
    g.ada_w = din("ada_w", [4, D, 3 * D])
    g.ada_bP = din("ada_bP", [128, 4, 48])
    g.pre_gP = din("pre_gP", [128, 4, KC])
    g.post_g = din("post_g", [4, D])
    g.y_out = nc.dram_tensor("y", [T, D], F32, kind="ExternalOutput").ap()
    g.hT_d = dscr("hT_d", [KC, 128, T], BF16)
    g.oT_d = dscr("oT_d", [KC, 128, T], BF16)
    g.gT_d = dscr("gT_d", [KC, 128, T], BF16)
    g.gg_d = dscr("gg_d", [4, 2, D], F32)

    glob = contextlib.ExitStack()
    g.glob = glob
    g.ident = S.sb(glob, [128, 128], BF16, "ident")
    g.identf = S.sb(glob, [128, 128], F32, "identf")
    g.ones_b = S.sb(glob, [128, 128], BF16, "ones")
    g.cont = S.sb(glob, [128, 2], F32, "cont")
    g.preA = S.sb(glob, [128, 4, 2, KC], F32, "preA")
    g.preB = S.sb(glob, [128, 4, 2, KC], F32, "preB")
    S.dma('sp', g.ident[:], g.ident_in[:, :], g.ident, writes=[g.ident])
    S.dma('sp', g.identf[:], g.identf_in[:, :], g.identf, writes=[g.identf])
    S.dma('sp', g.cont[:], g.cont_in[:, :], g.cont, writes=[g.cont])
    E.memset('dve', g.ones_b[:], 1.0, [g.ones_b])

    import os
    g.stop = int(os.environ.get("KSTOP", "99"))
    prologue(g, NL)
    mixers = [mla_mixer, diff_mixer, lru_mixer, rwkv_mixer]
    for l in range(NL):
        if g.stop < 1:
            break
        stage_pre(g, l, g.x_in if l == 0 else g.y_out)
        if g.stop < 2:
            break
        w_out = mixers[l % 4](g, l)
        if g.stop < 4:
            break
        stage_post(g, l, w_out, g.x_in if l == 0 else g.y_out)
    if NL == 0:
        raise ValueError
    S.emit()
    return g


def prologue(g, NL):
    S, E = g.S, g.E
    with Stage(S) as st:
        cT = st.sb([128, KC, 2], F32)
        csT = st.sb([128, KC, 2], F32)
        adab = st.sb([128, 4, 48], F32)
        preg = st.sb([128, 4, KC], F32)
        modT = st.sb([128, 4, 2, 48], F32)
        gsb = st.sb([128, 2, 16], F32)
        ggs = st.sb([32, 128], F32)
        S.dma('sp', cT[:], g.cT_in[:, :, :], cT, writes=[cT])
        S.dma('sp', adab[:], g.ada_bP[:, :, :], adab, writes=[adab])
        S.dma('sp', preg[:], g.pre_gP[:, :, :], preg, writes=[preg])
        E.act(csT[:], cT[:], AF.Silu, [cT], [csT])
        wring = st.ring([128, KC, 512], F32, 2, name="adaw")
        pm = st.ring([128, 512], F32, 2, psum=True)
        for l in range(NL):
            pmod = pm.next()
            for blk in range(12):
                w = wring.next()
                S.dma('sp', w[:], wview(g.ada_w[l])[:, :, blk * 512:(blk + 1) * 512], w, writes=[w])
                for fc in range(4):
                    ch = blk * 4 + fc
                    for kc in range(KC):
                        E.mm(pmod[:, ch * 2:ch * 2 + 2], w[:, kc, fc * 128:(fc + 1) * 128], csT[:, kc, :],
                             kc == 0, kc == KC - 1, [w, csT], [pmod])
            E.tt('dve', modT[:, l, :, :], pmod[:, 0:96].rearrange("p (c h) -> p h c", h=2),
                 adab[:, l, :].unsqueeze(1).to_broadcast([128, 2, 48]), ALU.add, [pmod, adab], [modT])
            E.ts('dve', g.preA[:, l, :, :], modT[:, l, :, 16:32], 1.0, None, ALU.add, None, [modT], [g.preA])
            E.tt('dve', g.preA[:, l, :, :], g.preA[:, l, :, :],
                 preg[:, l, :].unsqueeze(1).to_broadcast([128, 2, KC]), ALU.mult, [g.preA, preg], [g.preA])
            E.cp('dve', g.preB[:, l, :, :], modT[:, l, :, 0:16], [modT], [g.preB])
            E.cp('dve', gsb[:], modT[:, l, :, 32:48], [modT], [gsb])
            pt = pm.next()
            E.tr(pt[0:32, 0:128], gsb[:].rearrange("p h c -> p (h c)"), g.identf[:], [gsb, g.identf], [pt])
            E.cp('dve', ggs[:], pt[0:32, 0:128], [pt], [ggs])
            S.dma('pool', g.gg_d[l].rearrange("h (c p) -> (h c) p", p=128), ggs[:], ggs, reads=[ggs])


def stage_pre(g, l, x_src):
    S, E = g.S, g.E
    with Stage(S) as st:
        xr = st.ring([128, D], F32, 3, name="x")
        junkr = st.ring([128, D], BF16, 2, name="junk")
        xs = st.ring([128, D], BF16, 2, name="xs")
        ssr = st.ring([128, 4], F32, 4, name="ss")
        ptr = st.ring([128, 1024], BF16, 4, psum=True)
        tmpr = st.ring([128, 8, 128], F32, 3, name="tmp")
        hst = st.ring([128, KC, 512], BF16, 2, name="hst")
        for grp in range(8):
            hs = hst.next()
            for sub in range(4):
                tt = grp * 4 + sub
                half = tt // 16
                x = xr.next()
                S.dma('sp', x[:], x_src[tt * 128:(tt + 1) * 128, :], x, writes=[x])
                ss = ssr.next()
                junk = junkr.next()
                E.act(junk[:], x[:], AF.Square, [x], [ss, junk], accum=ss[:, 0:1])
                E.act(ss[:, 1:2], ss[:, 0:1], AF.Sqrt, [ss], [ss], scale=1.0 / D, bias=1e-6)
                E.recip(ss[:, 2:3], ss[:, 1:2], [ss], [ss])
                xb = xs.next()
                E.ts('dve', xb[:], x[:], ss[:, 2:3], None, ALU.mult, None, [x, ss], [xb])
                for hc in range(2):
                    pt = ptr.next()
                    for c8 in range(8):
                        c = hc * 8 + c8
                        E.tr(pt[:, c8 * 128:(c8 + 1) * 128], xb[:, c * 128:(c + 1) * 128], g.ident[:],
                             [xb, g.ident], [pt])
                    tm = tmpr.next()
                    E.tt('dve', tm[:], pt[:].rearrange("p (c t) -> p c t", t=128),
                         g.preA[:, l, half, hc * 8:hc * 8 + 8].unsqueeze(2).to_broadcast([128, 8, 128]),
                         ALU.mult, [pt, g.preA], [tm])
                    E.tt('dve', hs[:, hc * 8:hc * 8 + 8, sub * 128:(sub + 1) * 128], tm[:],
                         g.preB[:, l, half, hc * 8:hc * 8 + 8].unsqueeze(2).to_broadcast([128, 8, 128]),
                         ALU.add, [tm, g.preB], [hs])
            S.dma('pool', fmv(g.hT_d)[:, :, grp * 512:(grp + 1) * 512], hs[:], hs, reads=[hs])


def stage_post(g, l, w_out, x_src):
    S, E = g.S, g.E
    with Stage(S) as st:
        wl = WLoader(S, E, st)
        wo = st.sb([128, KC, D], BF16, "wo")
        wl.load(wo, lambda c0, c1: wo[:, :, c0:c1], wview(w_out), KC, D)
        gg = st.sb([128, 2, D], F32, "gg")
        pg = st.sb([128, D], F32, "pg")
        for h in range(2):
            S.dma('sp', gg[:, h, :], g.gg_d[l, h:h + 1, :].partition_broadcast(128), gg, writes=[gg])
        S.dma('sp', pg[:], g.post_g[l:l + 1, :].partition_broadcast(128), pg, writes=[pg])
        E.tt('dve', gg[:], gg[:], pg[:].unsqueeze(1).to_broadcast([128, 2, D]), ALU.mult, [gg, pg], [gg])
        oring = st.ring([128, KC, 512], BF16, 2, name="o")
        xr = st.ring([128, D], F32, 2, name="x")
        tmp = st.ring([128, D], F32, 2, name="t")
        pyr = st.ring([128, D], F32, 2, psum=True)
        junkr = st.ring([128, 512], BF16, 4, name="junk")
        ssr = st.ring([128, 8], F32, 4, name="ss")
        for grp in range(8):
            o = oring.next()
            S.dma('sp', o[:], fmv(g.oT_d)[:, :, grp * 512:(grp + 1) * 512], o, writes=[o])
            for sub in range(4):
                tt = grp * 4 + sub
                half = tt // 16
                y = pyr.next()
                for n4 in range(4):
                    for kc in range(KC):
                        E.mm(y[:, n4 * 512:(n4 + 1) * 512], o[:, kc, sub * 128:(sub + 1) * 128],
                             wo[:, kc, n4 * 512:(n4 + 1) * 512], kc == 0, kc == KC - 1, [o, wo], [(y, n4)])
                ss = ssr.next()
                for n4 in range(4):
                    junk = junkr.next()
                    E.act(junk[:], y[:, n4 * 512:(n4 + 1) * 512], AF.Square, [(y, n4)], [(ss, n4), junk],
                          accum=ss[:, n4:n4 + 1])
                E.red(ss[:, 4:5], ss[:, 0:4], ALU.add, [(ss, 0), (ss, 1), (ss, 2), (ss, 3)], [(ss, 4)])
                E.act(ss[:, 5:6], ss[:, 4:5], AF.Sqrt, [(ss, 4)], [(ss, 5)], scale=1.0 / D, bias=1e-6)
                E.recip(ss[:, 6:7], ss[:, 5:6], [(ss, 5)], [(ss, 6)])
                x = xr.next()
                S.dma('sp', x[:], x_src[tt * 128:(tt + 1) * 128, :], x, writes=[x])
                t = tmp.next()
                for n4 in range(4):
                    sl = slice(n4 * 512, (n4 + 1) * 512)
                    E.stt(t[:, sl], y[:, sl], ss[:, 6:7], gg[:, half, sl], ALU.mult, ALU.mult,
                          [(y, n4), (ss, 6), gg], [(t, n4)])
                    E.tt('dve', t[:, sl], t[:, sl], x[:, sl], ALU.add, [(t, n4), x], [(t, n4)])
                S.dma('pool', g.y_out[tt * 128:(tt + 1) * 128, :], t[:], t,
                      reads=[(t, 0), (t, 1), (t, 2), (t, 3)])


def mla_mixer(g, l):
    S, E = g.S, g.E
    din, dscr = g.din, g.dscr
    w_in = din("mla_w_in", [D, 3136])
    qgP = din("mla_qgP", [128, 4])
    kvgP = din("mla_kvgP", [128, 4])
    w_q = din("mla_w_q_up", [512, 3072])
    w_kv = din("mla_w_kv_up", [512, 4096])
    w_out = din("mla_w_out", [D, D])
    cos_in = din("mla_cosT", [64, T])
    sin_in = din("mla_sinT", [64, T])
    lat_d = dscr("lat_d", [8, 128, T], BF16)
    kpe_d = dscr("kpe_d", [64, T], BF16)
    scale = 192 ** -0.5

    for hf in range(2):
      with Stage(S) as st:
        T0 = hf * HALF
        hT = st.sb([128, KC, HALF], BF16, "hT")
        for c in range(KC):
            S.dma('sp', hT[:, c, :], g.hT_d[c, :, T0:T0 + HALF], hT, writes=[(hT, c)])
        wl = WLoader(S, E, st, nb=128)
        wq = st.ring([128, KC, 512], BF16, 2, name="wq")
        pz = st.ring([128, 512], F32, 4, psum=True)
        pss = st.ring([128, 512], F32, 2, psum=True)
        zr = st.ring([128, 512], F32, 6, name="z")
        sqr = st.ring([128, 512], BF16, 2, name="sq")
        sdr = st.ring([128, 512], F32, 2, name="sd")
        ostr = st.ring([128, 4, 512], BF16, 2, name="ost")
        gt = st.sb([128, 8], F32, "gt")
        S.dma('sp', gt[:, 0:4], qgP[:, :], gt, writes=[gt])
        S.dma('sp', gt[:, 4:8], kvgP[:, :], gt, writes=[gt])
        for gi in range(2):
            w = wq.next()
            wl.load(w, lambda c0, c1, w=w: w[:, :, c0:c1], wview(w_in)[:, :, gi * 512:(gi + 1) * 512], KC, 512)
            for tq in range(4):
                tsl = slice(tq * 512, (tq + 1) * 512)
                gsl = slice(T0 + tq * 512, T0 + (tq + 1) * 512)
                zc = [zr.next() for _ in range(4)]
                psum_ss = pss.next()
                for fc in range(4):
                    p = pz.next()
                    for kc in range(KC):
                        E.mm(p[:], w[:, kc, fc * 128:(fc + 1) * 128], hT[:, kc, tsl], kc == 0, kc == KC - 1,
                             [w, (hT, kc)], [p])
                    sq = sqr.next()
                    E.act(sq[:], p[:], AF.Square, [p], [sq])
                    E.cp('dve', zc[fc][:], p[:], [p], [zc[fc]])
                    E.mm(psum_ss[:], g.ones_b[:], sq[:], fc == 0, fc == 3, [g.ones_b, sq], [psum_ss])
                sd = sdr.next()
                E.act(sd[:], psum_ss[:], AF.Sqrt, [psum_ss], [sd], scale=1.0 / 512, bias=1e-6)
                E.recip(sd[:], sd[:], [sd], [sd])
                ost = ostr.next()
                for fc in range(4):
                    E.stt(ost[:, fc, :], zc[fc][:], gt[:, gi * 4 + fc:gi * 4 + fc + 1], sd[:], ALU.mult, ALU.mult,
                          [zc[fc], gt, sd], [ost])
                S.dma('pool', fmv(lat_d)[:, gi * 4:gi * 4 + 4, gsl], ost[:], ost, reads=[ost])
        wk = st.sb([128, KC, 128], BF16, "wk")
        wv = wview(w_in)
        wl.load(wk, lambda c0, c1: wk[:, :, c0:c1], wv[:, :, 1024:1088], KC, 64)
        wl.load(wk, lambda c0, c1: wk[:, :, 64 + c0:64 + c1], wv[:, :, 1056:1088], KC, 32, scale=-1.0)
        wl.load(wk, lambda c0, c1: wk[:, :, 96 + c0:96 + c1], wv[:, :, 1024:1056], KC, 32)
        cosr = st.ring([64, 512], F32, 2, name="cos")
        sinr = st.ring([64, 512], F32, 2, name="sin")
        kstr = st.ring([64, 512], BF16, 2, name="kst")
        t1r = st.ring([64, 512], F32, 2, name="t1")
        t2r = st.ring([64, 512], F32, 2, name="t2")
        for tq in range(4):
            tsl = slice(tq * 512, (tq + 1) * 512)
            gsl = slice(T0 + tq * 512, T0 + (tq + 1) * 512)
            cos, sin = cosr.next(), sinr.next()
            S.dma('sp', cos[:], cos_in[:, gsl], cos, writes=[cos])
            S.dma('sp', sin[:], sin_in[:, gsl], sin, writes=[sin])
            pa, pb = pz.next(), pz.next()
            for kc in range(KC):
                E.mm(pa[0:64, :], wk[:, kc, 0:64], hT[:, kc, tsl], kc == 0, kc == KC - 1, [wk, (hT, kc)], [pa])
            for kc in range(KC):
                E.mm(pb[0:64, :], wk[:, kc, 64:128], hT[:, kc, tsl], kc == 0, kc == KC - 1, [wk, (hT, kc)], [pb])
            t1, t2, kst = t1r.next(), t2r.next(), kstr.next()
            E.tt('dve', t1[:], pa[0:64, :], cos[:], ALU.mult, [pa, cos], [t1])
            E.tt('dve', t2[:], pb[0:64, :], sin[:], ALU.mult, [pb, sin], [t2])
            E.tt('dve', kst[:], t1[:], t2[:], ALU.add, [t1, t2], [kst])
            S.dma('pool', kpe_d[:, gsl], kst[:], kst, reads=[kst])
        gstr = st.ring([128, 4, 512], BF16, 2, name="gst")
        for gq in range(4):
            w = wq.next()
            wl.load(w, lambda c0, c1, w=w: w[:, :, c0:c1], wv[:, :, 1088 + gq * 512:1088 + (gq + 1) * 512], KC, 512)
            for tq in range(4):
                tsl = slice(tq * 512, (tq + 1) * 512)
                gsl = slice(T0 + tq * 512, T0 + (tq + 1) * 512)
                gs = gstr.next()
                for fc in range(4):
                    p = pz.next()
                    for kc in range(KC):
                        E.mm(p[:], w[:, kc, fc * 128:(fc + 1) * 128], hT[:, kc, tsl], kc == 0, kc == KC - 1,
                             [w, (hT, kc)], [p])
                    E.act(gs[:, fc, :], p[:], AF.Silu, [p], [gs])
                S.dma('pool', fmv(g.gT_d)[:, gq * 4:gq * 4 + 4, gsl], gs[:], gs, reads=[gs])

    if g.stop < 3:
        return w_out
    with Stage(S) as st:
        qn = st.sb([128, 4, T], BF16, "qn")
        kvn = st.sb([128, 4, T], BF16, "kvn")
        for c in range(4):
            S.dma('sp', qn[:, c, :], lat_d[c], qn, writes=[qn])
            S.dma('sp', kvn[:, c, :], lat_d[4 + c], kvn, writes=[kvn])
        kaug = st.sb([66, T], BF16, "kaug")
        S.dma('sp', kaug[0:64, :], kpe_d[:, :], kaug, writes=[kaug])
        S.dma('sp', kaug[64:66, :], g.seg_k[:, :], kaug, writes=[kaug])
        qaugr = st.ring([66, T], BF16, 2, name="qaug")
        for qa in qaugr.tiles:
            S.dma('sp', qa[64:66, :], g.seg_q[:, :], qa, writes=[(qa, 'seg')])
        cosr = st.ring([64, 512], F32, 2, name="cos")
        sinr = st.ring([64, 512], F32, 2, name="sin")
        wl = WLoader(S, E, st, kc=4, nb=256)
        wqr = st.ring([128, 4, 256], BF16, 2, name="wqh")
        wkvr = st.ring([128, 4, 256], BF16, 2, name="wkvh")
        qnpr = st.ring([128, T], BF16, 2, name="qnp")
        knpr = st.ring([128, T], BF16, 2, name="knp")
        Vr = st.ring([128, NT, 128], BF16, 1, name="V")
        ps_s = st.ring([128, 512], F32, 3, psum=True)
        ps_o = st.ring([128, 512], F32, 1, psum=True)
        ps_m = st.ring([128, 512], F32, 1, psum=True)
        ps_x = st.ring([128, 512], F32, 3, psum=True)
        sqr = st.ring([128, 512], BF16, 4, name="sq")
        t1r = st.ring([64, 512], F32, 2, name="t1")
        t2r = st.ring([64, 512], F32, 2, name="t2")
        pTr = st.ring([128, 512], BF16, 8, name="pT")
        sgr = st.ring([128, 512], BF16, 4, name="sg")
        rsr = st.ring([128, 512], F32, 2, name="rs")
        ofr = st.ring([128, 512], F32, 2, name="of")
        obr = st.ring([128, 512], BF16, 2, name="ob")
        glr = st.ring([128, 512], BF16, 2, name="gl")
        statr = st.ring([128, 24], F32, 2, name="stat")
        wqv = w_q.rearrange("(c p) n -> p c n", p=128)
        wkvv = w_kv.rearrange("(c p) n -> p c n", p=128)
        for h in range(16):
            wqh, wkvh = wqr.next(), wkvr.next()
            wl.load(wqh, lambda c0, c1, t=wqh: t[:, :, c0:c1], wqv[:, :, 192 * h:192 * h + 192], 4, 192)
            wl.load(wqh, lambda c0, c1, t=wqh: t[:, :, 192 + c0:192 + c1], wqv[:, :, 192 * h + 160:192 * h + 192],
                    4, 32, scale=-1.0)
            wl.load(wqh, lambda c0, c1, t=wqh: t[:, :, 224 + c0:224 + c1], wqv[:, :, 192 * h + 128:192 * h + 160],
                    4, 32)
            wl.load(wkvh, lambda c0, c1, t=wkvh: t[:, :, c0:c1], wkvv[:, :, 256 * h:256 * h + 256], 4, 256)
            qnp, knp, V, qaug, stat = qnpr.next(), knpr.next(), Vr.next(), qaugr.next(), statr.next()
            for tq in range(8):
                tsl = slice(tq * 512, (tq + 1) * 512)
                p = ps_x.next()
                for kc in range(4):
                    E.mm(p[:], wqh[:, kc, 0:128], qn[:, kc, tsl], kc == 0, kc == 3, [wqh, qn], [p])
                E.act(qnp[:, tsl], p[:], AF.Copy, [p], [(qnp, tq)])
                sq1 = sqr.next()
                E.cp('dve', sq1[:], p[:], [p], [sq1])
                E.tt('dve', sq1[:], sq1[:], sq1[:], ALU.mult, [sq1], [sq1])
                pa, pb = ps_x.next(), ps_x.next()
                for kc in range(4):
                    E.mm(pa[0:64, :], wqh[:, kc, 128:192], qn[:, kc, tsl], kc == 0, kc == 3, [wqh, qn], [pa])
                for kc in range(4):
                    E.mm(pb[0:64, :], wqh[:, kc, 192:256], qn[:, kc, tsl], kc == 0, kc == 3, [wqh, qn], [pb])
                t1, t2 = t1r.next(), t2r.next()
                cos, sin = cosr.next(), sinr.next()
                S.dma('sp', cos[:], cos_in[:, tsl], cos, writes=[cos])
                S.dma('sp', sin[:], sin_in[:, tsl], sin, writes=[sin])
                E.tt('dve', t1[:], pa[0:64, :], cos[:], ALU.mult, [pa, cos], [t1])
                E.tt('dve', t2[:], pb[0:64, :], sin[:], ALU.mult, [pb, sin], [t2])
                E.tt('dve', qaug[0:64, tsl], t1[:], t2[:], ALU.add, [t1, t2], [(qaug, tq)])
                sq2 = sqr.next()
                E.tt('dve', sq2[0:64, :], qaug[0:64, tsl], qaug[0:64, tsl], ALU.mult, [(qaug, tq)], [sq2])
                pn = ps_x.next()
                E.mm(pn[:], g.ones_b[:], sq1[:], True, False, [g.ones_b, sq1], [pn])
                E.mm(pn[:], g.ones_b[0:64, :], sq2[0:64, :], False, True, [g.ones_b, sq2], [pn])
                E.red(stat[:, tq:tq + 1], pn[:], ALU.max, [pn], [(stat, 'q', tq)])
                p = ps_x.next()
                for kc in range(4):
                    E.mm(p[:], wkvh[:, kc, 0:128], kvn[:, kc, tsl], kc == 0, kc == 3, [wkvh, kvn], [p])
                E.act(knp[:, tsl], p[:], AF.Copy, [p], [(knp, tq)])
                sq3 = sqr.next()
                E.cp('dve', sq3[:], p[:], [p], [sq3])
                E.tt('dve', sq3[:], sq3[:], sq3[:], ALU.mult, [sq3], [sq3])
                sq4 = sqr.next()
                E.tt('dve', sq4[0:64, :], kaug[0:64, tsl], kaug[0:64, tsl], ALU.mult, [kaug], [sq4])
                pn = ps_x.next()
                E.mm(pn[:], g.ones_b[:], sq3[:], True, False, [g.ones_b, sq3], [pn])
                E.mm(pn[:], g.ones_b[0:64, :], sq4[0:64, :], False, True, [g.ones_b, sq4], [pn])
                E.red(stat[:, 8 + tq:9 + tq], pn[:], ALU.max, [pn], [(stat, 'k', tq)])
                p = ps_x.next()
                for j in range(4):
                    tt = tq * 4 + j
                    for kc in range(4):
                        E.mm(p[:, j * 128:(j + 1) * 128], kvn[:, kc, tt * 128:(tt + 1) * 128], wkvh[:, kc, 128:256],
                             kc == 0, kc == 3, [wkvh, kvn], [p])
                E.act(V[:, tq * 4:tq * 4 + 4, :], p[:].rearrange("p (j d) -> p j d", d=128), AF.Copy, [p], [(V, tq)])
            qk = [(stat, 'q', t) for t in range(8)]
            kk = [(stat, 'k', t) for t in range(8)]
            E.red(stat[:, 16:17], stat[:, 0:8], ALU.max, qk, [(stat, 16)])
            E.red(stat[:, 17:18], stat[:, 8:16], ALU.max, kk, [(stat, 17)])
            E.tt('dve', stat[:, 18:19], stat[:, 16:17], stat[:, 17:18], ALU.mult, [(stat, 16), (stat, 17)], [(stat, 18)])
            E.act(stat[:, 19:20], stat[:, 18:19], AF.Sqrt, [(stat, 18)], [(stat, 19)])
            E.ts('dve', stat[:, 20:21], stat[:, 19:20], -scale, None, ALU.mult, None, [(stat, 19)], [(stat, 20)])
            negB = stat[:, 20:21]
            qkeys = [(qnp, t) for t in range(8)] + [(qaug, t) for t in range(8)] + [(qaug, 'seg')]
            kkeys = [(knp, t) for t in range(8)] + [kaug]
            vkeys = [(V, t) for t in range(8)]
            units = [(tq, kb) for tq in range(8) for kb in range(NT)]

            def qk_mm(u):
                tq, kb = units[u]
                ps = ps_s.next()
                tsl = slice(tq * 512, (tq + 1) * 512)
                ksl = slice(kb * 128, (kb + 1) * 128)
                E.mm(ps[:], knp[:, ksl], qnp[:, tsl], True, False, [(knp, kb // 4), (qnp, tq)], [ps])
                E.mm(ps[:], kaug[0:66, ksl], qaug[0:66, tsl], False, True, [kaug, (qaug, tq), (qaug, 'seg')], [ps])
                return ps

            pend = [qk_mm(0), qk_mm(1)]
            po = psm = None
            for u, (tq, kb) in enumerate(units):
                ps = pend.pop(0)
                if u + 2 < len(units):
                    pend.append(qk_mm(u + 2))
                if kb == 0:
                    po, psm = ps_o.next(), ps_m.next()
                pT = pTr.next()
                E.act(pT[:], ps[:], AF.Exp, [ps, (stat, 20)], [pT], scale=scale, bias=negB)
                E.mm(po[:], V[:, kb, :], pT[:], kb == 0, kb == NT - 1, [(V, kb // 4), pT], [po])
                E.mm(psm[:], g.ones_b[:], pT[:], kb == 0, kb == NT - 1, [g.ones_b, pT], [psm])
                if kb == NT - 1:
                    tsl = slice(tq * 512, (tq + 1) * 512)
                    rs, of, ob, gl = rsr.next(), ofr.next(), obr.next(), glr.next()
                    S.dma('sp', gl[:], g.gT_d[h, :, tsl], gl, writes=[gl])
                    E.recip(rs[:], psm[:], [psm], [rs])
                    E.tt('dve', of[:], po[:], rs[:], ALU.mult, [po, rs], [of])
                    E.tt('dve', ob[:], of[:], gl[:], ALU.mult, [of, gl], [ob])
                    S.dma('pool', g.oT_d[h, :, tsl], ob[:], ob, reads=[ob])
    return w_out


def diff_mixer(g, l):
    S, E = g.S, g.E
    din, dscr = g.din, g.dscr
    w_in = din("diff_w_in", [D, 8192])
    lam_in = din("diff_lam", [1, 256])
    gsub_in = din("diff_gsubP", [128, 1])
    w_out = din("diff_w_out", [D, D])
    cos_in = din("diff_cos", [128, NT, 8])
    sin_in = din("diff_sin", [128, NT, 8])
    qk_d = dscr("dqk_d", [2, KC, 128, T], BF16)
    v_d = dscr("dv_d", [T, D], BF16)
    scale = 64 ** -0.5
    lam_init = 0.8 - 0.6 * math.exp(-0.3 * l)
    wv = wview(w_in)

    for hf in range(2):
      with Stage(S) as st:
        T0 = hf * HALF
        hT = st.sb([128, KC, HALF], BF16, "hT")
        for c in range(KC):
            S.dma('sp', hT[:, c, :], g.hT_d[c, :, T0:T0 + HALF], hT, writes=[(hT, c)])
        wl = WLoader(S, E, st, nb=128)
        wq = st.ring([128, KC, 512], BF16, 2, name="wq")
        pz = st.ring([128, 512], F32, 4, psum=True)
        ptr = st.ring([128, 8, 128], BF16, 2, psum=True)
        ctab = st.sb([128, NT, 8], F32, "ctab")
        stab = st.sb([128, NT, 8], F32, "stab")
        S.dma('sp', ctab[:], cos_in[:, :, :], ctab, writes=[ctab])
        S.dma('sp', stab[:], sin_in[:, :, :], stab, writes=[stab])
        qtmr = st.ring([128, 512], BF16, 3, name="qtm")
        rr = st.ring([128, 4, 8, 8], F32, 3, name="rope")
        qTr = st.ring([128, 4, HALF], BF16, 2, name="qTst")
        for sel in range(2):
            for cb in range(4):
                w = wq.next()
                col0 = sel * 2048 + cb * 512
                wl.load(w, lambda c0, c1, w=w: w[:, :, c0:c1], wv[:, :, col0:col0 + 512], KC, 512)
                qT = qTr.next()
                for tt in range(16):
                    gtt = hf * 16 + tt
                    p = pz.next()
                    for kc in range(KC):
                        E.mm(p[:], hT[:, kc, tt * 128:(tt + 1) * 128], w[:, kc, :], kc == 0, kc == KC - 1,
                             [w, (hT, kc)], [p])
                    qtm = qtmr.next()
                    E.act(qtm[:], p[:], AF.Copy, [p], [qtm])
                    p3 = p[:].rearrange("p (h d) -> p h d", d=64)
                    q3 = qtm[:].rearrange("p (h d) -> p h d", d=64)
                    cs = ctab[:, gtt, :].unsqueeze(1).to_broadcast([128, 8, 8])
                    sn = stab[:, gtt, :].unsqueeze(1).to_broadcast([128, 8, 8])
                    r = rr.next()
                    E.tt('dve', r[:, 0], p3[:, :, 0:8], cs, ALU.mult, [p, ctab], [(r, 0)])
                    E.tt('dve', r[:, 1], p3[:, :, 8:16], sn, ALU.mult, [p, stab], [(r, 1)])
                    E.tt('dve', r[:, 2], p3[:, :, 8:16], cs, ALU.mult, [p, ctab], [(r, 2)])
                    E.tt('dve', r[:, 3], p3[:, :, 0:8], sn, ALU.mult, [p, stab], [(r, 3)])
                    E.tt('pool', q3[:, :, 0:8], r[:, 0], r[:, 1], ALU.subtract, [(r, 0), (r, 1), qtm], [qtm])
                    E.tt('pool', q3[:, :, 8:16], r[:, 2], r[:, 3], ALU.add, [(r, 2), (r, 3), qtm], [qtm])
                    pt = ptr.next()
                    for j in range(4):
                        E.tr(pt[:, j, :], qtm[:, j * 128:(j + 1) * 128], g.ident[:], [qtm, g.ident], [pt])
                    E.cp('dve', qT[:, :, tt * 128:(tt + 1) * 128], pt[:, 0:4, :], [pt], [(qT, tt)])
                S.dma('pool', fmv(qk_d[sel])[:, cb * 4:cb * 4 + 4, T0:T0 + HALF], qT[:], qT,
                      reads=[(qT, t) for t in range(16)])
        vstr = st.ring([128, 512], BF16, 3, name="vst")
        for cb in range(4):
            w = wq.next()
            col0 = 4096 + cb * 512
            wl.load(w, lambda c0, c1, w=w: w[:, :, c0:c1], wv[:, :, col0:col0 + 512], KC, 512)
            for tt in range(16):
                p = pz.next()
                for kc in range(KC):
                    E.mm(p[:], hT[:, kc, tt * 128:(tt + 1) * 128], w[:, kc, :], kc == 0, kc == KC - 1,
                         [w, (hT, kc)], [p])
                vs = vstr.next()
                E.act(vs[:], p[:], AF.Copy, [p], [vs])
                S.dma('pool', v_d[T0 + tt * 128:T0 + (tt + 1) * 128, cb * 512:(cb + 1) * 512], vs[:], vs, reads=[vs])
        gstr = st.ring([128, 4, 512], BF16, 2, name="gst")
        for gq in range(4):
            w = wq.next()
            wl.load(w, lambda c0, c1, w=w: w[:, :, c0:c1], wv[:, :, 6144 + gq * 512:6144 + (gq + 1) * 512], KC, 512)
            for tq in range(4):
                tsl = slice(tq * 512, (tq + 1) * 512)
                gsl = slice(T0 + tq * 512, T0 + (tq + 1) * 512)
                gs = gstr.next()
                for fc in range(4):
                    p = pz.next()
                    for kc in range(KC):
                        E.mm(p[:], w[:, kc, fc * 128:(fc + 1) * 128], hT[:, kc, tsl], kc == 0, kc == KC - 1,
                             [w, (hT, kc)], [p])
                    E.act(gs[:, fc, :], p[:], AF.Silu, [p], [gs])
                S.dma('pool', fmv(g.gT_d)[:, gq * 4:gq * 4 + 4, gsl], gs[:], gs, reads=[gs])
    if g.stop < 3:
        return w_out

    with Stage(S) as st:
        lam = st.sb([128, 256], F32, "lam")
        S.dma('sp', lam[:], lam_in[0:1, :].partition_broadcast(128), lam, writes=[lam])
        lt = st.sb([128, 128], F32, "lt")
        ls = st.sb([128, 8], F32, "ls")
        E.tt('dve', lt[:, 0:64], lam[:, 0:64], lam[:, 64:128], ALU.mult, [lam], [lt])
        E.tt('dve', lt[:, 64:128], lam[:, 128:192], lam[:, 192:256], ALU.mult, [lam], [lt])
        E.red(ls[:, 0:1], lt[:, 0:64], ALU.add, [lt], [(ls, 0)])
        E.red(ls[:, 1:2], lt[:, 64:128], ALU.add, [lt], [(ls, 1)])
        E.act(ls[:, 2:4], ls[:, 0:2], AF.Exp, [(ls, 0), (ls, 1)], [(ls, 2)])
        E.tt('dve', ls[:, 4:5], ls[:, 3:4], ls[:, 2:3], ALU.subtract, [(ls, 2)], [(ls, 4)])
        E.ts('dve', ls[:, 5:6], ls[:, 4:5], -lam_init, None, ALU.add, None, [(ls, 4)], [(ls, 5)])
        neglam = ls[:, 5:6]
        gsub = st.sb([128, 2], F32, "gsub")
        S.dma('sp', gsub[:, 0:1], gsub_in[:, :], gsub, writes=[gsub])
        E.ts('dve', gsub[:, 1:2], gsub[:, 0:1], 1.0 - lam_init, None, ALU.mult, None, [gsub], [(gsub, 1)])
        qr = [st.ring([66, T], BF16, 2, name=f"q{c}") for c in range(2)]
        kr = [st.ring([66, T], BF16, 2, name=f"k{c}") for c in range(2)]
        for c in range(2):
            for t_ in qr[c].tiles:
                S.dma('sp', t_[64:66, :], g.seg_q[:, :], t_, writes=[(t_, 'seg')])
            for t_ in kr[c].tiles:
                S.dma('sp', t_[64:66, :], g.seg_k[:, :], t_, writes=[(t_, 'seg')])
        Vr = st.ring([128, NT, 128], BF16, 2, name="V")
        ps_s = st.ring([128, 512], F32, 3, psum=True)
        ps_o = [st.ring([128, 512], F32, 1, psum=True) for _ in range(2)]
        ps_m = [st.ring([128, 512], F32, 1, psum=True) for _ in range(2)]
        ps_x = st.ring([128, 512], F32, 1, psum=True)
        sqr = st.ring([128, 512], BF16, 3, name="sq")
        pTr = st.ring([128, 512], BF16, 14, name="pT")
        sgr = st.ring([128, 512], BF16, 6, name="sg")
        rsr = st.ring([128, 512], F32, 2, name="rs")
        ofr = st.ring([128, 512], F32, 4, name="of")
        obr = st.ring([128, 512], BF16, 2, name="ob")
        glr = st.ring([128, 512], BF16, 2, name="gl")
        statr = st.ring([128, 48], F32, 2, name="stat")
        vview = v_d.rearrange("(n p) f -> p n f", p=128)
        for h in range(16):
            q = [qr[c].next() for c in range(2)]
            k = [kr[c].next() for c in range(2)]
            V, stat = Vr.next(), statr.next()
            for c in range(2):
                S.dma('sp', q[c][0:64, :], qk_d[0, h, 64 * c:64 * c + 64, :], q[c], writes=[(q[c], 'd')])
                S.dma('sp', k[c][0:64, :], qk_d[1, h, 64 * c:64 * c + 64, :], k[c], writes=[(k[c], 'd')])
            for j in range(4):
                S.dma('sp', V[:, j * 8:j * 8 + 8, :], vview[:, j * 8:j * 8 + 8, h * 128:(h + 1) * 128], V, writes=[V])
            for ti, tl in enumerate((q[0], q[1], k[0], k[1])):
                for tq in range(8):
                    tsl = slice(tq * 512, (tq + 1) * 512)
                    sq = sqr.next()
                    E.tt('dve', sq[0:64, :], tl[0:64, tsl], tl[0:64, tsl], ALU.mult, [(tl, 'd')], [sq])
                    pn = ps_x.next()
                    E.mm(pn[:], g.ones_b[0:64, :], sq[0:64, :], True, True, [g.ones_b, sq], [pn])
                    E.red(stat[:, ti * 8 + tq:ti * 8 + tq + 1], pn[:], ALU.max, [pn], [(stat, ti, tq)])
                E.red(stat[:, 32 + ti:33 + ti], stat[:, ti * 8:ti * 8 + 8], ALU.max,
                      [(stat, ti, t) for t in range(8)], [(stat, 32 + ti)])
            negB = []
            for c in range(2):
                E.tt('dve', stat[:, 36 + c:37 + c], stat[:, 32 + c:33 + c], stat[:, 34 + c:35 + c], ALU.mult,
                     [(stat, 32 + c), (stat, 34 + c)], [(stat, 36 + c)])
                E.act(stat[:, 38 + c:39 + c], stat[:, 36 + c:37 + c], AF.Sqrt, [(stat, 36 + c)], [(stat, 38 + c)])
                E.ts('dve', stat[:, 40 + c:41 + c], stat[:, 38 + c:39 + c], -scale, None, ALU.mult, None,
                     [(stat, 38 + c)], [(stat, 40 + c)])
                negB.append(stat[:, 40 + c:41 + c])
            units = [(tq, kb, c) for tq in range(8) for kb in range(NT) for c in range(2)]

            def qk_mm(u):
                tq, kb, c = units[u]
                ps = ps_s.next()
                E.mm(ps[:], k[c][0:66, kb * 128:(kb + 1) * 128], q[c][0:66, tq * 512:(tq + 1) * 512], True, True,
                     [(k[c], 'd'), (k[c], 'seg'), (q[c], 'd'), (q[c], 'seg')], [ps])
                return ps

            pend = [qk_mm(0), qk_mm(1)]
            po = [None, None]
            psm = [None, None]
            grp = [[], []]
            for u, (tq, kb, c) in enumerate(units):
                ps = pend.pop(0)
                if u + 2 < len(units):
                    pend.append(qk_mm(u + 2))
                if kb == 0:
                    po[c], psm[c] = ps_o[c].next(), ps_m[c].next()
                pT = pTr.next()
                E.act(pT[:], ps[:], AF.Exp, [ps, (stat, 40 + c)], [pT], scale=scale, bias=negB[c])
                E.mm(po[c][:], V[:, kb, :], pT[:], kb == 0, kb == NT - 1, [V, pT], [po[c]])
                E.mm(psm[c][:], g.ones_b[:], pT[:], kb == 0, kb == NT - 1, [g.ones_b, pT], [psm[c]])
                if kb == NT - 1 and c == 1:
                    tsl = slice(tq * 512, (tq + 1) * 512)
                    gl = glr.next()
                    S.dma('sp', gl[:], g.gT_d[h, :, tsl], gl, writes=[gl])
                    o = []
                    for cc in range(2):
                        rs, of = rsr.next(), ofr.next()
                        E.recip(rs[:], psm[cc][:], [psm[cc]], [rs])
                        E.tt('dve', of[:], po[cc][:], rs[:], ALU.mult, [po[cc], rs], [of])
                        o.append(of)
                    od = ofr.next()
                    E.stt(od[:], o[1][:], neglam, o[0][:], ALU.mult, ALU.add, [o[0], o[1], (ls, 5)], [od])
                    sq = sqr.next()
                    E.tt('dve', sq[:], od[:], od[:], ALU.mult, [od], [sq])
                    pn = ps_x.next()
                    E.mm(pn[:], g.ones_b[:], sq[:], True, True, [g.ones_b, sq], [pn])
                    rs = rsr.next()
                    E.act(rs[:], pn[:], AF.Sqrt, [pn], [rs], scale=1.0 / 128, bias=1e-5)
                    E.recip(rs[:], rs[:], [rs], [rs])
                    on = ofr.next()
                    E.stt(on[:], od[:], gsub[:, 1:2], rs[:], ALU.mult, ALU.mult, [od, (gsub, 1), rs], [on])
                    ob = obr.next()
                    E.tt('dve', ob[:], on[:], gl[:], ALU.mult, [on, gl], [ob])
                    S.dma('pool', g.oT_d[h, :, tsl], ob[:], ob, reads=[ob])
    return w_out


def lru_mixer(g, l):
    S, E = g.S, g.E
    din, dscr = g.din, g.dscr
    w_in = din("lru_w_in", [D, 2 * D])
    w_out = din("lru_w_out", [D, D])
    conv_in = din("lru_convP", [128, KC, 5])
    gw_in = din("lru_gate_w", [128, 2, 2, KC, 128])
    gb_in = din("lru_gate_bP", [128, 2, 2, KC])
    lam_in = din("lru_lamP", [128, 2, KC])
    xb_d = dscr("xb_d", [KC, 128, T], F32)
    wv = wview(w_in)

    for hf in range(2):
      with Stage(S) as st:
        T0 = hf * HALF
        hT = st.sb([128, KC, HALF], BF16, "hT")
        for c in range(KC):
            S.dma('sp', hT[:, c, :], g.hT_d[c, :, T0:T0 + HALF], hT, writes=[(hT, c)])
        wl = WLoader(S, E, st, nb=128)
        wq = st.ring([128, KC, 512], BF16, 2, name="wq")
        pz = st.ring([128, 512], F32, 4, psum=True)
        xstr = st.ring([128, 4, 512], F32, 2, name="xst")
        gstr = st.ring([128, 4, 512], BF16, 2, name="gst")
        for part in range(2):
            for gq in range(4):
                w = wq.next()
                col0 = part * 2048 + gq * 512
                wl.load(w, lambda c0, c1, w=w: w[:, :, c0:c1], wv[:, :, col0:col0 + 512], KC, 512)
                for tq in range(4):
                    tsl = slice(tq * 512, (tq + 1) * 512)
                    gsl = slice(T0 + tq * 512, T0 + (tq + 1) * 512)
                    stt_ = (xstr if part == 0 else gstr).next()
                    for fc in range(4):
                        p = pz.next()
                        for kc in range(KC):
                            E.mm(p[:], w[:, kc, fc * 128:(fc + 1) * 128], hT[:, kc, tsl], kc == 0, kc == KC - 1,
                                 [w, (hT, kc)], [p])
                        E.act(stt_[:, fc, :], p[:], AF.Copy if part == 0 else AF.Silu, [p], [stt_])
                    dst = fmv(xb_d if part == 0 else g.gT_d)[:, gq * 4:gq * 4 + 4, gsl]
                    S.dma('pool', dst, stt_[:], stt_, reads=[stt_])
    if g.stop < 3:
        return w_out

    with Stage(S) as st:
        cvp = st.sb([128, KC, 5], F32, "cvp")
        gb = st.sb([128, 2, 2, KC], F32, "gb")
        lam = st.sb([128, 2, KC], F32, "lam")
        sc = st.sb([128, 2, KC], F32, "sc")
        S.dma('sp', cvp[:], conv_in[:, :, :], cvp, writes=[cvp])
        S.dma('sp', gb[:], gb_in[:, :, :, :], gb, writes=[gb])
        S.dma('sp', lam[:], lam_in[:, :, :], lam, writes=[lam])
        E.act(sc[:], lam[:], AF.Exp, [lam], [sc], scale=-1.0)
        E.act(sc[:], sc[:], AF.Ln, [sc], [sc], bias=1.0)
        E.ts('dve', sc[:], sc[:], -8.0, None, ALU.mult, None, [sc], [sc])
        cont = g.cont
        xbp = st.sb([128, 2, HALF + 3], F32, "xbp")
        xc = st.sb([128, 2, HALF], F32, "xc")
        xcb = st.sb([128, T], BF16, "xcb")
        gTr = st.ring([128, T], BF16, 2, name="gT")
        ar = st.ring([128, T], F32, 2, name="a")
        ir = st.ring([128, T], F32, 2, name="i")
        mr = st.ring([128, T], F32, 2, name="m")
        hr = st.ring([128, T], F32, 2, name="hs")
        obr = st.ring([128, T], BF16, 1, name="ob")
        gwsr = st.ring([128, 2, 128], F32, 2, name="gws")
        gwr = st.ring([128, 2, 128], BF16, 2, name="gw")
        inr = st.ring([128, 2], F32, 4, name="init")
        pg = st.ring([128, 512], F32, 4, psum=True)
        E.memset('dve', xbp[:, 0, 0:1], 0.0, [(xbp, 'z')])
        E.memset('dve', xbp[:, 1, HALF + 1:HALF + 3], 0.0, [(xbp, 'z')])
        for n in range(KC):
            for h in range(2):
                S.dma('sp', xbp[:, h, 1:HALF + 1], xb_d[n, :, h * HALF:(h + 1) * HALF], xbp, writes=[(xbp, h)])
            gT = gTr.next()
            S.dma('sp', gT[:], g.gT_d[n], gT, writes=[gT])
            E.ts('dve', xbp[:, 0, HALF + 1:HALF + 3], xbp[:, 1, 1:3], cont[:, 0:1], None, ALU.mult, None,
                 [(xbp, 1), cont], [(xbp, 'h0')])
            E.ts('dve', xbp[:, 1, 0:1], xbp[:, 0, HALF:HALF + 1], cont[:, 0:1], None, ALU.mult, None,
                 [(xbp, 0), cont], [(xbp, 'h1')])
            xk = [(xbp, 0), (xbp, 1), (xbp, 'h0'), (xbp, 'h1'), (xbp, 'z')]
            E.ts('dve', xc[:], xbp[:, :, 0:HALF], cvp[:, n, 0:1], cvp[:, n, 4:5], ALU.mult, ALU.add,
                 xk + [cvp], [xc])
            for j in range(1, 4):
                E.stt(xc[:], xbp[:, :, j:j + HALF], cvp[:, n, j:j + 1], xc[:], ALU.mult, ALU.add, xk + [cvp, xc], [xc])
            xcf = xc[:].rearrange("p h t -> p (h t)")
            E.act(xcb[:], xcf, AF.Copy, [xc], [xcb])
            hs2 = []
            for d in range(2):
                gws, gw = gwsr.next(), gwr.next()
                S.dma('sp', gws[:], gw_in[:, d, :, n, :], gws, writes=[gws])
                E.cp('pool', gw[:], gws[:], [gws], [gw])
                a, it, mt, hs = ar.next(), ir.next(), mr.next(), hr.next()
                for tq in range(8):
                    tsl = slice(tq * 512, (tq + 1) * 512)
                    p_r, p_i = pg.next(), pg.next()
                    E.mm(p_r[:], gw[:, 0, :], xcb[:, tsl], True, True, [gw, xcb], [p_r])
                    E.mm(p_i[:], gw[:, 1, :], xcb[:, tsl], True, True, [gw, xcb], [p_i])
                    E.act(a[:, tsl], p_r[:], AF.Sigmoid, [p_r, gb], [(a, tq)], bias=gb[:, d, 0, n:n + 1])
                    E.act(it[:, tsl], p_i[:], AF.Sigmoid, [p_i, gb], [(it, tq)], bias=gb[:, d, 1, n:n + 1])
                ak = [(a, t) for t in range(8)]
                ik = [(it, t) for t in range(8)]
                E.act(a[:], a[:], AF.Exp, ak + [sc], ak, scale=sc[:, d, n:n + 1])
                E.act(mt[:], a[:], AF.Square, ak, [mt])
                E.act(mt[:], mt[:], AF.Sqrt, [mt], [mt], scale=-1.0, bias=1.0)
                first, mid = (0, HALF) if d == 0 else (T - 1, HALF - 1)
                E.memset('dve', mt[:, first:first + 1], 1.0, [mt])
                E.ts('dve', mt[:, mid:mid + 1], mt[:, mid:mid + 1], cont[:, 0:1], cont[:, 1:2], ALU.mult, ALU.add,
                     [mt, cont], [mt])
                E.tt('dve', it[:], it[:], xcf, ALU.mult, ik + [xc], ik)
                E.tt('dve', it[:], it[:], mt[:], ALU.mult, ik + [mt], ik)
                ini = inr.next()
                if d == 0:
                    E.scan(hs[:, 0:HALF], a[:, 0:HALF], it[:, 0:HALF], 0.0, ak + ik, [(hs, 0)])
                    E.ts('dve', ini[:, 0:1], hs[:, HALF - 1:HALF], cont[:, 0:1], None, ALU.mult, None,
                         [(hs, 0), cont], [ini])
                    E.scan(hs[:, HALF:T], a[:, HALF:T], it[:, HALF:T], ini[:, 0:1], ak + ik + [ini], [(hs, 1)])
                else:
                    E.scan(hs[:, HALF:T][:, ::-1], a[:, HALF:T][:, ::-1], it[:, HALF:T][:, ::-1], 0.0,
                           ak + ik, [(hs, 1)])
                    E.ts('dve', ini[:, 0:1], hs[:, HALF:HALF + 1], cont[:, 0:1], None, ALU.mult, None,
                         [(hs, 1), cont], [ini])
                    E.scan(hs[:, 0:HALF][:, ::-1], a[:, 0:HALF][:, ::-1], it[:, 0:HALF][:, ::-1], ini[:, 0:1],
                           ak + ik + [ini], [(hs, 0)])
                hs2.append(hs)
            hk = [(hs2[0], 0), (hs2[0], 1), (hs2[1], 0), (hs2[1], 1)]
            E.tt('dve', hs2[0][:], hs2[0][:], hs2[1][:], ALU.add, hk, [(hs2[0], 0), (hs2[0], 1)])
            ob = obr.next()
            E.tt('dve', ob[:], hs2[0][:], gT[:], ALU.mult, [(hs2[0], 0), (hs2[0], 1), gT], [ob])
            S.dma('pool', g.oT_d[n], ob[:], ob, reads=[ob])
    return w_out


CH = 64
NCH = T // CH


def rwkv_mixer(g, l):
    S, E = g.S, g.E
    din, dscr = g.din, g.dscr
    mu_in = din("rwkv_muP", [128, 6, KC])
    w_in = din("rwkv_w_in", [4, D, D])
    w0_in = din("rwkv_w0P", [128, 2, KC])
    a0_in = din("rwkv_a0P", [128, 2, KC])
    w1_in = din("rwkv_w1", [2, D, 96])
    w2_in = din("rwkv_w2", [2, 96, D])
    a1_in = din("rwkv_a1", [2, D, 96])
    a2_in = din("rwkv_a2", [2, 96, D])
    kk_in = din("rwkv_kkP", [128, KC])
    ka_in = din("rwkv_kaP", [128, KC])
    rk_in = din("rwkv_rkP", [128, KC])
    lng_in = din("rwkv_lngS", [2, 1024])
    lnb_in = din("rwkv_lnbS", [2, 1024])
    w_out = din("rwkv_w_out_perm", [D, D])
    mask4_in = din("rw_mask4", [2, 128, 512], BF16)
    maskT_in = din("rw_maskT", [2, 128, 128], BF16)
    bdm_in = din("rw_bdmask", [128, 128], BF16)
    lvm_in = din("rw_lvmask", [4, 128, 128], BF16)
    bones_in = din("rw_bones", [128, 128], BF16)
    r_d = dscr("rw_r", [KC, 128, T], F32)
    k_d = dscr("rw_k", [KC, 128, T], F32)
    a_d = dscr("rw_a", [2, KC, 128, T], F32)
    lw_d = dscr("rw_lw", [2, KC, 128, T], F32)
    v_st = dscr("rw_v", [2, T, 1024], BF16)
    sg_st = dscr("rw_sg", [2, T, 1024], BF16)
    X_d = dscr("rw_X", [2, NCH, 128, 16 * 4 * CH], BF16)
    rkr_d = dscr("rw_rkr", [NCH, 128, 16 * CH], BF16)
    o_st = dscr("rw_of", [2, T, 1024], F32)
    of_st = dscr("rw_ofin", [2, T, 1024], BF16)
    wc_d = dscr("rw_wc", [16, 128, KC * 512], BF16)
    w1c_d = dscr("rw_w1c", [4, 128, KC * 96], BF16)
    w2c_d = dscr("rw_w2c", [4, 96, D], BF16)
    cont = g.cont
    pcg = S.sb(g.glob, [128, 2, NCH, 16], F32, "pcg")

    for b in range(8):
      with Stage(S) as st:
        T0 = b * 512
        hx = st.sb([128, KC, 514], BF16, "hx")
        lo, hi = max(T0 - 1, 0), min(T0 + 513, T)
        S.dma('sp', hx[:, :, lo - T0 + 1:hi - T0 + 1], fmv(g.hT_d)[:, :, lo:hi], hx, writes=[hx])
        if b == 0:
            E.memset('dve', hx[:, :, 0:1], 0.0, [hx])
        if b == 7:
            E.memset('dve', hx[:, :, 513:514], 0.0, [hx])
        if b == 4:
            E.ts('dve', hx[:, :, 0:1], hx[:, :, 0:1], cont[:, 0:1], None, ALU.mult, None, [hx, cont], [hx])
        if b == 3:
            E.ts('dve', hx[:, :, 513:514], hx[:, :, 513:514], cont[:, 0:1], None, ALU.mult, None, [hx, cont], [hx])
        mu = st.sb([128, 6, KC], F32, "mu")
        S.dma('sp', mu[:], mu_in[:, :, :], mu, writes=[mu])
        bias0 = st.sb([128, 2, 2, KC], F32, "bias0")
        S.dma('sp', bias0[:, 0], w0_in[:, :, :], bias0, writes=[bias0])
        S.dma('sp', bias0[:, 1], a0_in[:, :, :], bias0, writes=[bias0])
        xx = st.sb([128, KC, 512], F32, "xx")
        tmpr = st.ring([128, 512], F32, 2, name="tmp")
        for kc in range(KC):
            tm = tmpr.next()
            E.tt('dve', tm[:], hx[:, kc, 0:512], hx[:, kc, 2:514], ALU.add, [hx], [tm])
            E.stt(xx[:, kc, :], tm[:], 0.5, hx[:, kc, 1:513], ALU.mult, ALU.subtract, [tm, hx], [(xx, kc)])
        xsr = st.ring([128, KC, 512], BF16, 2, name="xs")
        wl = WLoader(S, E, st, nb=256)
        wq = st.ring([128, KC, 512], BF16, 2, name="wq")
        pz = st.ring([128, 512], F32, 4, psum=True)
        pl = st.ring([128, 512], F32, 2, psum=True)

        def cached(tile, flat, cache_ap, fill):
            if b == 0:
                fill()
                S.dma('pool', cache_ap, flat, tile, reads=[tile])
            else:
                S.dma('sp', flat, cache_ap, tile, writes=[tile])

        fstr = st.ring([128, 4, 512], F32, 2, name="fst")
        vstr = [st.sb([128, 2, 16, 64], BF16, f"vst{i}") for i in range(4)]
        for m in range(6):
            xs = xsr.next()
            for kc in range(KC):
                E.stt(xs[:, kc, :], xx[:, kc, :], mu[:, m, kc:kc + 1], hx[:, kc, 1:513], ALU.mult, ALU.add,
                      [(xx, kc), mu, hx], [(xs, kc)])
            xk = [(xs, kc) for kc in range(KC)]
            if m < 2:
                dst_d = r_d if m == 0 else k_d
                for gq in range(4):
                    w = wq.next()
                    cached(w, w[:].rearrange("p c n -> p (c n)"), wc_d[m * 4 + gq],
                           lambda w=w, gq=gq: wl.load(w, lambda c0, c1, w=w: w[:, :, c0:c1],
                                                      wview(w_in[m])[:, :, gq * 512:(gq + 1) * 512], KC, 512))
                    fs = fstr.next()
                    for fc in range(4):
                        p = pz.next()
                        for kc in range(KC):
                            E.mm(p[:], w[:, kc, fc * 128:(fc + 1) * 128], xs[:, kc, :], kc == 0, kc == KC - 1,
                                 [w, (xs, kc)], [p])
                        E.act(fs[:, fc, :], p[:], AF.Copy, [p], [(fs, fc)])
                    S.dma('pool', fmv(dst_d)[:, gq * 4:gq * 4 + 4, T0:T0 + 512], fs[:], fs,
                          reads=[(fs, f) for f in range(4)])
            elif m < 4:
                dst_d = v_st if m == 2 else sg_st
                for n4 in range(4):
                    w = wq.next()
                    cached(w, w[:].rearrange("p c n -> p (c n)"), wc_d[m * 4 + n4],
                           lambda w=w, n4=n4: wl.load(w, lambda c0, c1, w=w: w[:, :, c0:c1],
                                                      wview(w_in[m])[:, :, n4 * 512:(n4 + 1) * 512], KC, 512))
                    for tt in range(4):
                        p = pz.next()
                        for kc in range(KC):
                            E.mm(p[:], xs[:, kc, tt * 128:(tt + 1) * 128], w[:, kc, :], kc == 0, kc == KC - 1,
                                 [w, (xs, kc)], [p])
                        vs = vstr[tt]
                        E.act(vs[:, :, n4 * 4:n4 * 4 + 4, :].rearrange("p h q i -> p q h i"),
                              p[:].rearrange("p (q h i) -> p q h i", h=2, i=64),
                              AF.Copy if m == 2 else AF.Silu, [p], [(vs, n4)])
                for tt in range(4):
                    for hh in range(2):
                        S.dma('pool', dst_d[hh, T0 + tt * 128:T0 + (tt + 1) * 128, :],
                              vstr[tt][:, hh, :, :].rearrange("p q i -> p (q i)"), vstr[tt],
                              reads=[(vstr[tt], n) for n in range(4)])
            else:
                which = m - 4
                l1_in, l2_in = (w1_in, w2_in) if which == 0 else (a1_in, a2_in)
                dst_d = lw_d if which == 0 else a_d
                for dr in range(2):
                    w1b = st.sb([128, KC, 96], BF16, "w1b") if (m == 4 and dr == 0) else w1b
                    cached(w1b, w1b[:].rearrange("p c n -> p (c n)"), w1c_d[which * 2 + dr],
                           lambda dr=dr: wl.load(w1b, lambda c0, c1: w1b[:, :, c0:c1], wview(l1_in[dr]), KC, 96))
                    p1 = pl.next()
                    for kc in range(KC):
                        E.mm(p1[0:96, :], w1b[:, kc, :], xs[:, kc, :], kc == 0, kc == KC - 1, [w1b, (xs, kc)], [p1])
                    tw = st.sb([96, 512], BF16, "tw") if (m == 4 and dr == 0) else tw
                    E.act(tw[:], p1[0:96, :], AF.Tanh if which == 0 else AF.Copy, [p1], [tw])
                    w2s = st.sb([96, D], F32, "w2s") if (m == 4 and dr == 0) else w2s
                    w2b = st.sb([96, D], BF16, "w2b") if (m == 4 and dr == 0) else w2b
                    def fill2(dr=dr):
                        S.dma('sp', w2s[:], l2_in[dr], w2s, writes=[w2s])
                        E.cp('pool', w2b[:], w2s[:], [w2s], [w2b])
                    cached(w2b, w2b[:], w2c_d[which * 2 + dr], fill2)
                    for gq in range(4):
                        fs = fstr.next()
                        for fc in range(4):
                            oc = gq * 4 + fc
                            p = pz.next()
                            E.mm(p[:], w2b[0:96, oc * 128:(oc + 1) * 128], tw[0:96, :], True, True, [w2b, tw], [p])
                            E.act(fs[:, fc, :], p[:], AF.Sigmoid, [p, bias0], [(fs, fc)],
                                  bias=bias0[:, which, dr, oc:oc + 1])
                            if which == 0:
                                E.ts('pool', fs[:, fc, :], fs[:, fc, :], -math.exp(-0.5), None, ALU.mult, None,
                                     [(fs, fc)], [(fs, fc)])
                        S.dma('pool', fmv(dst_d[dr])[:, gq * 4:gq * 4 + 4, T0:T0 + 512], fs[:], fs,
                              reads=[(fs, f) for f in range(4)])
    if g.stop < 3:
        return w_out

    with Stage(S) as st:
        vecs = st.sb([128, 3, KC], F32, "vecs")
        S.dma('sp', vecs[:, 0], kk_in[:, :], vecs, writes=[vecs])
        S.dma('sp', vecs[:, 1], ka_in[:, :], vecs, writes=[vecs])
        S.dma('sp', vecs[:, 2], rk_in[:, :], vecs, writes=[vecs])
        bones = st.sb([128, 128], BF16, "bones")
        S.dma('sp', bones[:], bones_in[:, :], bones, writes=[bones])
        cm = st.sb([128, 2, 4, 256], F32, "cm")
        E.memset('dve', cm[:], 1.0, [cm])
        for q in range(4):
            E.memset('dve', cm[:, 0, q, 0:256:64], 0.0, [cm])
            E.memset('dve', cm[:, 1, q, 63:256:64], 0.0, [cm])
        Xst = [st.sb([128, 4, 16, 4, CH], BF16, f"Xst{d}") for d in range(2)]
        rkst = st.sb([128, 4, 16, CH], BF16, "rkst")
        inr = st.ring([128, 6, 4, 256], F32, 2, name="inp")
        pn = st.ring([128, 512], F32, 2, psum=True)
        R = lambda nm, dt=F32, n=1: st.ring([128, 4, 256], dt, n, name=nm)
        nkr = R("nkkn")
        kkr, sqr, nrr, kknr, cr, e1r, e2r, e3r, t1r, kdr, kdsr = (R("kk"), R("sq", BF16), R("nr"), R("kkn"), R("c"),
                                                                 R("e1"), R("e2"), R("e3"), R("t1", F32, 3), R("kd"),
                                                                 R("kds"))
        v5 = lambda ap: ap.rearrange("p q (c t) -> p c q t", t=CH)
        for blk in range(16):
            tsl = slice(blk * 256, (blk + 1) * 256)
            for pg in range(4):
                it_ = inr.next()
                srcs = [r_d, k_d, a_d[0], a_d[1], lw_d[0], lw_d[1]]
                for i_, sd_ in enumerate(srcs):
                    S.dma('sp', it_[:, i_], fmv(sd_)[:, pg * 4:pg * 4 + 4, tsl], it_, writes=[(it_, i_)])
                ps4 = slice(pg * 4, pg * 4 + 4)
                vb = lambda j: vecs[:, j, ps4].unsqueeze(2).to_broadcast([128, 4, 256])
                r_, k_ = it_[:, 0], it_[:, 1]
                kk, sq, nr, kkn, kds = kkr.next(), sqr.next(), nrr.next(), kknr.next(), kdsr.next()
                E.tt('dve', kk[:], k_, vb(0), ALU.mult, [(it_, 1), vecs], [kk])
                E.tt('pool', sq[:], kk[:], kk[:], ALU.mult, [kk], [sq])
                for hq in range(2):
                    p = pn.next()
                    E.mm(p[:], bones[:], sq[:, 2 * hq:2 * hq + 2, :].rearrange("p q t -> p (q t)"), True, True,
                         [bones, sq], [p])
                    E.act(nr[:, 2 * hq:2 * hq + 2, :].rearrange("p q t -> p (q t)"), p[:], AF.Sqrt, [p], [(nr, hq)])
                nk = [(nr, 0), (nr, 1)]
                E.ts('dve', nr[:], nr[:], 1e-12, None, ALU.max, None, nk, nk)
                E.recip(nr[:], nr[:], nk, nk)
                E.tt('dve', kkn[:], kk[:], nr[:], ALU.mult, [kk] + nk, [kkn])
                nkkn = nkr.next()
                E.ts('pool', nkkn[:], kkn[:], -1.0, None, ALU.mult, None, [kkn], [nkkn])
                for dr in range(2):
                    a_, lw_ = it_[:, 2 + dr], it_[:, 4 + dr]
                    c, e1, e2, e3, t1, kd = cr.next(), e1r.next(), e2r.next(), e3r.next(), t1r.next(), kdr.next()
                    fl = lambda ap: ap.rearrange("p q t -> p (q t)")
                    if dr == 0:
                        E.scan(fl(c[:]), fl(cm[:, 0]), fl(lw_), 0.0, [cm, (it_, 4)], [c])
                    else:
                        E.scan(fl(c[:])[:, ::-1], fl(cm[:, 1])[:, ::-1], fl(lw_)[:, ::-1], 0.0, [cm, (it_, 5)], [c])
                    E.act(e1[:], c[:], AF.Exp, [c], [e1])
                    E.tt('pool', t1[:], c[:], lw_, ALU.subtract, [c, (it_, 4 + dr)], [t1])
                    E.act(e2[:], t1[:], AF.Exp, [t1], [e2])
                    E.act(e3[:], c[:], AF.Exp, [c], [e3], scale=-1.0)
                    X = Xst[dr]
                    E.tt('dve', X[:, :, ps4, 1, :], v5(r_), v5(e1[:]), ALU.mult, [(it_, 0), e1], [(X, pg, 1)])
                    E.tt('dve', X[:, :, ps4, 0, :], v5(nkkn[:]), v5(e2[:]), ALU.mult, [nkkn, e2], [(X, pg, 0)])
                    t2 = t1r.next()
                    E.tt('dve', t2[:], kkn[:], a_, ALU.mult, [kkn, (it_, 2 + dr)], [t2])
                    E.tt('dve', X[:, :, ps4, 2, :], v5(t2[:]), v5(e3[:]), ALU.mult, [t2, e3], [(X, pg, 2)])
                    t3 = t1r.next()
                    E.ts('dve', t3[:], a_, -1.0, None, ALU.add, None, [(it_, 2 + dr)], [t3])
                    E.tt('dve', t3[:], t3[:], vb(1), ALU.mult, [t3, vecs], [t3])
                    E.stt(kd[:], t3[:], 1.0, k_, ALU.add, ALU.mult, [t3, (it_, 1)], [kd])
                    E.tt('dve', X[:, :, ps4, 3, :], v5(kd[:]), v5(e3[:]), ALU.mult, [kd, e3], [(X, pg, 3)])
                    col = 63 if dr == 0 else 0
                    E.cp('pool', pcg[:, dr, blk * 4:blk * 4 + 4, ps4], e1[:, :, col:256:64].rearrange("p q c -> p c q"),
                         [e1], [(pcg, dr, blk, pg)])
                    if dr == 0:
                        E.cp('pool', kds[:], kd[:], [kd], [kds])
                    else:
                        E.tt('pool', kds[:], kds[:], kd[:], ALU.add, [kds, kd], [kds])
                E.tt('dve', kds[:], kds[:], vb(2), ALU.mult, [kds, vecs], [kds])
                E.tt('dve', rkst[:, :, ps4, :], v5(kds[:]), v5(r_), ALU.mult, [kds, (it_, 0)], [(rkst, pg)])
            for dr in range(2):
                X = Xst[dr]
                S.dma('pool', X_d[dr, blk * 4:blk * 4 + 4].rearrange("c p x -> p c x"),
                      X[:].rearrange("p c q x t -> p c (q x t)"), X,
                      reads=[(X, pg_, x) for pg_ in range(4) for x in range(4)])
            S.dma('pool', rkr_d[blk * 4:blk * 4 + 4].rearrange("c p x -> p c x"),
                  rkst[:].rearrange("p c q t -> p c (q t)"), rkst, reads=[(rkst, pg_) for pg_ in range(4)])

    import os
    if os.environ.get("RW_STOP") == "C1":
        return w_out
    for dr in range(1 if os.environ.get("RW_STOP") == "C2f" else 2):
      with Stage(S) as st:
        mask4 = st.sb([128, 512], BF16, "mask4")
        maskT = st.sb([128, 128], BF16, "maskT")
        bdm = st.sb([128, 2, 64], BF16, "bdm")
        S.dma('sp', mask4[:], mask4_in[dr], mask4, writes=[mask4])
        S.dma('sp', maskT[:], maskT_in[dr], maskT, writes=[maskT])
        S.dma('sp', bdm[:].rearrange("p h t -> p (h t)"), bdm_in[:, :], bdm, writes=[bdm])
        S32 = st.sb([128, 16, 64], F32, "S32")
        Sb = st.sb([128, 16, 64], BF16, "Sb")
        E.memset('dve', S32[:], 0.0, [(S32, 0), (S32, 1)])
        E.memset('dve', Sb[:], 0.0, [(Sb, 0), (Sb, 1)])
        Xr = st.ring([128, 16, 4, CH], BF16, 2, name="X")
        Vr = st.ring([128, 16, 64], BF16, 3, name="V")
        Ostr = st.ring([128, 16, 64], F32, 2, name="Ost")
        tmpr = st.ring([128, 8, 64], F32, 2, name="stmp")
        two = lambda shape, nm, dt=BF16: [st.sb(shape, dt, nm)] * 2
        ARs, Bbs, Kbs = two([128, 16, 256], "AR"), two([128, 16, 128], "Bb"), two([128, 16, 128], "Kb")
        G1s = two([128, 16, 512], "G1")
        sq16 = lambda nm: st.sb([128, 16, 128], BF16, nm)
        NTt, Nd, NTd, No, NTo, Mt, MTt, Tt, TTt, Yt, Zt = [sq16(n_) for n_ in
                                                          ("NT", "Nd", "NTd", "No", "NTo", "M", "MT", "T", "TT", "Y", "Z")]
        lvm = st.sb([128, 4, 128], BF16, "lvm")
        for i_ in range(4):
            S.dma('sp', lvm[:, i_, :], lvm_in[i_], lvm, writes=[lvm])
        Tbs = two([128, 16, 128], "Tb")
        BKTs, XTs, UTs = two([128, 16, 2, 128], "BKT"), two([128, 16, 64], "XT"), two([128, 16, 64], "UT")
        P1r = st.ring([128, 512], F32, 2, psum=True)
        Qr = st.ring([128, 512], F32, 4, psum=True)
        Sr = st.ring([128, 512], F32, 2, psum=True)
        if dr == 1:
            lng = st.sb([128, 16, 64], F32, "lng")
            lnb = st.sb([128, 16, 64], F32, "lnb")
            for hh in range(2):
                S.dma('sp', lng[hh * 64:(hh + 1) * 64].rearrange("p q i -> p (q i)"),
                      lng_in[hh:hh + 1, :].partition_broadcast(64), lng, writes=[lng])
                S.dma('sp', lnb[hh * 64:(hh + 1) * 64].rearrange("p q i -> p (q i)"),
                      lnb_in[hh:hh + 1, :].partition_broadcast(64), lnb, writes=[lnb])
            Ofr = st.ring([128, 16, 64], F32, 1, name="Of")
            sgr = st.ring([128, 16, 64], BF16, 2, name="sg")
            rkrr = st.ring([128, 16, CH], BF16, 2, name="rkr")
            rkbd = st.ring([128, 16, 2, 64], BF16, 1, name="rkbd")
            o2r = st.ring([128, 16, 64], F32, 1, name="o2")
            o3r = st.ring([128, 16, 64], F32, 1, name="o3")
            ofr = st.ring([128, 16, 64], BF16, 2, name="ofin")
            stt_r = st.ring([128, 8, 16], F32, 2, name="gnst")
        order = list(range(NCH)) if dr == 0 else list(range(NCH - 1, -1, -1))
        bd4 = bdm[:].unsqueeze(1).to_broadcast([128, 16, 2, 64])
        G4 = [list(range(4 * g_, 4 * g_ + 4)) for g_ in range(4)]
        q4 = lambda bank: bank[:].rearrange("p (j t) -> p j t", t=128)
        q8 = lambda bank: bank[:].rearrange("p (j t) -> p j t", t=64)

        def phaseG(ci, c):
            par = ci % 2
            AR, Bb, Kb, G1, BKT, Tb = ARs[par], Bbs[par], Kbs[par], G1s[par], BKTs[par], Tbs[par]
            X, V = Xr.next(), Vr.next()
            S.dma('sp', X[:].rearrange("p q x t -> p (q x t)"), X_d[dr, c], X, writes=[X])
            for hh in range(2):
                S.dma('sp', V[hh * 64:(hh + 1) * 64].rearrange("p q i -> p (q i)"),
                      v_st[hh, c * CH:(c + 1) * CH, :], V, writes=[V])
            x4 = lambda x: X[:, :, x, :].unsqueeze(2).to_broadcast([128, 16, 2, 64])
            ARv = AR[:].rearrange("p q (x h t) -> p q x h t", x=2, h=2)
            E.tt('dve', ARv[:, :, 0], x4(0), bd4, ALU.mult, [X, bdm], [(AR, 0)])
            E.tt('pool', ARv[:, :, 1], x4(1), bd4, ALU.mult, [X, bdm], [(AR, 1)])
            E.tt('dve', Bb[:].rearrange("p q (h t) -> p q h t", h=2), x4(2), bd4, ALU.mult, [X, bdm], [Bb])
            E.tt('pool', Kb[:].rearrange("p q (h t) -> p q h t", h=2), x4(3), bd4, ALU.mult, [X, bdm], [Kb])
            for q in range(16):
                p1 = P1r.next()
                E.mm(p1[:, 0:256], Bb[:, q, :], AR[:, q, :], True, True, [Bb, (AR, 0), (AR, 1)], [p1])
                E.mm(p1[:, 256:512], Kb[:, q, :], AR[:, q, :], True, True, [Kb, (AR, 0), (AR, 1)], [p1])
                E.tt('dve', G1[:, q, :], p1[:], mask4[:], ALU.mult, [p1, mask4], [(G1, q // 4)])
            m4 = lambda i_: lvm[:, i_, :].unsqueeze(1).to_broadcast([128, 4, 128])
            idb = g.ident[:].unsqueeze(1).to_broadcast([128, 4, 128])
            def chain(g_):
                gs_ = slice(4 * g_, 4 * g_ + 4)
                K_ = lambda t_: (t_, g_)

                def mm4(lhs, rhs, rk):
                    bank = Qr.next()
                    for j, q in enumerate(G4[g_]):
                        E.mm(bank[:, j * 128:(j + 1) * 128], lhs[:, q, :], rhs[:, q, :], True, True, rk, [bank])
                    return bank

                qb = Qr.next()
                for j, q in enumerate(G4[g_]):
                    E.mm(qb[:, j * 128:(j + 1) * 128], AR[:, q, 0:128], Bb[:, q, :], True, True, [(AR, 0), Bb], [qb])
                E.tt('dve', NTt[:, gs_, :], q4(qb), maskT[:].unsqueeze(1).to_broadcast([128, 4, 128]),
                     ALU.mult, [qb, maskT], [K_(NTt)])
                yield
                qt = Qr.next()
                qtb = qt[:].bitcast(BF16)
                for j, q in enumerate(G4[g_]):
                    E.tr(qtb[:, (2 * j) * 128:(2 * j + 1) * 128], Bb[:, q, :], g.ident[:], [Bb, g.ident], [qt])
                    E.tr(qtb[:, (2 * j + 1) * 128:(2 * j + 2) * 128], Kb[:, q, :], g.ident[:], [Kb, g.ident], [qt])
                E.act(BKT[:, gs_, :, :], qtb.rearrange("p (j w t) -> p j w t", w=2, t=128), AF.Copy, [qt], [K_(BKT)])
                Nn = G1[:, gs_, 0:128]
                E.tt('dve', Nd[:, gs_, :], Nn, m4(0), ALU.mult, [K_(G1), lvm], [K_(Nd)])
                E.tt('dve', NTd[:, gs_, :], NTt[:, gs_, :], m4(0), ALU.mult, [K_(NTt), lvm], [K_(NTd)])
                E.tt('dve', Tt[:, gs_, :], Nd[:, gs_, :], idb, ALU.add, [K_(Nd), g.ident], [K_(Tt)])
                E.tt('dve', TTt[:, gs_, :], NTd[:, gs_, :], idb, ALU.add, [K_(NTd), g.ident], [K_(TTt)])
                yield
                Mc, MTc = Nd, NTd
                for lvl in range(2):
                    ba = mm4(MTc, Mc, [K_(MTc), K_(Mc)])
                    bb = mm4(Mc, MTc, [K_(MTc), K_(Mc)])
                    E.act(Mt[:, gs_, :], q4(ba), AF.Copy, [ba], [K_(Mt)])
                    E.act(MTt[:, gs_, :], q4(bb), AF.Copy, [bb], [K_(MTt)])
                    yield
                    Mc, MTc = Mt, MTt
                    bc_ = mm4(MTc, Tt, [K_(MTc), K_(Tt)])
                    bd_ = mm4(Mc, TTt, [K_(Mc), K_(TTt)])
                    E.tt('dve', Tt[:, gs_, :], q4(bc_), Tt[:, gs_, :], ALU.add, [bc_, K_(Tt)], [K_(Tt)])
                    E.tt('dve', TTt[:, gs_, :], q4(bd_), TTt[:, gs_, :], ALU.add, [bd_, K_(TTt)], [K_(TTt)])
                    yield
                for mi in range(1, 4):
                    lastm = mi == 3
                    E.tt('dve', NTo[:, gs_, :], NTt[:, gs_, :], m4(mi), ALU.mult, [K_(NTt), lvm], [K_(NTo)])
                    by = mm4(NTo, Tt, [K_(NTo), K_(Tt)])
                    E.act(Yt[:, gs_, :], q4(by), AF.Copy, [by], [K_(Yt)])
                    if not lastm:
                        E.tt('dve', No[:, gs_, :], Nn, m4(mi), ALU.mult, [K_(G1), lvm], [K_(No)])
                        bz = mm4(No, TTt, [K_(No), K_(TTt)])
                        E.act(Zt[:, gs_, :], q4(bz), AF.Copy, [bz], [K_(Zt)])
                    yield
                    bc_ = mm4(TTt, Yt, [K_(TTt), K_(Yt)])
                    if not lastm:
                        bd_ = mm4(Tt, Zt, [K_(Tt), K_(Zt)])
                        E.tt('dve', Tt[:, gs_, :], q4(bc_), Tt[:, gs_, :], ALU.add, [bc_, K_(Tt)], [K_(Tt)])
                        E.tt('dve', TTt[:, gs_, :], q4(bd_), TTt[:, gs_, :], ALU.add, [bd_, K_(TTt)], [K_(TTt)])
                    else:
                        E.tt('dve', Tb[:, gs_, :], q4(bc_), Tt[:, gs_, :], ALU.add, [bc_, K_(Tt)], [K_(Tb)])
                    yield

            alive = [chain(g_) for g_ in range(4)]
            while alive:
                for gen in list(alive):
                    try:
                        next(gen)
                    except StopIteration:
                        alive.remove(gen)
            return V

        def phaseS(ci, c, V):
            par = ci % 2
            AR, G1, Tm, BKT, XT, UT = ARs[par], G1s[par], Tbs[par], BKTs[par], XTs[par], UTs[par]
            Ost = Ostr.next()
            H8 = [list(range(8 * h_, 8 * h_ + 8)) for h_ in range(2)]
            sk = lambda t_, h_: [(t_, 2 * h_), (t_, 2 * h_ + 1)]
            for h_ in range(2):
                sb = Sr.next()
                for j, q in enumerate(H8[h_]):
                    sl = sb[:, j * 64:(j + 1) * 64]
                    E.mm(sl, AR[:, q, 0:128], Sb[:, q, :], True, False, [(AR, 0), (Sb, h_)], [sb])
                    E.mm(sl, G1[:, q, 256:384], V[:, q, :], False, True, sk(G1, h_) + [V], [sb])
                E.act(XT[:, 8 * h_:8 * h_ + 8, :], q8(sb), AF.Copy, [sb], [(XT, h_)])
            for h_ in range(2):
                sb = Sr.next()
                for j, q in enumerate(H8[h_]):
                    E.mm(sb[:, j * 64:(j + 1) * 64], Tm[:, q, :], XT[:, q, :], True, True, sk(Tm, h_) + [(XT, h_)], [sb])
                E.act(UT[:, 8 * h_:8 * h_ + 8, :], q8(sb), AF.Copy, [sb], [(UT, h_)])
            for h_ in range(2):
                sb = Sr.next()
                for j, q in enumerate(H8[h_]):
                    sl = sb[:, j * 64:(j + 1) * 64]
                    E.mm(sl, AR[:, q, 128:256], Sb[:, q, :], True, False, [(AR, 1), (Sb, h_)], [sb])
                    E.mm(sl, G1[:, q, 128:256], UT[:, q, :], False, False, sk(G1, h_) + [(UT, h_)], [sb])
                    E.mm(sl, G1[:, q, 384:512], V[:, q, :], False, True, sk(G1, h_) + [V], [sb])
                E.cp('dve', Ost[:, 8 * h_:8 * h_ + 8, :], q8(sb), [sb], [(Ost, h_)])
            for h_ in range(2):
                sb = Sr.next()
                for j, q in enumerate(H8[h_]):
                    sl = sb[:, j * 64:(j + 1) * 64]
                    E.mm(sl, BKT[:, q, 0, :], UT[:, q, :], True, False, sk(BKT, h_) + [(UT, h_)], [sb])
                    E.mm(sl, BKT[:, q, 1, :], V[:, q, :], False, True, sk(BKT, h_) + [V], [sb])
                pcb = pcg[:, dr, c, 8 * h_:8 * h_ + 8].unsqueeze(2).to_broadcast([128, 8, 64])
                pk = [(pcg, dr, c // 4, q // 4) for q in H8[h_]]
                S8 = S32[:, 8 * h_:8 * h_ + 8, :]
                tm = tmpr.next()
                E.tt('pool', S8, S8, pcb, ALU.mult, [(S32, h_)] + pk, [(S32, h_)])
                E.tt('dve', tm[:], q8(sb), pcb, ALU.mult, [sb] + pk, [tm])
                E.tt('pool', S8, S8, tm[:], ALU.add, [(S32, h_), tm], [(S32, h_)])
                E.act(Sb[:, 8 * h_:8 * h_ + 8, :], S8, AF.Copy, [(S32, h_)], [(Sb, h_)])
            return Ost

        def reset_state():
            allk = [(S32, 0), (S32, 1)]
            E.ts('pool', S32[:], S32[:], cont[:, 0:1], None, ALU.mult, None, allk + [cont], allk)
            E.act(Sb[:], S32[:], AF.Copy, allk, [(Sb, 0), (Sb, 1)])

        def finalize(c, Ost, V):
            ok = [(Ost, 0), (Ost, 1)]
            if dr == 0:
                for hh in range(2):
                    S.dma('pool', o_st[hh, c * CH:(c + 1) * CH, :],
                          Ost[hh * 64:(hh + 1) * 64].rearrange("p q i -> p (q i)"), Ost, reads=ok)
                return
            Of, sg, rk, o2, o3, ofin, gs, rb = (Ofr.next(), sgr.next(), rkrr.next(), o2r.next(), o3r.next(), ofr.next(),
                                                stt_r.next(), rkbd.next())
            for hh in range(2):
                S.dma('sp', Of[hh * 64:(hh + 1) * 64].rearrange("p q i -> p (q i)"),
                      o_st[hh, c * CH:(c + 1) * CH, :], Of, writes=[Of])
                S.dma('sp', sg[hh * 64:(hh + 1) * 64].rearrange("p q i -> p (q i)"),
                      sg_st[hh, c * CH:(c + 1) * CH, :], sg, writes=[sg])
            S.dma('sp', rk[:].rearrange("p q t -> p (q t)"), rkr_d[c], rk, writes=[rk])
            E.tt('pool', rb[:], rk[:].unsqueeze(2).to_broadcast([128, 16, 2, 64]), bd4, ALU.mult, [rk, bdm], [rb])
            sb = Sr.next()
            for q in range(16):
                E.mm(sb[:, q:q + 1], rb[:, q].rearrange("p h t -> p (h t)"), g.ones_b[:, 0:1], True, True,
                     [rb, g.ones_b], [sb])
            E.cp('dve', gs[:, 7, :], sb[:, 0:16], [sb], [(gs, 7)])
            E.tt('pool', o2[:], Ost[:], Of[:], ALU.add, ok + [Of], [o2])
            bc = lambda j: gs[:, j, :].unsqueeze(2).to_broadcast([128, 16, 64])
            E.red(gs[:, 0, :], o2[:], ALU.add, [o2], [(gs, 0)])
            E.ts('dve', gs[:, 2, :], gs[:, 0, :], 1.0 / 64, None, ALU.mult, None, [(gs, 0)], [(gs, 2)])
            E.tt('dve', o2[:], o2[:], bc(2), ALU.subtract, [o2, (gs, 2)], [o2])
            E.tt('pool', o3[:], o2[:], o2[:], ALU.mult, [o2], [o3])
            E.red(gs[:, 1, :], o3[:], ALU.add, [o3], [(gs, 1)])
            E.act(gs[:, 5, :], gs[:, 1, :], AF.Sqrt, [(gs, 1)], [(gs, 5)], scale=1.0 / 64, bias=64e-5)
            E.recip(gs[:, 6, :], gs[:, 5, :], [(gs, 5)], [(gs, 6)])
            E.tt('dve', o2[:], o2[:], bc(6), ALU.mult, [o2, (gs, 6)], [o2])
            E.tt('pool', o2[:], o2[:], lng[:], ALU.mult, [o2, lng], [o2])
            E.tt('pool', o2[:], o2[:], lnb[:], ALU.add, [o2, lnb], [o2])
            E.tt('dve', o3[:], V[:], bc(7), ALU.mult, [V, (gs, 7)], [o3])
            E.tt('pool', o2[:], o2[:], o3[:], ALU.add, [o2, o3], [o2])
            E.tt('dve', ofin[:], o2[:], sg[:], ALU.mult, [o2, sg], [ofin])
            for hh in range(2):
                S.dma('pool', of_st[hh, c * CH:(c + 1) * CH, :],
                      ofin[hh * 64:(hh + 1) * 64].rearrange("p q i -> p (q i)"), ofin, reads=[ofin])

        for ci, c in enumerate(order):
            V = phaseG(ci, c)
            if ci == NCH // 2:
                reset_state()
            Ost = phaseS(ci, c, V)
            finalize(c, Ost, V)

    with Stage(S) as st:
        inr = st.ring([128, 1024], BF16, 3, name="oin")
        ptr = st.ring([128, 8, 128], BF16, 2, psum=True)
        ostr = [st.ring([128, 8, 512], BF16, 2, name=f"ost{hh}") for hh in range(2)]
        for grp in range(8):
            os_ = [ostr[hh].next() for hh in range(2)]
            for sub in range(4):
                tt = grp * 4 + sub
                for hh in range(2):
                    it_ = inr.next()
                    S.dma('sp', it_[:], of_st[hh, tt * 128:(tt + 1) * 128, :], it_, writes=[it_])
                    pt = ptr.next()
                    for q in range(8):
                        E.tr(pt[:, q, :], it_[:, q * 128:(q + 1) * 128], g.ident[:], [it_, g.ident], [pt])
                    if hh == 0:
                        E.cp('dve', os_[hh][:, :, sub * 128:(sub + 1) * 128], pt[:], [pt], [(os_[hh], sub)])
                    else:
                        E.act(os_[hh][:, :, sub * 128:(sub + 1) * 128], pt[:], AF.Copy, [pt], [(os_[hh], sub)])
            for hh in range(2):
                S.dma('pool', fmv(g.oT_d)[:, hh * 8:hh * 8 + 8, grp * 512:(grp + 1) * 512], os_[hh][:], os_[hh],
                      reads=[(os_[hh], s_) for s_ in range(4)])
    return w_out


def core_inputs_more(n, inp, pos, cont):
    if n == 'diff_w_in':
        return inp['diff_w_in'][0]
    if n == 'diff_lam':
        return inp['diff_lambda'][0].reshape(1, 256)
    if n == 'diff_gsubP':
        return _P(inp['diff_subln_g'][0])
    if n == 'diff_w_out':
        return inp['diff_w_out'][0]
    if n in ('diff_cos', 'diff_sin'):
        cs, sn = _rope_tables(pos, 16, 500000.0)
        t = (cs if n == 'diff_cos' else sn)[:, 0:8]
        return np.ascontiguousarray(t.reshape(NT, 128, 8).transpose(1, 0, 2))
    if n == 'lru_w_in':
        return inp['lru_w_in'][0]
    if n == 'lru_w_out':
        return inp['lru_w_out'][0]
    if n == 'lru_convP':
        cw = inp['lru_conv_w'][0]
        cb = inp['lru_conv_b'][0]
        a = np.concatenate([cw, cb[None, :]], axis=0)
        return np.ascontiguousarray(a.reshape(5, KC, 128).transpose(2, 1, 0))
    if n == 'lru_gate_w':
        return np.ascontiguousarray(inp['lru_gate_w'][0].transpose(3, 0, 1, 2, 4))
    if n == 'lru_gate_bP':
        return np.ascontiguousarray(inp['lru_gate_b'][0].reshape(2, 2, KC, 128).transpose(3, 0, 1, 2))
    if n == 'lru_lamP':
        return np.ascontiguousarray(inp['lru_lambda'][0].reshape(2, KC, 128).transpose(2, 0, 1))
    return core_inputs_rwkv(n, inp, pos, cont)


def _tri_masks():
    p = np.arange(128)
    same = (p[:, None] // 64) == (p[None, :] // 64)
    s_, t_ = p[:, None] % 64, p[None, :] % 64
    out4, outT = [], []
    for dr in range(2):
        strict = same & ((s_ < t_) if dr == 0 else (s_ > t_))
        incl = same & ((s_ <= t_) if dr == 0 else (s_ >= t_))
        out4.append(np.concatenate([strict, incl, strict, incl], axis=1))
        outT.append(strict.T)
    return np.stack(out4).astype(np.float32), np.stack(outT).astype(np.float32), same.astype(np.float32)


def core_inputs_rwkv(n, inp, pos, cont):
    if n == 'rwkv_muP':
        return np.ascontiguousarray(inp['rwkv_mu'][0].reshape(6, KC, 128).transpose(2, 0, 1))
    if n == 'rwkv_w_in':
        return inp['rwkv_w_in'][0]
    if n in ('rwkv_w0P', 'rwkv_a0P'):
        k = 'rwkv_w0' if n == 'rwkv_w0P' else 'rwkv_a0'
        return np.ascontiguousarray(inp[k][0].reshape(2, KC, 128).transpose(2, 0, 1))
    if n in ('rwkv_w1', 'rwkv_w2', 'rwkv_a1', 'rwkv_a2'):
        return inp[n][0]
    if n in ('rwkv_kkP', 'rwkv_kaP', 'rwkv_rkP'):
        k = {'rwkv_kkP': 'rwkv_k_k', 'rwkv_kaP': 'rwkv_k_a', 'rwkv_rkP': 'rwkv_r_k'}[n]
        return _P(inp[k][0].reshape(-1))
    if n in ('rwkv_lngS', 'rwkv_lnbS'):
        v = inp['rwkv_ln_g' if n == 'rwkv_lngS' else 'rwkv_ln_b'][0]
        return np.ascontiguousarray(v.reshape(16, 2, 64).transpose(1, 0, 2).reshape(2, 1024))
    if n == 'rwkv_w_out_perm':
        cp, e, i = np.meshgrid(np.arange(16), np.arange(2), np.arange(64), indexing='ij')
        hh, q = cp // 8, cp % 8
        perm = ((2 * q + e) * 128 + hh * 64 + i).reshape(-1)
        return np.ascontiguousarray(inp['rwkv_w_out'][0][perm, :])
    if n == 'rw_mask4':
        return _bf(_tri_masks()[0])
    if n == 'rw_maskT':
        return _bf(_tri_masks()[1])
    if n == 'rw_lvmask':
        p = np.arange(128)
        blk = lambda b: ((p[:, None] // b) == (p[None, :] // b)).astype(np.float32)
        return _bf(np.stack([blk(8), blk(16) - blk(8), blk(32) - blk(16), blk(64) - blk(32)]))
    if n in ('rw_bdmask', 'rw_bones'):
        return _bf(_tri_masks()[2])
    raise KeyError(n)


def _P(v):
    return np.ascontiguousarray(np.asarray(v, np.float32).reshape(-1, 128).T)


def _bf(a):
    return np.asarray(a, np.float32).astype(ml_dtypes.bfloat16)


def _rope_tables(pos, dim, theta):
    inv = (1.0 / (np.float32(theta) ** (np.arange(0, dim, 2, dtype=np.float32) / np.float32(dim)))).astype(np.float32)
    ang = pos.astype(np.float32)[:, None] * inv[None, :]
    ang = np.concatenate([ang, ang], axis=-1).astype(np.float32)
    return np.cos(ang).astype(np.float32), np.sin(ang).astype(np.float32)


def core_inputs(i, inp, names):
    if i < 4:
        x = inp['x_sample'][i]
        c2 = np.stack([inp['c_sample'][i], inp['c_sample'][i]])
        cont = 1.0
        pos = np.arange(T)
    else:
        j = i - 4
        x = np.concatenate([inp['x_prompt'][2 * j], inp['x_prompt'][2 * j + 1]], axis=0)
        c2 = np.stack([inp['c_prompt'][2 * j], inp['c_prompt'][2 * j + 1]])
        cont = 0.0
        pos = np.arange(T) % HALF
    hq = (np.arange(T) >= HALF).astype(np.float32)
    d = {}
    for n in names:
        if n == 'x':
            v = np.ascontiguousarray(x, np.float32)
        elif n == 'cT':
            v = np.ascontiguousarray(c2.reshape(2, KC, 128).transpose(2, 1, 0), np.float32)
        elif n == 'cont':
            v = np.tile(np.array([[cont, 1.0 - cont]], np.float32), (128, 1))
        elif n == 'seg_q':
            v = _bf(np.stack([BIG * hq, BIG * (1 - hq)]) * (1.0 - cont))
        elif n == 'seg_k':
            v = _bf(np.stack([-(1 - hq), -hq]))
        elif n == 'ident':
            v = _bf(np.eye(128))
        elif n == 'identf':
            v = np.eye(128, dtype=np.float32)
        elif n == 'ada_w':
            v = inp['ada_w']
        elif n == 'ada_bP':
            v = np.ascontiguousarray(np.stack([_P(inp['ada_b'][l]) for l in range(4)], axis=1))
        elif n == 'pre_gP':
            v = np.ascontiguousarray(np.stack([_P(inp['norm_pre_g'][l]) for l in range(4)], axis=1))
        elif n == 'post_g':
            v = inp['norm_post_g']
        elif n == 'mla_w_in':
            v = inp['mla_w_in'][0]
        elif n == 'mla_qgP':
            v = _P(inp['mla_q_norm_g'][0])
        elif n == 'mla_kvgP':
            v = _P(inp['mla_kv_norm_g'][0])
        elif n == 'mla_w_q_up':
            v = inp['mla_w_q_up'][0]
        elif n == 'mla_w_kv_up':
            v = inp['mla_w_kv_up'][0]
        elif n == 'mla_w_out':
            v = inp['mla_w_out'][0]
        elif n in ('mla_cosT', 'mla_sinT'):
            cs, sn = _rope_tables(pos, 64, 10000.0)
            v = np.ascontiguousarray((cs if n == 'mla_cosT' else sn).T)
        else:
            v = core_inputs_more(n, inp, pos, cont)
        d[n] = np.ascontiguousarray(v)
    return d


_PROG = {}


def run_layers(inputs, NL=4):
    if NL not in _PROG:
        _PROG[NL] = build(NL)
    g = _PROG[NL]
    inp = {k: np.asarray(v) for k, v in inputs.items()}
    in_maps = [core_inputs(i, inp, g.in_names) for i in range(8)]
    res = run_bass_kernel_spmd(g.nc, in_maps, core_ids=list(range(8)))
    return [r["y"] for r in res.results]


def kernel(**inputs):
    ys = run_layers(inputs, 4)
    y_sample = np.stack([np.asarray(ys[i], np.float32) for i in range(4)])
    y_prompt = np.stack([np.asarray(ys[4 + j // 2], np.float32)[(j % 2) * HALF:(j % 2 + 1) * HALF] for j in range(8)])
    return (y_prompt, y_sample)
```
